# Optimizing a Trainium2 kernel written in Bass

```python
import math
import jax, jax.numpy as jnp
from jax import lax
import numpy as np

D_MODEL = 2048
BATCH = 1
SEQ = 8192
DEPTH = 4

N_A_LAYERS = DEPTH // 2
N_B_LAYERS = DEPTH - N_A_LAYERS

NSA_HEADS = 16
NSA_KV_HEADS = 4
NSA_GROUP = NSA_HEADS // NSA_KV_HEADS
NSA_HEAD_DIM = D_MODEL // NSA_HEADS
CMP_BLOCK = 32
CMP_STRIDE = 16
CMP_HIDDEN = 4 * NSA_HEAD_DIM
SEL_BLOCK = 64
SEL_TOPK = 16
WINDOW = 512
FORCE_SCORE = 1.0e4
NSA_IN_COLS = NSA_HEADS * NSA_HEAD_DIM + 6 * NSA_KV_HEADS * NSA_HEAD_DIM + 3 * NSA_HEADS

DIFF_HEADS = 16
DIFF_KV_HEADS = 4
DIFF_GROUP = DIFF_HEADS // DIFF_KV_HEADS
DIFF_HEAD_DIM = D_MODEL // (2 * DIFF_HEADS)
DIFF_KV_COLS = DIFF_KV_HEADS * 2 * DIFF_HEAD_DIM * 2

D_FF = 4 * D_MODEL

ROPE_THETA = 500000.0
ROPE_FRACTION = 4
Q_BLOCK = 128
NORM_EPS = 1e-6

kernel_name = "yoco_nsa_diffattn_hybrid"


def rms_norm(x, g):
    xf = x.astype(jnp.float32)
    y = xf * lax.rsqrt(jnp.mean(xf * xf, axis=-1, keepdims=True) + NORM_EPS)
    return (y * g.astype(jnp.float32)).astype(x.dtype)


def rope_tables(seq, head_dim):
    rot = head_dim // ROPE_FRACTION
    inv = 1.0 / (ROPE_THETA ** (jnp.arange(0, rot, 2, dtype=jnp.float32) / rot))
    ang = jnp.arange(seq, dtype=jnp.float32)[:, None] * inv[None, :]
    return jnp.cos(ang), jnp.sin(ang)


def apply_partial_rope(x, cos, sin):
    half = cos.shape[-1]
    x1, x2, xp = x[..., :half], x[..., half:2 * half], x[..., 2 * half:]
    c, s = cos.astype(x.dtype), sin.astype(x.dtype)
    return jnp.concatenate([x1 * c - x2 * s, x1 * s + x2 * c, xp], axis=-1)


def masked_softmax(s, valid):
    s = jnp.where(valid, s.astype(jnp.float32), -jnp.inf)
    m = jnp.max(s, axis=-1, keepdims=True)
    m = jnp.where(jnp.isfinite(m), m, 0.0)
    e = jnp.exp(s - m)
    den = jnp.sum(e, axis=-1, keepdims=True)
    return e / jnp.maximum(den, 1e-30)


def sq_relu_mlp(h, w_up, w_down):
    return jnp.square(jax.nn.relu(h @ w_up)) @ w_down


def nsa_mixer(h, w_in, cmp_pos, cmp_w1, cmp_w2, w_out, cos, sin):
    B, S, _ = h.shape
    H, Hk, G, dk = NSA_HEADS, NSA_KV_HEADS, NSA_GROUP, NSA_HEAD_DIM
    kvw = Hk * dk
    scale = dk ** -0.5
    proj = h @ w_in
    q = proj[..., :H * dk].reshape(B, S, Hk, G, dk).transpose(0, 2, 3, 1, 4)
    parts = [proj[..., H * dk + i * kvw:H * dk + (i + 1) * kvw]
             .reshape(B, S, Hk, dk).transpose(0, 2, 1, 3) for i in range(6)]
    k_c, v_c, k_s, v_s, k_w, v_w = parts
    gates = jax.nn.sigmoid(proj[..., H * dk + 6 * kvw:].astype(jnp.float32))
    gates = gates.reshape(B, S, Hk, G, 3).transpose(0, 2, 3, 1, 4)

    n_cmp = (S - CMP_BLOCK) // CMP_STRIDE + 1
    idx = jnp.arange(n_cmp)[:, None] * CMP_STRIDE + jnp.arange(CMP_BLOCK)[None, :]

    def compress(t, j):
        blk = (t[:, :, idx] + cmp_pos[j]).reshape(B, Hk, n_cmp, CMP_BLOCK * dk)
        return jax.nn.gelu(blk @ cmp_w1[j]) @ cmp_w2[j]

    kc, vc = compress(k_c, 0), compress(v_c, 1)
    cmp_end = idx[:, -1]

    n_sel = S // SEL_BLOCK
    topk = min(SEL_TOPK, n_sel)
    cs = jnp.arange(n_cmp) * CMP_STRIDE
    ss = jnp.arange(n_sel) * SEL_BLOCK
    overlap = ((cs[:, None] < ss[None, :] + SEL_BLOCK) &
               (cs[:, None] + CMP_BLOCK > ss[None, :])).astype(jnp.float32)

    q_rot = apply_partial_rope(q, cos, sin)
    ks_blocks = apply_partial_rope(k_s, cos, sin).reshape(B, Hk, n_sel, SEL_BLOCK, dk)
    vs_blocks = v_s.reshape(B, Hk, n_sel, SEL_BLOCK, dk)
    pad = ((0, 0), (0, 0), (WINDOW, 0), (0, 0))
    kw_pad = jnp.pad(apply_partial_rope(k_w, cos, sin), pad)
    vw_pad = jnp.pad(v_w, pad)
    gather = jax.vmap(jax.vmap(lambda blocks, ix: blocks[ix]))
    sel_ids = jnp.arange(n_sel)

    def block_fn(qb):
        start = qb * Q_BLOCK
        t = start + jnp.arange(Q_BLOCK)
        qr = lax.dynamic_slice_in_dim(q, start, Q_BLOCK, axis=3)
        qo = lax.dynamic_slice_in_dim(q_rot, start, Q_BLOCK, axis=3)
        g = lax.dynamic_slice_in_dim(gates, start, Q_BLOCK, axis=3)

        s_c = jnp.einsum('bhgqd,bhnd->bhgqn', qr, kc) * scale
        p_c = masked_softmax(s_c, cmp_end[None, :] <= t[:, None])
        o_c = jnp.einsum('bhgqn,bhnd->bhgqd', p_c.astype(vc.dtype), vc)

        imp = jnp.einsum('bhgqn,ns->bhqs', p_c, overlap)
        cur = t // SEL_BLOCK
        forced = ((sel_ids[None, :] == 0) | (sel_ids[None, :] == cur[:, None]) |
                  (sel_ids[None, :] == cur[:, None] - 1))
        imp = jnp.where(forced, FORCE_SCORE, imp)
        imp = jnp.where(ss[None, :] <= t[:, None], imp, -1.0)
        _, sel_idx = lax.top_k(imp, topk)
        k_g = gather(ks_blocks, sel_idx).reshape(B, Hk, Q_BLOCK, topk * SEL_BLOCK, dk)
        v_g = gather(vs_blocks, sel_idx).reshape(B, Hk, Q_BLOCK, topk * SEL_BLOCK, dk)
        key_pos = sel_idx[..., None] * SEL_BLOCK + jnp.arange(SEL_BLOCK)
        valid_s = (key_pos <= t[:, None, None]).reshape(B, Hk, 1, Q_BLOCK, topk * SEL_BLOCK)
        s_s = jnp.einsum('bhgqd,bhqkd->bhgqk', qo, k_g) * scale
        p_s = masked_softmax(s_s, valid_s)
        o_s = jnp.einsum('bhgqk,bhqkd->bhgqd', p_s.astype(v_g.dtype), v_g)

        kw = lax.dynamic_slice_in_dim(kw_pad, start, Q_BLOCK + WINDOW, axis=2)
        vw = lax.dynamic_slice_in_dim(vw_pad, start, Q_BLOCK + WINDOW, axis=2)
        kpos = start - WINDOW + jnp.arange(Q_BLOCK + WINDOW)
        valid_w = ((kpos[None, :] <= t[:, None]) & (kpos[None, :] > t[:, None] - WINDOW) &
                   (kpos[None, :] >= 0))
        s_w = jnp.einsum('bhgqd,bhkd->bhgqk', qo, kw) * scale
        p_w = masked_softmax(s_w, valid_w)
        o_w = jnp.einsum('bhgqk,bhkd->bhgqd', p_w.astype(vw.dtype), vw)

        gg = g.astype(o_c.dtype)
        return gg[..., 0:1] * o_c + gg[..., 1:2] * o_s + gg[..., 2:3] * o_w

    o = lax.map(block_fn, jnp.arange(S // Q_BLOCK))
    o = o.transpose(1, 0, 4, 2, 3, 5).reshape(B, S, H * dk)
    return o @ w_out


def shared_kv(x, g, w_kv, cos, sin):
    B, S, _ = x.shape
    Hk, d = DIFF_KV_HEADS, DIFF_HEAD_DIM
    kv = rms_norm(x, g) @ w_kv
    kcols = Hk * 2 * d
    k = kv[..., :kcols].reshape(B, S, Hk, 2, d).transpose(0, 2, 3, 1, 4)
    k = apply_partial_rope(k, cos, sin)
    v = kv[..., kcols:].reshape(B, S, Hk, 2 * d).transpose(0, 2, 1, 3)
    return k, v


def diff_mixer(h, w_q, lam_vecs, subln_g, w_out, k_sh, v_sh, cos, sin, lambda_init):
    B, S, _ = h.shape
    H, Hk, G, d = DIFF_HEADS, DIFF_KV_HEADS, DIFF_GROUP, DIFF_HEAD_DIM
    scale = d ** -0.5
    q = (h @ w_q).reshape(B, S, Hk, G, 2, d).transpose(0, 2, 3, 4, 1, 5)
    q = apply_partial_rope(q, cos, sin)
    lv = lam_vecs.astype(jnp.float32)
    lam = jnp.exp(jnp.sum(lv[0] * lv[1])) - jnp.exp(jnp.sum(lv[2] * lv[3])) + lambda_init
    kpos = jnp.arange(S)

    def block_fn(qb):
        start = qb * Q_BLOCK
        t = start + jnp.arange(Q_BLOCK)
        qblk = lax.dynamic_slice_in_dim(q, start, Q_BLOCK, axis=4)
        s = jnp.einsum('bhgcqd,bhckd->bhgcqk', qblk, k_sh) * scale
        p = masked_softmax(s, kpos[None, :] <= t[:, None])
        a = p[:, :, :, 0] - lam * p[:, :, :, 1]
        return jnp.einsum('bhgqk,bhkd->bhgqd', a.astype(v_sh.dtype), v_sh)

    o = lax.map(block_fn, jnp.arange(S // Q_BLOCK))
    o = o.transpose(1, 0, 4, 2, 3, 5).reshape(B, S, H, 2 * d)
    o = rms_norm(o, subln_g) * (1.0 - lambda_init)
    return o.reshape(B, S, H * 2 * d) @ w_out


def setup_inputs(seed: int = 0) -> dict:
    key = jax.random.key(seed)
    ks = jax.random.split(key, 20)
    f32 = jnp.float32
    nrm = lambda k, shape, s: jax.random.normal(k, shape, f32) * s
    dk, d = NSA_HEAD_DIM, DIFF_HEAD_DIM
    return {
        "x": nrm(ks[0], (BATCH, SEQ, D_MODEL), 1.0),
        "attn_norm_g": 1.0 + nrm(ks[1], (DEPTH, D_MODEL), 0.02),
        "mlp_norm_g": 1.0 + nrm(ks[2], (DEPTH, D_MODEL), 0.02),
        "final_norm_g": 1.0 + nrm(ks[3], (D_MODEL,), 0.02),
        "nsa_w_in": nrm(ks[4], (N_A_LAYERS, D_MODEL, NSA_IN_COLS), D_MODEL ** -0.5),
        "nsa_cmp_pos": nrm(ks[5], (N_A_LAYERS, 2, CMP_BLOCK, dk), 0.1),
        "nsa_cmp_w1": nrm(ks[6], (N_A_LAYERS, 2, CMP_BLOCK * dk, CMP_HIDDEN), (CMP_BLOCK * dk) ** -0.5),
        "nsa_cmp_w2": nrm(ks[7], (N_A_LAYERS, 2, CMP_HIDDEN, dk), CMP_HIDDEN ** -0.5),
        "nsa_w_out": nrm(ks[8], (N_A_LAYERS, D_MODEL, D_MODEL), D_MODEL ** -0.5),
        "kv_norm_g": 1.0 + nrm(ks[9], (D_MODEL,), 0.02),
        "kv_w_shared": nrm(ks[10], (D_MODEL, DIFF_KV_COLS), D_MODEL ** -0.5),
        "diff_w_q": nrm(ks[11], (N_B_LAYERS, D_MODEL, 2 * DIFF_HEADS * d), D_MODEL ** -0.5),
        "diff_lambda": nrm(ks[12], (N_B_LAYERS, 4, d), 0.1),
        "diff_subln_g": 1.0 + nrm(ks[13], (N_B_LAYERS, 2 * d), 0.02),
        "diff_w_out": nrm(ks[14], (N_B_LAYERS, DIFF_HEADS * 2 * d, D_MODEL), D_MODEL ** -0.5),
        "mlp_w_up": nrm(ks[15], (DEPTH, D_MODEL, D_FF), D_MODEL ** -0.5),
        "mlp_w_down": nrm(ks[16], (DEPTH, D_FF, D_MODEL), D_FF ** -0.5),
    }


def reference(x, attn_norm_g, mlp_norm_g, final_norm_g, nsa_w_in, nsa_cmp_pos, nsa_cmp_w1,
              nsa_cmp_w2, nsa_w_out, kv_norm_g, kv_w_shared, diff_w_q, diff_lambda,
              diff_subln_g, diff_w_out, mlp_w_up, mlp_w_down):
    S = x.shape[1]
    cos_a, sin_a = rope_tables(S, NSA_HEAD_DIM)
    cos_b, sin_b = rope_tables(S, DIFF_HEAD_DIM)
    k_sh, v_sh = None, None
    for layer in range(DEPTH):
        h = rms_norm(x, attn_norm_g[layer])
        if layer < N_A_LAYERS:
            x = x + nsa_mixer(h, nsa_w_in[layer], nsa_cmp_pos[layer], nsa_cmp_w1[layer],
                              nsa_cmp_w2[layer], nsa_w_out[layer], cos_a, sin_a)
        else:
            j = layer - N_A_LAYERS
            if j == 0:
                k_sh, v_sh = shared_kv(x, kv_norm_g, kv_w_shared, cos_b, sin_b)
            lambda_init = 0.8 - 0.6 * math.exp(-0.3 * layer)
            x = x + diff_mixer(h, diff_w_q[j], diff_lambda[j], diff_subln_g[j], diff_w_out[j],
                               k_sh, v_sh, cos_b, sin_b, lambda_init)
        x = x + sq_relu_mlp(rms_norm(x, mlp_norm_g[layer]), mlp_w_up[layer], mlp_w_down[layer])
    return rms_norm(x, final_norm_g)
```

```python
import math
from contextlib import ExitStack

import numpy as np
import ml_dtypes
import concourse.bass as bass
import concourse.mybir as mybir
from concourse.bass_utils import run_bass_kernel_spmd

F32 = mybir.dt.float32
BF16 = mybir.dt.bfloat16
AF = mybir.ActivationFunctionType
ALU = mybir.AluOpType
AX = mybir.AxisListType
NPBF = ml_dtypes.bfloat16

NCORES = 8
S = 8192
D = 2048
TPC = S // NCORES
NTT = TPC // 128
EPS = 1e-6
NEG = -30000.0
ROPE_THETA = 500000.0

ENGS = ("pe", "act", "dve", "pool", "sp")


class Op:
    __slots__ = ("eng", "fn", "deps", "signalled", "sig", "dma_sem", "dma_val")


class Prog:
    def __init__(self, nc):
        self.nc = nc
        self.ops = []
        self.last_w = {}
        self.readers = {}
        self.dma_cum = {}
        self.rot = {}

    def rr(self, name, n):
        i = self.rot.get(name, 0)
        self.rot[name] = i + 1
        return i % n

    def op(self, eng, fn, reads=(), writes=(), dma_sem=None, ndma=1):
        o = Op()
        o.eng = eng
        o.fn = fn
        o.signalled = False
        o.sig = 0
        o.dma_sem = dma_sem
        o.dma_val = 0
        deps = set()
        for k in reads:
            w = self.last_w.get(k)
            if w is not None:
                deps.add(w)
        for k in writes:
            w = self.last_w.get(k)
            if w is not None:
                deps.add(w)
            for r in self.readers.get(k, ()):
                deps.add(r)
        o.deps = [d for d in deps
                  if not (d.eng == "pe" and eng == "pe" and d.dma_sem is None and dma_sem is None)]
        if dma_sem is not None:
            self.dma_cum[dma_sem] = self.dma_cum.get(dma_sem, 0) + 16 * ndma
            o.dma_val = self.dma_cum[dma_sem]
        for d in o.deps:
            if d.dma_sem is None:
                d.signalled = True
        for k in writes:
            self.last_w[k] = o
            self.readers[k] = []
        for k in reads:
            if k not in writes:
                self.readers.setdefault(k, []).append(o)
        self.ops.append(o)
        return o

    def emit(self, stack, final_waits=()):
        nc = self.nc
        cnt = {e: 0 for e in ENGS}
        for o in self.ops:
            if o.dma_sem is None and o.signalled:
                cnt[o.eng] += 1
                o.sig = cnt[o.eng]
        esem = {e: stack.enter_context(nc.semaphore("s_" + e)) for e in ENGS}
        dsem = {}
        for i, k in enumerate(self.dma_cum):
            dsem[k] = stack.enter_context(nc.semaphore("d%d" % i))
        block = stack.enter_context(nc.Block())
        per = {e: [o for o in self.ops if o.eng == e] for e in ENGS}
        dma_cum = self.dma_cum

        def run(e, eng):
            waited = {}
            for o in per[e]:
                need = {}
                for d in o.deps:
                    if d.dma_sem is not None:
                        key = ("d", d.dma_sem)
                        v = d.dma_val
                    else:
                        key = ("e", d.eng)
                        v = d.sig
                    if v > need.get(key, 0):
                        need[key] = v
                for key, v in need.items():
                    if v > waited.get(key, 0):
                        sem = dsem[key[1]] if key[0] == "d" else esem[key[1]]
                        eng.wait_ge(sem, v)
                        waited[key] = v
                r = o.fn(eng)
                if o.dma_sem is not None:
                    rs = r if isinstance(r, (list, tuple)) else [r]
                    for ins in rs:
                        ins.then_inc(dsem[o.dma_sem], 16)
                elif o.signalled:
                    r.then_inc(esem[e], 1)
            if e == "sp":
                for k in final_waits:
                    if k in dsem:
                        eng.wait_ge(dsem[k], dma_cum[k])

        @block.tensor
        def _(eng):
            run("pe", eng)

        @block.scalar
        def _(eng):
            run("act", eng)

        @block.vector
        def _(eng):
            run("dve", eng)

        @block.gpsimd
        def _(eng):
            run("pool", eng)

        @block.sync
        def _(eng):
            run("sp", eng)


def rope_consts(head_chunk, tok0, ntok):
    rot = head_chunk // 4
    half = rot // 2
    inv = 1.0 / (ROPE_THETA ** (np.arange(0, rot, 2, dtype=np.float32) / np.float32(rot)))
    inv = inv.astype(np.float32)
    pos = np.arange(tok0, tok0 + ntok, dtype=np.float32)
    ang = (pos[None, :] * inv[:, None]).astype(np.float32)
    cos = np.cos(ang).astype(np.float32)
    sin = np.sin(ang).astype(np.float32)
    C = np.ones((128, ntok), np.float32)
    Sn = np.zeros((128, ntok), np.float32)
    Rm = np.zeros((128, 128), np.float32)
    for base in range(0, 128, head_chunk):
        for j in range(half):
            C[base + j] = cos[j]
            C[base + half + j] = cos[j]
            Sn[base + j] = sin[j]
            Sn[base + half + j] = sin[j]
            Rm[base + j, base + half + j] = -1.0
            Rm[base + half + j, base + j] = 1.0
    return C, Sn, np.ascontiguousarray(Rm.T)


def build_dense(oproj, mlp, final, proj):
    nc = bass.Bass("TRN2", target_bir_lowering=False)
    T = TPC
    dr = {}

    def din(name, shape, dt=F32):
        dr[name] = nc.dram_tensor(name, list(shape), dt, kind="ExternalInput")
        return dr[name]

    def dout(name, shape, dt=F32):
        dr[name] = nc.dram_tensor(name, list(shape), dt, kind="ExternalOutput")
        return dr[name]

    x_d = din("x", [T, D])
    ident_d = din("ident", [128, 128])
    if oproj:
        oT_d = din("oT", [D, T], BF16)
        wo_d = din("w_o", [D, D])
    if mlp:
        gm_d = din("g_mlp", [128, 16])
        wu_d = din("w_up", [D, 4 * D])
        wd_d = din("w_down", [4 * D, D])
    if final:
        gf_d = din("g_final", [D])
        y_d = dout("y", [T, D])
    else:
        xo_d = dout("x_out", [T, D])
    if proj:
        ga_d = din("g_attn", [128, 16])
        cos_d = din("cosT", [128, T])
        sin_d = din("sinT", [128, T])
        rt_d = din("rotT", [128, 128])
    if proj == "nsa":
        win_d = din("w_in", [D, 5168])
        qT_d = dout("qT", [D, T], BF16)
        qrT_d = dout("qrT", [D, T], BF16)
        kcT_d = dout("kcT", [512, T], BF16)
        vcT_d = dout("vcT", [512, T], BF16)
        ksT_d = dout("ksT", [512, T], BF16)
        kwT_d = dout("kwT", [512, T], BF16)
        vs_d = dout("vs", [T, 512], BF16)
        vw_d = dout("vw", [T, 512], BF16)
        gT_d = dout("gT", [48, T])
    if proj in ("diff", "diffkv"):
        wq_d = din("w_q", [D, D])
        dqT_d = dout("dqT", [D, T], BF16)
    if proj == "diffkv":
        gk_d = din("g_kv", [128, 16])
        wkv_d = din("w_kv", [D, 1024])
        dkT_d = dout("dkT", [512, T], BF16)
        dv_d = dout("dv", [T, 512], BF16)

    with ExitStack() as st:
        sb = lambda name, shape, dt: st.enter_context(nc.sbuf_tensor(name, list(shape), dt))
        x = sb("x_sb", [128, NTT, D], F32)
        big1 = sb("big1", [128, 16, T], BF16)
        big2 = sb("big2", [128, 16, T], BF16)
        NWB = 3
        wb = sb("wb", [128, NWB, 16, 512], BF16)
        ident = sb("ident_sb", [128, 128], F32)
        xh = sb("xh", [128, 1, D], F32)
        ssq = sb("ssq", [128, 32], F32)
        rstd = sb("rstd", [128, 32], F32)
        gT = sb("gT_sb", [128, 4, 16], F32)
        NST = 4
        stg = sb("stg", [128, NST, 512], BF16)
        r32 = sb("r32", [128, 2, 512], F32)
        if proj:
            cosT = sb("cos_sb", [128, T], F32)
            sinT = sb("sin_sb", [128, T], F32)
            rotT = sb("rot_sb", [128, 128], F32)
            t1 = sb("t1", [128, 1, 512], F32)
            t2 = sb("t2", [128, 1, 512], F32)
            gst = sb("gst", [48, 1, 512], F32)
        if final:
            gfb = sb("gfb", [128, D], F32)
        psb = [st.enter_context(nc.psum_tensor("ps%d" % i, [128, 512], F32)) for i in range(8)]

        P = Prog(nc)
        out_sems = []
        B2 = [("big2", fc) for fc in range(16)]

        P.op("sp", lambda e: e.dma_start(out=x[:], in_=x_d.ap().rearrange("(t p) c -> p t c", p=128)),
             writes=[("x", t) for t in range(NTT)], dma_sem="ldx")
        P.op("act", lambda e: e.dma_start(out=ident[:], in_=ident_d.ap()), writes=["ident"], dma_sem="ldc")
        gains = {}

        def load_gain(name, d):
            gi = len(gains)
            gains[name] = gi
            P.op("act", lambda e: e.dma_start(out=gT[:, gi, :], in_=d.ap()),
                 writes=[("gain", gi)], dma_sem="ldg")

        if mlp:
            load_gain("mlp", gm_d)
        if proj:
            load_gain("attn", ga_d)
            P.op("act", lambda e: e.dma_start(out=cosT[:], in_=cos_d.ap()), writes=["cos"], dma_sem="ldc")
            P.op("act", lambda e: e.dma_start(out=sinT[:], in_=sin_d.ap()), writes=["sin"], dma_sem="ldc")
            P.op("act", lambda e: e.dma_start(out=rotT[:], in_=rt_d.ap()), writes=["rot"], dma_sem="ldc")
        if proj == "diffkv":
            load_gain("kv", gk_d)
        if final:
            P.op("act", lambda e: e.dma_start(out=gfb[:], in_=gf_d.ap().partition_broadcast(128)),
                 writes=["gfb"], dma_sem="ldc")
        if oproj:
            P.op("sp", lambda e: e.dma_start(out=big2[:], in_=oT_d.ap().rearrange("(k p) t -> p k t", p=128)),
                 writes=B2, dma_sem="ldo")

        def load_wblock(wd, r0, c0, ncols=512):
            i = P.rr("wb", NWB)
            src = wd.ap()[r0:r0 + D, c0:c0 + ncols].rearrange("(k p) c -> p k c", p=128)
            P.op("pool", lambda e: e.dma_start(out=wb[:, i, :, 0:ncols], in_=src),
                 writes=[("wb", i)], dma_sem=("wb", i))
            return i

        def next_ps():
            return P.rr("ps", 4)

        def mm_group(pb, pso, pairs, reads):
            def fn(e):
                n = len(pairs)
                ins = None
                for i, (l, r) in enumerate(pairs):
                    ins = e.matmul(pso, l, r, start=(i == 0), stop=(i == n - 1))
                return ins
            P.op("pe", fn, reads=reads, writes=[("ps", pb)])

        def resid_add(pb, tt, cc):
            xs = x[:, tt, cc * 512:(cc + 1) * 512]
            P.op("dve", lambda e: e.tensor_tensor(out=xs, in0=xs, in1=psb[pb][:], op=ALU.add),
                 reads=[("ps", pb), ("x", tt)], writes=[("x", tt)])

        if oproj:
            for cc in range(4):
                wi = load_wblock(wo_d, 0, cc * 512)
                for tt in range(NTT):
                    pb = next_ps()
                    mm_group(pb, psb[pb][:], [(big2[:, kc, tt * 128:(tt + 1) * 128], wb[:, wi, kc, :]) for kc in range(16)],
                             reads=B2 + [("wb", wi)])
                    resid_add(pb, tt, cc)

        nstat = [0]

        def make_hT(gname):
            import os
            HT = int(os.environ.get("HT", "9"))
            gi = gains[gname]
            for tt in range(NTT):
                _make_hT_tile(gi, tt, HT)

        def _make_hT_tile(gi, tt, HT):
            import os
            if True:
                si = nstat[0]
                nstat[0] += 1
                xi = P.rr("xh", 1)
                P.op("dve", lambda e, si=si: e.memset(ssq[:, si:si + 1], 0.0), writes=[("ssq", si)])
                P.op("act", lambda e, si=si, tt=tt: e.activation(out=xh[:, 0, :], in_=x[:, tt, :], func=AF.Square,
                                                                  accum_out=ssq[:, si:si + 1]),
                     reads=[("x", tt), ("ssq", si)], writes=[("xh", 0), ("ssq", si)])
                P.op("dve", lambda e, si=si: e.tensor_scalar(out=rstd[:, si:si + 1], in0=ssq[:, si:si + 1],
                                                             scalar1=1.0 / D, scalar2=EPS, op0=ALU.mult, op1=ALU.add),
                     reads=[("ssq", si)], writes=[("rstd", si)])
                if HT < 2:
                    return
                P.op("act", lambda e, si=si: e.activation(out=rstd[:, si:si + 1], in_=rstd[:, si:si + 1], func=AF.Sqrt),
                     reads=[("rstd", si)], writes=[("rstd", si)])
                P.op("dve", lambda e, si=si: e.reciprocal(out=rstd[:, si:si + 1], in_=rstd[:, si:si + 1]),
                     reads=[("rstd", si)], writes=[("rstd", si)])
                if HT < 3:
                    return
                P.op("act", lambda e, si=si, tt=tt, xi=xi: e.activation(out=xh[:, xi, :], in_=x[:, tt, :], func=AF.Identity,
                                                                       scale=rstd[:, si:si + 1]),
                     reads=[("x", tt), ("rstd", si)], writes=[("xh", xi)])
                if HT < 4:
                    return
                for q4 in range(4):
                    pb = 4 + P.rr("pst", 2)

                    def tfn(e, q4=q4, xi=xi, pb=pb):
                        ins = None
                        for j in range(4):
                            kc = q4 * 4 + j
                            ins = e.transpose(out=psb[pb][:, j * 128:(j + 1) * 128],
                                              in_=xh[:, xi, kc * 128:(kc + 1) * 128], identity=ident[:])
                        return ins
                    P.op("pe", tfn, reads=[("xh", xi), "ident"], writes=[("ps", pb)])
                    if HT < 5:
                        continue
                    for j in range(4):
                        kc = q4 * 4 + j
                        dst = big1[:, kc, tt * 128:(tt + 1) * 128]
                        src = psb[pb][:, j * 128:(j + 1) * 128]
                        EV = int(os.environ.get("EV", "2"))
                        if (q4 % 2 == 0 and EV == 2) or EV == 0:
                            P.op("dve", lambda e, dst=dst, src=src, kc=kc: e.tensor_scalar(
                                out=dst, in0=src, scalar1=gT[:, gi, kc:kc + 1], scalar2=None, op0=ALU.mult),
                                reads=[("ps", pb), ("gain", gi)], writes=[("big1", tt, q4 % 2)])
                        else:
                            P.op("act", lambda e, dst=dst, src=src, kc=kc: e.activation(
                                out=dst, in_=src, func=AF.Identity, scale=gT[:, gi, kc:kc + 1]),
                                reads=[("ps", pb), ("gain", gi)], writes=[("big1", tt, q4 % 2)])

        hT_reads = [("big1", t, u) for t in range(NTT) for u in range(2)]

        if mlp:
            make_hT("mlp")
            for qt in range(4):
                for blk in range(4):
                    wi = load_wblock(wu_d, 0, qt * 2048 + blk * 512)
                    for j in range(4):
                        fc = blk * 4 + j
                        for tg in range(2):
                            pb = next_ps()
                            mm_group(pb, psb[pb][:],
                                     [(wb[:, wi, kc, j * 128:(j + 1) * 128], big1[:, kc, tg * 512:(tg + 1) * 512])
                                      for kc in range(16)],
                                     reads=hT_reads + [("wb", wi)])
                            ri = P.rr("r32", 2)
                            P.op("act", lambda e, pb=pb, ri=ri: e.activation(out=r32[:, ri, :], in_=psb[pb][:], func=AF.Relu),
                                 reads=[("ps", pb)], writes=[("r32", ri)])
                            eng = "pool" if (fc + tg) % 2 == 0 else "dve"
                            P.op(eng, lambda e, ri=ri, fc=fc, tg=tg: e.tensor_tensor(
                                out=big2[:, fc, tg * 512:(tg + 1) * 512], in0=r32[:, ri, :], in1=r32[:, ri, :], op=ALU.mult),
                                reads=[("r32", ri)], writes=[("big2", fc)])
                for cc in range(4):
                    wi = load_wblock(wd_d, qt * 2048, cc * 512)
                    for tt in range(NTT):
                        pb = next_ps()
                        mm_group(pb, psb[pb][:],
                                 [(big2[:, fc, tt * 128:(tt + 1) * 128], wb[:, wi, fc, :]) for fc in range(16)],
                                 reads=B2 + [("wb", wi)])
                        resid_add(pb, tt, cc)

        if final:
            for tt in range(NTT):
                si = nstat[0]
                nstat[0] += 1
                yi = 0
                P.op("dve", lambda e, si=si: e.memset(ssq[:, si:si + 1], 0.0), writes=[("ssq", si)])
                P.op("act", lambda e, si=si, tt=tt: e.activation(out=xh[:, 0, :], in_=x[:, tt, :], func=AF.Square,
                                                                  accum_out=ssq[:, si:si + 1]),
                     reads=[("x", tt), ("ssq", si)], writes=[("xh", 0), ("ssq", si)])
                P.op("dve", lambda e, si=si: e.tensor_scalar(out=rstd[:, si:si + 1], in0=ssq[:, si:si + 1],
                                                             scalar1=1.0 / D, scalar2=EPS, op0=ALU.mult, op1=ALU.add),
                     reads=[("ssq", si)], writes=[("rstd", si)])
                P.op("act", lambda e, si=si: e.activation(out=rstd[:, si:si + 1], in_=rstd[:, si:si + 1], func=AF.Sqrt),
                     reads=[("rstd", si)], writes=[("rstd", si)])
                P.op("dve", lambda e, si=si: e.reciprocal(out=rstd[:, si:si + 1], in_=rstd[:, si:si + 1]),
                     reads=[("rstd", si)], writes=[("rstd", si)])
                P.op("dve", lambda e, si=si, tt=tt, yi=yi: e.scalar_tensor_tensor(
                    out=xh[:, 0, :], in0=x[:, tt, :], scalar=rstd[:, si:si + 1], in1=gfb[:], op0=ALU.mult, op1=ALU.mult),
                    reads=[("x", tt), ("rstd", si), "gfb"], writes=[("xh", 0)])
                P.op("sp", lambda e, tt=tt, yi=yi: e.dma_start(out=y_d.ap()[tt * 128:(tt + 1) * 128, :], in_=xh[:, 0, :]),
                     reads=[("xh", 0)], dma_sem="yst")
            out_sems += ["yst"]
        else:
            P.op("sp", lambda e: e.dma_start(out=xo_d.ap().rearrange("(t p) c -> p t c", p=128), in_=x[:]),
                 reads=[("x", t) for t in range(NTT)], dma_sem="stx")
            out_sems.append("stx")

        def stage_out(dst_ap, src_fn, eng, reads):
            si = P.rr("stg", NST)
            P.op(eng, lambda e: src_fn(e, stg[:, si, :]), reads=reads, writes=[("stg", si)])
            P.op("sp", lambda e: e.dma_start(out=dst_ap, in_=stg[:, si, :]), reads=[("stg", si)], dma_sem=("stg", si))

        def proj_fm(wi, j, mode, outs):
            for tg in range(2):
                pb = next_ps()
                mm_group(pb, psb[pb][:],
                         [(wb[:, wi, kc, j * 128:(j + 1) * 128], big1[:, kc, tg * 512:(tg + 1) * 512]) for kc in range(16)],
                         reads=hT_reads + [("wb", wi)])
                tsl = slice(tg * 512, (tg + 1) * 512)
                if "rope" in outs:
                    ri = P.rr("r32", 2)
                    P.op("act", lambda e, pb=pb, ri=ri: e.activation(out=r32[:, ri, :], in_=psb[pb][:], func=AF.Copy),
                         reads=[("ps", pb)], writes=[("r32", ri)])
                    if "plain" in outs:
                        dd, r0 = outs["plain"]
                        stage_out(dd.ap()[r0:r0 + 128, tsl],
                                  lambda e, o, ri=ri: e.tensor_copy(out=o, in_=r32[:, ri, :]), "pool", [("r32", ri)])
                    pr = 6 + P.rr("psr", 2)
                    P.op("pe", lambda e, pr=pr, ri=ri: e.matmul(psb[pr][:], rotT[:], r32[:, ri, :], start=True, stop=True),
                         reads=[("r32", ri), "rot"], writes=[("ps", pr)])
                    ti = P.rr("t12", 1)
                    P.op("dve", lambda e, ri=ri, ti=ti, tsl=tsl: e.tensor_tensor(out=t1[:, ti, :], in0=r32[:, ri, :],
                                                                                 in1=cosT[:, tsl], op=ALU.mult),
                         reads=[("r32", ri), "cos"], writes=[("t1", ti)])
                    P.op("dve", lambda e, pr=pr, ti=ti, tsl=tsl: e.tensor_tensor(out=t2[:, ti, :], in0=psb[pr][:],
                                                                                 in1=sinT[:, tsl], op=ALU.mult),
                         reads=[("ps", pr), "sin"], writes=[("t2", ti)])
                    dd, r0 = outs["rope"]
                    stage_out(dd.ap()[r0:r0 + 128, tsl],
                              lambda e, o, ti=ti: e.tensor_tensor(out=o, in0=t1[:, ti, :], in1=t2[:, ti, :], op=ALU.add),
                              "pool", [("t1", ti), ("t2", ti)])
                else:
                    dd, r0 = outs["plain"]
                    stage_out(dd.ap()[r0:r0 + 128, tsl],
                              lambda e, o, pb=pb: e.activation(out=o, in_=psb[pb][:], func=AF.Copy), "act", [("ps", pb)])

        def proj_tm(wi, dd, c0):
            for tt in range(NTT):
                pb = next_ps()
                mm_group(pb, psb[pb][:],
                         [(big1[:, kc, tt * 128:(tt + 1) * 128], wb[:, wi, kc, :]) for kc in range(16)],
                         reads=hT_reads + [("wb", wi)])
                stage_out(dd.ap()[tt * 128:(tt + 1) * 128, c0:c0 + 512],
                          lambda e, o, pb=pb: e.activation(out=o, in_=psb[pb][:], func=AF.Copy), "act", [("ps", pb)])

        import os
        DBG = int(os.environ.get("DBG", "9"))
        if proj == "nsa" and DBG >= 2:
            make_hT("attn")
        if proj == "nsa" and DBG >= 3:
            for b in range(4 if DBG >= 4 else 1):
                wi = load_wblock(win_d, 0, b * 512)
                for j in range(4):
                    r0 = (b * 4 + j) * 128
                    proj_fm(wi, j, "both", {"plain": (qT_d, r0), "rope": (qrT_d, r0)})
        if proj == "nsa" and DBG >= 5:
            specs = [(kcT_d, False), (vcT_d, False), (ksT_d, True), (None, "vs"), (kwT_d, True), (None, "vw")]
            for pi, (dd, mode) in enumerate(specs):
                wi = load_wblock(win_d, 0, 2048 + pi * 512)
                if dd is None:
                    proj_tm(wi, vs_d if mode == "vs" else vw_d, 0)
                else:
                    for j in range(4):
                        proj_fm(wi, j, "x", {"rope": (dd, j * 128)} if mode else {"plain": (dd, j * 128)})
            wi = load_wblock(win_d, 0, 2048 + 6 * 512, 48)
            for tg in range(2):
                pb = next_ps()
                mm_group(pb, psb[pb][0:48, :],
                         [(wb[:, wi, kc, 0:48], big1[:, kc, tg * 512:(tg + 1) * 512]) for kc in range(16)],
                         reads=hT_reads + [("wb", wi)])
                gi2 = P.rr("gst", 1)
                P.op("act", lambda e, pb=pb, gi2=gi2: e.activation(out=gst[:, gi2, :], in_=psb[pb][0:48, :], func=AF.Sigmoid),
                     reads=[("ps", pb)], writes=[("gst", gi2)])
                P.op("sp", lambda e, gi2=gi2, tg=tg: e.dma_start(out=gT_d.ap()[:, tg * 512:(tg + 1) * 512], in_=gst[:, gi2, :]),
                     reads=[("gst", gi2)], dma_sem=("gst", gi2))
            out_sems += [("gst", 0)]
        if proj in ("diff", "diffkv"):
            make_hT("attn")
            for b in range(4):
                wi = load_wblock(wq_d, 0, b * 512)
                for j in range(4):
                    proj_fm(wi, j, "x", {"rope": (dqT_d, (b * 4 + j) * 128)})
        if proj == "diffkv":
            make_hT("kv")
            wi = load_wblock(wkv_d, 0, 0)
            for j in range(4):
                proj_fm(wi, j, "x", {"rope": (dkT_d, j * 128)})
            wi = load_wblock(wkv_d, 0, 512)
            proj_tm(wi, dv_d, 0)
        if proj:
            out_sems += [("stg", i) for i in range(NST)]

        P.emit(st, final_waits=out_sems)
    return nc


_DENSE_CACHE = {}


def gain_fm(g):
    return np.ascontiguousarray(np.asarray(g, np.float32).reshape(16, 128).T)


def get_dense(oproj, mlp, final, proj):
    key = (oproj, mlp, final, proj)
    if key not in _DENSE_CACHE:
        _DENSE_CACHE[key] = build_dense(*key)
    return _DENSE_CACHE[key]


NSLOT = 32
BIGV = 10000.0


def slot_qb(i, half):
    m = i // 2
    if i % 2 == 0:
        return 4 * m + (0 if half == 0 else 1), 4 * m + 1
    return 4 * m + (3 if half == 0 else 2), 4 * m + 3


def build_nsa_attn():
    nc = bass.Bass("TRN2", target_bir_lowering=False)
    din = lambda name, shape, dt=F32: nc.dram_tensor(name, list(shape), dt, kind="ExternalInput")
    kcT_d = din("kcT", [128, S], BF16)
    vcT_d = din("vcT", [128, S], BF16)
    ksT_d = din("ksT", [128, S], BF16)
    kwT_d = din("kwT", [128, S], BF16)
    vs_d = din("vs", [S, 128], BF16)
    vw_d = din("vw", [S, 128], BF16)
    qT_d = din("qT", [128, NSLOT, 512], BF16)
    qrT_d = din("qrT", [128, NSLOT, 512], BF16)
    g3_d = din("g3", [3, NSLOT, 512])
    w1_d = din("w1", [2, 4096, 512])
    w2_d = din("w2", [2, 512, 128])
    posT_d = din("posT", [2, 128, 32])
    identf_d = din("identf", [128, 128])
    i4_d = din("i4", [128, 512], BF16)
    ones_d = din("ones", [128, 128], BF16)
    acon_d = din("acon4", [128, 16, 128], BF16)
    ov_d = din("ov", [128, 4, 128], BF16)
    tailm_d = din("tailm", [128, 4, 128], BF16)
    winm_d = din("winm", [128, 8, 128], BF16)
    cmask_d = din("cmask", [128, NSLOT, 128], BF16)
    slotc_d = din("slotc", [128, NSLOT, 256])
    sel3_d = din("sel3", [3, 3, 128])
    oT_d = nc.dram_tensor("oT", [128, NSLOT, 512], BF16, kind="ExternalOutput")
    import os
    DBGA = int(os.environ.get("DBGA", "0"))
    if DBGA:
        dbg_d = nc.dram_tensor("dbg", [128, 2, 8, 512], F32, kind="ExternalOutput")
    scale = 128.0 ** -0.5

    with ExitStack() as st:
        sb = lambda name, shape, dt: st.enter_context(nc.sbuf_tensor(name, list(shape), dt))
        ksT = sb("ksT_sb", [128, S], BF16)
        kwT = sb("kwT_sb", [128, S], BF16)
        vs = sb("vs_sb", [128, 64, 128], BF16)
        vw = sb("vw_sb", [128, 64, 128], BF16)
        xc = sb("xc", [128, S], BF16)
        w1sb = sb("w1sb", [128, 32, 512], BF16)
        w2sb = sb("w2sb", [128, 4, 128], BF16)
        posT = sb("posT_sb", [128, 32], F32)
        XL = sb("XL", [128, 2, 512], BF16)
        hidT = sb("hidT", [128, 4, 512], BF16)
        tA = sb("tA", [128, 2, 512], F32)
        tB = sb("tB", [128, 2, 512], F32)
        kcmpT = sb("kcmpT", [128, 512], BF16)
        vcmp = sb("vcmp", [128, 4, 128], BF16)
        identf = sb("identf_sb", [128, 128], F32)
        i4 = sb("i4_sb", [128, 512], BF16)
        ones = sb("ones_sb", [128, 128], BF16)
        acon = sb("acon_sb", [128, 16, 128], BF16)
        ov = sb("ov_sb", [128, 4, 128], BF16)
        tailm = sb("tailm_sb", [128, 4, 128], BF16)
        winm = sb("winm_sb", [128, 8, 128], BF16)
        sel3 = sb("sel3_sb", [3, 3, 128], F32)
        qsb = sb("qsb", [128, 2, 512], BF16)
        qrsb = sb("qrsb", [128, 2, 512], BF16)
        g3sb = sb("g3sb", [3, 2, 512], F32)
        cmsb = sb("cmsb", [128, 2, 128], BF16)
        slc = sb("slc", [128, 2, 256], F32)
        Ec = sb("Ec", [128, 4, 512], BF16)
        NE = 3
        Eb = sb("Eb", [128, NE, 512], BF16)
        Pn = sb("Pn", [128, 4, 512], BF16)
        rden = sb("rden", [128, 2, 512], F32)
        wgt = sb("wgt", [128, 512], F32)
        tmp = sb("tmp", [128, 512], F32)
        acc = sb("acc", [128, 2, 512], F32)
        ost = sb("ost", [128, 2, 512], BF16)
        impm = sb("impm", [128, 128], F32)
        impm2 = sb("impm2", [128, 128], F32)
        t8 = sb("t8", [128, 16], F32)
        nsel = sb("nsel", [128, 128], F32)
        nselT = sb("nselT", [128, 2, 4, 128], BF16)
        PS = [st.enter_context(nc.psum_tensor("ps%d" % i, [128, 512], F32)) for i in range(8)]
        Sb, Ob, Db, M0, M1 = (0, 1), (2, 3), (4, 5), 6, 7

        P = Prog(nc)
        ld = lambda eng, dst, src, key, sem: P.op(eng, lambda e: e.dma_start(out=dst, in_=src), writes=[key], dma_sem=sem)
        ld("sp", ksT[:], ksT_d.ap(), "ksT", "l_ks")
        ld("sp", kwT[:], kwT_d.ap(), "kwT", "l_kw")
        ld("sp", vs[:], vs_d.ap().rearrange("(t p) d -> p t d", p=128), "vs", "l_vs")
        ld("sp", vw[:], vw_d.ap().rearrange("(t p) d -> p t d", p=128), "vw", "l_vw")
        ld("act", identf[:], identf_d.ap(), "identf", "l_c0")
        ld("act", i4[:], i4_d.ap(), "i4", "l_c1")
        ld("act", ones[:], ones_d.ap(), "ones", "l_c2")
        ld("act", acon[:], acon_d.ap(), "acon", "l_c3")
        ld("act", ov[:], ov_d.ap(), "ov", "l_c4")
        ld("act", tailm[:], tailm_d.ap(), "tailm", "l_c5")
        ld("act", winm[:], winm_d.ap(), "winm", "l_c6")
        ld("act", sel3[:], sel3_d.ap(), "sel3", "l_c7")
        P.op("dve", lambda e: e.memset(XL[:], 0.0), writes=[("XL", 0), ("XL", 1)])

        GC = math.sqrt(2.0 / math.pi)
        for jv in range(2):
            src_d = kcT_d if jv == 0 else vcT_d
            ld("sp", xc[:], src_d.ap(), "xc", "l_xc")
            for q4 in range(4):
                P.op("pool", lambda e, q4=q4, jv=jv: e.dma_start(
                    out=w1sb[:, q4 * 8:(q4 + 1) * 8, :],
                    in_=w1_d.ap()[jv, q4 * 1024:(q4 + 1) * 1024, :].rearrange("(l p) h -> p l h", p=128)),
                    writes=[("w1", q4)], dma_sem=("l_w1", q4))
            P.op("pool", lambda e, jv=jv: e.dma_start(out=w2sb[:], in_=w2_d.ap()[jv].rearrange("(c p) d -> p c d", p=128)),
                 writes=["w2"], dma_sem="l_w2")
            ld("act", posT[:], posT_d.ap()[jv], "posT", "l_pos")
            for l in range(32):
                xi = P.rr("xl", 2)
                src = bass.AP(xc, l, [[S, 128], [16, 511]])
                P.op("dve", lambda e, xi=xi, src=src, l=l: e.tensor_scalar(
                    out=XL[:, xi, 0:511], in0=src, scalar1=posT[:, l:l + 1], scalar2=None, op0=ALU.add),
                    reads=["xc", "posT"], writes=[("XL", xi)])

                def fn(e, xi=xi, l=l):
                    ins = None
                    for hc in range(4):
                        ins = e.matmul(PS[hc][:], w1sb[:, l, hc * 128:(hc + 1) * 128], XL[:, xi, :],
                                       start=(l == 0), stop=(l == 31))
                    return ins
                P.op("pe", fn, reads=[("XL", xi), ("w1", l // 8)], writes=[("H", hc) for hc in range(4)])
            for hc in range(4):
                ti = P.rr("tAB", 2)
                P.op("act", lambda e, hc=hc, ti=ti: e.activation(out=tA[:, ti, :], in_=PS[hc][:], func=AF.Square),
                     reads=[("H", hc)], writes=[("tA", ti)])
                P.op("dve", lambda e, ti=ti: e.tensor_scalar(out=tA[:, ti, :], in0=tA[:, ti, :], scalar1=0.044715, scalar2=1.0,
                                                             op0=ALU.mult, op1=ALU.add),
                     reads=[("tA", ti)], writes=[("tA", ti)])
                P.op("dve", lambda e, hc=hc, ti=ti: e.tensor_tensor(out=tB[:, ti, :], in0=tA[:, ti, :], in1=PS[hc][:], op=ALU.mult),
                     reads=[("tA", ti), ("H", hc)], writes=[("tB", ti)])
                P.op("act", lambda e, ti=ti: e.activation(out=tB[:, ti, :], in_=tB[:, ti, :], func=AF.Tanh, scale=GC),
                     reads=[("tB", ti)], writes=[("tB", ti)])
                P.op("dve", lambda e, ti=ti: e.tensor_scalar(out=tB[:, ti, :], in0=tB[:, ti, :], scalar1=1.0, scalar2=0.5,
                                                             op0=ALU.add, op1=ALU.mult),
                     reads=[("tB", ti)], writes=[("tB", ti)])
                P.op("dve", lambda e, hc=hc, ti=ti: e.tensor_tensor(out=hidT[:, hc, :], in0=tB[:, ti, :], in1=PS[hc][:], op=ALU.mult),
                     reads=[("tB", ti), ("H", hc)], writes=[("hidT", hc)])
            hid_reads = [("hidT", hc) for hc in range(4)]
            if jv == 0:
                def fn(e):
                    ins = None
                    for hc in range(4):
                        ins = e.matmul(PS[4][:], w2sb[:, hc, :], hidT[:, hc, :], start=(hc == 0), stop=(hc == 3))
                    return ins
                P.op("pe", fn, reads=hid_reads + ["w2"], writes=[("ps", 4)])
                P.op("act", lambda e: e.activation(out=kcmpT[:], in_=PS[4][:], func=AF.Copy), reads=[("ps", 4)], writes=["kcmpT"])
            else:
                def fn(e):
                    ins = None
                    for nt in range(4):
                        for hc in range(4):
                            ins = e.matmul(PS[5][:, nt * 128:(nt + 1) * 128], hidT[:, hc, nt * 128:(nt + 1) * 128], w2sb[:, hc, :],
                                           start=(hc == 0), stop=(hc == 3))
                    return ins
                P.op("pe", fn, reads=hid_reads + ["w2"], writes=[("ps", 5)])
                P.op("act", lambda e: e.activation(out=vcmp[:].rearrange("p a b -> p (a b)"), in_=PS[5][:], func=AF.Copy),
                     reads=[("ps", 5)], writes=["vcmp"])
        ALLPS = [("H", h) for h in range(4)] + [("ps", 4), ("ps", 5)]
        PK = {0: ("S", 0), 1: ("S", 1), 2: ("O", 0), 3: ("O", 1), 4: ("D", 0), 5: ("D", 1)}
        P.op("pe", lambda e: e.matmul(PS[7][:, 0:128], ones[:], ones[:], start=True, stop=True),
             reads=["ones"], writes=ALLPS + [PK[i] for i in range(6)] + ["M1"])

        def branch(tiles, qbuf_key, q_ap, ob, db):
            n = len(tiles)
            for ti_, (kl, kkey, extra, vl, vkey) in enumerate(tiles):
                sbk = P.rr("S", 2)
                ei = P.rr("E", NE)

                def sfn(e, kl=kl, extra=extra, sbk=sbk):
                    ins = e.matmul(PS[Sb[sbk]][:], kl, q_ap, start=True, stop=(len(extra) == 0))
                    for xi_, (l_, r_, tp, _) in enumerate(extra):
                        kw = {} if tp is None else {"tile_position": tp}
                        ins = e.matmul(PS[Sb[sbk]][:], l_, r_, start=False, stop=(xi_ == len(extra) - 1), **kw)
                    return ins
                xkeys = [k for x_ in extra for k in x_[3]]
                P.op("pe", sfn, reads=[kkey, qbuf_key] + xkeys, writes=[("S", sbk)])
                P.op("act", lambda e, sbk=sbk, ei=ei: e.activation(out=Eb[:, ei, :], in_=PS[Sb[sbk]][:], func=AF.Exp, scale=scale),
                     reads=[("S", sbk)], writes=[("E", ei)])

                def ofn(e, vl=vl, ei=ei, ti_=ti_):
                    e.matmul(PS[Ob[ob]][:], vl, Eb[:, ei, :], start=(ti_ == 0), stop=(ti_ == n - 1))
                    return e.matmul(PS[Db[db]][:], ones[:], Eb[:, ei, :], start=(ti_ == 0), stop=(ti_ == n - 1))
                P.op("pe", ofn, reads=[vkey, ("E", ei), "ones"], writes=[("O", ob), ("D", db)])

        def rden_of(db, rb):
            P.op("dve", lambda e: e.tensor_scalar(out=rden[:, rb, :], in0=PS[Db[db]][:], scalar1=1e-30, scalar2=None, op0=ALU.max),
                 reads=[("D", db)], writes=[("rden", rb)])
            P.op("dve", lambda e: e.reciprocal(out=rden[:, rb, :], in_=rden[:, rb, :]), reads=[("rden", rb)], writes=[("rden", rb)])

        cur_slot = [0]

        def dump(src_ap, key, idx):
            if DBGA and cur_slot[0] < 2:
                sl = cur_slot[0]
                P.op("sp", lambda e: e.dma_start(out=dbg_d.ap()[:, sl, idx, :], in_=src_ap), reads=[key], dma_sem="dbg")

        def combine(bi, ob, rb, qi, ai, first):
            P.op("pe", lambda e: e.matmul(PS[M1][:], sel3[:, bi, :], g3sb[:, qi, :], start=True, stop=True),
                 reads=["sel3", ("g3", qi)], writes=["M1"])
            dump(rden[:, rb, :], ("rden", rb), bi * 2)
            P.op("dve", lambda e: e.tensor_tensor(out=wgt[:], in0=rden[:, rb, :], in1=PS[M1][:], op=ALU.mult),
                 reads=[("rden", rb), "M1"], writes=["wgt"])
            dump(wgt[:], "wgt", bi * 2 + 1)
            if first:
                P.op("dve", lambda e: e.tensor_tensor(out=acc[:, ai, :], in0=wgt[:], in1=PS[Ob[ob]][:], op=ALU.mult),
                     reads=["wgt", ("O", ob)], writes=[("acc", ai)])
            else:
                P.op("dve", lambda e: e.tensor_tensor(out=tmp[:], in0=wgt[:], in1=PS[Ob[ob]][:], op=ALU.mult),
                     reads=["wgt", ("O", ob)], writes=["tmp"])
                P.op("pool", lambda e: e.tensor_tensor(out=acc[:, ai, :], in0=acc[:, ai, :], in1=tmp[:], op=ALU.add),
                     reads=["tmp", ("acc", ai)], writes=[("acc", ai)])

        def slot_loads(i):
            qi = i % 2
            ld("sp", qsb[:, qi, :], qT_d.ap()[:, i, :], ("q", qi), ("l_q", qi))
            ld("sp", qrsb[:, qi, :], qrT_d.ap()[:, i, :], ("qr", qi), ("l_qr", qi))
            ld("sp", g3sb[:, qi, :], g3_d.ap()[:, i, :], ("g3", qi), ("l_g3", qi))
            ld("sp", cmsb[:, qi, :], cmask_d.ap()[:, i, :], ("cm", qi), ("l_cm", qi))
            ld("sp", slc[:, qi, :], slotc_d.ap()[:, i, :], ("slc", qi), ("l_sl", qi))

        for i in (range(NSLOT) if DBGA == 0 else (range(2) if DBGA == 1 else (1, 2))):
            cur_slot[0] = i if DBGA < 2 else i - 1
            par = i % 2
            qbmax = 4 * (i // 2) + (1 if par == 0 else 3)
            qi = i % 2
            if i == 0 or DBGA:
                slot_loads(i)
            if i + 1 < NSLOT and not DBGA:
                slot_loads(i + 1)
            ai = P.rr("acc", 2)

            nkt = qbmax // 16 + 1
            obc, dbc = P.rr("O", 2), P.rr("D", 2)
            for kc_ in range(nkt):
                sbk = P.rr("S", 2)
                last = kc_ == nkt - 1

                def sfn(e, kc_=kc_, sbk=sbk, last=last, qi=qi):
                    ins = e.matmul(PS[Sb[sbk]][:], kcmpT[:, kc_ * 128:(kc_ + 1) * 128], qsb[:, qi, :], start=True, stop=not last)
                    if last:
                        ins = e.matmul(PS[Sb[sbk]][:], cmsb[:, qi, :], i4[:], start=False, stop=True)
                    return ins
                P.op("pe", sfn, reads=["kcmpT", ("q", qi), ("cm", qi), "i4"], writes=[("S", sbk)])
                P.op("act", lambda e, sbk=sbk, kc_=kc_: e.activation(out=Ec[:, kc_, :], in_=PS[Sb[sbk]][:], func=AF.Exp, scale=scale),
                     reads=[("S", sbk)], writes=[("Ec", kc_)])

                def ofn(e, kc_=kc_, last=last, obc=obc, dbc=dbc):
                    e.matmul(PS[Ob[obc]][:], vcmp[:, kc_, :], Ec[:, kc_, :], start=(kc_ == 0), stop=last)
                    return e.matmul(PS[Db[dbc]][:], ones[:], Ec[:, kc_, :], start=(kc_ == 0), stop=last)
                P.op("pe", ofn, reads=["vcmp", ("Ec", kc_), "ones"], writes=[("O", obc), ("D", dbc)])
            rbc = P.rr("rden", 2)
            rden_of(dbc, rbc)
            for kc_ in range(nkt):
                P.op("pool", lambda e, kc_=kc_, rbc=rbc: e.tensor_tensor(out=Pn[:, kc_, :], in0=Ec[:, kc_, :], in1=rden[:, rbc, :], op=ALU.mult),
                     reads=[("Ec", kc_), ("rden", rbc)], writes=[("Pn", kc_)])

            def ifn(e, nkt=nkt):
                ins = None
                tot = nkt * 4
                c_ = 0
                for kc_ in range(nkt):
                    for g in range(4):
                        ins = e.matmul(PS[M0][:, 0:128], Pn[:, kc_, g * 128:(g + 1) * 128], ov[:, kc_, :],
                                       start=(c_ == 0), stop=(c_ == tot - 1))
                        c_ += 1
                return ins
            P.op("pe", ifn, reads=[("Pn", k) for k in range(nkt)] + ["ov"], writes=["M0a"])
            P.op("dve", lambda e, qi=qi: e.tensor_tensor(out=impm[:], in0=PS[M0][:, 0:128], in1=slc[:, qi, 0:128], op=ALU.mult),
                 reads=["M0a", ("slc", qi)], writes=["impm"])
            P.op("dve", lambda e, qi=qi: e.tensor_tensor(out=impm[:], in0=impm[:], in1=slc[:, qi, 128:256], op=ALU.add),
                 reads=["impm", ("slc", qi)], writes=["impm"])
            P.op("dve", lambda e: e.max(out=t8[:, 0:8], in_=impm[:]), reads=["impm"], writes=["t8a"])
            P.op("dve", lambda e: e.match_replace(out=impm2[:], in_to_replace=t8[:, 0:8], in_values=impm[:], imm_value=-1.0e9),
                 reads=["impm", "t8a"], writes=["impm2"])
            P.op("dve", lambda e: e.max(out=t8[:, 8:16], in_=impm2[:]), reads=["impm2"], writes=["t8b"])
            P.op("dve", lambda e: e.tensor_scalar(out=nsel[:], in0=impm[:], scalar1=t8[:, 15:16], scalar2=None, op0=ALU.is_lt),
                 reads=["impm", "t8b"], writes=["nsel"])
            P.op("pe", lambda e: e.transpose(out=PS[M0][:, 128:256], in_=nsel[:], identity=identf[:]),
                 reads=["nsel", "identf"], writes=["M0b"])
            ni = P.rr("nselT", 2)
            for g in range(4):
                P.op("dve", lambda e, g=g, ni=ni: e.tensor_copy(out=nselT[:, ni, g, :], in_=PS[M0][:, 128:256]),
                     reads=["M0b"], writes=[("nselT", ni)])
            combine(0, obc, rbc, qi, ai, True)

            tiles = []
            for kt in range(qbmax + 1):
                j, r = kt // 16, kt % 16
                extra = [(acon[32 * j:32 * j + 32, r, :], nselT[32 * j:32 * j + 32, ni, :, :].rearrange("p a b -> p (a b)"),
                          (32 * j, 0), ["acon", ("nselT", ni)])]
                if kt == qbmax - 1:
                    extra.append((tailm[:, par * 2 + 0, :], i4[:], None, ["tailm", "i4"]))
                if kt == qbmax:
                    extra.append((tailm[:, par * 2 + 1, :], i4[:], None, ["tailm", "i4"]))
                tiles.append((ksT[:, kt * 128:(kt + 1) * 128], "ksT", extra, vs[:, kt, :], "vs"))
            obs, dbs = P.rr("O", 2), P.rr("D", 2)
            branch(tiles, ("qr", qi), qrsb[:, qi, :], obs, dbs)
            rbs = P.rr("rden", 2)
            rden_of(dbs, rbs)
            combine(1, obs, rbs, qi, ai, False)

            tiles = []
            for jj in range(6):
                kt = qbmax - 5 + jj
                if kt < 0:
                    continue
                extra = []
                if jj in (0, 1, 4, 5):
                    extra.append((winm[:, par * 4 + (0, 1, None, None, 2, 3)[jj], :], i4[:], None, ["winm", "i4"]))
                tiles.append((kwT[:, kt * 128:(kt + 1) * 128], "kwT", extra, vw[:, kt, :], "vw"))
            obw, dbw = P.rr("O", 2), P.rr("D", 2)
            branch(tiles, ("qr", qi), qrsb[:, qi, :], obw, dbw)
            rbw = P.rr("rden", 2)
            rden_of(dbw, rbw)
            combine(2, obw, rbw, qi, ai, False)

            dump(acc[:, ai, :], ("acc", ai), 6)
            oi = P.rr("ost", 2)
            P.op("act", lambda e, oi=oi, ai=ai: e.activation(out=ost[:, oi, :], in_=acc[:, ai, :], func=AF.Copy),
                 reads=[("acc", ai)], writes=[("ost", oi)])
            P.op("sp", lambda e, oi=oi, i=i: e.dma_start(out=oT_d.ap()[:, i, :], in_=ost[:, oi, :]),
                 reads=[("ost", oi)], dma_sem=("st_o", oi))
        P.emit(st, final_waits=[("st_o", 0), ("st_o", 1), "dbg"])
    return nc


def nsa_attn_consts(half):
    c = {}
    c["identf"] = np.eye(128, dtype=np.float32)
    c["i4"] = np.tile(np.eye(128, dtype=np.float32), (1, 4)).astype(NPBF)
    c["ones"] = np.ones((128, 128), np.float32).astype(NPBF)
    p = np.arange(128)
    k = np.arange(128)
    acon = np.zeros((128, 16, 128), np.float32)
    for r in range(16):
        acon[:, r, :] = np.where((p[:, None] % 32) == 2 * r + (k[None, :] >= 64), NEG, 0.0)
    c["acon4"] = acon.astype(NPBF)
    ov = np.zeros((128, 4, 128), np.float32)
    for kt in range(4):
        n = 128 * kt + p
        cs = n * 16
        ss = np.arange(128) * 64
        o = (cs[:, None] < ss[None, :] + 64) & (cs[:, None] + 32 > ss[None, :]) & (n[:, None] <= 510)
        ov[:, kt, :] = o
    c["ov"] = ov.astype(NPBF)
    q = p[:, None]
    kk = k[None, :]
    zero = np.zeros((128, 128), np.float32)
    allneg = np.full((128, 128), NEG, np.float32)
    caus = np.where(kk <= q, 0.0, NEG).astype(np.float32)
    winold = np.where(kk > q, 0.0, NEG).astype(np.float32)
    tail = np.zeros((128, 4, 128), np.float32)
    winm = np.zeros((128, 8, 128), np.float32)
    for par in range(2):
        higher = (par == 1) if half == 0 else (par == 0)
        if higher:
            tail[:, par * 2 + 0] = zero
            tail[:, par * 2 + 1] = caus
            w = [allneg, winold, zero, caus]
        else:
            tail[:, par * 2 + 0] = caus
            tail[:, par * 2 + 1] = allneg
            w = [winold, zero, caus, allneg]
        for x_ in range(4):
            winm[:, par * 4 + x_] = w[x_]
    c["tailm"] = tail.astype(NPBF)
    c["winm"] = winm.astype(NPBF)
    cmask = np.zeros((128, NSLOT, 128), np.float32)
    slotc = np.zeros((128, NSLOT, 256), np.float32)
    s_ = np.arange(128)[None, :]
    for i in range(NSLOT):
        qb, qbmax = slot_qb(i, half)
        t = 128 * qb + p[:, None]
        ktc = qbmax // 16
        n = 128 * ktc + k[None, :]
        cmask[:, i, :] = np.where(16 * n + 31 <= t, 0.0, NEG)
        cur = t // 64
        m1 = np.ones((128, 128), np.float32)
        m2 = np.zeros((128, 128), np.float32)
        f0 = (s_ == 0) & (s_ <= cur)
        m1[np.broadcast_to(f0, m1.shape)] = 0.0
        m2[np.broadcast_to(f0, m2.shape)] = BIGV + 2
        fp = (s_ == cur - 1)
        m1[fp] = 0.0
        m2[fp] = BIGV + 1
        fc = (s_ == cur)
        m1[fc] = 0.0
        m2[fc] = BIGV
        nc_ = s_ > cur
        m1[nc_] = 0.0
        m2[nc_] = (-1.0 - np.broadcast_to(s_, m2.shape))[nc_]
        slotc[:, i, 0:128] = m1
        slotc[:, i, 128:256] = m2
    c["cmask"] = cmask.astype(NPBF)
    c["slotc"] = slotc
    sel3 = np.zeros((3, 3, 128), np.float32)
    for b in range(3):
        sel3[b, b, :] = 1.0
    c["sel3"] = sel3
    return c


_PROG = {}
_IDENT = np.eye(128, dtype=np.float32)


def _run(nc, maps):
    res = run_bass_kernel_spmd(nc, maps, core_ids=list(range(NCORES)))
    return res.results


def _cat(res, name, axis):
    return np.concatenate([np.asarray(r[name]) for r in res], axis=axis)


def dense_maps(xs, inp, layer_done, oT_full, proj, layer_next):
    maps = []
    for c in range(NCORES):
        m = {"x": xs[c], "ident": _IDENT}
        if layer_done is not None:
            L = layer_done
            m["oT"] = np.ascontiguousarray(oT_full[:, c * TPC:(c + 1) * TPC])
            m["w_o"] = inp["nsa_w_out"][L] if L < 2 else inp["diff_w_out"][L - 2]
            m["g_mlp"] = gain_fm(inp["mlp_norm_g"][L])
            m["w_up"] = inp["mlp_w_up"][L]
            m["w_down"] = inp["mlp_w_down"][L]
        if proj is None:
            m["g_final"] = np.asarray(inp["final_norm_g"], np.float32)
        else:
            m["g_attn"] = gain_fm(inp["attn_norm_g"][layer_next])
            C, Sn, RT = rope_consts(128 if proj == "nsa" else 64, c * TPC, TPC)
            m["cosT"], m["sinT"], m["rotT"] = C, Sn, RT
            if proj == "nsa":
                m["w_in"] = inp["nsa_w_in"][layer_next]
            else:
                m["w_q"] = inp["diff_w_q"][layer_next - 2]
            if proj == "diffkv":
                m["g_kv"] = gain_fm(inp["kv_norm_g"])
                m["w_kv"] = inp["kv_w_shared"]
        maps.append(m)
    return maps


def nsa_attn_maps(res, inp, layer):
    qT = _cat(res, "qT", 1).reshape(16, 128, 64, 128)
    qrT = _cat(res, "qrT", 1).reshape(16, 128, 64, 128)
    kcT, vcT = _cat(res, "kcT", 1), _cat(res, "vcT", 1)
    ksT, kwT = _cat(res, "ksT", 1), _cat(res, "kwT", 1)
    vs, vw = _cat(res, "vs", 0), _cat(res, "vw", 0)
    gT = _cat(res, "gT", 1)
    maps = []
    for c in range(NCORES):
        hk, half = c // 2, c % 2
        qbs = [slot_qb(i, half)[0] for i in range(NSLOT)]
        m = dict(nsa_attn_consts(half))
        rs = slice(hk * 128, (hk + 1) * 128)
        m["kcT"] = np.ascontiguousarray(kcT[rs])
        m["vcT"] = np.ascontiguousarray(vcT[rs])
        m["ksT"] = np.ascontiguousarray(ksT[rs])
        m["kwT"] = np.ascontiguousarray(kwT[rs])
        m["vs"] = np.ascontiguousarray(vs[:, rs])
        m["vw"] = np.ascontiguousarray(vw[:, rs])
        m["qT"] = np.ascontiguousarray(qT[4 * hk:4 * hk + 4][:, :, qbs, :].transpose(1, 2, 0, 3)).reshape(128, NSLOT, 512)
        m["qrT"] = np.ascontiguousarray(qrT[4 * hk:4 * hk + 4][:, :, qbs, :].transpose(1, 2, 0, 3)).reshape(128, NSLOT, 512)
        gv = gT[hk * 12:(hk + 1) * 12].reshape(4, 3, 64, 128)[:, :, qbs, :]
        m["g3"] = np.ascontiguousarray(gv.transpose(1, 2, 0, 3)).reshape(3, NSLOT, 512)
        m["w1"] = inp["nsa_cmp_w1"][layer]
        m["w2"] = inp["nsa_cmp_w2"][layer]
        m["posT"] = np.ascontiguousarray(np.asarray(inp["nsa_cmp_pos"][layer]).transpose(0, 2, 1))
        maps.append(m)
    return maps


def nsa_attn_gather(res):
    oT = np.zeros((16, 128, 64, 128), NPBF)
    for c in range(NCORES):
        hk, half = c // 2, c % 2
        qbs = [slot_qb(i, half)[0] for i in range(NSLOT)]
        o = np.asarray(res[c]["oT"]).reshape(128, NSLOT, 4, 128).transpose(2, 0, 1, 3)
        oT[4 * hk:4 * hk + 4][:, :, qbs, :] = o
    return oT.reshape(2048, 8192)


def build_diff_attn():
    nc = bass.Bass("TRN2", target_bir_lowering=False)
    din = lambda name, shape, dt=F32: nc.dram_tensor(name, list(shape), dt, kind="ExternalInput")
    kT_d = din("kT", [128, S], BF16)
    v_d = din("v", [S, 128], BF16)
    qT_d = din("qT", [128, 64, 512], BF16)
    lamv_d = din("lamv", [128, 256])
    g_d = din("subg", [128, 1])
    linit_d = din("linit", [128, 2])
    i4_d = din("i4", [128, 512], BF16)
    ones_d = din("ones", [128, 128], BF16)
    onesf_d = din("onesf", [128, 128])
    caus_d = din("causT", [128, 128], BF16)
    oT_d = nc.dram_tensor("oT", [128, 64, 256], BF16, kind="ExternalOutput")
    scale = 64.0 ** -0.5
    with ExitStack() as st:
        sb = lambda name, shape, dt: st.enter_context(nc.sbuf_tensor(name, list(shape), dt))
        kT = sb("kT_sb", [128, S], BF16)
        v = sb("v_sb", [128, 64, 128], BF16)
        qsb = sb("q_sb", [128, 64, 512], BF16)
        lamv = sb("lamv_sb", [128, 256], F32)
        gcol = sb("gcol", [128, 1], F32)
        linit = sb("linit_sb", [128, 2], F32)
        i4 = sb("i4_sb", [128, 512], BF16)
        ones = sb("ones_sb", [128, 128], BF16)
        onesf = sb("onesf_sb", [128, 128], F32)
        caus = sb("caus_sb", [128, 128], BF16)
        lw = sb("lw", [128, 128], F32)
        ls = sb("ls", [128, 4], F32)
        nlam = sb("nlam", [128, 1], F32)
        gsc = sb("gsc", [128, 1], F32)
        NE = 3
        Eb = sb("Eb", [128, NE, 512], BF16)
        rden = sb("rden", [128, 512], F32)
        A = sb("A", [128, 512], F32)
        o = sb("o", [128, 256], F32)
        sq = sb("sq", [128, 256], F32)
        rs = sb("rs", [128, 256], F32)
        on = sb("on", [128, 256], F32)
        ost = sb("ost", [128, 2, 256], BF16)
        PS = [st.enter_context(nc.psum_tensor("ps%d" % i, [128, 512], F32)) for i in range(8)]
        Sb, Ob, Db, M0 = (0, 1), (2, 3), (4, 5), 6
        P = Prog(nc)
        ld = lambda eng, dst, src, key, sem: P.op(eng, lambda e: e.dma_start(out=dst, in_=src), writes=[key], dma_sem=sem)
        ld("sp", kT[:], kT_d.ap(), "kT", "l_k")
        ld("sp", v[:], v_d.ap().rearrange("(t p) d -> p t d", p=128), "v", "l_v")
        ld("sp", qsb[:], qT_d.ap(), "q", "l_q")
        ld("act", lamv[:], lamv_d.ap(), "lamv", "l_c0")
        ld("act", gcol[:], g_d.ap(), "gcol", "l_c1")
        ld("act", linit[:], linit_d.ap(), "linit", "l_c2")
        ld("act", i4[:], i4_d.ap(), "i4", "l_c3")
        ld("act", ones[:], ones_d.ap(), "ones", "l_c4")
        ld("act", onesf[:], onesf_d.ap(), "onesf", "l_c5")
        ld("act", caus[:], caus_d.ap(), "caus", "l_c6")
        P.op("dve", lambda e: e.tensor_tensor(out=lw[:, 0:64], in0=lamv[:, 0:64], in1=lamv[:, 64:128], op=ALU.mult),
             reads=["lamv"], writes=["lw0"])
        P.op("dve", lambda e: e.tensor_tensor(out=lw[:, 64:128], in0=lamv[:, 128:192], in1=lamv[:, 192:256], op=ALU.mult),
             reads=["lamv"], writes=["lw1"])
        P.op("dve", lambda e: e.reduce_sum(out=ls[:, 0:1], in_=lw[:, 0:64], axis=AX.X), reads=["lw0"], writes=["ls0"])
        P.op("dve", lambda e: e.reduce_sum(out=ls[:, 1:2], in_=lw[:, 64:128], axis=AX.X), reads=["lw1"], writes=["ls1"])
        P.op("act", lambda e: e.activation(out=ls[:, 2:4], in_=ls[:, 0:2], func=AF.Exp), reads=["ls0", "ls1"], writes=["ls23"])
        P.op("dve", lambda e: e.tensor_tensor(out=nlam[:], in0=ls[:, 3:4], in1=ls[:, 2:3], op=ALU.subtract),
             reads=["ls23"], writes=["nlam"])
        P.op("dve", lambda e: e.tensor_tensor(out=nlam[:], in0=nlam[:], in1=linit[:, 0:1], op=ALU.subtract),
             reads=["nlam", "linit"], writes=["nlam"])
        P.op("dve", lambda e: e.tensor_tensor(out=gsc[:], in0=gcol[:], in1=linit[:, 1:2], op=ALU.mult),
             reads=["gcol", "linit"], writes=["gsc"])

        import os
        NQ = int(os.environ.get("DIFFNQ", "64"))
        DSTEP = int(os.environ.get("DSTEP", "9"))
        for qb in range(NQ):
            ob, db = P.rr("O", 2), P.rr("D", 2)
            n = qb + 1
            for kt in range(n):
                sbk = P.rr("S", 2)
                ei = P.rr("E", NE)
                diag = kt == qb

                def sfn(e, kt=kt, sbk=sbk, diag=diag, qb=qb):
                    ksl = slice(kt * 128, (kt + 1) * 128)
                    ins = e.matmul(PS[Sb[sbk]][:], kT[:, ksl], qsb[:, qb, :], start=True, stop=not diag)
                    if diag:
                        ins = e.matmul(PS[Sb[sbk]][:], caus[:], i4[:], start=False, stop=True)
                    return ins
                P.op("pe", sfn, reads=["kT", "q", "caus", "i4"], writes=[("S", sbk)])
                P.op("act", lambda e, sbk=sbk, ei=ei: e.activation(out=Eb[:, ei, :], in_=PS[Sb[sbk]][:], func=AF.Exp, scale=scale),
                     reads=[("S", sbk)], writes=[("E", ei)])

                def ofn(e, kt=kt, ei=ei, n=n, ob=ob, db=db):
                    e.matmul(PS[Ob[ob]][:], v[:, kt, :], Eb[:, ei, :], start=(kt == 0), stop=(kt == n - 1))
                    return e.matmul(PS[Db[db]][:], ones[:], Eb[:, ei, :], start=(kt == 0), stop=(kt == n - 1))
                P.op("pe", ofn, reads=["v", ("E", ei), "ones"], writes=[("O", ob), ("D", db)])
            P.op("dve", lambda e, db=db: e.tensor_scalar(out=rden[:], in0=PS[Db[db]][:], scalar1=1e-30, scalar2=None, op0=ALU.max),
                 reads=[("D", db)], writes=["rden"])
            P.op("dve", lambda e: e.reciprocal(out=rden[:], in_=rden[:]), reads=["rden"], writes=["rden"])
            P.op("dve", lambda e, ob=ob: e.tensor_tensor(out=A[:], in0=rden[:], in1=PS[Ob[ob]][:], op=ALU.mult),
                 reads=["rden", ("O", ob)], writes=["A"])
            P.op("dve", lambda e: e.scalar_tensor_tensor(out=o[:], in0=A[:, 256:512], scalar=nlam[:, 0:1], in1=A[:, 0:256],
                                                         op0=ALU.mult, op1=ALU.add),
                 reads=["A", "nlam"], writes=["o"])
            P.op("pool", lambda e: e.tensor_tensor(out=sq[:], in0=o[:], in1=o[:], op=ALU.mult), reads=["o"], writes=["sq"])
            P.op("pe", lambda e: e.matmul(PS[M0][:, 0:256], onesf[:], sq[:], start=True, stop=True),
                 reads=["sq", "onesf"], writes=["M0"])
            P.op("dve", lambda e: e.tensor_scalar(out=rs[:], in0=PS[M0][:, 0:256], scalar1=1.0 / 128, scalar2=EPS,
                                                  op0=ALU.mult, op1=ALU.add), reads=["M0"], writes=["rs"])
            P.op("act", lambda e: e.activation(out=rs[:], in_=rs[:], func=AF.Sqrt), reads=["rs"], writes=["rs"])
            P.op("dve", lambda e: e.reciprocal(out=rs[:], in_=rs[:]), reads=["rs"], writes=["rs"])
            P.op("dve", lambda e: e.tensor_tensor(out=on[:], in0=o[:], in1=rs[:], op=ALU.mult), reads=["o", "rs"], writes=["on"])
            oi = P.rr("ost", 2)
            P.op("act", lambda e, oi=oi: e.activation(out=ost[:, oi, :], in_=on[:], func=AF.Identity, scale=gsc[:, 0:1]),
                 reads=["on", "gsc"], writes=[("ost", oi)])
            P.op("sp", lambda e, oi=oi, qb=qb: e.dma_start(out=oT_d.ap()[:, qb, :], in_=ost[:, oi, :]),
                 reads=[("ost", oi)], dma_sem=("st_o", oi))
        P.emit(st, final_waits=[("st_o", 0), ("st_o", 1)])
    return nc


def diff_attn_maps(dqT, dkT, dv, inp, layer):
    j = layer - 2
    li = 0.8 - 0.6 * math.exp(-0.3 * layer)
    q4 = dqT.reshape(16, 128, 64, 128)
    p = np.arange(128)
    consts = {
        "i4": np.tile(np.eye(128, dtype=np.float32), (1, 4)).astype(NPBF),
        "ones": np.ones((128, 128), np.float32).astype(NPBF),
        "onesf": np.ones((128, 128), np.float32),
        "causT": np.where(p[None, :] <= p[:, None], 0.0, NEG).astype(np.float32).astype(NPBF),
        "lamv": np.ascontiguousarray(np.broadcast_to(np.asarray(inp["diff_lambda"][j], np.float32).reshape(1, 256), (128, 256))),
        "subg": np.ascontiguousarray(np.asarray(inp["diff_subln_g"][j], np.float32).reshape(128, 1)),
        "linit": np.ascontiguousarray(np.broadcast_to(np.array([[li, 1.0 - li]], np.float32), (128, 2))),
    }
    maps = []
    for c in range(NCORES):
        hk = c // 2
        m = dict(consts)
        m["kT"] = np.ascontiguousarray(dkT[hk * 128:(hk + 1) * 128])
        m["v"] = np.ascontiguousarray(dv[:, hk * 128:(hk + 1) * 128])
        qq = np.ascontiguousarray(q4[2 * c:2 * c + 2].transpose(1, 2, 0, 3))
        qz = np.zeros((128, 64, 2, 2, 128), NPBF)
        qz[0:64, :, 0] = qq[0:64]
        qz[64:128, :, 1] = qq[64:128]
        m["qT"] = qz.reshape(128, 64, 512)
        maps.append(m)
    return maps


def diff_attn_gather(res):
    oT = np.zeros((16, 128, 64, 128), NPBF)
    for c in range(NCORES):
        o = np.asarray(res[c]["oT"]).reshape(128, 64, 2, 128).transpose(2, 0, 1, 3)
        oT[2 * c:2 * c + 2] = o
    return oT.reshape(2048, 8192)


def kernel(x, attn_norm_g, mlp_norm_g, final_norm_g, nsa_w_in, nsa_cmp_pos, nsa_cmp_w1, nsa_cmp_w2, nsa_w_out,
           kv_norm_g, kv_w_shared, diff_w_q, diff_lambda, diff_subln_g, diff_w_out, mlp_w_up, mlp_w_down, _debug=None):
    inp = dict(attn_norm_g=attn_norm_g, mlp_norm_g=mlp_norm_g, final_norm_g=final_norm_g, nsa_w_in=nsa_w_in,
               nsa_cmp_pos=nsa_cmp_pos, nsa_cmp_w1=nsa_cmp_w1, nsa_cmp_w2=nsa_cmp_w2, nsa_w_out=nsa_w_out,
               kv_norm_g=kv_norm_g, kv_w_shared=kv_w_shared, diff_w_q=diff_w_q, diff_lambda=diff_lambda,
               diff_subln_g=diff_subln_g, diff_w_out=diff_w_out, mlp_w_up=mlp_w_up, mlp_w_down=mlp_w_down)
    inp = {k: np.asarray(v, np.float32) for k, v in inp.items()}
    x2 = np.asarray(x, np.float32).reshape(S, D)
    xs = [np.ascontiguousarray(x2[c * TPC:(c + 1) * TPC]) for c in range(NCORES)]
    if "nsa" not in _PROG:
        _PROG["nsa"] = build_nsa_attn()
        _PROG["diff"] = build_diff_attn()
    dbg = {}
    res = _run(get_dense(False, False, False, "nsa"), dense_maps(xs, inp, None, None, "nsa", 0))
    dkT = dv = None
    for layer in range(4):
        if layer < 2:
            ra = _run(_PROG["nsa"], nsa_attn_maps(res, inp, layer))
            oT = nsa_attn_gather(ra)
        else:
            if layer == 2:
                dkT, dv = _cat(res, "dkT", 1), _cat(res, "dv", 0)
            ra = _run(_PROG["diff"], diff_attn_maps(_cat(res, "dqT", 1), dkT, dv, inp, layer))
            oT = diff_attn_gather(ra)
        if _debug is not None:
            dbg["oT%d" % layer] = oT
        nxt = [("nsa", 1), ("diffkv", 2), ("diff", 3), (None, None)][layer]
        res = _run(get_dense(True, True, nxt[0] is None, nxt[0]), dense_maps(xs, inp, layer, oT, nxt[0], nxt[1]))
        if nxt[0] is not None:
            xs = [np.asarray(r["x_out"]) for r in res]
            if _debug is not None:
                dbg["x%d" % layer] = np.concatenate(xs, 0)
    y = np.concatenate([np.asarray(r["y"]) for r in res], 0).reshape(1, S, D).astype(np.float32)
    if _debug is not None:
        _debug.update(dbg)
    return y
```

```python
import math
from contextlib import ExitStack

import numpy as np
import ml_dtypes
import concourse.bass as bass
import concourse.mybir as mybir
from concourse.bass_utils import run_bass_kernel_spmd

F32 = mybir.dt.float32
BF16 = mybir.dt.bfloat16
AF = mybir.ActivationFunctionType
ALU = mybir.AluOpType
AX = mybir.AxisListType
NPBF = ml_dtypes.bfloat16

NCORES = 8
S = 8192
D = 2048
TPC = S // NCORES
NTT = TPC // 128
EPS = 1e-6
NEG = -30000.0
ROPE_THETA = 500000.0

ENGS = ("pe", "act", "dve", "pool", "sp")


class Op:
    __slots__ = ("eng", "fn", "deps", "signalled", "sig", "dma_sem", "dma_val")


class Prog:
    def __init__(self, nc):
        self.nc = nc
        self.ops = []
        self.last_w = {}
        self.readers = {}
        self.dma_cum = {}
        self.rot = {}

    def rr(self, name, n):
        i = self.rot.get(name, 0)
        self.rot[name] = i + 1
        return i % n

    def op(self, eng, fn, reads=(), writes=(), dma_sem=None, ndma=1):
        o = Op()
        o.eng = eng
        o.fn = fn
        o.signalled = False
        o.sig = 0
        o.dma_sem = dma_sem
        o.dma_val = 0
        deps = set()
        for k in reads:
            w = self.last_w.get(k)
            if w is not None:
                deps.add(w)
        for k in writes:
            w = self.last_w.get(k)
            if w is not None:
                deps.add(w)
            for r in self.readers.get(k, ()):
                deps.add(r)
        o.deps = [d for d in deps
                  if not (d.eng == "pe" and eng == "pe" and d.dma_sem is None and dma_sem is None)]
        if dma_sem is not None:
            self.dma_cum[dma_sem] = self.dma_cum.get(dma_sem, 0) + 16 * ndma
            o.dma_val = self.dma_cum[dma_sem]
        for d in o.deps:
            if d.dma_sem is None:
                d.signalled = True
        for k in writes:
            self.last_w[k] = o
            self.readers[k] = []
        for k in reads:
            if k not in writes:
                self.readers.setdefault(k, []).append(o)
        self.ops.append(o)
        return o

    def emit(self, stack, final_waits=()):
        nc = self.nc
        cnt = {e: 0 for e in ENGS}
        for o in self.ops:
            if o.dma_sem is None and o.signalled:
                cnt[o.eng] += 1
                o.sig = cnt[o.eng]
        esem = {e: stack.enter_context(nc.semaphore("s_" + e)) for e in ENGS}
        dsem = {}
        for i, k in enumerate(self.dma_cum):
            dsem[k] = stack.enter_context(nc.semaphore("d%d" % i))
        block = stack.enter_context(nc.Block())
        per = {e: [o for o in self.ops if o.eng == e] for e in ENGS}
        dma_cum = self.dma_cum

        def run(e, eng):
            waited = {}
            for o in per[e]:
                need = {}
                for d in o.deps:
                    if d.dma_sem is not None:
                        key = ("d", d.dma_sem)
                        v = d.dma_val
                    else:
                        key = ("e", d.eng)
                        v = d.sig
                    if v > need.get(key, 0):
                        need[key] = v
                for key, v in need.items():
                    if v > waited.get(key, 0):
                        sem = dsem[key[1]] if key[0] == "d" else esem[key[1]]
                        eng.wait_ge(sem, v)
                        waited[key] = v
                r = o.fn(eng)
                if o.dma_sem is not None:
                    rs = r if isinstance(r, (list, tuple)) else [r]
                    for ins in rs:
                        ins.then_inc(dsem[o.dma_sem], 16)
                elif o.signalled:
                    r.then_inc(esem[e], 1)
            if e == "sp":
                for k in final_waits:
                    if k in dsem:
                        eng.wait_ge(dsem[k], dma_cum[k])

        @block.tensor
        def _(eng):
            run("pe", eng)

        @block.scalar
        def _(eng):
            run("act", eng)

        @block.vector
        def _(eng):
            run("dve", eng)

        @block.gpsimd
        def _(eng):
            run("pool", eng)

        @block.sync
        def _(eng):
            run("sp", eng)


def rope_consts(head_chunk, tok0, ntok):
    rot = head_chunk // 4
    half = rot // 2
    inv = 1.0 / (ROPE_THETA ** (np.arange(0, rot, 2, dtype=np.float32) / np.float32(rot)))
    inv = inv.astype(np.float32)
    pos = np.arange(tok0, tok0 + ntok, dtype=np.float32)
    ang = (pos[None, :] * inv[:, None]).astype(np.float32)
    cos = np.cos(ang).astype(np.float32)
    sin = np.sin(ang).astype(np.float32)
    C = np.ones((128, ntok), np.float32)
    Sn = np.zeros((128, ntok), np.float32)
    Rm = np.zeros((128, 128), np.float32)
    for base in range(0, 128, head_chunk):
        for j in range(half):
            C[base + j] = cos[j]
            C[base + half + j] = cos[j]
            Sn[base + j] = sin[j]
            Sn[base + half + j] = sin[j]
            Rm[base + j, base + half + j] = -1.0
            Rm[base + half + j, base + j] = 1.0
    return C, Sn, np.ascontiguousarray(Rm.T)


def build_dense(oproj, mlp, final, proj):
    nc = bass.Bass("TRN2", target_bir_lowering=False)
    T = TPC
    dr = {}

    def din(name, shape, dt=F32):
        dr[name] = nc.dram_tensor(name, list(shape), dt, kind="ExternalInput")
        return dr[name]

    def dout(name, shape, dt=F32):
        dr[name] = nc.dram_tensor(name, list(shape), dt, kind="ExternalOutput")
        return dr[name]

    x_d = din("x", [T, D])
    ident_d = din("ident", [128, 128])
    if oproj:
        oT_d = din("oT", [D, T], BF16)
        wo_d = din("w_o", [D, D])
    if mlp:
        gm_d = din("g_mlp", [128, 16])
        wu_d = din("w_up", [D, 4 * D])
        wd_d = din("w_down", [4 * D, D])
    if final:
        gf_d = din("g_final", [D])
        y_d = dout("y", [T, D])
    else:
        xo_d = dout("x_out", [T, D])
    if proj:
        ga_d = din("g_attn", [128, 16])
        cos_d = din("cosT", [128, T])
        sin_d = din("sinT", [128, T])
        rt_d = din("rotT", [128, 128])
    if proj == "nsa":
        win_d = din("w_in", [D, 5168])
        qT_d = dout("qT", [D, T], BF16)
        qrT_d = dout("qrT", [D, T], BF16)
        kcT_d = dout("kcT", [512, T], BF16)
        vcT_d = dout("vcT", [512, T], BF16)
        ksT_d = dout("ksT", [512, T], BF16)
        kwT_d = dout("kwT", [512, T], BF16)
        vs_d = dout("vs", [T, 512], BF16)
        vw_d = dout("vw", [T, 512], BF16)
        gT_d = dout("gT", [48, T])
    if proj in ("diff", "diffkv"):
        wq_d = din("w_q", [D, D])
        dqT_d = dout("dqT", [D, T], BF16)
    if proj == "diffkv":
        gk_d = din("g_kv", [128, 16])
        wkv_d = din("w_kv", [D, 1024])
        dkT_d = dout("dkT", [512, T], BF16)
        dv_d = dout("dv", [T, 512], BF16)

    with ExitStack() as st:
        sb = lambda name, shape, dt: st.enter_context(nc.sbuf_tensor(name, list(shape), dt))
        x = sb("x_sb", [128, NTT, D], F32)
        big1 = sb("big1", [128, 16, T], BF16)
        big2 = sb("big2", [128, 16, T], BF16)
        NWB = 3
        wb = sb("wb", [128, NWB, 16, 512], BF16)
        ident = sb("ident_sb", [128, 128], F32)
        xh = sb("xh", [128, 1, D], F32)
        ssq = sb("ssq", [128, 32], F32)
        rstd = sb("rstd", [128, 32], F32)
        gT = sb("gT_sb", [128, 4, 16], F32)
        NST = 4
        stg = sb("stg", [128, NST, 512], BF16)
        r32 = sb("r32", [128, 2, 512], F32)
        if proj:
            cosT = sb("cos_sb", [128, T], F32)
            sinT = sb("sin_sb", [128, T], F32)
            rotT = sb("rot_sb", [128, 128], F32)
            t1 = sb("t1", [128, 1, 512], F32)
            t2 = sb("t2", [128, 1, 512], F32)
            gst = sb("gst", [48, 1, 512], F32)
        if final:
            gfb = sb("gfb", [128, D], F32)
        psb = [st.enter_context(nc.psum_tensor("ps%d" % i, [128, 512], F32)) for i in range(8)]

        P = Prog(nc)
        out_sems = []
        B2 = [("big2", fc) for fc in range(16)]

        P.op("sp", lambda e: e.dma_start(out=x[:], in_=x_d.ap().rearrange("(t p) c -> p t c", p=128)),
             writes=[("x", t) for t in range(NTT)], dma_sem="ldx")
        P.op("act", lambda e: e.dma_start(out=ident[:], in_=ident_d.ap()), writes=["ident"], dma_sem="ldc")
        gains = {}

        def load_gain(name, d):
            gi = len(gains)
            gains[name] = gi
            P.op("act", lambda e: e.dma_start(out=gT[:, gi, :], in_=d.ap()),
                 writes=[("gain", gi)], dma_sem="ldg")

        if mlp:
            load_gain("mlp", gm_d)
        if proj:
            load_gain("attn", ga_d)
            P.op("act", lambda e: e.dma_start(out=cosT[:], in_=cos_d.ap()), writes=["cos"], dma_sem="ldc")
            P.op("act", lambda e: e.dma_start(out=sinT[:], in_=sin_d.ap()), writes=["sin"], dma_sem="ldc")
            P.op("act", lambda e: e.dma_start(out=rotT[:], in_=rt_d.ap()), writes=["rot"], dma_sem="ldc")
        if proj == "diffkv":
            load_gain("kv", gk_d)
        if final:
            P.op("act", lambda e: e.dma_start(out=gfb[:], in_=gf_d.ap().partition_broadcast(128)),
                 writes=["gfb"], dma_sem="ldc")
        if oproj:
            P.op("sp", lambda e: e.dma_start(out=big2[:], in_=oT_d.ap().rearrange("(k p) t -> p k t", p=128)),
                 writes=B2, dma_sem="ldo")

        def load_wblock(wd, r0, c0, ncols=512):
            i = P.rr("wb", NWB)
            src = wd.ap()[r0:r0 + D, c0:c0 + ncols].rearrange("(k p) c -> p k c", p=128)
            P.op("pool", lambda e: e.dma_start(out=wb[:, i, :, 0:ncols], in_=src),
                 writes=[("wb", i)], dma_sem=("wb", i))
            return i

        def next_ps():
            return P.rr("ps", 4)

        def mm_group(pb, pso, pairs, reads):
            def fn(e):
                n = len(pairs)
                ins = None
                for i, (l, r) in enumerate(pairs):
                    ins = e.matmul(pso, l, r, start=(i == 0), stop=(i == n - 1))
                return ins
            P.op("pe", fn, reads=reads, writes=[("ps", pb)])

        def resid_add(pb, tt, cc):
            xs = x[:, tt, cc * 512:(cc + 1) * 512]
            P.op("dve", lambda e: e.tensor_tensor(out=xs, in0=xs, in1=psb[pb][:], op=ALU.add),
                 reads=[("ps", pb), ("x", tt)], writes=[("x", tt)])

        if oproj:
            for cc in range(4):
                wi = load_wblock(wo_d, 0, cc * 512)
                for tt in range(NTT):
                    pb = next_ps()
                    mm_group(pb, psb[pb][:], [(big2[:, kc, tt * 128:(tt + 1) * 128], wb[:, wi, kc, :]) for kc in range(16)],
                             reads=B2 + [("wb", wi)])
                    resid_add(pb, tt, cc)

        nstat = [0]

        def make_hT(gname):
            import os
            HT = int(os.environ.get("HT", "9"))
            gi = gains[gname]
            for tt in range(NTT):
                _make_hT_tile(gi, tt, HT)

        def _make_hT_tile(gi, tt, HT):
            import os
            if True:
                si = nstat[0]
                nstat[0] += 1
                xi = P.rr("xh", 1)
                P.op("dve", lambda e, si=si: e.memset(ssq[:, si:si + 1], 0.0), writes=[("ssq", si)])
                P.op("act", lambda e, si=si, tt=tt: e.activation(out=xh[:, 0, :], in_=x[:, tt, :], func=AF.Square,
                                                                  accum_out=ssq[:, si:si + 1]),
                     reads=[("x", tt), ("ssq", si)], writes=[("xh", 0), ("ssq", si)])
                P.op("dve", lambda e, si=si: e.tensor_scalar(out=rstd[:, si:si + 1], in0=ssq[:, si:si + 1],
                                                             scalar1=1.0 / D, scalar2=EPS, op0=ALU.mult, op1=ALU.add),
                     reads=[("ssq", si)], writes=[("rstd", si)])
                if HT < 2:
                    return
                P.op("act", lambda e, si=si: e.activation(out=rstd[:, si:si + 1], in_=rstd[:, si:si + 1], func=AF.Sqrt),
                     reads=[("rstd", si)], writes=[("rstd", si)])
                P.op("dve", lambda e, si=si: e.reciprocal(out=rstd[:, si:si + 1], in_=rstd[:, si:si + 1]),
                     reads=[("rstd", si)], writes=[("rstd", si)])
                if HT < 3:
                    return
                P.op("act", lambda e, si=si, tt=tt, xi=xi: e.activation(out=xh[:, xi, :], in_=x[:, tt, :], func=AF.Identity,
                                                                       scale=rstd[:, si:si + 1]),
                     reads=[("x", tt), ("rstd", si)], writes=[("xh", xi)])
                if HT < 4:
                    return
                for q4 in range(4):
                    pb = 4 + P.rr("pst", 2)

                    def tfn(e, q4=q4, xi=xi, pb=pb):
                        ins = None
                        for j in range(4):
                            kc = q4 * 4 + j
                            ins = e.transpose(out=psb[pb][:, j * 128:(j + 1) * 128],
                                              in_=xh[:, xi, kc * 128:(kc + 1) * 128], identity=ident[:])
                        return ins
                    P.op("pe", tfn, reads=[("xh", xi), "ident"], writes=[("ps", pb)])
                    if HT < 5:
                        continue
                    for j in range(4):
                        kc = q4 * 4 + j
                        dst = big1[:, kc, tt * 128:(tt + 1) * 128]
                        src = psb[pb][:, j * 128:(j + 1) * 128]
                        EV = int(os.environ.get("EV", "2"))
                        if (q4 % 2 == 0 and EV == 2) or EV == 0:
                            P.op("dve", lambda e, dst=dst, src=src, kc=kc: e.tensor_scalar(
                                out=dst, in0=src, scalar1=gT[:, gi, kc:kc + 1], scalar2=None, op0=ALU.mult),
                                reads=[("ps", pb), ("gain", gi)], writes=[("big1", tt, q4 % 2)])
                        else:
                            P.op("act", lambda e, dst=dst, src=src, kc=kc: e.activation(
                                out=dst, in_=src, func=AF.Identity, scale=gT[:, gi, kc:kc + 1]),
                                reads=[("ps", pb), ("gain", gi)], writes=[("big1", tt, q4 % 2)])

        hT_reads = [("big1", t, u) for t in range(NTT) for u in range(2)]

        if mlp:
            make_hT("mlp")
            for qt in range(4):
                for blk in range(4):
                    wi = load_wblock(wu_d, 0, qt * 2048 + blk * 512)
                    for j in range(4):
                        fc = blk * 4 + j
                        for tg in range(2):
                            pb = next_ps()
                            mm_group(pb, psb[pb][:],
                                     [(wb[:, wi, kc, j * 128:(j + 1) * 128], big1[:, kc, tg * 512:(tg + 1) * 512])
                                      for kc in range(16)],
                                     reads=hT_reads + [("wb", wi)])
                            ri = P.rr("r32", 2)
                            P.op("act", lambda e, pb=pb, ri=ri: e.activation(out=r32[:, ri, :], in_=psb[pb][:], func=AF.Relu),
                                 reads=[("ps", pb)], writes=[("r32", ri)])
                            eng = "pool" if (fc + tg) % 2 == 0 else "dve"
                            P.op(eng, lambda e, ri=ri, fc=fc, tg=tg: e.tensor_tensor(
                                out=big2[:, fc, tg * 512:(tg + 1) * 512], in0=r32[:, ri, :], in1=r32[:, ri, :], op=ALU.mult),
                                reads=[("r32", ri)], writes=[("big2", fc)])
                for cc in range(4):
                    wi = load_wblock(wd_d, qt * 2048, cc * 512)
                    for tt in range(NTT):
                        pb = next_ps()
                        mm_group(pb, psb[pb][:],
                                 [(big2[:, fc, tt * 128:(tt + 1) * 128], wb[:, wi, fc, :]) for fc in range(16)],
                                 reads=B2 + [("wb", wi)])
                        resid_add(pb, tt, cc)

        if final:
            for tt in range(NTT):
                si = nstat[0]
                nstat[0] += 1
                yi = 0
                P.op("dve", lambda e, si=si: e.memset(ssq[:, si:si + 1], 0.0), writes=[("ssq", si)])
                P.op("act", lambda e, si=si, tt=tt: e.activation(out=xh[:, 0, :], in_=x[:, tt, :], func=AF.Square,
                                                                  accum_out=ssq[:, si:si + 1]),
                     reads=[("x", tt), ("ssq", si)], writes=[("xh", 0), ("ssq", si)])
                P.op("dve", lambda e, si=si: e.tensor_scalar(out=rstd[:, si:si + 1], in0=ssq[:, si:si + 1],
                                                             scalar1=1.0 / D, scalar2=EPS, op0=ALU.mult, op1=ALU.add),
                     reads=[("ssq", si)], writes=[("rstd", si)])
                P.op("act", lambda e, si=si: e.activation(out=rstd[:, si:si + 1], in_=rstd[:, si:si + 1], func=AF.Sqrt),
                     reads=[("rstd", si)], writes=[("rstd", si)])
                P.op("dve", lambda e, si=si: e.reciprocal(out=rstd[:, si:si + 1], in_=rstd[:, si:si + 1]),
                     reads=[("rstd", si)], writes=[("rstd", si)])
                P.op("dve", lambda e, si=si, tt=tt, yi=yi: e.scalar_tensor_tensor(
                    out=xh[:, 0, :], in0=x[:, tt, :], scalar=rstd[:, si:si + 1], in1=gfb[:], op0=ALU.mult, op1=ALU.mult),
                    reads=[("x", tt), ("rstd", si), "gfb"], writes=[("xh", 0)])
                P.op("sp", lambda e, tt=tt, yi=yi: e.dma_start(out=y_d.ap()[tt * 128:(tt + 1) * 128, :], in_=xh[:, 0, :]),
                     reads=[("xh", 0)], dma_sem="yst")
            out_sems += ["yst"]
        else:
            P.op("sp", lambda e: e.dma_start(out=xo_d.ap().rearrange("(t p) c -> p t c", p=128), in_=x[:]),
                 reads=[("x", t) for t in range(NTT)], dma_sem="stx")
            out_sems.append("stx")

        def stage_out(dst_ap, src_fn, eng, reads):
            si = P.rr("stg", NST)
            P.op(eng, lambda e: src_fn(e, stg[:, si, :]), reads=reads, writes=[("stg", si)])
            P.op("sp", lambda e: e.dma_start(out=dst_ap, in_=stg[:, si, :]), reads=[("stg", si)], dma_sem=("stg", si))

        def proj_fm(wi, j, mode, outs):
            for tg in range(2):
                pb = next_ps()
                mm_group(pb, psb[pb][:],
                         [(wb[:, wi, kc, j * 128:(j + 1) * 128], big1[:, kc, tg * 512:(tg + 1) * 512]) for kc in range(16)],
                         reads=hT_reads + [("wb", wi)])
                tsl = slice(tg * 512, (tg + 1) * 512)
                if "rope" in outs:
                    ri = P.rr("r32", 2)
                    P.op("act", lambda e, pb=pb, ri=ri: e.activation(out=r32[:, ri, :], in_=psb[pb][:], func=AF.Copy),
                         reads=[("ps", pb)], writes=[("r32", ri)])
                    if "plain" in outs:
                        dd, r0 = outs["plain"]
                        stage_out(dd.ap()[r0:r0 + 128, tsl],
                                  lambda e, o, ri=ri: e.tensor_copy(out=o, in_=r32[:, ri, :]), "pool", [("r32", ri)])
                    pr = 6 + P.rr("psr", 2)
                    P.op("pe", lambda e, pr=pr, ri=ri: e.matmul(psb[pr][:], rotT[:], r32[:, ri, :], start=True, stop=True),
                         reads=[("r32", ri), "rot"], writes=[("ps", pr)])
                    ti = P.rr("t12", 1)
                    P.op("dve", lambda e, ri=ri, ti=ti, tsl=tsl: e.tensor_tensor(out=t1[:, ti, :], in0=r32[:, ri, :],
                                                                                 in1=cosT[:, tsl], op=ALU.mult),
                         reads=[("r32", ri), "cos"], writes=[("t1", ti)])
                    P.op("dve", lambda e, pr=pr, ti=ti, tsl=tsl: e.tensor_tensor(out=t2[:, ti, :], in0=psb[pr][:],
                                                                                 in1=sinT[:, tsl], op=ALU.mult),
                         reads=[("ps", pr), "sin"], writes=[("t2", ti)])
                    dd, r0 = outs["rope"]
                    stage_out(dd.ap()[r0:r0 + 128, tsl],
                              lambda e, o, ti=ti: e.tensor_tensor(out=o, in0=t1[:, ti, :], in1=t2[:, ti, :], op=ALU.add),
                              "pool", [("t1", ti), ("t2", ti)])
                else:
                    dd, r0 = outs["plain"]
                    stage_out(dd.ap()[r0:r0 + 128, tsl],
                              lambda e, o, pb=pb: e.activation(out=o, in_=psb[pb][:], func=AF.Copy), "act", [("ps", pb)])

        def proj_tm(wi, dd, c0):
            for tt in range(NTT):
                pb = next_ps()
                mm_group(pb, psb[pb][:],
                         [(big1[:, kc, tt * 128:(tt + 1) * 128], wb[:, wi, kc, :]) for kc in range(16)],
                         reads=hT_reads + [("wb", wi)])
                stage_out(dd.ap()[tt * 128:(tt + 1) * 128, c0:c0 + 512],
                          lambda e, o, pb=pb: e.activation(out=o, in_=psb[pb][:], func=AF.Copy), "act", [("ps", pb)])

        import os
        DBG = int(os.environ.get("DBG", "9"))
        if proj == "nsa" and DBG >= 2:
            make_hT("attn")
        if proj == "nsa" and DBG >= 3:
            for b in range(4 if DBG >= 4 else 1):
                wi = load_wblock(win_d, 0, b * 512)
                for j in range(4):
                    r0 = (b * 4 + j) * 128
                    proj_fm(wi, j, "both", {"plain": (qT_d, r0), "rope": (qrT_d, r0)})
        if proj == "nsa" and DBG >= 5:
            specs = [(kcT_d, False), (vcT_d, False), (ksT_d, True), (None, "vs"), (kwT_d, True), (None, "vw")]
            for pi, (dd, mode) in enumerate(specs):
                wi = load_wblock(win_d, 0, 2048 + pi * 512)
                if dd is None:
                    proj_tm(wi, vs_d if mode == "vs" else vw_d, 0)
                else:
                    for j in range(4):
                        proj_fm(wi, j, "x", {"rope": (dd, j * 128)} if mode else {"plain": (dd, j * 128)})
            wi = load_wblock(win_d, 0, 2048 + 6 * 512, 48)
            for tg in range(2):
                pb = next_ps()
                mm_group(pb, psb[pb][0:48, :],
                         [(wb[:, wi, kc, 0:48], big1[:, kc, tg * 512:(tg + 1) * 512]) for kc in range(16)],
                         reads=hT_reads + [("wb", wi)])
                gi2 = P.rr("gst", 1)
                P.op("act", lambda e, pb=pb, gi2=gi2: e.activation(out=gst[:, gi2, :], in_=psb[pb][0:48, :], func=AF.Sigmoid),
                     reads=[("ps", pb)], writes=[("gst", gi2)])
                P.op("sp", lambda e, gi2=gi2, tg=tg: e.dma_start(out=gT_d.ap()[:, tg * 512:(tg + 1) * 512], in_=gst[:, gi2, :]),
                     reads=[("gst", gi2)], dma_sem=("gst", gi2))
            out_sems += [("gst", 0)]
        if proj in ("diff", "diffkv"):
            make_hT("attn")
            for b in range(4):
                wi = load_wblock(wq_d, 0, b * 512)
                for j in range(4):
                    proj_fm(wi, j, "x", {"rope": (dqT_d, (b * 4 + j) * 128)})
        if proj == "diffkv":
            make_hT("kv")
            wi = load_wblock(wkv_d, 0, 0)
            for j in range(4):
                proj_fm(wi, j, "x", {"rope": (dkT_d, j * 128)})
            wi = load_wblock(wkv_d, 0, 512)
            proj_tm(wi, dv_d, 0)
        if proj:
            out_sems += [("stg", i) for i in range(NST)]

        P.emit(st, final_waits=out_sems)
    return nc


_DENSE_CACHE = {}


def gain_fm(g):
    return np.ascontiguousarray(np.asarray(g, np.float32).reshape(16, 128).T)


def get_dense(oproj, mlp, final, proj):
    key = (oproj, mlp, final, proj)
    if key not in _DENSE_CACHE:
        _DENSE_CACHE[key] = build_dense(*key)
    return _DENSE_CACHE[key]


NSLOT = 32
BIGV = 10000.0


def slot_qb(i, half):
    m = i // 2
    if i % 2 == 0:
        return 4 * m + (0 if half == 0 else 1), 4 * m + 1
    return 4 * m + (3 if half == 0 else 2), 4 * m + 3


def build_nsa_attn():
    nc = bass.Bass("TRN2", target_bir_lowering=False)
    din = lambda name, shape, dt=F32: nc.dram_tensor(name, list(shape), dt, kind="ExternalInput")
    kcT_d = din("kcT", [128, S], BF16)
    vcT_d = din("vcT", [128, S], BF16)
    ksT_d = din("ksT", [128, S], BF16)
    kwT_d = din("kwT", [128, S], BF16)
    vs_d = din("vs", [S, 128], BF16)
    vw_d = din("vw", [S, 128], BF16)
    qT_d = din("qT", [128, NSLOT, 512], BF16)
    qrT_d = din("qrT", [128, NSLOT, 512], BF16)
    g3_d = din("g3", [3, NSLOT, 512])
    w1_d = din("w1", [2, 4096, 512])
    w2_d = din("w2", [2, 512, 128])
    posT_d = din("posT", [2, 128, 32])
    identf_d = din("identf", [128, 128])
    i4_d = din("i4", [128, 512], BF16)
    ones_d = din("ones", [128, 128], BF16)
    acon_d = din("acon4", [128, 16, 128], BF16)
    ov_d = din("ov", [128, 4, 128], BF16)
    tailm_d = din("tailm", [128, 4, 128], BF16)
    winm_d = din("winm", [128, 8, 128], BF16)
    cmask_d = din("cmask", [128, NSLOT, 128], BF16)
    slotc_d = din("slotc", [128, NSLOT, 256])
    sel3_d = din("sel3", [3, 3, 128])
    oT_d = nc.dram_tensor("oT", [128, NSLOT, 512], BF16, kind="ExternalOutput")
    import os
    DBGA = int(os.environ.get("DBGA", "0"))
    if DBGA:
        dbg_d = nc.dram_tensor("dbg", [128, 2, 8, 512], F32, kind="ExternalOutput")
    scale = 128.0 ** -0.5

    with ExitStack() as st:
        sb = lambda name, shape, dt: st.enter_context(nc.sbuf_tensor(name, list(shape), dt))
        ksT = sb("ksT_sb", [128, S], BF16)
        kwT = sb("kwT_sb", [128, S], BF16)
        vs = sb("vs_sb", [128, 64, 128], BF16)
        vw = sb("vw_sb", [128, 64, 128], BF16)
        xc = sb("xc", [128, S], BF16)
        w1sb = sb("w1sb", [128, 32, 512], BF16)
        w2sb = sb("w2sb", [128, 4, 128], BF16)
        posT = sb("posT_sb", [128, 32], F32)
        XL = sb("XL", [128, 2, 512], BF16)
        hidT = sb("hidT", [128, 4, 512], BF16)
        tA = sb("tA", [128, 2, 512], F32)
        tB = sb("tB", [128, 2, 512], F32)
        kcmpT = sb("kcmpT", [128, 512], BF16)
        vcmp = sb("vcmp", [128, 4, 128], BF16)
        identf = sb("identf_sb", [128, 128], F32)
        i4 = sb("i4_sb", [128, 512], BF16)
        ones = sb("ones_sb", [128, 128], BF16)
        acon = sb("acon_sb", [128, 16, 128], BF16)
        ov = sb("ov_sb", [128, 4, 128], BF16)
        tailm = sb("tailm_sb", [128, 4, 128], BF16)
        winm = sb("winm_sb", [128, 8, 128], BF16)
        sel3 = sb("sel3_sb", [3, 3, 128], F32)
        qsb = sb("qsb", [128, 2, 512], BF16)
        qrsb = sb("qrsb", [128, 2, 512], BF16)
        g3sb = sb("g3sb", [3, 2, 512], F32)
        cmsb = sb("cmsb", [128, 2, 128], BF16)
        slc = sb("slc", [128, 2, 256], F32)
        Ec = sb("Ec", [128, 4, 512], BF16)
        NE = 3
        Eb = sb("Eb", [128, NE, 512], BF16)
        Pn = sb("Pn", [128, 4, 512], BF16)
        rden = sb("rden", [128, 2, 512], F32)
        wgt = sb("wgt", [128, 512], F32)
        tmp = sb("tmp", [128, 512], F32)
        acc = sb("acc", [128, 2, 512], F32)
        ost = sb("ost", [128, 2, 512], BF16)
        impm = sb("impm", [128, 128], F32)
        impm2 = sb("impm2", [128, 128], F32)
        t8 = sb("t8", [128, 16], F32)
        nsel = sb("nsel", [128, 128], F32)
        nselT = sb("nselT", [128, 2, 4, 128], BF16)
        PS = [st.enter_context(nc.psum_tensor("ps%d" % i, [128, 512], F32)) for i in range(8)]
        Sb, Ob, Db, M0, M1 = (0, 1, 6), (2, 3), (4, 5), 7, 7

        P = Prog(nc)
        ld = lambda eng, dst, src, key, sem: P.op(eng, lambda e: e.dma_start(out=dst, in_=src), writes=[key], dma_sem=sem)
        ld("sp", ksT[:], ksT_d.ap(), "ksT", "l_ks")
        ld("sp", kwT[:], kwT_d.ap(), "kwT", "l_kw")
        ld("sp", vs[:], vs_d.ap().rearrange("(t p) d -> p t d", p=128), "vs", "l_vs")
        ld("sp", vw[:], vw_d.ap().rearrange("(t p) d -> p t d", p=128), "vw", "l_vw")
        ld("act", identf[:], identf_d.ap(), "identf", "l_c0")
        ld("act", i4[:], i4_d.ap(), "i4", "l_c1")
        ld("act", ones[:], ones_d.ap(), "ones", "l_c2")
        ld("act", acon[:], acon_d.ap(), "acon", "l_c3")
        ld("act", ov[:], ov_d.ap(), "ov", "l_c4")
        ld("act", tailm[:], tailm_d.ap(), "tailm", "l_c5")
        ld("act", winm[:], winm_d.ap(), "winm", "l_c6")
        ld("act", sel3[:], sel3_d.ap(), "sel3", "l_c7")
        P.op("dve", lambda e: e.memset(XL[:], 0.0), writes=[("XL", 0), ("XL", 1)])

        GC = math.sqrt(2.0 / math.pi)
        for jv in range(2):
            src_d = kcT_d if jv == 0 else vcT_d
            ld("sp", xc[:], src_d.ap(), "xc", "l_xc")
            for q4 in range(4):
                P.op("pool", lambda e, q4=q4, jv=jv: e.dma_start(
                    out=w1sb[:, q4 * 8:(q4 + 1) * 8, :],
                    in_=w1_d.ap()[jv, q4 * 1024:(q4 + 1) * 1024, :].rearrange("(l p) h -> p l h", p=128)),
                    writes=[("w1", q4)], dma_sem=("l_w1", q4))
            P.op("pool", lambda e, jv=jv: e.dma_start(out=w2sb[:], in_=w2_d.ap()[jv].rearrange("(c p) d -> p c d", p=128)),
                 writes=["w2"], dma_sem="l_w2")
            ld("act", posT[:], posT_d.ap()[jv], "posT", "l_pos")
            for l in range(32):
                xi = P.rr("xl", 2)
                src = bass.AP(xc, l, [[S, 128], [16, 511]])
                P.op("dve", lambda e, xi=xi, src=src, l=l: e.tensor_scalar(
                    out=XL[:, xi, 0:511], in0=src, scalar1=posT[:, l:l + 1], scalar2=None, op0=ALU.add),
                    reads=["xc", "posT"], writes=[("XL", xi)])

                def fn(e, xi=xi, l=l):
                    ins = None
                    for hc in range(4):
                        ins = e.matmul(PS[hc][:], w1sb[:, l, hc * 128:(hc + 1) * 128], XL[:, xi, :],
                                       start=(l == 0), stop=(l == 31))
                    return ins
                P.op("pe", fn, reads=[("XL", xi), ("w1", l // 8)], writes=[("H", hc) for hc in range(4)])
            for hc in range(4):
                ti = P.rr("tAB", 2)
                P.op("act", lambda e, hc=hc, ti=ti: e.activation(out=tA[:, ti, :], in_=PS[hc][:], func=AF.Square),
                     reads=[("H", hc)], writes=[("tA", ti)])
                P.op("dve", lambda e, ti=ti: e.tensor_scalar(out=tA[:, ti, :], in0=tA[:, ti, :], scalar1=0.044715, scalar2=1.0,
                                                             op0=ALU.mult, op1=ALU.add),
                     reads=[("tA", ti)], writes=[("tA", ti)])
                P.op("dve", lambda e, hc=hc, ti=ti: e.tensor_tensor(out=tB[:, ti, :], in0=tA[:, ti, :], in1=PS[hc][:], op=ALU.mult),
                     reads=[("tA", ti), ("H", hc)], writes=[("tB", ti)])
                P.op("act", lambda e, ti=ti: e.activation(out=tB[:, ti, :], in_=tB[:, ti, :], func=AF.Tanh, scale=GC),
                     reads=[("tB", ti)], writes=[("tB", ti)])
                P.op("dve", lambda e, ti=ti: e.tensor_scalar(out=tB[:, ti, :], in0=tB[:, ti, :], scalar1=1.0, scalar2=0.5,
                                                             op0=ALU.add, op1=ALU.mult),
                     reads=[("tB", ti)], writes=[("tB", ti)])
                P.op("dve", lambda e, hc=hc, ti=ti: e.tensor_tensor(out=hidT[:, hc, :], in0=tB[:, ti, :], in1=PS[hc][:], op=ALU.mult),
                     reads=[("tB", ti), ("H", hc)], writes=[("hidT", hc)])
            hid_reads = [("hidT", hc) for hc in range(4)]
            if jv == 0:
                def fn(e):
                    ins = None
                    for hc in range(4):
                        ins = e.matmul(PS[4][:], w2sb[:, hc, :], hidT[:, hc, :], start=(hc == 0), stop=(hc == 3))
                    return ins
                P.op("pe", fn, reads=hid_reads + ["w2"], writes=[("ps", 4)])
                P.op("act", lambda e: e.activation(out=kcmpT[:], in_=PS[4][:], func=AF.Copy), reads=[("ps", 4)], writes=["kcmpT"])
            else:
                def fn(e):
                    ins = None
                    for nt in range(4):
                        for hc in range(4):
                            ins = e.matmul(PS[5][:, nt * 128:(nt + 1) * 128], hidT[:, hc, nt * 128:(nt + 1) * 128], w2sb[:, hc, :],
                                           start=(hc == 0), stop=(hc == 3))
                    return ins
                P.op("pe", fn, reads=hid_reads + ["w2"], writes=[("ps", 5)])
                P.op("act", lambda e: e.activation(out=vcmp[:].rearrange("p a b -> p (a b)"), in_=PS[5][:], func=AF.Copy),
                     reads=[("ps", 5)], writes=["vcmp"])
        ALLPS = [("H", h) for h in range(4)] + [("ps", 4), ("ps", 5)]
        PK = {0: ("S", 0), 1: ("S", 1), 2: ("O", 0), 3: ("O", 1), 4: ("D", 0), 5: ("D", 1), 6: ("S", 2)}
        P.op("pe", lambda e: e.matmul(PS[7][:, 0:128], ones[:], ones[:], start=True, stop=True),
             reads=["ones"], writes=ALLPS + [PK[i] for i in range(7)] + ["M"])

        def branch(tiles, qbuf_key, q_ap, ob, db):
            n = len(tiles)
            slots = {}

            def emit_s(ti_):
                kl, kkey, extra, vl, vkey = tiles[ti_]
                sbk = P.rr("S", 3)
                slots[ti_] = sbk

                def sfn(e, kl=kl, extra=extra, sbk=sbk):
                    ins = e.matmul(PS[Sb[sbk]][:], kl, q_ap, start=True, stop=(len(extra) == 0))
                    for xi_, (l_, r_, tp, _) in enumerate(extra):
                        kw = {} if tp is None else {"tile_position": tp}
                        ins = e.matmul(PS[Sb[sbk]][:], l_, r_, start=False, stop=(xi_ == len(extra) - 1), **kw)
                    return ins
                xkeys = [k for x_ in extra for k in x_[3]]
                P.op("pe", sfn, reads=[kkey, qbuf_key] + xkeys, writes=[("S", sbk)])

            def emit_rest(ti_):
                kl, kkey, extra, vl, vkey = tiles[ti_]
                sbk = slots[ti_]
                ei = P.rr("E", NE)
                P.op("act", lambda e, sbk=sbk, ei=ei: e.activation(out=Eb[:, ei, :], in_=PS[Sb[sbk]][:], func=AF.Exp, scale=scale),
                     reads=[("S", sbk)], writes=[("E", ei)])

                def ofn(e, vl=vl, ei=ei, ti_=ti_):
                    e.matmul(PS[Ob[ob]][:], vl, Eb[:, ei, :], start=(ti_ == 0), stop=(ti_ == n - 1))
                    return e.matmul(PS[Db[db]][:], ones[:], Eb[:, ei, :], start=(ti_ == 0), stop=(ti_ == n - 1))
                P.op("pe", ofn, reads=[vkey, ("E", ei), "ones"], writes=[("O", ob), ("D", db)])

            emit_s(0)
            if n > 1:
                emit_s(1)
            for ti_ in range(n):
                if ti_ + 2 < n:
                    emit_s(ti_ + 2)
                emit_rest(ti_)

        def rden_of(db, rb):
            P.op("dve", lambda e: e.tensor_scalar(out=rden[:, rb, :], in0=PS[Db[db]][:], scalar1=1e-30, scalar2=None, op0=ALU.max),
                 reads=[("D", db)], writes=[("rden", rb)])
            P.op("dve", lambda e: e.reciprocal(out=rden[:, rb, :], in_=rden[:, rb, :]), reads=[("rden", rb)], writes=[("rden", rb)])

        cur_slot = [0]

        def dump(src_ap, key, idx):
            if DBGA and cur_slot[0] < 2:
                sl = cur_slot[0]
                P.op("sp", lambda e: e.dma_start(out=dbg_d.ap()[:, sl, idx, :], in_=src_ap), reads=[key], dma_sem="dbg")

        def combine(bi, ob, rb, qi, ai, first):
            P.op("pe", lambda e: e.matmul(PS[M1][:], sel3[:, bi, :], g3sb[:, qi, :], start=True, stop=True),
                 reads=["sel3", ("g3", qi)], writes=["M"])
            dump(rden[:, rb, :], ("rden", rb), bi * 2)
            P.op("dve", lambda e: e.tensor_tensor(out=wgt[:], in0=rden[:, rb, :], in1=PS[M1][:], op=ALU.mult),
                 reads=[("rden", rb), "M"], writes=["wgt"])
            dump(wgt[:], "wgt", bi * 2 + 1)
            if first:
                P.op("dve", lambda e: e.tensor_tensor(out=acc[:, ai, :], in0=wgt[:], in1=PS[Ob[ob]][:], op=ALU.mult),
                     reads=["wgt", ("O", ob)], writes=[("acc", ai)])
            else:
                P.op("dve", lambda e: e.tensor_tensor(out=tmp[:], in0=wgt[:], in1=PS[Ob[ob]][:], op=ALU.mult),
                     reads=["wgt", ("O", ob)], writes=["tmp"])
                P.op("pool", lambda e: e.tensor_tensor(out=acc[:, ai, :], in0=acc[:, ai, :], in1=tmp[:], op=ALU.add),
                     reads=["tmp", ("acc", ai)], writes=[("acc", ai)])

        def slot_loads(i):
            qi = i % 2
            ld("sp", qsb[:, qi, :], qT_d.ap()[:, i, :], ("q", qi), ("l_q", qi))
            ld("sp", qrsb[:, qi, :], qrT_d.ap()[:, i, :], ("qr", qi), ("l_qr", qi))
            ld("sp", g3sb[:, qi, :], g3_d.ap()[:, i, :], ("g3", qi), ("l_g3", qi))
            ld("sp", cmsb[:, qi, :], cmask_d.ap()[:, i, :], ("cm", qi), ("l_cm", qi))
            ld("sp", slc[:, qi, :], slotc_d.ap()[:, i, :], ("slc", qi), ("l_sl", qi))

        for i in (range(NSLOT) if DBGA == 0 else (range(2) if DBGA == 1 else (1, 2))):
            cur_slot[0] = i if DBGA < 2 else i - 1
            par = i % 2
            qbmax = 4 * (i // 2) + (1 if par == 0 else 3)
            qi = i % 2
            if i == 0 or DBGA:
                slot_loads(i)
            if i + 1 < NSLOT and not DBGA:
                slot_loads(i + 1)
            ai = P.rr("acc", 2)

            nkt = qbmax // 16 + 1
            obc, dbc = P.rr("O", 2), P.rr("D", 2)
            for kc_ in range(nkt):
                sbk = P.rr("S", 3)
                last = kc_ == nkt - 1

                def sfn(e, kc_=kc_, sbk=sbk, last=last, qi=qi):
                    ins = e.matmul(PS[Sb[sbk]][:], kcmpT[:, kc_ * 128:(kc_ + 1) * 128], qsb[:, qi, :], start=True, stop=not last)
                    if last:
                        ins = e.matmul(PS[Sb[sbk]][:], cmsb[:, qi, :], i4[:], start=False, stop=True)
                    return ins
                P.op("pe", sfn, reads=["kcmpT", ("q", qi), ("cm", qi), "i4"], writes=[("S", sbk)])
                P.op("act", lambda e, sbk=sbk, kc_=kc_: e.activation(out=Ec[:, kc_, :], in_=PS[Sb[sbk]][:], func=AF.Exp, scale=scale),
                     reads=[("S", sbk)], writes=[("Ec", kc_)])

                def ofn(e, kc_=kc_, last=last, obc=obc, dbc=dbc):
                    e.matmul(PS[Ob[obc]][:], vcmp[:, kc_, :], Ec[:, kc_, :], start=(kc_ == 0), stop=last)
                    return e.matmul(PS[Db[dbc]][:], ones[:], Ec[:, kc_, :], start=(kc_ == 0), stop=last)
                P.op("pe", ofn, reads=["vcmp", ("Ec", kc_), "ones"], writes=[("O", obc), ("D", dbc)])
            rbc = P.rr("rden", 2)
            rden_of(dbc, rbc)
            for kc_ in range(nkt):
                P.op("pool", lambda e, kc_=kc_, rbc=rbc: e.tensor_tensor(out=Pn[:, kc_, :], in0=Ec[:, kc_, :], in1=rden[:, rbc, :], op=ALU.mult),
                     reads=[("Ec", kc_), ("rden", rbc)], writes=[("Pn", kc_)])

            tiles = []
            for jj in range(6):
                kt = qbmax - 5 + jj
                if kt < 0:
                    continue
                extra = []
                if jj in (0, 1, 4, 5):
                    extra.append((winm[:, par * 4 + (0, 1, None, None, 2, 3)[jj], :], i4[:], None, ["winm", "i4"]))
                tiles.append((kwT[:, kt * 128:(kt + 1) * 128], "kwT", extra, vw[:, kt, :], "vw"))
            obw, dbw = P.rr("O", 2), P.rr("D", 2)
            branch(tiles, ("qr", qi), qrsb[:, qi, :], obw, dbw)
            def ifn(e, nkt=nkt):
                ins = None
                tot = nkt * 4
                c_ = 0
                for kc_ in range(nkt):
                    for g in range(4):
                        ins = e.matmul(PS[M0][:, 0:128], Pn[:, kc_, g * 128:(g + 1) * 128], ov[:, kc_, :],
                                       start=(c_ == 0), stop=(c_ == tot - 1))
                        c_ += 1
                return ins
            P.op("pe", ifn, reads=[("Pn", k) for k in range(nkt)] + ["ov"], writes=["M"])
            P.op("dve", lambda e, qi=qi: e.tensor_tensor(out=impm[:], in0=PS[M0][:, 0:128], in1=slc[:, qi, 0:128], op=ALU.mult),
                 reads=["M", ("slc", qi)], writes=["impm"])
            P.op("dve", lambda e, qi=qi: e.tensor_tensor(out=impm[:], in0=impm[:], in1=slc[:, qi, 128:256], op=ALU.add),
                 reads=["impm", ("slc", qi)], writes=["impm"])
            P.op("dve", lambda e: e.max(out=t8[:, 0:8], in_=impm[:]), reads=["impm"], writes=["t8a"])
            P.op("dve", lambda e: e.match_replace(out=impm2[:], in_to_replace=t8[:, 0:8], in_values=impm[:], imm_value=-1.0e9),
                 reads=["impm", "t8a"], writes=["impm2"])
            P.op("dve", lambda e: e.max(out=t8[:, 8:16], in_=impm2[:]), reads=["impm2"], writes=["t8b"])
            P.op("dve", lambda e: e.tensor_scalar(out=nsel[:], in0=impm[:], scalar1=t8[:, 15:16], scalar2=None, op0=ALU.is_lt),
                 reads=["impm", "t8b"], writes=["nsel"])
            P.op("pe", lambda e: e.transpose(out=PS[M0][:, 128:256], in_=nsel[:], identity=identf[:]),
                 reads=["nsel", "identf"], writes=["M"])
            ni = P.rr("nselT", 2)
            for g in range(4):
                P.op("dve", lambda e, g=g, ni=ni: e.tensor_copy(out=nselT[:, ni, g, :], in_=PS[M0][:, 128:256]),
                     reads=["M"], writes=[("nselT", ni)])
            combine(0, obc, rbc, qi, ai, True)
            rbw = P.rr("rden", 2)
            rden_of(dbw, rbw)
            combine(2, obw, rbw, qi, ai, False)


            tiles = []
            for kt in range(qbmax + 1):
                j, r = kt // 16, kt % 16
                extra = [(acon[32 * j:32 * j + 32, r, :], nselT[32 * j:32 * j + 32, ni, :, :].rearrange("p a b -> p (a b)"),
                          (32 * j, 0), ["acon", ("nselT", ni)])]
                if kt == qbmax - 1:
                    extra.append((tailm[:, par * 2 + 0, :], i4[:], None, ["tailm", "i4"]))
                if kt == qbmax:
                    extra.append((tailm[:, par * 2 + 1, :], i4[:], None, ["tailm", "i4"]))
                tiles.append((ksT[:, kt * 128:(kt + 1) * 128], "ksT", extra, vs[:, kt, :], "vs"))
            obs, dbs = P.rr("O", 2), P.rr("D", 2)
            branch(tiles, ("qr", qi), qrsb[:, qi, :], obs, dbs)
            rbs = P.rr("rden", 2)
            rden_of(dbs, rbs)
            combine(1, obs, rbs, qi, ai, False)

            dump(acc[:, ai, :], ("acc", ai), 6)
            oi = P.rr("ost", 2)
            P.op("act", lambda e, oi=oi, ai=ai: e.activation(out=ost[:, oi, :], in_=acc[:, ai, :], func=AF.Copy),
                 reads=[("acc", ai)], writes=[("ost", oi)])
            P.op("sp", lambda e, oi=oi, i=i: e.dma_start(out=oT_d.ap()[:, i, :], in_=ost[:, oi, :]),
                 reads=[("ost", oi)], dma_sem=("st_o", oi))
        P.emit(st, final_waits=[("st_o", 0), ("st_o", 1), "dbg"])
    return nc


def nsa_attn_consts(half):
    c = {}
    c["identf"] = np.eye(128, dtype=np.float32)
    c["i4"] = np.tile(np.eye(128, dtype=np.float32), (1, 4)).astype(NPBF)
    c["ones"] = np.ones((128, 128), np.float32).astype(NPBF)
    p = np.arange(128)
    k = np.arange(128)
    acon = np.zeros((128, 16, 128), np.float32)
    for r in range(16):
        acon[:, r, :] = np.where((p[:, None] % 32) == 2 * r + (k[None, :] >= 64), NEG, 0.0)
    c["acon4"] = acon.astype(NPBF)
    ov = np.zeros((128, 4, 128), np.float32)
    for kt in range(4):
        n = 128 * kt + p
        cs = n * 16
        ss = np.arange(128) * 64
        o = (cs[:, None] < ss[None, :] + 64) & (cs[:, None] + 32 > ss[None, :]) & (n[:, None] <= 510)
        ov[:, kt, :] = o
    c["ov"] = ov.astype(NPBF)
    q = p[:, None]
    kk = k[None, :]
    zero = np.zeros((128, 128), np.float32)
    allneg = np.full((128, 128), NEG, np.float32)
    caus = np.where(kk <= q, 0.0, NEG).astype(np.float32)
    winold = np.where(kk > q, 0.0, NEG).astype(np.float32)
    tail = np.zeros((128, 4, 128), np.float32)
    winm = np.zeros((128, 8, 128), np.float32)
    for par in range(2):
        higher = (par == 1) if half == 0 else (par == 0)
        if higher:
            tail[:, par * 2 + 0] = zero
            tail[:, par * 2 + 1] = caus
            w = [allneg, winold, zero, caus]
        else:
            tail[:, par * 2 + 0] = caus
            tail[:, par * 2 + 1] = allneg
            w = [winold, zero, caus, allneg]
        for x_ in range(4):
            winm[:, par * 4 + x_] = w[x_]
    c["tailm"] = tail.astype(NPBF)
    c["winm"] = winm.astype(NPBF)
    cmask = np.zeros((128, NSLOT, 128), np.float32)
    slotc = np.zeros((128, NSLOT, 256), np.float32)
    s_ = np.arange(128)[None, :]
    for i in range(NSLOT):
        qb, qbmax = slot_qb(i, half)
        t = 128 * qb + p[:, None]
        ktc = qbmax // 16
        n = 128 * ktc + k[None, :]
        cmask[:, i, :] = np.where(16 * n + 31 <= t, 0.0, NEG)
        cur = t // 64
        m1 = np.ones((128, 128), np.float32)
        m2 = np.zeros((128, 128), np.float32)
        f0 = (s_ == 0) & (s_ <= cur)
        m1[np.broadcast_to(f0, m1.shape)] = 0.0
        m2[np.broadcast_to(f0, m2.shape)] = BIGV + 2
        fp = (s_ == cur - 1)
        m1[fp] = 0.0
        m2[fp] = BIGV + 1
        fc = (s_ == cur)
        m1[fc] = 0.0
        m2[fc] = BIGV
        nc_ = s_ > cur
        m1[nc_] = 0.0
        m2[nc_] = (-1.0 - np.broadcast_to(s_, m2.shape))[nc_]
        slotc[:, i, 0:128] = m1
        slotc[:, i, 128:256] = m2
    c["cmask"] = cmask.astype(NPBF)
    c["slotc"] = slotc
    sel3 = np.zeros((3, 3, 128), np.float32)
    for b in range(3):
        sel3[b, b, :] = 1.0
    c["sel3"] = sel3
    return c


_PROG = {}
_IDENT = np.eye(128, dtype=np.float32)


def _run(nc, maps):
    res = run_bass_kernel_spmd(nc, maps, core_ids=list(range(NCORES)))
    return res.results


def _cat(res, name, axis):
    return np.concatenate([np.asarray(r[name]) for r in res], axis=axis)


def dense_maps(xs, inp, layer_done, oT_full, proj, layer_next):
    maps = []
    for c in range(NCORES):
        m = {"x": xs[c], "ident": _IDENT}
        if layer_done is not None:
            L = layer_done
            m["oT"] = np.ascontiguousarray(oT_full[:, c * TPC:(c + 1) * TPC])
            m["w_o"] = inp["nsa_w_out"][L] if L < 2 else inp["diff_w_out"][L - 2]
            m["g_mlp"] = gain_fm(inp["mlp_norm_g"][L])
            m["w_up"] = inp["mlp_w_up"][L]
            m["w_down"] = inp["mlp_w_down"][L]
        if proj is None:
            m["g_final"] = np.asarray(inp["final_norm_g"], np.float32)
        else:
            m["g_attn"] = gain_fm(inp["attn_norm_g"][layer_next])
            C, Sn, RT = rope_consts(128 if proj == "nsa" else 64, c * TPC, TPC)
            m["cosT"], m["sinT"], m["rotT"] = C, Sn, RT
            if proj == "nsa":
                m["w_in"] = inp["nsa_w_in"][layer_next]
            else:
                m["w_q"] = inp["diff_w_q"][layer_next - 2]
            if proj == "diffkv":
                m["g_kv"] = gain_fm(inp["kv_norm_g"])
                m["w_kv"] = inp["kv_w_shared"]
        maps.append(m)
    return maps


def nsa_attn_maps(res, inp, layer):
    qT = _cat(res, "qT", 1).reshape(16, 128, 64, 128)
    qrT = _cat(res, "qrT", 1).reshape(16, 128, 64, 128)
    kcT, vcT = _cat(res, "kcT", 1), _cat(res, "vcT", 1)
    ksT, kwT = _cat(res, "ksT", 1), _cat(res, "kwT", 1)
    vs, vw = _cat(res, "vs", 0), _cat(res, "vw", 0)
    gT = _cat(res, "gT", 1)
    maps = []
    for c in range(NCORES):
        hk, half = c // 2, c % 2
        qbs = [slot_qb(i, half)[0] for i in range(NSLOT)]
        m = dict(nsa_attn_consts(half))
        rs = slice(hk * 128, (hk + 1) * 128)
        m["kcT"] = np.ascontiguousarray(kcT[rs])
        m["vcT"] = np.ascontiguousarray(vcT[rs])
        m["ksT"] = np.ascontiguousarray(ksT[rs])
        m["kwT"] = np.ascontiguousarray(kwT[rs])
        m["vs"] = np.ascontiguousarray(vs[:, rs])
        m["vw"] = np.ascontiguousarray(vw[:, rs])
        m["qT"] = np.ascontiguousarray(qT[4 * hk:4 * hk + 4][:, :, qbs, :].transpose(1, 2, 0, 3)).reshape(128, NSLOT, 512)
        m["qrT"] = np.ascontiguousarray(qrT[4 * hk:4 * hk + 4][:, :, qbs, :].transpose(1, 2, 0, 3)).reshape(128, NSLOT, 512)
        gv = gT[hk * 12:(hk + 1) * 12].reshape(4, 3, 64, 128)[:, :, qbs, :]
        m["g3"] = np.ascontiguousarray(gv.transpose(1, 2, 0, 3)).reshape(3, NSLOT, 512)
        m["w1"] = inp["nsa_cmp_w1"][layer]
        m["w2"] = inp["nsa_cmp_w2"][layer]
        m["posT"] = np.ascontiguousarray(np.asarray(inp["nsa_cmp_pos"][layer]).transpose(0, 2, 1))
        maps.append(m)
    return maps


def nsa_attn_gather(res):
    oT = np.zeros((16, 128, 64, 128), NPBF)
    for c in range(NCORES):
        hk, half = c // 2, c % 2
        qbs = [slot_qb(i, half)[0] for i in range(NSLOT)]
        o = np.asarray(res[c]["oT"]).reshape(128, NSLOT, 4, 128).transpose(2, 0, 1, 3)
        oT[4 * hk:4 * hk + 4][:, :, qbs, :] = o
    return oT.reshape(2048, 8192)


def build_diff_attn():
    nc = bass.Bass("TRN2", target_bir_lowering=False)
    din = lambda name, shape, dt=F32: nc.dram_tensor(name, list(shape), dt, kind="ExternalInput")
    kT_d = din("kT", [128, S], BF16)
    v_d = din("v", [S, 128], BF16)
    qT_d = din("qT", [128, 64, 512], BF16)
    lamv_d = din("lamv", [128, 256])
    g_d = din("subg", [128, 1])
    linit_d = din("linit", [128, 2])
    i4_d = din("i4", [128, 512], BF16)
    ones_d = din("ones", [128, 128], BF16)
    onesf_d = din("onesf", [128, 128])
    caus_d = din("causT", [128, 128], BF16)
    oT_d = nc.dram_tensor("oT", [128, 64, 256], BF16, kind="ExternalOutput")
    scale = 64.0 ** -0.5
    with ExitStack() as st:
        sb = lambda name, shape, dt: st.enter_context(nc.sbuf_tensor(name, list(shape), dt))
        kT = sb("kT_sb", [128, S], BF16)
        v = sb("v_sb", [128, 64, 128], BF16)
        qsb = sb("q_sb", [128, 64, 512], BF16)
        lamv = sb("lamv_sb", [128, 256], F32)
        gcol = sb("gcol", [128, 1], F32)
        linit = sb("linit_sb", [128, 2], F32)
        i4 = sb("i4_sb", [128, 512], BF16)
        ones = sb("ones_sb", [128, 128], BF16)
        onesf = sb("onesf_sb", [128, 128], F32)
        caus = sb("caus_sb", [128, 128], BF16)
        lw = sb("lw", [128, 128], F32)
        ls = sb("ls", [128, 4], F32)
        nlam = sb("nlam", [128, 1], F32)
        gsc = sb("gsc", [128, 1], F32)
        NE = 3
        Eb = sb("Eb", [128, NE, 512], BF16)
        rden = sb("rden", [128, 512], F32)
        A = sb("A", [128, 512], F32)
        o = sb("o", [128, 256], F32)
        sq = sb("sq", [128, 256], F32)
        rs = sb("rs", [128, 256], F32)
        on = sb("on", [128, 256], F32)
        ost = sb("ost", [128, 2, 256], BF16)
        PS = [st.enter_context(nc.psum_tensor("ps%d" % i, [128, 512], F32)) for i in range(8)]
        Sb, Ob, Db, M0 = (0, 1, 2), (3, 4), (5, 6), 7
        P = Prog(nc)
        ld = lambda eng, dst, src, key, sem: P.op(eng, lambda e: e.dma_start(out=dst, in_=src), writes=[key], dma_sem=sem)
        ld("sp", kT[:], kT_d.ap(), "kT", "l_k")
        ld("sp", v[:], v_d.ap().rearrange("(t p) d -> p t d", p=128), "v", "l_v")
        ld("sp", qsb[:], qT_d.ap(), "q", "l_q")
        ld("act", lamv[:], lamv_d.ap(), "lamv", "l_c0")
        ld("act", gcol[:], g_d.ap(), "gcol", "l_c1")
        ld("act", linit[:], linit_d.ap(), "linit", "l_c2")
        ld("act", i4[:], i4_d.ap(), "i4", "l_c3")
        ld("act", ones[:], ones_d.ap(), "ones", "l_c4")
        ld("act", onesf[:], onesf_d.ap(), "onesf", "l_c5")
        ld("act", caus[:], caus_d.ap(), "caus", "l_c6")
        P.op("dve", lambda e: e.tensor_tensor(out=lw[:, 0:64], in0=lamv[:, 0:64], in1=lamv[:, 64:128], op=ALU.mult),
             reads=["lamv"], writes=["lw0"])
        P.op("dve", lambda e: e.tensor_tensor(out=lw[:, 64:128], in0=lamv[:, 128:192], in1=lamv[:, 192:256], op=ALU.mult),
             reads=["lamv"], writes=["lw1"])
        P.op("dve", lambda e: e.reduce_sum(out=ls[:, 0:1], in_=lw[:, 0:64], axis=AX.X), reads=["lw0"], writes=["ls0"])
        P.op("dve", lambda e: e.reduce_sum(out=ls[:, 1:2], in_=lw[:, 64:128], axis=AX.X), reads=["lw1"], writes=["ls1"])
        P.op("act", lambda e: e.activation(out=ls[:, 2:4], in_=ls[:, 0:2], func=AF.Exp), reads=["ls0", "ls1"], writes=["ls23"])
        P.op("dve", lambda e: e.tensor_tensor(out=nlam[:], in0=ls[:, 3:4], in1=ls[:, 2:3], op=ALU.subtract),
             reads=["ls23"], writes=["nlam"])
        P.op("dve", lambda e: e.tensor_tensor(out=nlam[:], in0=nlam[:], in1=linit[:, 0:1], op=ALU.subtract),
             reads=["nlam", "linit"], writes=["nlam"])
        P.op("dve", lambda e: e.tensor_tensor(out=gsc[:], in0=gcol[:], in1=linit[:, 1:2], op=ALU.mult),
             reads=["gcol", "linit"], writes=["gsc"])

        import os
        NQ = int(os.environ.get("DIFFNQ", "64"))
        DSTEP = int(os.environ.get("DSTEP", "9"))
        for qb in range(NQ):
            ob, db = P.rr("O", 2), P.rr("D", 2)
            n = qb + 1
            sl_ = {}

            def emit_s(kt, qb=qb):
                sbk = P.rr("S", 3)
                sl_[kt] = sbk
                diag = kt == qb

                def sfn(e, kt=kt, sbk=sbk, diag=diag, qb=qb):
                    ksl = slice(kt * 128, (kt + 1) * 128)
                    ins = e.matmul(PS[Sb[sbk]][:], kT[:, ksl], qsb[:, qb, :], start=True, stop=not diag)
                    if diag:
                        ins = e.matmul(PS[Sb[sbk]][:], caus[:], i4[:], start=False, stop=True)
                    return ins
                P.op("pe", sfn, reads=["kT", "q", "caus", "i4"], writes=[("S", sbk)])

            def emit_rest(kt, n=n, ob=ob, db=db):
                sbk = sl_[kt]
                ei = P.rr("E", NE)
                P.op("act", lambda e, sbk=sbk, ei=ei: e.activation(out=Eb[:, ei, :], in_=PS[Sb[sbk]][:], func=AF.Exp, scale=scale),
                     reads=[("S", sbk)], writes=[("E", ei)])

                def ofn(e, kt=kt, ei=ei, n=n, ob=ob, db=db):
                    e.matmul(PS[Ob[ob]][:], v[:, kt, :], Eb[:, ei, :], start=(kt == 0), stop=(kt == n - 1))
                    return e.matmul(PS[Db[db]][:], ones[:], Eb[:, ei, :], start=(kt == 0), stop=(kt == n - 1))
                P.op("pe", ofn, reads=["v", ("E", ei), "ones"], writes=[("O", ob), ("D", db)])

            emit_s(0)
            if n > 1:
                emit_s(1)
            for kt in range(n):
                if kt + 2 < n:
                    emit_s(kt + 2)
                emit_rest(kt)
            P.op("dve", lambda e, db=db: e.tensor_scalar(out=rden[:], in0=PS[Db[db]][:], scalar1=1e-30, scalar2=None, op0=ALU.max),
                 reads=[("D", db)], writes=["rden"])
            P.op("dve", lambda e: e.reciprocal(out=rden[:], in_=rden[:]), reads=["rden"], writes=["rden"])
            P.op("dve", lambda e, ob=ob: e.tensor_tensor(out=A[:], in0=rden[:], in1=PS[Ob[ob]][:], op=ALU.mult),
                 reads=["rden", ("O", ob)], writes=["A"])
            P.op("dve", lambda e: e.scalar_tensor_tensor(out=o[:], in0=A[:, 256:512], scalar=nlam[:, 0:1], in1=A[:, 0:256],
                                                         op0=ALU.mult, op1=ALU.add),
                 reads=["A", "nlam"], writes=["o"])
            P.op("pool", lambda e: e.tensor_tensor(out=sq[:], in0=o[:], in1=o[:], op=ALU.mult), reads=["o"], writes=["sq"])
            P.op("pe", lambda e: e.matmul(PS[M0][:, 0:256], onesf[:], sq[:], start=True, stop=True),
                 reads=["sq", "onesf"], writes=["M0"])
            P.op("dve", lambda e: e.tensor_scalar(out=rs[:], in0=PS[M0][:, 0:256], scalar1=1.0 / 128, scalar2=EPS,
                                                  op0=ALU.mult, op1=ALU.add), reads=["M0"], writes=["rs"])
            P.op("act", lambda e: e.activation(out=rs[:], in_=rs[:], func=AF.Ln), reads=["rs"], writes=["rs"])
            P.op("act", lambda e: e.activation(out=rs[:], in_=rs[:], func=AF.Exp, scale=-0.5), reads=["rs"], writes=["rs"])
            P.op("dve", lambda e: e.tensor_tensor(out=on[:], in0=o[:], in1=rs[:], op=ALU.mult), reads=["o", "rs"], writes=["on"])
            oi = P.rr("ost", 2)
            P.op("act", lambda e, oi=oi: e.activation(out=ost[:, oi, :], in_=on[:], func=AF.Identity, scale=gsc[:, 0:1]),
                 reads=["on", "gsc"], writes=[("ost", oi)])
            P.op("sp", lambda e, oi=oi, qb=qb: e.dma_start(out=oT_d.ap()[:, qb, :], in_=ost[:, oi, :]),
                 reads=[("ost", oi)], dma_sem=("st_o", oi))
        P.emit(st, final_waits=[("st_o", 0), ("st_o", 1)])
    return nc


def diff_attn_maps(dqT, dkT, dv, inp, layer):
    j = layer - 2
    li = 0.8 - 0.6 * math.exp(-0.3 * layer)
    q4 = dqT.reshape(16, 128, 64, 128)
    p = np.arange(128)
    consts = {
        "i4": np.tile(np.eye(128, dtype=np.float32), (1, 4)).astype(NPBF),
        "ones": np.ones((128, 128), np.float32).astype(NPBF),
        "onesf": np.ones((128, 128), np.float32),
        "causT": np.where(p[None, :] <= p[:, None], 0.0, NEG).astype(np.float32).astype(NPBF),
        "lamv": np.ascontiguousarray(np.broadcast_to(np.asarray(inp["diff_lambda"][j], np.float32).reshape(1, 256), (128, 256))),
        "subg": np.ascontiguousarray(np.asarray(inp["diff_subln_g"][j], np.float32).reshape(128, 1)),
        "linit": np.ascontiguousarray(np.broadcast_to(np.array([[li, 1.0 - li]], np.float32), (128, 2))),
    }
    maps = []
    for c in range(NCORES):
        hk = c // 2
        m = dict(consts)
        m["kT"] = np.ascontiguousarray(dkT[hk * 128:(hk + 1) * 128])
        m["v"] = np.ascontiguousarray(dv[:, hk * 128:(hk + 1) * 128])
        qq = np.ascontiguousarray(q4[2 * c:2 * c + 2].transpose(1, 2, 0, 3))
        qz = np.zeros((128, 64, 2, 2, 128), NPBF)
        qz[0:64, :, 0] = qq[0:64]
        qz[64:128, :, 1] = qq[64:128]
        m["qT"] = qz.reshape(128, 64, 512)
        maps.append(m)
    return maps


def diff_attn_gather(res):
    oT = np.zeros((16, 128, 64, 128), NPBF)
    for c in range(NCORES):
        o = np.asarray(res[c]["oT"]).reshape(128, 64, 2, 128).transpose(2, 0, 1, 3)
        oT[2 * c:2 * c + 2] = o
    return oT.reshape(2048, 8192)


def kernel(x, attn_norm_g, mlp_norm_g, final_norm_g, nsa_w_in, nsa_cmp_pos, nsa_cmp_w1, nsa_cmp_w2, nsa_w_out,
           kv_norm_g, kv_w_shared, diff_w_q, diff_lambda, diff_subln_g, diff_w_out, mlp_w_up, mlp_w_down, _debug=None):
    inp = dict(attn_norm_g=attn_norm_g, mlp_norm_g=mlp_norm_g, final_norm_g=final_norm_g, nsa_w_in=nsa_w_in,
               nsa_cmp_pos=nsa_cmp_pos, nsa_cmp_w1=nsa_cmp_w1, nsa_cmp_w2=nsa_cmp_w2, nsa_w_out=nsa_w_out,
               kv_norm_g=kv_norm_g, kv_w_shared=kv_w_shared, diff_w_q=diff_w_q, diff_lambda=diff_lambda,
               diff_subln_g=diff_subln_g, diff_w_out=diff_w_out, mlp_w_up=mlp_w_up, mlp_w_down=mlp_w_down)
    inp = {k: np.asarray(v, np.float32) for k, v in inp.items()}
    x2 = np.asarray(x, np.float32).reshape(S, D)
    xs = [np.ascontiguousarray(x2[c * TPC:(c + 1) * TPC]) for c in range(NCORES)]
    if "nsa" not in _PROG:
        _PROG["nsa"] = build_nsa_attn()
        _PROG["diff"] = build_diff_attn()
    dbg = {}
    res = _run(get_dense(False, False, False, "nsa"), dense_maps(xs, inp, None, None, "nsa", 0))
    dkT = dv = None
    for layer in range(4):
        if layer < 2:
            ra = _run(_PROG["nsa"], nsa_attn_maps(res, inp, layer))
            oT = nsa_attn_gather(ra)
        else:
            if layer == 2:
                dkT, dv = _cat(res, "dkT", 1), _cat(res, "dv", 0)
            ra = _run(_PROG["diff"], diff_attn_maps(_cat(res, "dqT", 1), dkT, dv, inp, layer))
            oT = diff_attn_gather(ra)
        if _debug is not None:
            dbg["oT%d" % layer] = oT
        nxt = [("nsa", 1), ("diffkv", 2), ("diff", 3), (None, None)][layer]
        res = _run(get_dense(True, True, nxt[0] is None, nxt[0]), dense_maps(xs, inp, layer, oT, nxt[0], nxt[1]))
        if nxt[0] is not None:
            xs = [np.asarray(r["x_out"]) for r in res]
            if _debug is not None:
                dbg["x%d" % layer] = np.concatenate(xs, 0)
    y = np.concatenate([np.asarray(r["y"]) for r in res], 0).reshape(1, S, D).astype(np.float32)
    if _debug is not None:
        _debug.update(dbg)
    return y
```

```python
import math
from contextlib import ExitStack

import numpy as np
import ml_dtypes
import concourse.bass as bass
import concourse.mybir as mybir
from concourse.bass_utils import run_bass_kernel_spmd

F32 = mybir.dt.float32
BF16 = mybir.dt.bfloat16
AF = mybir.ActivationFunctionType
ALU = mybir.AluOpType
AX = mybir.AxisListType
NPBF = ml_dtypes.bfloat16

NCORES = 8
S = 8192
D = 2048
TPC = S // NCORES
NTT = TPC // 128
EPS = 1e-6
NEG = -30000.0
ROPE_THETA = 500000.0

ENGS = ("pe", "act", "dve", "pool", "sp")


class Op:
    __slots__ = ("eng", "fn", "deps", "signalled", "sig", "dma_sem", "dma_val")


class Prog:
    def __init__(self, nc):
        self.nc = nc
        self.ops = []
        self.last_w = {}
        self.readers = {}
        self.dma_cum = {}
        self.rot = {}
        self.dry = False

    def rr(self, name, n):
        if self.dry:
            return 0
        i = self.rot.get(name, 0)
        self.rot[name] = i + 1
        return i % n

    def op(self, eng, fn, reads=(), writes=(), dma_sem=None, ndma=1):
        if self.dry:
            return None
        o = Op()
        o.eng = eng
        o.fn = fn
        o.signalled = False
        o.sig = 0
        o.dma_sem = dma_sem
        o.dma_val = 0
        deps = set()
        for k in reads:
            w = self.last_w.get(k)
            if w is not None:
                deps.add(w)
        for k in writes:
            w = self.last_w.get(k)
            if w is not None:
                deps.add(w)
            for r in self.readers.get(k, ()):
                deps.add(r)
        o.deps = [d for d in deps
                  if not (d.eng == "pe" and eng == "pe" and d.dma_sem is None and dma_sem is None)]
        if dma_sem is not None:
            self.dma_cum[dma_sem] = self.dma_cum.get(dma_sem, 0) + 16 * ndma
            o.dma_val = self.dma_cum[dma_sem]
        for d in o.deps:
            if d.dma_sem is None:
                d.signalled = True
        for k in writes:
            self.last_w[k] = o
            self.readers[k] = []
        for k in reads:
            if k not in writes:
                self.readers.setdefault(k, []).append(o)
        self.ops.append(o)
        return o

    def emit(self, stack, final_waits=()):
        nc = self.nc
        cnt = {e: 0 for e in ENGS}
        for o in self.ops:
            if o.dma_sem is None and o.signalled:
                cnt[o.eng] += 1
                o.sig = cnt[o.eng]
        esem = {e: stack.enter_context(nc.semaphore("s_" + e)) for e in ENGS}
        dsem = {}
        for i, k in enumerate(self.dma_cum):
            dsem[k] = stack.enter_context(nc.semaphore("d%d" % i))
        block = stack.enter_context(nc.Block())
        per = {e: [o for o in self.ops if o.eng == e] for e in ENGS}
        dma_cum = self.dma_cum

        def run(e, eng):
            waited = {}
            for o in per[e]:
                need = {}
                for d in o.deps:
                    if d.dma_sem is not None:
                        key = ("d", d.dma_sem)
                        v = d.dma_val
                    else:
                        key = ("e", d.eng)
                        v = d.sig
                    if v > need.get(key, 0):
                        need[key] = v
                for key, v in need.items():
                    if v > waited.get(key, 0):
                        sem = dsem[key[1]] if key[0] == "d" else esem[key[1]]
                        eng.wait_ge(sem, v)
                        waited[key] = v
                r = o.fn(eng)
                if o.dma_sem is not None:
                    rs = r if isinstance(r, (list, tuple)) else [r]
                    for ins in rs:
                        ins.then_inc(dsem[o.dma_sem], 16)
                elif o.signalled:
                    r.then_inc(esem[e], 1)
            if e == "sp":
                for k in final_waits:
                    if k in dsem:
                        eng.wait_ge(dsem[k], dma_cum[k])

        @block.tensor
        def _(eng):
            run("pe", eng)

        @block.scalar
        def _(eng):
            run("act", eng)

        @block.vector
        def _(eng):
            run("dve", eng)

        @block.gpsimd
        def _(eng):
            run("pool", eng)

        @block.sync
        def _(eng):
            run("sp", eng)


def rope_consts(head_chunk, tok0, ntok):
    rot = head_chunk // 4
    half = rot // 2
    inv = 1.0 / (ROPE_THETA ** (np.arange(0, rot, 2, dtype=np.float32) / np.float32(rot)))
    inv = inv.astype(np.float32)
    pos = np.arange(tok0, tok0 + ntok, dtype=np.float32)
    ang = (pos[None, :] * inv[:, None]).astype(np.float32)
    cos = np.cos(ang).astype(np.float32)
    sin = np.sin(ang).astype(np.float32)
    C = np.ones((128, ntok), np.float32)
    Sn = np.zeros((128, ntok), np.float32)
    Rm = np.zeros((128, 128), np.float32)
    for base in range(0, 128, head_chunk):
        for j in range(half):
            C[base + j] = cos[j]
            C[base + half + j] = cos[j]
            Sn[base + j] = sin[j]
            Sn[base + half + j] = sin[j]
            Rm[base + j, base + half + j] = -1.0
            Rm[base + half + j, base + j] = 1.0
    return C, Sn, np.ascontiguousarray(Rm.T)


def build_dense(oproj, mlp, final, proj):
    nc = bass.Bass("TRN2", target_bir_lowering=False)
    T = TPC
    dr = {}

    def din(name, shape, dt=F32):
        dr[name] = nc.dram_tensor(name, list(shape), dt, kind="ExternalInput")
        return dr[name]

    def dout(name, shape, dt=F32):
        dr[name] = nc.dram_tensor(name, list(shape), dt, kind="ExternalOutput")
        return dr[name]

    x_d = din("x", [T, D])
    ident_d = din("ident", [128, 128])
    if oproj:
        oT_d = din("oT", [D, T], BF16)
        wo_d = din("w_o", [D, D])
    if mlp:
        gm_d = din("g_mlp", [128, 16])
        wu_d = din("w_up", [D, 4 * D])
        wd_d = din("w_down", [4 * D, D])
    if final:
        gf_d = din("g_final", [D])
        y_d = dout("y", [T, D])
    else:
        xo_d = dout("x_out", [T, D])
    if proj:
        ga_d = din("g_attn", [128, 16])
        cos_d = din("cosT", [128, T])
        sin_d = din("sinT", [128, T])
        rt_d = din("rotT", [128, 128])
    if proj == "nsa":
        win_d = din("w_in", [D, 5168])
        qT_d = dout("qT", [D, T], BF16)
        qrT_d = dout("qrT", [D, T], BF16)
        kcT_d = dout("kcT", [512, T], BF16)
        vcT_d = dout("vcT", [512, T], BF16)
        ksT_d = dout("ksT", [512, T], BF16)
        kwT_d = dout("kwT", [512, T], BF16)
        vs_d = dout("vs", [T, 512], BF16)
        vw_d = dout("vw", [T, 512], BF16)
        gT_d = dout("gT", [48, T])
    if proj in ("diff", "diffkv"):
        wq_d = din("w_q", [D, D])
        dqT_d = dout("dqT", [D, T], BF16)
    if proj == "diffkv":
        gk_d = din("g_kv", [128, 16])
        wkv_d = din("w_kv", [D, 1024])
        dkT_d = dout("dkT", [512, T], BF16)
        dv_d = dout("dv", [T, 512], BF16)

    with ExitStack() as st:
        sb = lambda name, shape, dt: st.enter_context(nc.sbuf_tensor(name, list(shape), dt))
        x = sb("x_sb", [128, NTT, D], F32)
        big1 = sb("big1", [128, 16, T], BF16)
        big2 = sb("big2", [128, 16, T], BF16)
        NWB = 2
        wb = sb("wb", [128, NWB, 16, 512], BF16)
        wst = sb("wst", [128, 2, 4, 512], F32)
        ident = sb("ident_sb", [128, 128], F32)
        xh = sb("xh", [128, 1, D], F32)
        ssq = sb("ssq", [128, 32], F32)
        rstd = sb("rstd", [128, 32], F32)
        gT = sb("gT_sb", [128, 4, 16], F32)
        NST = 4
        stg = sb("stg", [128, NST, 512], BF16)
        r32 = sb("r32", [128, 2, 512], F32)
        if proj:
            cosT = sb("cos_sb", [128, T], F32)
            sinT = sb("sin_sb", [128, T], F32)
            rotT = sb("rot_sb", [128, 128], F32)
            t1 = sb("t1", [128, 1, 512], F32)
            t2 = sb("t2", [128, 1, 512], F32)
            gst = sb("gst", [48, 1, 512], F32)
        if final:
            gfb = sb("gfb", [128, D], F32)
        psb = [st.enter_context(nc.psum_tensor("ps%d" % i, [128, 512], F32)) for i in range(8)]

        P = Prog(nc)
        out_sems = []
        B2 = [("big2", fc) for fc in range(16)]

        P.op("sp", lambda e: e.dma_start(out=x[:], in_=x_d.ap().rearrange("(t p) c -> p t c", p=128)),
             writes=[("x", t) for t in range(NTT)], dma_sem="ldx")
        P.op("act", lambda e: e.dma_start(out=ident[:], in_=ident_d.ap()), writes=["ident"], dma_sem="ldc")
        gains = {}

        def load_gain(name, d):
            gi = len(gains)
            gains[name] = gi
            P.op("act", lambda e: e.dma_start(out=gT[:, gi, :], in_=d.ap()),
                 writes=[("gain", gi)], dma_sem="ldg")

        if mlp:
            load_gain("mlp", gm_d)
        if proj:
            load_gain("attn", ga_d)
            P.op("act", lambda e: e.dma_start(out=cosT[:], in_=cos_d.ap()), writes=["cos"], dma_sem="ldc")
            P.op("act", lambda e: e.dma_start(out=sinT[:], in_=sin_d.ap()), writes=["sin"], dma_sem="ldc")
            P.op("act", lambda e: e.dma_start(out=rotT[:], in_=rt_d.ap()), writes=["rot"], dma_sem="ldc")
        if proj == "diffkv":
            load_gain("kv", gk_d)
        if final:
            P.op("act", lambda e: e.dma_start(out=gfb[:], in_=gf_d.ap().partition_broadcast(128)),
                 writes=["gfb"], dma_sem="ldc")
        if oproj:
            P.op("sp", lambda e: e.dma_start(out=big2[:], in_=oT_d.ap().rearrange("(k p) t -> p k t", p=128)),
                 writes=B2, dma_sem="ldo")

        wlist = []
        wpos = [0]

        def issue_wblock(n):
            wd, r0, c0, ncols = wlist[n]
            i = n % NWB
            for qq in range(4):
                si = P.rr("wst", 2)
                src = wd.ap()[r0 + qq * 512:r0 + (qq + 1) * 512, c0:c0 + ncols].rearrange("(k p) c -> p k c", p=128)
                P.op("sp", lambda e, si=si, src=src: e.dma_start(out=wst[:, si, :, 0:ncols], in_=src),
                     writes=[("wst", si)], dma_sem=("wst", si))
                P.op("pool", lambda e, si=si, qq=qq: e.tensor_copy(out=wb[:, i, qq * 4:(qq + 1) * 4, 0:ncols],
                                                                   in_=wst[:, si, :, 0:ncols]),
                     reads=[("wst", si)], writes=[("wb", i)])

        def load_wblock(wd, r0, c0, ncols=512):
            if P.dry:
                wlist.append((wd, r0, c0, ncols))
                return 0
            n = wpos[0]
            wpos[0] += 1
            if n == 0:
                issue_wblock(0)
            if n + 1 < len(wlist):
                issue_wblock(n + 1)
            return n % NWB

        def next_ps():
            return P.rr("ps", 4)

        def mm_group(pb, pso, pairs, reads):
            def fn(e):
                n = len(pairs)
                ins = None
                for i, (l, r) in enumerate(pairs):
                    ins = e.matmul(pso, l, r, start=(i == 0), stop=(i == n - 1))
                return ins
            P.op("pe", fn, reads=reads, writes=[("ps", pb)])

        def resid_add(pb, tt, cc):
            xs = x[:, tt, cc * 512:(cc + 1) * 512]
            P.op("dve", lambda e: e.tensor_tensor(out=xs, in0=xs, in1=psb[pb][:], op=ALU.add),
                 reads=[("ps", pb), ("x", tt)], writes=[("x", tt)])

        nstat = [0]
        for PASS in (0, 1):
            P.dry = (PASS == 0)
            nstat[0] = 0
            if oproj:
                for cc in range(4):
                    wi = load_wblock(wo_d, 0, cc * 512)
                    for tt in range(NTT):
                        pb = next_ps()
                        mm_group(pb, psb[pb][:], [(big2[:, kc, tt * 128:(tt + 1) * 128], wb[:, wi, kc, :]) for kc in range(16)],
                                 reads=B2 + [("wb", wi)])
                        resid_add(pb, tt, cc)


            def make_hT(gname):
                import os
                HT = int(os.environ.get("HT", "9"))
                gi = gains[gname]
                for tt in range(NTT):
                    _make_hT_tile(gi, tt, HT)

            def _make_hT_tile(gi, tt, HT):
                import os
                if True:
                    si = nstat[0]
                    nstat[0] += 1
                    xi = P.rr("xh", 1)
                    P.op("dve", lambda e, si=si: e.memset(ssq[:, si:si + 1], 0.0), writes=[("ssq", si)])
                    P.op("act", lambda e, si=si, tt=tt: e.activation(out=xh[:, 0, :], in_=x[:, tt, :], func=AF.Square,
                                                                      accum_out=ssq[:, si:si + 1]),
                         reads=[("x", tt), ("ssq", si)], writes=[("xh", 0), ("ssq", si)])
                    P.op("dve", lambda e, si=si: e.tensor_scalar(out=rstd[:, si:si + 1], in0=ssq[:, si:si + 1],
                                                                 scalar1=1.0 / D, scalar2=EPS, op0=ALU.mult, op1=ALU.add),
                         reads=[("ssq", si)], writes=[("rstd", si)])
                    if HT < 2:
                        return
                    P.op("act", lambda e, si=si: e.activation(out=rstd[:, si:si + 1], in_=rstd[:, si:si + 1], func=AF.Sqrt),
                         reads=[("rstd", si)], writes=[("rstd", si)])
                    P.op("dve", lambda e, si=si: e.reciprocal(out=rstd[:, si:si + 1], in_=rstd[:, si:si + 1]),
                         reads=[("rstd", si)], writes=[("rstd", si)])
                    if HT < 3:
                        return
                    P.op("act", lambda e, si=si, tt=tt, xi=xi: e.activation(out=xh[:, xi, :], in_=x[:, tt, :], func=AF.Identity,
                                                                           scale=rstd[:, si:si + 1]),
                         reads=[("x", tt), ("rstd", si)], writes=[("xh", xi)])
                    if HT < 4:
                        return
                    for q4 in range(4):
                        pb = 4 + P.rr("pst", 2)

                        def tfn(e, q4=q4, xi=xi, pb=pb):
                            ins = None
                            for j in range(4):
                                kc = q4 * 4 + j
                                ins = e.transpose(out=psb[pb][:, j * 128:(j + 1) * 128],
                                                  in_=xh[:, xi, kc * 128:(kc + 1) * 128], identity=ident[:])
                            return ins
                        P.op("pe", tfn, reads=[("xh", xi), "ident"], writes=[("ps", pb)])
                        if HT < 5:
                            continue
                        for j in range(4):
                            kc = q4 * 4 + j
                            dst = big1[:, kc, tt * 128:(tt + 1) * 128]
                            src = psb[pb][:, j * 128:(j + 1) * 128]
                            EV = int(os.environ.get("EV", "2"))
                            if (q4 % 2 == 0 and EV == 2) or EV == 0:
                                P.op("dve", lambda e, dst=dst, src=src, kc=kc: e.tensor_scalar(
                                    out=dst, in0=src, scalar1=gT[:, gi, kc:kc + 1], scalar2=None, op0=ALU.mult),
                                    reads=[("ps", pb), ("gain", gi)], writes=[("big1", tt, q4 % 2)])
                            else:
                                P.op("act", lambda e, dst=dst, src=src, kc=kc: e.activation(
                                    out=dst, in_=src, func=AF.Identity, scale=gT[:, gi, kc:kc + 1]),
                                    reads=[("ps", pb), ("gain", gi)], writes=[("big1", tt, q4 % 2)])

            hT_reads = [("big1", t, u) for t in range(NTT) for u in range(2)]

            if mlp:
                make_hT("mlp")
                for qt in range(4):
                    for blk in range(4):
                        wi = load_wblock(wu_d, 0, qt * 2048 + blk * 512)
                        for j in range(4):
                            fc = blk * 4 + j
                            for tg in range(2):
                                pb = next_ps()
                                mm_group(pb, psb[pb][:],
                                         [(wb[:, wi, kc, j * 128:(j + 1) * 128], big1[:, kc, tg * 512:(tg + 1) * 512])
                                          for kc in range(16)],
                                         reads=hT_reads + [("wb", wi)])
                                ri = P.rr("r32", 2)
                                P.op("act", lambda e, pb=pb, ri=ri: e.activation(out=r32[:, ri, :], in_=psb[pb][:], func=AF.Relu),
                                     reads=[("ps", pb)], writes=[("r32", ri)])
                                if (fc + tg) % 2 == 0:
                                    P.op("act", lambda e, ri=ri, fc=fc, tg=tg: e.activation(
                                        out=big2[:, fc, tg * 512:(tg + 1) * 512], in_=r32[:, ri, :], func=AF.Square),
                                        reads=[("r32", ri)], writes=[("big2", fc)])
                                else:
                                    P.op("dve", lambda e, ri=ri, fc=fc, tg=tg: e.tensor_tensor(
                                        out=big2[:, fc, tg * 512:(tg + 1) * 512], in0=r32[:, ri, :], in1=r32[:, ri, :], op=ALU.mult),
                                        reads=[("r32", ri)], writes=[("big2", fc)])
                    for cc in range(4):
                        wi = load_wblock(wd_d, qt * 2048, cc * 512)
                        for tt in range(NTT):
                            pb = next_ps()
                            mm_group(pb, psb[pb][:],
                                     [(big2[:, fc, tt * 128:(tt + 1) * 128], wb[:, wi, fc, :]) for fc in range(16)],
                                     reads=B2 + [("wb", wi)])
                            resid_add(pb, tt, cc)

            if final:
                for tt in range(NTT):
                    si = nstat[0]
                    nstat[0] += 1
                    yi = 0
                    P.op("dve", lambda e, si=si: e.memset(ssq[:, si:si + 1], 0.0), writes=[("ssq", si)])
                    P.op("act", lambda e, si=si, tt=tt: e.activation(out=xh[:, 0, :], in_=x[:, tt, :], func=AF.Square,
                                                                      accum_out=ssq[:, si:si + 1]),
                         reads=[("x", tt), ("ssq", si)], writes=[("xh", 0), ("ssq", si)])
                    P.op("dve", lambda e, si=si: e.tensor_scalar(out=rstd[:, si:si + 1], in0=ssq[:, si:si + 1],
                                                                 scalar1=1.0 / D, scalar2=EPS, op0=ALU.mult, op1=ALU.add),
                         reads=[("ssq", si)], writes=[("rstd", si)])
                    P.op("act", lambda e, si=si: e.activation(out=rstd[:, si:si + 1], in_=rstd[:, si:si + 1], func=AF.Sqrt),
                         reads=[("rstd", si)], writes=[("rstd", si)])
                    P.op("dve", lambda e, si=si: e.reciprocal(out=rstd[:, si:si + 1], in_=rstd[:, si:si + 1]),
                         reads=[("rstd", si)], writes=[("rstd", si)])
                    P.op("dve", lambda e, si=si, tt=tt, yi=yi: e.scalar_tensor_tensor(
                        out=xh[:, 0, :], in0=x[:, tt, :], scalar=rstd[:, si:si + 1], in1=gfb[:], op0=ALU.mult, op1=ALU.mult),
                        reads=[("x", tt), ("rstd", si), "gfb"], writes=[("xh", 0)])
                    P.op("sp", lambda e, tt=tt, yi=yi: e.dma_start(out=y_d.ap()[tt * 128:(tt + 1) * 128, :], in_=xh[:, 0, :]),
                         reads=[("xh", 0)], dma_sem="yst")
                out_sems += ["yst"]
            else:
                P.op("sp", lambda e: e.dma_start(out=xo_d.ap().rearrange("(t p) c -> p t c", p=128), in_=x[:]),
                     reads=[("x", t) for t in range(NTT)], dma_sem="stx")
                out_sems.append("stx")

            def stage_out(dst_ap, src_fn, eng, reads):
                si = P.rr("stg", NST)
                P.op(eng, lambda e: src_fn(e, stg[:, si, :]), reads=reads, writes=[("stg", si)])
                P.op("sp", lambda e: e.dma_start(out=dst_ap, in_=stg[:, si, :]), reads=[("stg", si)], dma_sem=("stg", si))

            def proj_fm(wi, j, mode, outs):
                for tg in range(2):
                    pb = next_ps()
                    mm_group(pb, psb[pb][:],
                             [(wb[:, wi, kc, j * 128:(j + 1) * 128], big1[:, kc, tg * 512:(tg + 1) * 512]) for kc in range(16)],
                             reads=hT_reads + [("wb", wi)])
                    tsl = slice(tg * 512, (tg + 1) * 512)
                    if "rope" in outs:
                        ri = P.rr("r32", 2)
                        P.op("act", lambda e, pb=pb, ri=ri: e.activation(out=r32[:, ri, :], in_=psb[pb][:], func=AF.Copy),
                             reads=[("ps", pb)], writes=[("r32", ri)])
                        if "plain" in outs:
                            dd, r0 = outs["plain"]
                            stage_out(dd.ap()[r0:r0 + 128, tsl],
                                      lambda e, o, ri=ri: e.tensor_copy(out=o, in_=r32[:, ri, :]), "dve", [("r32", ri)])
                        pr = 6 + P.rr("psr", 2)
                        P.op("pe", lambda e, pr=pr, ri=ri: e.matmul(psb[pr][:], rotT[:], r32[:, ri, :], start=True, stop=True),
                             reads=[("r32", ri), "rot"], writes=[("ps", pr)])
                        ti = P.rr("t12", 1)
                        P.op("dve", lambda e, ri=ri, ti=ti, tsl=tsl: e.tensor_tensor(out=t1[:, ti, :], in0=r32[:, ri, :],
                                                                                     in1=cosT[:, tsl], op=ALU.mult),
                             reads=[("r32", ri), "cos"], writes=[("t1", ti)])
                        P.op("dve", lambda e, pr=pr, ti=ti, tsl=tsl: e.tensor_tensor(out=t2[:, ti, :], in0=psb[pr][:],
                                                                                     in1=sinT[:, tsl], op=ALU.mult),
                             reads=[("ps", pr), "sin"], writes=[("t2", ti)])
                        dd, r0 = outs["rope"]
                        stage_out(dd.ap()[r0:r0 + 128, tsl],
                                  lambda e, o, ti=ti: e.tensor_tensor(out=o, in0=t1[:, ti, :], in1=t2[:, ti, :], op=ALU.add),
                                  "dve", [("t1", ti), ("t2", ti)])
                    else:
                        dd, r0 = outs["plain"]
                        stage_out(dd.ap()[r0:r0 + 128, tsl],
                                  lambda e, o, pb=pb: e.activation(out=o, in_=psb[pb][:], func=AF.Copy), "act", [("ps", pb)])

            def proj_tm(wi, dd, c0):
                for tt in range(NTT):
                    pb = next_ps()
                    mm_group(pb, psb[pb][:],
                             [(big1[:, kc, tt * 128:(tt + 1) * 128], wb[:, wi, kc, :]) for kc in range(16)],
                             reads=hT_reads + [("wb", wi)])
                    stage_out(dd.ap()[tt * 128:(tt + 1) * 128, c0:c0 + 512],
                              lambda e, o, pb=pb: e.activation(out=o, in_=psb[pb][:], func=AF.Copy), "act", [("ps", pb)])

            import os
            DBG = int(os.environ.get("DBG", "9"))
            if proj == "nsa" and DBG >= 2:
                make_hT("attn")
            if proj == "nsa" and DBG >= 3:
                for b in range(4 if DBG >= 4 else 1):
                    wi = load_wblock(win_d, 0, b * 512)
                    for j in range(4):
                        r0 = (b * 4 + j) * 128
                        proj_fm(wi, j, "both", {"plain": (qT_d, r0), "rope": (qrT_d, r0)})
            if proj == "nsa" and DBG >= 5:
                specs = [(kcT_d, False), (vcT_d, False), (ksT_d, True), (None, "vs"), (kwT_d, True), (None, "vw")]
                for pi, (dd, mode) in enumerate(specs):
                    wi = load_wblock(win_d, 0, 2048 + pi * 512)
                    if dd is None:
                        proj_tm(wi, vs_d if mode == "vs" else vw_d, 0)
                    else:
                        for j in range(4):
                            proj_fm(wi, j, "x", {"rope": (dd, j * 128)} if mode else {"plain": (dd, j * 128)})
                wi = load_wblock(win_d, 0, 2048 + 6 * 512, 48)
                for tg in range(2):
                    pb = next_ps()
                    mm_group(pb, psb[pb][0:48, :],
                             [(wb[:, wi, kc, 0:48], big1[:, kc, tg * 512:(tg + 1) * 512]) for kc in range(16)],
                             reads=hT_reads + [("wb", wi)])
                    gi2 = P.rr("gst", 1)
                    P.op("act", lambda e, pb=pb, gi2=gi2: e.activation(out=gst[:, gi2, :], in_=psb[pb][0:48, :], func=AF.Sigmoid),
                         reads=[("ps", pb)], writes=[("gst", gi2)])
                    P.op("sp", lambda e, gi2=gi2, tg=tg: e.dma_start(out=gT_d.ap()[:, tg * 512:(tg + 1) * 512], in_=gst[:, gi2, :]),
                         reads=[("gst", gi2)], dma_sem=("gst", gi2))
                out_sems += [("gst", 0)]
            if proj in ("diff", "diffkv"):
                make_hT("attn")
                for b in range(4):
                    wi = load_wblock(wq_d, 0, b * 512)
                    for j in range(4):
                        proj_fm(wi, j, "x", {"rope": (dqT_d, (b * 4 + j) * 128)})
            if proj == "diffkv":
                make_hT("kv")
                wi = load_wblock(wkv_d, 0, 0)
                for j in range(4):
                    proj_fm(wi, j, "x", {"rope": (dkT_d, j * 128)})
                wi = load_wblock(wkv_d, 0, 512)
                proj_tm(wi, dv_d, 0)
            if proj:
                out_sems += [("stg", i) for i in range(NST)]

        P.emit(st, final_waits=out_sems)
    return nc


_DENSE_CACHE = {}


def gain_fm(g):
    return np.ascontiguousarray(np.asarray(g, np.float32).reshape(16, 128).T)


def get_dense(oproj, mlp, final, proj):
    key = (oproj, mlp, final, proj)
    if key not in _DENSE_CACHE:
        _DENSE_CACHE[key] = build_dense(*key)
    return _DENSE_CACHE[key]


NSLOT = 32
BIGV = 10000.0


def slot_qb(i, half):
    m = i // 2
    if i % 2 == 0:
        return 4 * m + (0 if half == 0 else 1), 4 * m + 1
    return 4 * m + (3 if half == 0 else 2), 4 * m + 3


def build_nsa_attn():
    nc = bass.Bass("TRN2", target_bir_lowering=False)
    din = lambda name, shape, dt=F32: nc.dram_tensor(name, list(shape), dt, kind="ExternalInput")
    kcT_d = din("kcT", [128, S], BF16)
    vcT_d = din("vcT", [128, S], BF16)
    ksT_d = din("ksT", [128, S], BF16)
    kwT_d = din("kwT", [128, S], BF16)
    vs_d = din("vs", [S, 128], BF16)
    vw_d = din("vw", [S, 128], BF16)
    qT_d = din("qT", [128, NSLOT, 512], BF16)
    qrT_d = din("qrT", [128, NSLOT, 512], BF16)
    g3_d = din("g3", [3, NSLOT, 512])
    w1_d = din("w1", [2, 4096, 512])
    w2_d = din("w2", [2, 512, 128])
    posT_d = din("posT", [2, 128, 32])
    identf_d = din("identf", [128, 128])
    i4_d = din("i4", [128, 512], BF16)
    ones_d = din("ones", [128, 128], BF16)
    acon_d = din("acon4", [128, 16, 128], BF16)
    ov_d = din("ov", [128, 4, 128], BF16)
    tailm_d = din("tailm", [128, 4, 128], BF16)
    winm_d = din("winm", [128, 8, 128], BF16)
    cmask_d = din("cmask", [128, NSLOT, 128], BF16)
    slotc_d = din("slotc", [128, NSLOT, 256])
    sel3_d = din("sel3", [3, 3, 128])
    oT_d = nc.dram_tensor("oT", [128, NSLOT, 512], BF16, kind="ExternalOutput")
    import os
    DBGA = int(os.environ.get("DBGA", "0"))
    if DBGA:
        dbg_d = nc.dram_tensor("dbg", [128, 2, 8, 512], F32, kind="ExternalOutput")
    scale = 128.0 ** -0.5

    with ExitStack() as st:
        sb = lambda name, shape, dt: st.enter_context(nc.sbuf_tensor(name, list(shape), dt))
        ksT = sb("ksT_sb", [128, S], BF16)
        kwT = sb("kwT_sb", [128, S], BF16)
        vs = sb("vs_sb", [128, 64, 128], BF16)
        vw = sb("vw_sb", [128, 64, 128], BF16)
        xc = sb("xc", [128, S], BF16)
        w1sb = sb("w1sb", [128, 32, 512], BF16)
        w2sb = sb("w2sb", [128, 4, 128], BF16)
        posT = sb("posT_sb", [128, 32], F32)
        XL = sb("XL", [128, 2, 512], BF16)
        hidT = sb("hidT", [128, 4, 512], BF16)
        tA = sb("tA", [128, 2, 512], F32)
        tB = sb("tB", [128, 2, 512], F32)
        kcmpT = sb("kcmpT", [128, 512], BF16)
        vcmp = sb("vcmp", [128, 4, 128], BF16)
        identf = sb("identf_sb", [128, 128], F32)
        i4 = sb("i4_sb", [128, 512], BF16)
        ones = sb("ones_sb", [128, 128], BF16)
        acon = sb("acon_sb", [128, 16, 128], BF16)
        ov = sb("ov_sb", [128, 4, 128], BF16)
        tailm = sb("tailm_sb", [128, 4, 128], BF16)
        winm = sb("winm_sb", [128, 8, 128], BF16)
        sel3 = sb("sel3_sb", [3, 3, 128], F32)
        qsb = sb("qsb", [128, 2, 512], BF16)
        qrsb = sb("qrsb", [128, 2, 512], BF16)
        g3sb = sb("g3sb", [3, 2, 512], F32)
        cmsb = sb("cmsb", [128, 2, 128], BF16)
        slc = sb("slc", [128, 2, 256], F32)
        Ec = sb("Ec", [128, 4, 512], BF16)
        NE = 3
        Eb = sb("Eb", [128, NE, 512], BF16)
        Pn = sb("Pn", [128, 4, 512], BF16)
        rden = sb("rden", [128, 3, 512], F32)
        wgt = sb("wgt", [128, 512], F32)
        tmp = sb("tmp", [128, 512], F32)
        acc = sb("acc", [128, 2, 512], F32)
        ost = sb("ost", [128, 2, 512], BF16)
        impm = sb("impm", [128, 128], F32)
        impm2 = sb("impm2", [128, 128], F32)
        t8 = sb("t8", [128, 16], F32)
        nsel = sb("nsel", [128, 128], F32)
        nselT = sb("nselT", [128, 2, 4, 128], BF16)
        PS = [st.enter_context(nc.psum_tensor("ps%d" % i, [128, 512], F32)) for i in range(8)]
        Sb, Ob, Db, M0, M1 = (0, 1, 6), (2, 3), (4, 5), 7, 7

        P = Prog(nc)
        ld = lambda eng, dst, src, key, sem: P.op(eng, lambda e: e.dma_start(out=dst, in_=src), writes=[key], dma_sem=sem)
        ld("sp", ksT[:], ksT_d.ap(), "ksT", "l_ks")
        ld("sp", kwT[:], kwT_d.ap(), "kwT", "l_kw")
        ld("sp", vs[:], vs_d.ap().rearrange("(t p) d -> p t d", p=128), "vs", "l_vs")
        ld("sp", vw[:], vw_d.ap().rearrange("(t p) d -> p t d", p=128), "vw", "l_vw")
        ld("act", identf[:], identf_d.ap(), "identf", "l_c0")
        ld("act", i4[:], i4_d.ap(), "i4", "l_c1")
        ld("act", ones[:], ones_d.ap(), "ones", "l_c2")
        ld("act", acon[:], acon_d.ap(), "acon", "l_c3")
        ld("act", ov[:], ov_d.ap(), "ov", "l_c4")
        ld("act", tailm[:], tailm_d.ap(), "tailm", "l_c5")
        ld("act", winm[:], winm_d.ap(), "winm", "l_c6")
        ld("act", sel3[:], sel3_d.ap(), "sel3", "l_c7")
        P.op("dve", lambda e: e.memset(XL[:], 0.0), writes=[("XL", 0), ("XL", 1)])

        GC = math.sqrt(2.0 / math.pi)
        for jv in range(2):
            src_d = kcT_d if jv == 0 else vcT_d
            ld("sp", xc[:], src_d.ap(), "xc", "l_xc")
            for q4 in range(4):
                P.op("pool", lambda e, q4=q4, jv=jv: e.dma_start(
                    out=w1sb[:, q4 * 8:(q4 + 1) * 8, :],
                    in_=w1_d.ap()[jv, q4 * 1024:(q4 + 1) * 1024, :].rearrange("(l p) h -> p l h", p=128)),
                    writes=[("w1", q4)], dma_sem=("l_w1", q4))
            P.op("pool", lambda e, jv=jv: e.dma_start(out=w2sb[:], in_=w2_d.ap()[jv].rearrange("(c p) d -> p c d", p=128)),
                 writes=["w2"], dma_sem="l_w2")
            ld("act", posT[:], posT_d.ap()[jv], "posT", "l_pos")
            for l in range(32):
                xi = P.rr("xl", 2)
                src = bass.AP(xc, l, [[S, 128], [16, 511]])
                P.op("dve", lambda e, xi=xi, src=src, l=l: e.tensor_scalar(
                    out=XL[:, xi, 0:511], in0=src, scalar1=posT[:, l:l + 1], scalar2=None, op0=ALU.add),
                    reads=["xc", "posT"], writes=[("XL", xi)])

                def fn(e, xi=xi, l=l):
                    ins = None
                    for hc in range(4):
                        ins = e.matmul(PS[hc][:], w1sb[:, l, hc * 128:(hc + 1) * 128], XL[:, xi, :],
                                       start=(l == 0), stop=(l == 31))
                    return ins
                P.op("pe", fn, reads=[("XL", xi), ("w1", l // 8)], writes=[("H", hc) for hc in range(4)])
            for hc in range(4):
                ti = P.rr("tAB", 2)
                P.op("act", lambda e, hc=hc, ti=ti: e.activation(out=tA[:, ti, :], in_=PS[hc][:], func=AF.Square),
                     reads=[("H", hc)], writes=[("tA", ti)])
                P.op("dve", lambda e, ti=ti: e.tensor_scalar(out=tA[:, ti, :], in0=tA[:, ti, :], scalar1=0.044715, scalar2=1.0,
                                                             op0=ALU.mult, op1=ALU.add),
                     reads=[("tA", ti)], writes=[("tA", ti)])
                P.op("dve", lambda e, hc=hc, ti=ti: e.tensor_tensor(out=tB[:, ti, :], in0=tA[:, ti, :], in1=PS[hc][:], op=ALU.mult),
                     reads=[("tA", ti), ("H", hc)], writes=[("tB", ti)])
                P.op("act", lambda e, ti=ti: e.activation(out=tB[:, ti, :], in_=tB[:, ti, :], func=AF.Tanh, scale=GC),
                     reads=[("tB", ti)], writes=[("tB", ti)])
                P.op("dve", lambda e, ti=ti: e.tensor_scalar(out=tB[:, ti, :], in0=tB[:, ti, :], scalar1=1.0, scalar2=0.5,
                                                             op0=ALU.add, op1=ALU.mult),
                     reads=[("tB", ti)], writes=[("tB", ti)])
                P.op("dve", lambda e, hc=hc, ti=ti: e.tensor_tensor(out=hidT[:, hc, :], in0=tB[:, ti, :], in1=PS[hc][:], op=ALU.mult),
                     reads=[("tB", ti), ("H", hc)], writes=[("hidT", hc)])
            hid_reads = [("hidT", hc) for hc in range(4)]
            if jv == 0:
                def fn(e):
                    ins = None
                    for hc in range(4):
                        ins = e.matmul(PS[4][:], w2sb[:, hc, :], hidT[:, hc, :], start=(hc == 0), stop=(hc == 3))
                    return ins
                P.op("pe", fn, reads=hid_reads + ["w2"], writes=[("ps", 4)])
                P.op("act", lambda e: e.activation(out=kcmpT[:], in_=PS[4][:], func=AF.Copy), reads=[("ps", 4)], writes=["kcmpT"])
            else:
                def fn(e):
                    ins = None
                    for nt in range(4):
                        for hc in range(4):
                            ins = e.matmul(PS[5][:, nt * 128:(nt + 1) * 128], hidT[:, hc, nt * 128:(nt + 1) * 128], w2sb[:, hc, :],
                                           start=(hc == 0), stop=(hc == 3))
                    return ins
                P.op("pe", fn, reads=hid_reads + ["w2"], writes=[("ps", 5)])
                P.op("act", lambda e: e.activation(out=vcmp[:].rearrange("p a b -> p (a b)"), in_=PS[5][:], func=AF.Copy),
                     reads=[("ps", 5)], writes=["vcmp"])
        ALLPS = [("H", h) for h in range(4)] + [("ps", 4), ("ps", 5)]
        PK = {0: ("S", 0), 1: ("S", 1), 2: ("O", 0), 3: ("O", 1), 4: ("D", 0), 5: ("D", 1), 6: ("S", 2)}
        P.op("pe", lambda e: e.matmul(PS[7][:, 0:128], ones[:], ones[:], start=True, stop=True),
             reads=["ones"], writes=ALLPS + [PK[i] for i in range(7)] + ["M"])

        def branch(tiles, qbuf_key, q_ap, ob, db, hooks=None):
            n = len(tiles)
            slots = {}

            def emit_s(ti_):
                kl, kkey, extra, vl, vkey = tiles[ti_]
                sbk = P.rr("S", 3)
                slots[ti_] = sbk

                def sfn(e, kl=kl, extra=extra, sbk=sbk):
                    ins = e.matmul(PS[Sb[sbk]][:], kl, q_ap, start=True, stop=(len(extra) == 0))
                    for xi_, (l_, r_, tp, _) in enumerate(extra):
                        kw = {} if tp is None else {"tile_position": tp}
                        ins = e.matmul(PS[Sb[sbk]][:], l_, r_, start=False, stop=(xi_ == len(extra) - 1), **kw)
                    return ins
                xkeys = [k for x_ in extra for k in x_[3]]
                P.op("pe", sfn, reads=[kkey, qbuf_key] + xkeys, writes=[("S", sbk)])

            def emit_rest(ti_):
                kl, kkey, extra, vl, vkey = tiles[ti_]
                sbk = slots[ti_]
                ei = P.rr("E", NE)
                P.op("act", lambda e, sbk=sbk, ei=ei: e.activation(out=Eb[:, ei, :], in_=PS[Sb[sbk]][:], func=AF.Exp, scale=scale),
                     reads=[("S", sbk)], writes=[("E", ei)])

                def ofn(e, vl=vl, ei=ei, ti_=ti_):
                    e.matmul(PS[Ob[ob]][:], vl, Eb[:, ei, :], start=(ti_ == 0), stop=(ti_ == n - 1))
                    return e.matmul(PS[Db[db]][:], ones[:], Eb[:, ei, :], start=(ti_ == 0), stop=(ti_ == n - 1))
                P.op("pe", ofn, reads=[vkey, ("E", ei), "ones"], writes=[("O", ob), ("D", db)])

            emit_s(0)
            if n > 1:
                emit_s(1)
            for ti_ in range(n):
                if ti_ + 2 < n:
                    emit_s(ti_ + 2)
                emit_rest(ti_)
                if hooks and ti_ in hooks:
                    for h_ in hooks[ti_]:
                        h_()

        def rden_of(db, rb):
            P.op("dve", lambda e: e.tensor_scalar(out=rden[:, rb, :], in0=PS[Db[db]][:], scalar1=1e-30, scalar2=None, op0=ALU.max),
                 reads=[("D", db)], writes=[("rden", rb)])
            P.op("dve", lambda e: e.reciprocal(out=rden[:, rb, :], in_=rden[:, rb, :]), reads=[("rden", rb)], writes=[("rden", rb)])

        cur_slot = [0]

        def dump(src_ap, key, idx):
            if DBGA and cur_slot[0] < 2:
                sl = cur_slot[0]
                P.op("sp", lambda e: e.dma_start(out=dbg_d.ap()[:, sl, idx, :], in_=src_ap), reads=[key], dma_sem="dbg")

        def combine(bi, ob, rb, qi, ai, first):
            P.op("pe", lambda e: e.matmul(PS[M1][:], sel3[:, bi, :], g3sb[:, qi, :], start=True, stop=True),
                 reads=["sel3", ("g3", qi)], writes=["M"])
            dump(rden[:, rb, :], ("rden", rb), bi * 2)
            P.op("dve", lambda e: e.tensor_tensor(out=wgt[:], in0=rden[:, rb, :], in1=PS[M1][:], op=ALU.mult),
                 reads=[("rden", rb), "M"], writes=["wgt"])
            dump(wgt[:], "wgt", bi * 2 + 1)
            if first:
                P.op("dve", lambda e: e.tensor_tensor(out=acc[:, ai, :], in0=wgt[:], in1=PS[Ob[ob]][:], op=ALU.mult),
                     reads=["wgt", ("O", ob)], writes=[("acc", ai)])
            else:
                P.op("dve", lambda e: e.tensor_tensor(out=tmp[:], in0=wgt[:], in1=PS[Ob[ob]][:], op=ALU.mult),
                     reads=["wgt", ("O", ob)], writes=["tmp"])
                P.op("pool", lambda e: e.tensor_tensor(out=acc[:, ai, :], in0=acc[:, ai, :], in1=tmp[:], op=ALU.add),
                     reads=["tmp", ("acc", ai)], writes=[("acc", ai)])

        def slot_loads(i):
            qi = i % 2
            ld("sp", qsb[:, qi, :], qT_d.ap()[:, i, :], ("q", qi), ("l_q", qi))
            ld("sp", qrsb[:, qi, :], qrT_d.ap()[:, i, :], ("qr", qi), ("l_qr", qi))
            ld("sp", g3sb[:, qi, :], g3_d.ap()[:, i, :], ("g3", qi), ("l_g3", qi))
            ld("sp", cmsb[:, qi, :], cmask_d.ap()[:, i, :], ("cm", qi), ("l_cm", qi))
            ld("sp", slc[:, qi, :], slotc_d.ap()[:, i, :], ("slc", qi), ("l_sl", qi))

        NSL = int(os.environ.get("NSL", str(NSLOT)))
        SL = {}

        def slot_info(i):
            par = i % 2
            return par, 4 * (i // 2) + (1 if par == 0 else 3), i % 2

        def stage_a(i):
            par, qbmax, qi = slot_info(i)
            nkt = qbmax // 16 + 1
            obc, dbc = P.rr("O", 2), P.rr("D", 2)
            for kc_ in range(nkt):
                sbk = P.rr("S", 3)
                last = kc_ == nkt - 1

                def sfn(e, kc_=kc_, sbk=sbk, last=last, qi=qi):
                    ins = e.matmul(PS[Sb[sbk]][:], kcmpT[:, kc_ * 128:(kc_ + 1) * 128], qsb[:, qi, :], start=True, stop=not last)
                    if last:
                        ins = e.matmul(PS[Sb[sbk]][:], cmsb[:, qi, :], i4[:], start=False, stop=True)
                    return ins
                P.op("pe", sfn, reads=["kcmpT", ("q", qi), ("cm", qi), "i4"], writes=[("S", sbk)])
                P.op("act", lambda e, sbk=sbk, kc_=kc_: e.activation(out=Ec[:, kc_, :], in_=PS[Sb[sbk]][:], func=AF.Exp, scale=scale),
                     reads=[("S", sbk)], writes=[("Ec", kc_)])

                def ofn(e, kc_=kc_, last=last, obc=obc, dbc=dbc):
                    e.matmul(PS[Ob[obc]][:], vcmp[:, kc_, :], Ec[:, kc_, :], start=(kc_ == 0), stop=last)
                    return e.matmul(PS[Db[dbc]][:], ones[:], Ec[:, kc_, :], start=(kc_ == 0), stop=last)
                P.op("pe", ofn, reads=["vcmp", ("Ec", kc_), "ones"], writes=[("O", obc), ("D", dbc)])
            rbc = P.rr("rden", 3)
            rden_of(dbc, rbc)
            for kc_ in range(nkt):
                P.op("pool", lambda e, kc_=kc_, rbc=rbc: e.tensor_tensor(out=Pn[:, kc_, :], in0=Ec[:, kc_, :], in1=rden[:, rbc, :], op=ALU.mult),
                     reads=[("Ec", kc_), ("rden", rbc)], writes=[("Pn", kc_)])
            SL[i] = dict(nkt=nkt, obc=obc, rbc=rbc)
            combine(0, obc, rbc, qi, i % 2, True)

        def stage_b(i):
            par, qbmax, qi = slot_info(i)
            nkt = SL[i]["nkt"]

            def ifn(e, nkt=nkt):
                ins = None
                tot = nkt * 4
                c_ = 0
                for kc_ in range(nkt):
                    for g in range(4):
                        ins = e.matmul(PS[M0][:, 0:128], Pn[:, kc_, g * 128:(g + 1) * 128], ov[:, kc_, :],
                                       start=(c_ == 0), stop=(c_ == tot - 1))
                        c_ += 1
                return ins
            P.op("pe", ifn, reads=[("Pn", k) for k in range(nkt)] + ["ov"], writes=["M"])
            P.op("dve", lambda e, qi=qi: e.tensor_tensor(out=impm[:], in0=PS[M0][:, 0:128], in1=slc[:, qi, 0:128], op=ALU.mult),
                 reads=["M", ("slc", qi)], writes=["impm"])
            P.op("dve", lambda e, qi=qi: e.tensor_tensor(out=impm[:], in0=impm[:], in1=slc[:, qi, 128:256], op=ALU.add),
                 reads=["impm", ("slc", qi)], writes=["impm"])
            P.op("dve", lambda e: e.max(out=t8[:, 0:8], in_=impm[:]), reads=["impm"], writes=["t8a"])
            P.op("dve", lambda e: e.match_replace(out=impm2[:], in_to_replace=t8[:, 0:8], in_values=impm[:], imm_value=-1.0e9),
                 reads=["impm", "t8a"], writes=["impm2"])
            P.op("dve", lambda e: e.max(out=t8[:, 8:16], in_=impm2[:]), reads=["impm2"], writes=["t8b"])
            P.op("dve", lambda e: e.tensor_scalar(out=nsel[:], in0=impm[:], scalar1=t8[:, 15:16], scalar2=None, op0=ALU.is_lt),
                 reads=["impm", "t8b"], writes=["nsel"])

        def stage_c(i):
            par, qbmax, qi = slot_info(i)
            P.op("pe", lambda e: e.transpose(out=PS[M0][:, 128:256], in_=nsel[:], identity=identf[:]),
                 reads=["nsel", "identf"], writes=["M"])
            ni = P.rr("nselT", 2)
            for g in range(4):
                P.op("dve", lambda e, g=g, ni=ni: e.tensor_copy(out=nselT[:, ni, g, :], in_=PS[M0][:, 128:256]),
                     reads=["M"], writes=[("nselT", ni)])
            SL[i]["ni"] = ni

        if NSL > 0:
            slot_loads(0)
            stage_a(0)
            stage_b(0)
            stage_c(0)
        for i in range(NSL):
            par, qbmax, qi = slot_info(i)
            ai = i % 2
            nxt = i + 1 < NSL
            if nxt:
                slot_loads(i + 1)
                stage_a(i + 1)

            tiles = []
            for jj in range(6):
                kt = qbmax - 5 + jj
                if kt < 0:
                    continue
                extra = []
                if jj in (0, 1, 4, 5):
                    extra.append((winm[:, par * 4 + (0, 1, None, None, 2, 3)[jj], :], i4[:], None, ["winm", "i4"]))
                tiles.append((kwT[:, kt * 128:(kt + 1) * 128], "kwT", extra, vw[:, kt, :], "vw"))
            obw, dbw = P.rr("O", 2), P.rr("D", 2)
            branch(tiles, ("qr", qi), qrsb[:, qi, :], obw, dbw)
            rbw = P.rr("rden", 3)
            rden_of(dbw, rbw)
            combine(2, obw, rbw, qi, ai, False)

            ni = SL[i]["ni"]
            tiles = []
            for kt in range(qbmax + 1):
                j, r = kt // 16, kt % 16
                extra = [(acon[32 * j:32 * j + 32, r, :], nselT[32 * j:32 * j + 32, ni, :, :].rearrange("p a b -> p (a b)"),
                          (32 * j, 0), ["acon", ("nselT", ni)])]
                if kt == qbmax - 1:
                    extra.append((tailm[:, par * 2 + 0, :], i4[:], None, ["tailm", "i4"]))
                if kt == qbmax:
                    extra.append((tailm[:, par * 2 + 1, :], i4[:], None, ["tailm", "i4"]))
                tiles.append((ksT[:, kt * 128:(kt + 1) * 128], "ksT", extra, vs[:, kt, :], "vs"))
            hooks = {}
            if nxt:
                nt_ = len(tiles)
                hooks.setdefault(min(1, nt_ - 1), []).append(lambda i=i: stage_b(i + 1))
                hooks.setdefault(min(7, nt_ - 1), []).append(lambda i=i: stage_c(i + 1))
            obs, dbs = P.rr("O", 2), P.rr("D", 2)
            branch(tiles, ("qr", qi), qrsb[:, qi, :], obs, dbs, hooks)
            rbs = P.rr("rden", 3)
            rden_of(dbs, rbs)
            combine(1, obs, rbs, qi, ai, False)

            oi = P.rr("ost", 2)
            P.op("act", lambda e, oi=oi, ai=ai: e.activation(out=ost[:, oi, :], in_=acc[:, ai, :], func=AF.Copy),
                 reads=[("acc", ai)], writes=[("ost", oi)])
            P.op("sp", lambda e, oi=oi, i=i: e.dma_start(out=oT_d.ap()[:, i, :], in_=ost[:, oi, :]),
                 reads=[("ost", oi)], dma_sem=("st_o", oi))
        P.emit(st, final_waits=[("st_o", 0), ("st_o", 1), "dbg"])
    return nc


def nsa_attn_consts(half):
    c = {}
    c["identf"] = np.eye(128, dtype=np.float32)
    c["i4"] = np.tile(np.eye(128, dtype=np.float32), (1, 4)).astype(NPBF)
    c["ones"] = np.ones((128, 128), np.float32).astype(NPBF)
    p = np.arange(128)
    k = np.arange(128)
    acon = np.zeros((128, 16, 128), np.float32)
    for r in range(16):
        acon[:, r, :] = np.where((p[:, None] % 32) == 2 * r + (k[None, :] >= 64), NEG, 0.0)
    c["acon4"] = acon.astype(NPBF)
    ov = np.zeros((128, 4, 128), np.float32)
    for kt in range(4):
        n = 128 * kt + p
        cs = n * 16
        ss = np.arange(128) * 64
        o = (cs[:, None] < ss[None, :] + 64) & (cs[:, None] + 32 > ss[None, :]) & (n[:, None] <= 510)
        ov[:, kt, :] = o
    c["ov"] = ov.astype(NPBF)
    q = p[:, None]
    kk = k[None, :]
    zero = np.zeros((128, 128), np.float32)
    allneg = np.full((128, 128), NEG, np.float32)
    caus = np.where(kk <= q, 0.0, NEG).astype(np.float32)
    winold = np.where(kk > q, 0.0, NEG).astype(np.float32)
    tail = np.zeros((128, 4, 128), np.float32)
    winm = np.zeros((128, 8, 128), np.float32)
    for par in range(2):
        higher = (par == 1) if half == 0 else (par == 0)
        if higher:
            tail[:, par * 2 + 0] = zero
            tail[:, par * 2 + 1] = caus
            w = [allneg, winold, zero, caus]
        else:
            tail[:, par * 2 + 0] = caus
            tail[:, par * 2 + 1] = allneg
            w = [winold, zero, caus, allneg]
        for x_ in range(4):
            winm[:, par * 4 + x_] = w[x_]
    c["tailm"] = tail.astype(NPBF)
    c["winm"] = winm.astype(NPBF)
    cmask = np.zeros((128, NSLOT, 128), np.float32)
    slotc = np.zeros((128, NSLOT, 256), np.float32)
    s_ = np.arange(128)[None, :]
    for i in range(NSLOT):
        qb, qbmax = slot_qb(i, half)
        t = 128 * qb + p[:, None]
        ktc = qbmax // 16
        n = 128 * ktc + k[None, :]
        cmask[:, i, :] = np.where(16 * n + 31 <= t, 0.0, NEG)
        cur = t // 64
        m1 = np.ones((128, 128), np.float32)
        m2 = np.zeros((128, 128), np.float32)
        f0 = (s_ == 0) & (s_ <= cur)
        m1[np.broadcast_to(f0, m1.shape)] = 0.0
        m2[np.broadcast_to(f0, m2.shape)] = BIGV + 2
        fp = (s_ == cur - 1)
        m1[fp] = 0.0
        m2[fp] = BIGV + 1
        fc = (s_ == cur)
        m1[fc] = 0.0
        m2[fc] = BIGV
        nc_ = s_ > cur
        m1[nc_] = 0.0
        m2[nc_] = (-1.0 - np.broadcast_to(s_, m2.shape))[nc_]
        slotc[:, i, 0:128] = m1
        slotc[:, i, 128:256] = m2
    c["cmask"] = cmask.astype(NPBF)
    c["slotc"] = slotc
    sel3 = np.zeros((3, 3, 128), np.float32)
    for b in range(3):
        sel3[b, b, :] = 1.0
    c["sel3"] = sel3
    return c


_PROG = {}
_IDENT = np.eye(128, dtype=np.float32)


def _run(nc, maps):
    res = run_bass_kernel_spmd(nc, maps, core_ids=list(range(NCORES)))
    return res.results


def _cat(res, name, axis):
    return np.concatenate([np.asarray(r[name]) for r in res], axis=axis)


def dense_maps(xs, inp, layer_done, oT_full, proj, layer_next):
    maps = []
    for c in range(NCORES):
        m = {"x": xs[c], "ident": _IDENT}
        if layer_done is not None:
            L = layer_done
            m["oT"] = np.ascontiguousarray(oT_full[:, c * TPC:(c + 1) * TPC])
            m["w_o"] = inp["nsa_w_out"][L] if L < 2 else inp["diff_w_out"][L - 2]
            m["g_mlp"] = gain_fm(inp["mlp_norm_g"][L])
            m["w_up"] = inp["mlp_w_up"][L]
            m["w_down"] = inp["mlp_w_down"][L]
        if proj is None:
            m["g_final"] = np.asarray(inp["final_norm_g"], np.float32)
        else:
            m["g_attn"] = gain_fm(inp["attn_norm_g"][layer_next])
            C, Sn, RT = rope_consts(128 if proj == "nsa" else 64, c * TPC, TPC)
            m["cosT"], m["sinT"], m["rotT"] = C, Sn, RT
            if proj == "nsa":
                m["w_in"] = inp["nsa_w_in"][layer_next]
            else:
                m["w_q"] = inp["diff_w_q"][layer_next - 2]
            if proj == "diffkv":
                m["g_kv"] = gain_fm(inp["kv_norm_g"])
                m["w_kv"] = inp["kv_w_shared"]
        maps.append(m)
    return maps


def nsa_attn_maps(res, inp, layer):
    qT = _cat(res, "qT", 1).reshape(16, 128, 64, 128)
    qrT = _cat(res, "qrT", 1).reshape(16, 128, 64, 128)
    kcT, vcT = _cat(res, "kcT", 1), _cat(res, "vcT", 1)
    ksT, kwT = _cat(res, "ksT", 1), _cat(res, "kwT", 1)
    vs, vw = _cat(res, "vs", 0), _cat(res, "vw", 0)
    gT = _cat(res, "gT", 1)
    maps = []
    for c in range(NCORES):
        hk, half = c // 2, c % 2
        qbs = [slot_qb(i, half)[0] for i in range(NSLOT)]
        m = dict(nsa_attn_consts(half))
        rs = slice(hk * 128, (hk + 1) * 128)
        m["kcT"] = np.ascontiguousarray(kcT[rs])
        m["vcT"] = np.ascontiguousarray(vcT[rs])
        m["ksT"] = np.ascontiguousarray(ksT[rs])
        m["kwT"] = np.ascontiguousarray(kwT[rs])
        m["vs"] = np.ascontiguousarray(vs[:, rs])
        m["vw"] = np.ascontiguousarray(vw[:, rs])
        m["qT"] = np.ascontiguousarray(qT[4 * hk:4 * hk + 4][:, :, qbs, :].transpose(1, 2, 0, 3)).reshape(128, NSLOT, 512)
        m["qrT"] = np.ascontiguousarray(qrT[4 * hk:4 * hk + 4][:, :, qbs, :].transpose(1, 2, 0, 3)).reshape(128, NSLOT, 512)
        gv = gT[hk * 12:(hk + 1) * 12].reshape(4, 3, 64, 128)[:, :, qbs, :]
        m["g3"] = np.ascontiguousarray(gv.transpose(1, 2, 0, 3)).reshape(3, NSLOT, 512)
        m["w1"] = inp["nsa_cmp_w1"][layer]
        m["w2"] = inp["nsa_cmp_w2"][layer]
        m["posT"] = np.ascontiguousarray(np.asarray(inp["nsa_cmp_pos"][layer]).transpose(0, 2, 1))
        maps.append(m)
    return maps


def nsa_attn_gather(res):
    oT = np.zeros((16, 128, 64, 128), NPBF)
    for c in range(NCORES):
        hk, half = c // 2, c % 2
        qbs = [slot_qb(i, half)[0] for i in range(NSLOT)]
        o = np.asarray(res[c]["oT"]).reshape(128, NSLOT, 4, 128).transpose(2, 0, 1, 3)
        oT[4 * hk:4 * hk + 4][:, :, qbs, :] = o
    return oT.reshape(2048, 8192)


def build_diff_attn():
    nc = bass.Bass("TRN2", target_bir_lowering=False)
    din = lambda name, shape, dt=F32: nc.dram_tensor(name, list(shape), dt, kind="ExternalInput")
    kT_d = din("kT", [128, S], BF16)
    v_d = din("v", [S, 128], BF16)
    qT_d = din("qT", [128, 64, 512], BF16)
    lamv_d = din("lamv", [128, 256])
    g_d = din("subg", [128, 1])
    linit_d = din("linit", [128, 2])
    i4_d = din("i4", [128, 512], BF16)
    ones_d = din("ones", [128, 128], BF16)
    onesf_d = din("onesf", [128, 128])
    caus_d = din("causT", [128, 128], BF16)
    oT_d = nc.dram_tensor("oT", [128, 64, 256], BF16, kind="ExternalOutput")
    scale = 64.0 ** -0.5
    with ExitStack() as st:
        sb = lambda name, shape, dt: st.enter_context(nc.sbuf_tensor(name, list(shape), dt))
        kT = sb("kT_sb", [128, S], BF16)
        v = sb("v_sb", [128, 64, 128], BF16)
        qsb = sb("q_sb", [128, 64, 512], BF16)
        lamv = sb("lamv_sb", [128, 256], F32)
        gcol = sb("gcol", [128, 1], F32)
        linit = sb("linit_sb", [128, 2], F32)
        i4 = sb("i4_sb", [128, 512], BF16)
        ones = sb("ones_sb", [128, 128], BF16)
        onesf = sb("onesf_sb", [128, 128], F32)
        caus = sb("caus_sb", [128, 128], BF16)
        lw = sb("lw", [128, 128], F32)
        ls = sb("ls", [128, 4], F32)
        nlam = sb("nlam", [128, 1], F32)
        gsc = sb("gsc", [128, 1], F32)
        NE = 3
        Eb = sb("Eb", [128, NE, 512], BF16)
        rden = sb("rden", [128, 512], F32)
        A = sb("A", [128, 512], F32)
        o = sb("o", [128, 256], F32)
        sq = sb("sq", [128, 256], F32)
        rs = sb("rs", [128, 256], F32)
        on = sb("on", [128, 256], F32)
        ost = sb("ost", [128, 2, 256], BF16)
        PS = [st.enter_context(nc.psum_tensor("ps%d" % i, [128, 512], F32)) for i in range(8)]
        Sb, Ob, Db, M0 = (0, 1, 2), (3, 4), (5, 6), 7
        P = Prog(nc)
        ld = lambda eng, dst, src, key, sem: P.op(eng, lambda e: e.dma_start(out=dst, in_=src), writes=[key], dma_sem=sem)
        ld("sp", kT[:], kT_d.ap(), "kT", "l_k")
        ld("sp", v[:], v_d.ap().rearrange("(t p) d -> p t d", p=128), "v", "l_v")
        ld("sp", qsb[:], qT_d.ap(), "q", "l_q")
        ld("act", lamv[:], lamv_d.ap(), "lamv", "l_c0")
        ld("act", gcol[:], g_d.ap(), "gcol", "l_c1")
        ld("act", linit[:], linit_d.ap(), "linit", "l_c2")
        ld("act", i4[:], i4_d.ap(), "i4", "l_c3")
        ld("act", ones[:], ones_d.ap(), "ones", "l_c4")
        ld("act", onesf[:], onesf_d.ap(), "onesf", "l_c5")
        ld("act", caus[:], caus_d.ap(), "caus", "l_c6")
        P.op("dve", lambda e: e.tensor_tensor(out=lw[:, 0:64], in0=lamv[:, 0:64], in1=lamv[:, 64:128], op=ALU.mult),
             reads=["lamv"], writes=["lw0"])
        P.op("dve", lambda e: e.tensor_tensor(out=lw[:, 64:128], in0=lamv[:, 128:192], in1=lamv[:, 192:256], op=ALU.mult),
             reads=["lamv"], writes=["lw1"])
        P.op("dve", lambda e: e.reduce_sum(out=ls[:, 0:1], in_=lw[:, 0:64], axis=AX.X), reads=["lw0"], writes=["ls0"])
        P.op("dve", lambda e: e.reduce_sum(out=ls[:, 1:2], in_=lw[:, 64:128], axis=AX.X), reads=["lw1"], writes=["ls1"])
        P.op("act", lambda e: e.activation(out=ls[:, 2:4], in_=ls[:, 0:2], func=AF.Exp), reads=["ls0", "ls1"], writes=["ls23"])
        P.op("dve", lambda e: e.tensor_tensor(out=nlam[:], in0=ls[:, 3:4], in1=ls[:, 2:3], op=ALU.subtract),
             reads=["ls23"], writes=["nlam"])
        P.op("dve", lambda e: e.tensor_tensor(out=nlam[:], in0=nlam[:], in1=linit[:, 0:1], op=ALU.subtract),
             reads=["nlam", "linit"], writes=["nlam"])
        P.op("dve", lambda e: e.tensor_tensor(out=gsc[:], in0=gcol[:], in1=linit[:, 1:2], op=ALU.mult),
             reads=["gcol", "linit"], writes=["gsc"])

        import os
        NQ = int(os.environ.get("DIFFNQ", "64"))
        DSTEP = int(os.environ.get("DSTEP", "9"))
        for qb in range(NQ):
            ob, db = P.rr("O", 2), P.rr("D", 2)
            n = qb + 1
            sl_ = {}

            def emit_s(kt, qb=qb):
                sbk = P.rr("S", 3)
                sl_[kt] = sbk
                diag = kt == qb

                def sfn(e, kt=kt, sbk=sbk, diag=diag, qb=qb):
                    ksl = slice(kt * 128, (kt + 1) * 128)
                    ins = e.matmul(PS[Sb[sbk]][:], kT[:, ksl], qsb[:, qb, :], start=True, stop=not diag)
                    if diag:
                        ins = e.matmul(PS[Sb[sbk]][:], caus[:], i4[:], start=False, stop=True)
                    return ins
                P.op("pe", sfn, reads=["kT", "q", "caus", "i4"], writes=[("S", sbk)])

            def emit_rest(kt, n=n, ob=ob, db=db):
                sbk = sl_[kt]
                ei = P.rr("E", NE)
                P.op("act", lambda e, sbk=sbk, ei=ei: e.activation(out=Eb[:, ei, :], in_=PS[Sb[sbk]][:], func=AF.Exp, scale=scale),
                     reads=[("S", sbk)], writes=[("E", ei)])

                def ofn(e, kt=kt, ei=ei, n=n, ob=ob, db=db):
                    e.matmul(PS[Ob[ob]][:], v[:, kt, :], Eb[:, ei, :], start=(kt == 0), stop=(kt == n - 1))
                    return e.matmul(PS[Db[db]][:], ones[:], Eb[:, ei, :], start=(kt == 0), stop=(kt == n - 1))
                P.op("pe", ofn, reads=["v", ("E", ei), "ones"], writes=[("O", ob), ("D", db)])

            emit_s(0)
            if n > 1:
                emit_s(1)
            for kt in range(n):
                if kt + 2 < n:
                    emit_s(kt + 2)
                emit_rest(kt)
            P.op("dve", lambda e, db=db: e.tensor_scalar(out=rden[:], in0=PS[Db[db]][:], scalar1=1e-30, scalar2=None, op0=ALU.max),
                 reads=[("D", db)], writes=["rden"])
            P.op("dve", lambda e: e.reciprocal(out=rden[:], in_=rden[:]), reads=["rden"], writes=["rden"])
            P.op("dve", lambda e, ob=ob: e.tensor_tensor(out=A[:], in0=rden[:], in1=PS[Ob[ob]][:], op=ALU.mult),
                 reads=["rden", ("O", ob)], writes=["A"])
            P.op("dve", lambda e: e.scalar_tensor_tensor(out=o[:], in0=A[:, 256:512], scalar=nlam[:, 0:1], in1=A[:, 0:256],
                                                         op0=ALU.mult, op1=ALU.add),
                 reads=["A", "nlam"], writes=["o"])
            P.op("pool", lambda e: e.tensor_tensor(out=sq[:], in0=o[:], in1=o[:], op=ALU.mult), reads=["o"], writes=["sq"])
            P.op("pe", lambda e: e.matmul(PS[M0][:, 0:256], onesf[:], sq[:], start=True, stop=True),
                 reads=["sq", "onesf"], writes=["M0"])
            P.op("dve", lambda e: e.tensor_scalar(out=rs[:], in0=PS[M0][:, 0:256], scalar1=1.0 / 128, scalar2=EPS,
                                                  op0=ALU.mult, op1=ALU.add), reads=["M0"], writes=["rs"])
            P.op("act", lambda e: e.activation(out=rs[:], in_=rs[:], func=AF.Ln), reads=["rs"], writes=["rs"])
            P.op("act", lambda e: e.activation(out=rs[:], in_=rs[:], func=AF.Exp, scale=-0.5), reads=["rs"], writes=["rs"])
            P.op("dve", lambda e: e.tensor_tensor(out=on[:], in0=o[:], in1=rs[:], op=ALU.mult), reads=["o", "rs"], writes=["on"])
            oi = P.rr("ost", 2)
            P.op("act", lambda e, oi=oi: e.activation(out=ost[:, oi, :], in_=on[:], func=AF.Identity, scale=gsc[:, 0:1]),
                 reads=["on", "gsc"], writes=[("ost", oi)])
            P.op("sp", lambda e, oi=oi, qb=qb: e.dma_start(out=oT_d.ap()[:, qb, :], in_=ost[:, oi, :]),
                 reads=[("ost", oi)], dma_sem=("st_o", oi))
        P.emit(st, final_waits=[("st_o", 0), ("st_o", 1)])
    return nc


def diff_attn_maps(dqT, dkT, dv, inp, layer):
    j = layer - 2
    li = 0.8 - 0.6 * math.exp(-0.3 * layer)
    q4 = dqT.reshape(16, 128, 64, 128)
    p = np.arange(128)
    consts = {
        "i4": np.tile(np.eye(128, dtype=np.float32), (1, 4)).astype(NPBF),
        "ones": np.ones((128, 128), np.float32).astype(NPBF),
        "onesf": np.ones((128, 128), np.float32),
        "causT": np.where(p[None, :] <= p[:, None], 0.0, NEG).astype(np.float32).astype(NPBF),
        "lamv": np.ascontiguousarray(np.broadcast_to(np.asarray(inp["diff_lambda"][j], np.float32).reshape(1, 256), (128, 256))),
        "subg": np.ascontiguousarray(np.asarray(inp["diff_subln_g"][j], np.float32).reshape(128, 1)),
        "linit": np.ascontiguousarray(np.broadcast_to(np.array([[li, 1.0 - li]], np.float32), (128, 2))),
    }
    maps = []
    for c in range(NCORES):
        hk = c // 2
        m = dict(consts)
        m["kT"] = np.ascontiguousarray(dkT[hk * 128:(hk + 1) * 128])
        m["v"] = np.ascontiguousarray(dv[:, hk * 128:(hk + 1) * 128])
        qq = np.ascontiguousarray(q4[2 * c:2 * c + 2].transpose(1, 2, 0, 3))
        qz = np.zeros((128, 64, 2, 2, 128), NPBF)
        qz[0:64, :, 0] = qq[0:64]
        qz[64:128, :, 1] = qq[64:128]
        m["qT"] = qz.reshape(128, 64, 512)
        maps.append(m)
    return maps


def diff_attn_gather(res):
    oT = np.zeros((16, 128, 64, 128), NPBF)
    for c in range(NCORES):
        o = np.asarray(res[c]["oT"]).reshape(128, 64, 2, 128).transpose(2, 0, 1, 3)
        oT[2 * c:2 * c + 2] = o
    return oT.reshape(2048, 8192)


def kernel(x, attn_norm_g, mlp_norm_g, final_norm_g, nsa_w_in, nsa_cmp_pos, nsa_cmp_w1, nsa_cmp_w2, nsa_w_out,
           kv_norm_g, kv_w_shared, diff_w_q, diff_lambda, diff_subln_g, diff_w_out, mlp_w_up, mlp_w_down, _debug=None):
    inp = dict(attn_norm_g=attn_norm_g, mlp_norm_g=mlp_norm_g, final_norm_g=final_norm_g, nsa_w_in=nsa_w_in,
               nsa_cmp_pos=nsa_cmp_pos, nsa_cmp_w1=nsa_cmp_w1, nsa_cmp_w2=nsa_cmp_w2, nsa_w_out=nsa_w_out,
               kv_norm_g=kv_norm_g, kv_w_shared=kv_w_shared, diff_w_q=diff_w_q, diff_lambda=diff_lambda,
               diff_subln_g=diff_subln_g, diff_w_out=diff_w_out, mlp_w_up=mlp_w_up, mlp_w_down=mlp_w_down)
    inp = {k: np.asarray(v, np.float32) for k, v in inp.items()}
    x2 = np.asarray(x, np.float32).reshape(S, D)
    xs = [np.ascontiguousarray(x2[c * TPC:(c + 1) * TPC]) for c in range(NCORES)]
    if "nsa" not in _PROG:
        _PROG["nsa"] = build_nsa_attn()
        _PROG["diff"] = build_diff_attn()
    dbg = {}
    res = _run(get_dense(False, False, False, "nsa"), dense_maps(xs, inp, None, None, "nsa", 0))
    dkT = dv = None
    for layer in range(4):
        if layer < 2:
            ra = _run(_PROG["nsa"], nsa_attn_maps(res, inp, layer))
            oT = nsa_attn_gather(ra)
        else:
            if layer == 2:
                dkT, dv = _cat(res, "dkT", 1), _cat(res, "dv", 0)
            ra = _run(_PROG["diff"], diff_attn_maps(_cat(res, "dqT", 1), dkT, dv, inp, layer))
            oT = diff_attn_gather(ra)
        if _debug is not None:
            dbg["oT%d" % layer] = oT
        nxt = [("nsa", 1), ("diffkv", 2), ("diff", 3), (None, None)][layer]
        res = _run(get_dense(True, True, nxt[0] is None, nxt[0]), dense_maps(xs, inp, layer, oT, nxt[0], nxt[1]))
        if nxt[0] is not None:
            xs = [np.asarray(r["x_out"]) for r in res]
            if _debug is not None:
                dbg["x%d" % layer] = np.concatenate(xs, 0)
    y = np.concatenate([np.asarray(r["y"]) for r in res], 0).reshape(1, S, D).astype(np.float32)
    if _debug is not None:
        _debug.update(dbg)
    return y
```

```python
import math
from contextlib import ExitStack

import numpy as np
import ml_dtypes
import concourse.bass as bass
import concourse.mybir as mybir
from concourse.bass_utils import run_bass_kernel_spmd

F32 = mybir.dt.float32
BF16 = mybir.dt.bfloat16
AF = mybir.ActivationFunctionType
ALU = mybir.AluOpType
AX = mybir.AxisListType
NPBF = ml_dtypes.bfloat16

NCORES = 8
S = 8192
D = 2048
TPC = S // NCORES
NTT = TPC // 128
EPS = 1e-6
NEG = -30000.0
ROPE_THETA = 500000.0

ENGS = ("pe", "act", "dve", "pool", "sp")


class Op:
    __slots__ = ("eng", "fn", "deps", "signalled", "sig", "dma_sem", "dma_val")


class Prog:
    def __init__(self, nc):
        self.nc = nc
        self.ops = []
        self.last_w = {}
        self.readers = {}
        self.dma_cum = {}
        self.rot = {}
        self.dry = False

    def rr(self, name, n):
        if self.dry:
            return 0
        i = self.rot.get(name, 0)
        self.rot[name] = i + 1
        return i % n

    def op(self, eng, fn, reads=(), writes=(), dma_sem=None, ndma=1):
        if self.dry:
            return None
        o = Op()
        o.eng = eng
        o.fn = fn
        o.signalled = False
        o.sig = 0
        o.dma_sem = dma_sem
        o.dma_val = 0
        deps = set()
        for k in reads:
            w = self.last_w.get(k)
            if w is not None:
                deps.add(w)
        for k in writes:
            w = self.last_w.get(k)
            if w is not None:
                deps.add(w)
            for r in self.readers.get(k, ()):
                deps.add(r)
        o.deps = [d for d in deps
                  if not (d.eng == "pe" and eng == "pe" and d.dma_sem is None and dma_sem is None)]
        if dma_sem is not None:
            self.dma_cum[dma_sem] = self.dma_cum.get(dma_sem, 0) + 16 * ndma
            o.dma_val = self.dma_cum[dma_sem]
        for d in o.deps:
            if d.dma_sem is None:
                d.signalled = True
        for k in writes:
            self.last_w[k] = o
            self.readers[k] = []
        for k in reads:
            if k not in writes:
                self.readers.setdefault(k, []).append(o)
        self.ops.append(o)
        return o

    def emit(self, stack, final_waits=()):
        nc = self.nc
        cnt = {e: 0 for e in ENGS}
        for o in self.ops:
            if o.dma_sem is None and o.signalled:
                cnt[o.eng] += 1
                o.sig = cnt[o.eng]
        esem = {e: stack.enter_context(nc.semaphore("s_" + e)) for e in ENGS}
        dsem = {}
        for i, k in enumerate(self.dma_cum):
            dsem[k] = stack.enter_context(nc.semaphore("d%d" % i))
        block = stack.enter_context(nc.Block())
        per = {e: [o for o in self.ops if o.eng == e] for e in ENGS}
        dma_cum = self.dma_cum

        def run(e, eng):
            waited = {}
            for o in per[e]:
                need = {}
                for d in o.deps:
                    if d.dma_sem is not None:
                        key = ("d", d.dma_sem)
                        v = d.dma_val
                    else:
                        key = ("e", d.eng)
                        v = d.sig
                    if v > need.get(key, 0):
                        need[key] = v
                for key, v in need.items():
                    if v > waited.get(key, 0):
                        sem = dsem[key[1]] if key[0] == "d" else esem[key[1]]
                        eng.wait_ge(sem, v)
                        waited[key] = v
                r = o.fn(eng)
                if o.dma_sem is not None:
                    rs = r if isinstance(r, (list, tuple)) else [r]
                    for ins in rs:
                        ins.then_inc(dsem[o.dma_sem], 16)
                elif o.signalled:
                    r.then_inc(esem[e], 1)
            if e == "sp":
                for k in final_waits:
                    if k in dsem:
                        eng.wait_ge(dsem[k], dma_cum[k])

        @block.tensor
        def _(eng):
            run("pe", eng)

        @block.scalar
        def _(eng):
            run("act", eng)

        @block.vector
        def _(eng):
            run("dve", eng)

        @block.gpsimd
        def _(eng):
            run("pool", eng)

        @block.sync
        def _(eng):
            run("sp", eng)


def rope_consts(head_chunk, tok0, ntok):
    rot = head_chunk // 4
    half = rot // 2
    inv = 1.0 / (ROPE_THETA ** (np.arange(0, rot, 2, dtype=np.float32) / np.float32(rot)))
    inv = inv.astype(np.float32)
    pos = np.arange(tok0, tok0 + ntok, dtype=np.float32)
    ang = (pos[None, :] * inv[:, None]).astype(np.float32)
    cos = np.cos(ang).astype(np.float32)
    sin = np.sin(ang).astype(np.float32)
    C = np.ones((128, ntok), np.float32)
    Sn = np.zeros((128, ntok), np.float32)
    Rm = np.zeros((128, 128), np.float32)
    for base in range(0, 128, head_chunk):
        for j in range(half):
            C[base + j] = cos[j]
            C[base + half + j] = cos[j]
            Sn[base + j] = sin[j]
            Sn[base + half + j] = sin[j]
            Rm[base + j, base + half + j] = -1.0
            Rm[base + half + j, base + j] = 1.0
    return C, Sn, np.ascontiguousarray(Rm.T)


def build_dense(oproj, mlp, final, proj):
    nc = bass.Bass("TRN2", target_bir_lowering=False)
    T = TPC
    dr = {}

    def din(name, shape, dt=F32):
        dr[name] = nc.dram_tensor(name, list(shape), dt, kind="ExternalInput")
        return dr[name]

    def dout(name, shape, dt=F32):
        dr[name] = nc.dram_tensor(name, list(shape), dt, kind="ExternalOutput")
        return dr[name]

    x_d = din("x", [T, D])
    ident_d = din("ident", [128, 128])
    if oproj:
        oT_d = din("oT", [D, T], BF16)
        wo_d = din("w_o", [D, D])
    if mlp:
        gm_d = din("g_mlp", [128, 16])
        wu_d = din("w_up", [D, 4 * D])
        wd_d = din("w_down", [4 * D, D])
    if final:
        gf_d = din("g_final", [D])
        y_d = dout("y", [T, D])
    else:
        xo_d = dout("x_out", [T, D])
    if proj:
        ga_d = din("g_attn", [128, 16])
        cos_d = din("cosT", [128, T])
        sin_d = din("sinT", [128, T])
        rt_d = din("rotT", [128, 128])
    if proj == "nsa":
        win_d = din("w_in", [D, 5168])
        qT_d = dout("qT", [D, T], BF16)
        qrT_d = dout("qrT", [D, T], BF16)
        kcT_d = dout("kcT", [512, T], BF16)
        vcT_d = dout("vcT", [512, T], BF16)
        ksT_d = dout("ksT", [512, T], BF16)
        kwT_d = dout("kwT", [512, T], BF16)
        vs_d = dout("vs", [T, 512], BF16)
        vw_d = dout("vw", [T, 512], BF16)
        gT_d = dout("gT", [48, T])
    if proj in ("diff", "diffkv"):
        wq_d = din("w_q", [D, D])
        dqT_d = dout("dqT", [D, T], BF16)
    if proj == "diffkv":
        gk_d = din("g_kv", [128, 16])
        wkv_d = din("w_kv", [D, 1024])
        dkT_d = dout("dkT", [512, T], BF16)
        dv_d = dout("dv", [T, 512], BF16)

    with ExitStack() as st:
        sb = lambda name, shape, dt: st.enter_context(nc.sbuf_tensor(name, list(shape), dt))
        x = sb("x_sb", [128, NTT, D], F32)
        big1 = sb("big1", [128, 16, T], BF16)
        big2 = sb("big2", [128, 16, T], BF16)
        NWB = 2
        wb = sb("wb", [128, NWB, 16, 512], BF16)
        wst = sb("wst", [128, 2, 4, 512], F32)
        ident = sb("ident_sb", [128, 128], F32)
        xh = sb("xh", [128, 1, D], F32)
        ssq = sb("ssq", [128, 32], F32)
        rstd = sb("rstd", [128, 32], F32)
        gT = sb("gT_sb", [128, 4, 16], F32)
        NST = 4
        stg = sb("stg", [128, NST, 512], BF16)
        r32 = sb("r32", [128, 2, 512], F32)
        if proj:
            cosT = sb("cos_sb", [128, T], F32)
            sinT = sb("sin_sb", [128, T], F32)
            rotT = sb("rot_sb", [128, 128], F32)
            t1 = sb("t1", [128, 1, 512], F32)
            t2 = sb("t2", [128, 1, 512], F32)
            gst = sb("gst", [48, 1, 512], F32)
        if final:
            gfb = sb("gfb", [128, D], F32)
        psb = [st.enter_context(nc.psum_tensor("ps%d" % i, [128, 512], F32)) for i in range(8)]

        P = Prog(nc)
        out_sems = []
        B2 = [("big2", fc) for fc in range(16)]

        P.op("sp", lambda e: e.dma_start(out=x[:], in_=x_d.ap().rearrange("(t p) c -> p t c", p=128)),
             writes=[("x", t) for t in range(NTT)], dma_sem="ldx")
        P.op("act", lambda e: e.dma_start(out=ident[:], in_=ident_d.ap()), writes=["ident"], dma_sem="ld_ident")
        gains = {}

        def load_gain(name, d):
            gi = len(gains)
            gains[name] = gi
            P.op("act", lambda e: e.dma_start(out=gT[:, gi, :], in_=d.ap()),
                 writes=[("gain", gi)], dma_sem=("ldg", gi))

        if mlp:
            load_gain("mlp", gm_d)
        if proj:
            load_gain("attn", ga_d)
            P.op("act", lambda e: e.dma_start(out=cosT[:], in_=cos_d.ap()), writes=["cos"], dma_sem="ld_cos")
            P.op("act", lambda e: e.dma_start(out=sinT[:], in_=sin_d.ap()), writes=["sin"], dma_sem="ld_sin")
            P.op("act", lambda e: e.dma_start(out=rotT[:], in_=rt_d.ap()), writes=["rot"], dma_sem="ld_rot")
        if proj == "diffkv":
            load_gain("kv", gk_d)
        if final:
            P.op("act", lambda e: e.dma_start(out=gfb[:], in_=gf_d.ap().partition_broadcast(128)),
                 writes=["gfb"], dma_sem="ld_gfb")
        if oproj:
            P.op("sp", lambda e: e.dma_start(out=big2[:], in_=oT_d.ap().rearrange("(k p) t -> p k t", p=128)),
                 writes=B2, dma_sem="ldo")

        wlist = []
        wpos = [0]

        def issue_wblock(n):
            wd, r0, c0, ncols = wlist[n]
            i = n % NWB
            for qq in range(4):
                si = P.rr("wst", 2)
                src = wd.ap()[r0 + qq * 512:r0 + (qq + 1) * 512, c0:c0 + ncols].rearrange("(k p) c -> p k c", p=128)
                P.op("sp", lambda e, si=si, src=src: e.dma_start(out=wst[:, si, :, 0:ncols], in_=src),
                     writes=[("wst", si)], dma_sem=("wst", si))
                P.op("pool", lambda e, si=si, qq=qq: e.tensor_copy(out=wb[:, i, qq * 4:(qq + 1) * 4, 0:ncols],
                                                                   in_=wst[:, si, :, 0:ncols]),
                     reads=[("wst", si)], writes=[("wb", i)])

        def load_wblock(wd, r0, c0, ncols=512):
            if P.dry:
                wlist.append((wd, r0, c0, ncols))
                return 0
            n = wpos[0]
            wpos[0] += 1
            if n == 0:
                issue_wblock(0)
            if n + 1 < len(wlist):
                issue_wblock(n + 1)
            return n % NWB

        def next_ps():
            return P.rr("ps", 4)

        def mm_group(pb, pso, pairs, reads):
            def fn(e):
                n = len(pairs)
                ins = None
                for i, (l, r) in enumerate(pairs):
                    ins = e.matmul(pso, l, r, start=(i == 0), stop=(i == n - 1))
                return ins
            P.op("pe", fn, reads=reads, writes=[("ps", pb)])

        def resid_add(pb, tt, cc):
            xs = x[:, tt, cc * 512:(cc + 1) * 512]
            P.op("dve", lambda e: e.tensor_tensor(out=xs, in0=xs, in1=psb[pb][:], op=ALU.add),
                 reads=[("ps", pb), ("x", tt)], writes=[("x", tt)])

        nstat = [0]
        for PASS in (0, 1):
            P.dry = (PASS == 0)
            nstat[0] = 0
            if oproj:
                for cc in range(4):
                    wi = load_wblock(wo_d, 0, cc * 512)
                    for tt in range(NTT):
                        pb = next_ps()
                        mm_group(pb, psb[pb][:], [(big2[:, kc, tt * 128:(tt + 1) * 128], wb[:, wi, kc, :]) for kc in range(16)],
                                 reads=B2 + [("wb", wi)])
                        resid_add(pb, tt, cc)


            def make_hT(gname):
                HT = 9
                gi = gains[gname]
                for tt in range(NTT):
                    _make_hT_tile(gi, tt, HT)

            def _make_hT_tile(gi, tt, HT):
                if True:
                    si = nstat[0]
                    nstat[0] += 1
                    xi = P.rr("xh", 1)
                    P.op("dve", lambda e, si=si: e.memset(ssq[:, si:si + 1], 0.0), writes=[("ssq", si)])
                    P.op("act", lambda e, si=si, tt=tt: e.activation(out=xh[:, 0, :], in_=x[:, tt, :], func=AF.Square,
                                                                      accum_out=ssq[:, si:si + 1]),
                         reads=[("x", tt), ("ssq", si)], writes=[("xh", 0), ("ssq", si)])
                    P.op("dve", lambda e, si=si: e.tensor_scalar(out=rstd[:, si:si + 1], in0=ssq[:, si:si + 1],
                                                                 scalar1=1.0 / D, scalar2=EPS, op0=ALU.mult, op1=ALU.add),
                         reads=[("ssq", si)], writes=[("rstd", si)])
                    if HT < 2:
                        return
                    P.op("act", lambda e, si=si: e.activation(out=rstd[:, si:si + 1], in_=rstd[:, si:si + 1], func=AF.Sqrt),
                         reads=[("rstd", si)], writes=[("rstd", si)])
                    P.op("dve", lambda e, si=si: e.reciprocal(out=rstd[:, si:si + 1], in_=rstd[:, si:si + 1]),
                         reads=[("rstd", si)], writes=[("rstd", si)])
                    if HT < 3:
                        return
                    P.op("act", lambda e, si=si, tt=tt, xi=xi: e.activation(out=xh[:, xi, :], in_=x[:, tt, :], func=AF.Identity,
                                                                           scale=rstd[:, si:si + 1]),
                         reads=[("x", tt), ("rstd", si)], writes=[("xh", xi)])
                    if HT < 4:
                        return
                    for q4 in range(4):
                        pb = 4 + P.rr("pst", 2)

                        def tfn(e, q4=q4, xi=xi, pb=pb):
                            ins = None
                            for j in range(4):
                                kc = q4 * 4 + j
                                ins = e.transpose(out=psb[pb][:, j * 128:(j + 1) * 128],
                                                  in_=xh[:, xi, kc * 128:(kc + 1) * 128], identity=ident[:])
                            return ins
                        P.op("pe", tfn, reads=[("xh", xi), "ident"], writes=[("ps", pb)])
                        if HT < 5:
                            continue
                        for j in range(4):
                            kc = q4 * 4 + j
                            dst = big1[:, kc, tt * 128:(tt + 1) * 128]
                            src = psb[pb][:, j * 128:(j + 1) * 128]
                            EV = 2
                            if (q4 % 2 == 0 and EV == 2) or EV == 0:
                                P.op("dve", lambda e, dst=dst, src=src, kc=kc: e.tensor_scalar(
                                    out=dst, in0=src, scalar1=gT[:, gi, kc:kc + 1], scalar2=None, op0=ALU.mult),
                                    reads=[("ps", pb), ("gain", gi)], writes=[("big1", tt, q4 % 2)])
                            else:
                                P.op("act", lambda e, dst=dst, src=src, kc=kc: e.activation(
                                    out=dst, in_=src, func=AF.Identity, scale=gT[:, gi, kc:kc + 1]),
                                    reads=[("ps", pb), ("gain", gi)], writes=[("big1", tt, q4 % 2)])

            hT_reads = [("big1", t, u) for t in range(NTT) for u in range(2)]

            if mlp:
                make_hT("mlp")
                for qt in range(4):
                    for blk in range(4):
                        wi = load_wblock(wu_d, 0, qt * 2048 + blk * 512)
                        for j in range(4):
                            fc = blk * 4 + j
                            for tg in range(2):
                                pb = next_ps()
                                mm_group(pb, psb[pb][:],
                                         [(wb[:, wi, kc, j * 128:(j + 1) * 128], big1[:, kc, tg * 512:(tg + 1) * 512])
                                          for kc in range(16)],
                                         reads=hT_reads + [("wb", wi)])
                                ri = P.rr("r32", 2)
                                P.op("act", lambda e, pb=pb, ri=ri: e.activation(out=r32[:, ri, :], in_=psb[pb][:], func=AF.Relu),
                                     reads=[("ps", pb)], writes=[("r32", ri)])
                                if (fc + tg) % 2 == 0:
                                    P.op("act", lambda e, ri=ri, fc=fc, tg=tg: e.activation(
                                        out=big2[:, fc, tg * 512:(tg + 1) * 512], in_=r32[:, ri, :], func=AF.Square),
                                        reads=[("r32", ri)], writes=[("big2", fc)])
                                else:
                                    P.op("dve", lambda e, ri=ri, fc=fc, tg=tg: e.tensor_tensor(
                                        out=big2[:, fc, tg * 512:(tg + 1) * 512], in0=r32[:, ri, :], in1=r32[:, ri, :], op=ALU.mult),
                                        reads=[("r32", ri)], writes=[("big2", fc)])
                    for cc in range(4):
                        wi = load_wblock(wd_d, qt * 2048, cc * 512)
                        for tt in range(NTT):
                            pb = next_ps()
                            mm_group(pb, psb[pb][:],
                                     [(big2[:, fc, tt * 128:(tt + 1) * 128], wb[:, wi, fc, :]) for fc in range(16)],
                                     reads=B2 + [("wb", wi)])
                            resid_add(pb, tt, cc)

            if final:
                for tt in range(NTT):
                    si = nstat[0]
                    nstat[0] += 1
                    yi = 0
                    P.op("dve", lambda e, si=si: e.memset(ssq[:, si:si + 1], 0.0), writes=[("ssq", si)])
                    P.op("act", lambda e, si=si, tt=tt: e.activation(out=xh[:, 0, :], in_=x[:, tt, :], func=AF.Square,
                                                                      accum_out=ssq[:, si:si + 1]),
                         reads=[("x", tt), ("ssq", si)], writes=[("xh", 0), ("ssq", si)])
                    P.op("dve", lambda e, si=si: e.tensor_scalar(out=rstd[:, si:si + 1], in0=ssq[:, si:si + 1],
                                                                 scalar1=1.0 / D, scalar2=EPS, op0=ALU.mult, op1=ALU.add),
                         reads=[("ssq", si)], writes=[("rstd", si)])
                    P.op("act", lambda e, si=si: e.activation(out=rstd[:, si:si + 1], in_=rstd[:, si:si + 1], func=AF.Sqrt),
                         reads=[("rstd", si)], writes=[("rstd", si)])
                    P.op("dve", lambda e, si=si: e.reciprocal(out=rstd[:, si:si + 1], in_=rstd[:, si:si + 1]),
                         reads=[("rstd", si)], writes=[("rstd", si)])
                    P.op("dve", lambda e, si=si, tt=tt, yi=yi: e.scalar_tensor_tensor(
                        out=xh[:, 0, :], in0=x[:, tt, :], scalar=rstd[:, si:si + 1], in1=gfb[:], op0=ALU.mult, op1=ALU.mult),
                        reads=[("x", tt), ("rstd", si), "gfb"], writes=[("xh", 0)])
                    P.op("sp", lambda e, tt=tt, yi=yi: e.dma_start(out=y_d.ap()[tt * 128:(tt + 1) * 128, :], in_=xh[:, 0, :]),
                         reads=[("xh", 0)], dma_sem="yst")
                out_sems += ["yst"]
            else:
                P.op("sp", lambda e: e.dma_start(out=xo_d.ap().rearrange("(t p) c -> p t c", p=128), in_=x[:]),
                     reads=[("x", t) for t in range(NTT)], dma_sem="stx")
                out_sems.append("stx")

            def stage_out(dst_ap, src_fn, eng, reads):
                si = P.rr("stg", NST)
                P.op(eng, lambda e: src_fn(e, stg[:, si, :]), reads=reads, writes=[("stg", si)])
                P.op("sp", lambda e: e.dma_start(out=dst_ap, in_=stg[:, si, :]), reads=[("stg", si)], dma_sem=("stg", si))

            def proj_fm(wi, j, mode, outs):
                for tg in range(2):
                    pb = next_ps()
                    mm_group(pb, psb[pb][:],
                             [(wb[:, wi, kc, j * 128:(j + 1) * 128], big1[:, kc, tg * 512:(tg + 1) * 512]) for kc in range(16)],
                             reads=hT_reads + [("wb", wi)])
                    tsl = slice(tg * 512, (tg + 1) * 512)
                    if "rope" in outs:
                        ri = P.rr("r32", 2)
                        P.op("act", lambda e, pb=pb, ri=ri: e.activation(out=r32[:, ri, :], in_=psb[pb][:], func=AF.Copy),
                             reads=[("ps", pb)], writes=[("r32", ri)])
                        if "plain" in outs:
                            dd, r0 = outs["plain"]
                            stage_out(dd.ap()[r0:r0 + 128, tsl],
                                      lambda e, o, ri=ri: e.tensor_copy(out=o, in_=r32[:, ri, :]), "dve", [("r32", ri)])
                        pr = 6 + P.rr("psr", 2)
                        P.op("pe", lambda e, pr=pr, ri=ri: e.matmul(psb[pr][:], rotT[:], r32[:, ri, :], start=True, stop=True),
                             reads=[("r32", ri), "rot"], writes=[("ps", pr)])
                        ti = P.rr("t12", 1)
                        P.op("dve", lambda e, ri=ri, ti=ti, tsl=tsl: e.tensor_tensor(out=t1[:, ti, :], in0=r32[:, ri, :],
                                                                                     in1=cosT[:, tsl], op=ALU.mult),
                             reads=[("r32", ri), "cos"], writes=[("t1", ti)])
                        P.op("dve", lambda e, pr=pr, ti=ti, tsl=tsl: e.tensor_tensor(out=t2[:, ti, :], in0=psb[pr][:],
                                                                                     in1=sinT[:, tsl], op=ALU.mult),
                             reads=[("ps", pr), "sin"], writes=[("t2", ti)])
                        dd, r0 = outs["rope"]
                        stage_out(dd.ap()[r0:r0 + 128, tsl],
                                  lambda e, o, ti=ti: e.tensor_tensor(out=o, in0=t1[:, ti, :], in1=t2[:, ti, :], op=ALU.add),
                                  "dve", [("t1", ti), ("t2", ti)])
                    else:
                        dd, r0 = outs["plain"]
                        stage_out(dd.ap()[r0:r0 + 128, tsl],
                                  lambda e, o, pb=pb: e.activation(out=o, in_=psb[pb][:], func=AF.Copy), "act", [("ps", pb)])

            def proj_tm(wi, dd, c0):
                for tt in range(NTT):
                    pb = next_ps()
                    mm_group(pb, psb[pb][:],
                             [(big1[:, kc, tt * 128:(tt + 1) * 128], wb[:, wi, kc, :]) for kc in range(16)],
                             reads=hT_reads + [("wb", wi)])
                    stage_out(dd.ap()[tt * 128:(tt + 1) * 128, c0:c0 + 512],
                              lambda e, o, pb=pb: e.activation(out=o, in_=psb[pb][:], func=AF.Copy), "act", [("ps", pb)])

            DBG = 9
            if proj == "nsa" and DBG >= 2:
                make_hT("attn")
            if proj == "nsa" and DBG >= 3:
                for b in range(4 if DBG >= 4 else 1):
                    wi = load_wblock(win_d, 0, b * 512)
                    for j in range(4):
                        r0 = (b * 4 + j) * 128
                        proj_fm(wi, j, "both", {"plain": (qT_d, r0), "rope": (qrT_d, r0)})
            if proj == "nsa" and DBG >= 5:
                specs = [(kcT_d, False), (vcT_d, False), (ksT_d, True), (None, "vs"), (kwT_d, True), (None, "vw")]
                for pi, (dd, mode) in enumerate(specs):
                    wi = load_wblock(win_d, 0, 2048 + pi * 512)
                    if dd is None:
                        proj_tm(wi, vs_d if mode == "vs" else vw_d, 0)
                    else:
                        for j in range(4):
                            proj_fm(wi, j, "x", {"rope": (dd, j * 128)} if mode else {"plain": (dd, j * 128)})
                wi = load_wblock(win_d, 0, 2048 + 6 * 512, 48)
                for tg in range(2):
                    pb = next_ps()
                    mm_group(pb, psb[pb][0:48, :],
                             [(wb[:, wi, kc, 0:48], big1[:, kc, tg * 512:(tg + 1) * 512]) for kc in range(16)],
                             reads=hT_reads + [("wb", wi)])
                    gi2 = P.rr("gst", 1)
                    P.op("act", lambda e, pb=pb, gi2=gi2: e.activation(out=gst[:, gi2, :], in_=psb[pb][0:48, :], func=AF.Sigmoid),
                         reads=[("ps", pb)], writes=[("gst", gi2)])
                    P.op("sp", lambda e, gi2=gi2, tg=tg: e.dma_start(out=gT_d.ap()[:, tg * 512:(tg + 1) * 512], in_=gst[:, gi2, :]),
                         reads=[("gst", gi2)], dma_sem=("gst", gi2))
                out_sems += [("gst", 0)]
            if proj in ("diff", "diffkv"):
                make_hT("attn")
                for b in range(4):
                    wi = load_wblock(wq_d, 0, b * 512)
                    for j in range(4):
                        proj_fm(wi, j, "x", {"rope": (dqT_d, (b * 4 + j) * 128)})
            if proj == "diffkv":
                make_hT("kv")
                wi = load_wblock(wkv_d, 0, 0)
                for j in range(4):
                    proj_fm(wi, j, "x", {"rope": (dkT_d, j * 128)})
                wi = load_wblock(wkv_d, 0, 512)
                proj_tm(wi, dv_d, 0)
            if proj:
                out_sems += [("stg", i) for i in range(NST)]

        P.emit(st, final_waits=out_sems)
    return nc


_DENSE_CACHE = {}


def gain_fm(g):
    return np.ascontiguousarray(np.asarray(g, np.float32).reshape(16, 128).T)


def get_dense(oproj, mlp, final, proj):
    key = (oproj, mlp, final, proj)
    if key not in _DENSE_CACHE:
        _DENSE_CACHE[key] = build_dense(*key)
    return _DENSE_CACHE[key]


NSLOT = 32
BIGV = 10000.0


def slot_qb(i, half):
    m = i // 2
    if i % 2 == 0:
        return 4 * m + (0 if half == 0 else 1), 4 * m + 1
    return 4 * m + (3 if half == 0 else 2), 4 * m + 3


def build_nsa_attn():
    nc = bass.Bass("TRN2", target_bir_lowering=False)
    din = lambda name, shape, dt=F32: nc.dram_tensor(name, list(shape), dt, kind="ExternalInput")
    kcT_d = din("kcT", [128, S], BF16)
    vcT_d = din("vcT", [128, S], BF16)
    ksT_d = din("ksT", [128, S], BF16)
    kwT_d = din("kwT", [128, S], BF16)
    vs_d = din("vs", [S, 128], BF16)
    vw_d = din("vw", [S, 128], BF16)
    qT_d = din("qT", [128, NSLOT, 512], BF16)
    qrT_d = din("qrT", [128, NSLOT, 512], BF16)
    g3_d = din("g3", [3, NSLOT, 512])
    w1_d = din("w1", [2, 4096, 512])
    w2_d = din("w2", [2, 512, 128])
    posT_d = din("posT", [2, 128, 32])
    identf_d = din("identf", [128, 128])
    i4_d = din("i4", [128, 512], BF16)
    ones_d = din("ones", [128, 128], BF16)
    acon_d = din("acon4", [128, 16, 128], BF16)
    ov_d = din("ov", [128, 4, 128], BF16)
    tailm_d = din("tailm", [128, 4, 128], BF16)
    winm_d = din("winm", [128, 8, 128], BF16)
    cmask_d = din("cmask", [128, NSLOT, 128], BF16)
    slotc_d = din("slotc", [128, NSLOT, 256])
    sel3_d = din("sel3", [3, 3, 128])
    oT_d = nc.dram_tensor("oT", [128, NSLOT, 512], BF16, kind="ExternalOutput")
    DBGA = 0
    if DBGA:
        dbg_d = nc.dram_tensor("dbg", [128, 2, 8, 512], F32, kind="ExternalOutput")
    scale = 128.0 ** -0.5

    with ExitStack() as st:
        sb = lambda name, shape, dt: st.enter_context(nc.sbuf_tensor(name, list(shape), dt))
        ksT = sb("ksT_sb", [128, S], BF16)
        kwT = sb("kwT_sb", [128, S], BF16)
        vs = sb("vs_sb", [128, 64, 128], BF16)
        vw = sb("vw_sb", [128, 64, 128], BF16)
        xc = sb("xc", [128, S], BF16)
        w1sb = sb("w1sb", [128, 32, 512], BF16)
        w2sb = sb("w2sb", [128, 4, 128], BF16)
        posT = sb("posT_sb", [128, 32], F32)
        XL = sb("XL", [128, 2, 512], BF16)
        hidT = sb("hidT", [128, 4, 512], BF16)
        tA = sb("tA", [128, 2, 512], F32)
        tB = sb("tB", [128, 2, 512], F32)
        kcmpT = sb("kcmpT", [128, 512], BF16)
        vcmp = sb("vcmp", [128, 4, 128], BF16)
        identf = sb("identf_sb", [128, 128], F32)
        i4 = sb("i4_sb", [128, 512], BF16)
        ones = sb("ones_sb", [128, 128], BF16)
        acon = sb("acon_sb", [128, 16, 128], BF16)
        ov = sb("ov_sb", [128, 4, 128], BF16)
        tailm = sb("tailm_sb", [128, 4, 128], BF16)
        winm = sb("winm_sb", [128, 8, 128], BF16)
        sel3 = sb("sel3_sb", [3, 3, 128], F32)
        qsb = sb("qsb", [128, 2, 512], BF16)
        qrsb = sb("qrsb", [128, 2, 512], BF16)
        g3sb = sb("g3sb", [3, 2, 512], F32)
        cmsb = sb("cmsb", [128, 2, 128], BF16)
        slc = sb("slc", [128, 2, 256], F32)
        Ec = sb("Ec", [128, 4, 512], BF16)
        NE = 3
        Eb = sb("Eb", [128, NE, 512], BF16)
        Pn = sb("Pn", [128, 4, 512], BF16)
        rden = sb("rden", [128, 3, 512], F32)
        wgt = sb("wgt", [128, 512], F32)
        tmp = sb("tmp", [128, 512], F32)
        acc = sb("acc", [128, 2, 512], F32)
        ost = sb("ost", [128, 2, 512], BF16)
        impm = sb("impm", [128, 128], F32)
        impm2 = sb("impm2", [128, 128], F32)
        t8 = sb("t8", [128, 16], F32)
        nsel = sb("nsel", [128, 128], F32)
        nselT = sb("nselT", [128, 2, 4, 128], BF16)
        PS = [st.enter_context(nc.psum_tensor("ps%d" % i, [128, 512], F32)) for i in range(8)]
        Sb, Ob, Db, M0, M1 = (0, 1, 6), (2, 3), (4, 5), 7, 7

        P = Prog(nc)
        ld = lambda eng, dst, src, key, sem: P.op(eng, lambda e: e.dma_start(out=dst, in_=src), writes=[key], dma_sem=sem)
        ld("sp", ksT[:], ksT_d.ap(), "ksT", "l_ks")
        ld("sp", kwT[:], kwT_d.ap(), "kwT", "l_kw")
        ld("sp", vs[:], vs_d.ap().rearrange("(t p) d -> p t d", p=128), "vs", "l_vs")
        ld("sp", vw[:], vw_d.ap().rearrange("(t p) d -> p t d", p=128), "vw", "l_vw")
        ld("act", identf[:], identf_d.ap(), "identf", "l_c0")
        ld("act", i4[:], i4_d.ap(), "i4", "l_c1")
        ld("act", ones[:], ones_d.ap(), "ones", "l_c2")
        ld("act", acon[:], acon_d.ap(), "acon", "l_c3")
        ld("act", ov[:], ov_d.ap(), "ov", "l_c4")
        ld("act", tailm[:], tailm_d.ap(), "tailm", "l_c5")
        ld("act", winm[:], winm_d.ap(), "winm", "l_c6")
        ld("act", sel3[:], sel3_d.ap(), "sel3", "l_c7")
        P.op("dve", lambda e: e.memset(XL[:], 0.0), writes=[("XL", 0), ("XL", 1)])

        GC = math.sqrt(2.0 / math.pi)
        for jv in range(2):
            src_d = kcT_d if jv == 0 else vcT_d
            ld("sp", xc[:], src_d.ap(), "xc", "l_xc")
            for q4 in range(4):
                P.op("pool", lambda e, q4=q4, jv=jv: e.dma_start(
                    out=w1sb[:, q4 * 8:(q4 + 1) * 8, :],
                    in_=w1_d.ap()[jv, q4 * 1024:(q4 + 1) * 1024, :].rearrange("(l p) h -> p l h", p=128)),
                    writes=[("w1", q4)], dma_sem=("l_w1", q4))
            P.op("pool", lambda e, jv=jv: e.dma_start(out=w2sb[:], in_=w2_d.ap()[jv].rearrange("(c p) d -> p c d", p=128)),
                 writes=["w2"], dma_sem="l_w2")
            ld("act", posT[:], posT_d.ap()[jv], "posT", "l_pos")
            for l in range(32):
                xi = P.rr("xl", 2)
                src = bass.AP(xc, l, [[S, 128], [16, 511]])
                P.op("dve", lambda e, xi=xi, src=src, l=l: e.tensor_scalar(
                    out=XL[:, xi, 0:511], in0=src, scalar1=posT[:, l:l + 1], scalar2=None, op0=ALU.add),
                    reads=["xc", "posT"], writes=[("XL", xi)])

                def fn(e, xi=xi, l=l):
                    ins = None
                    for hc in range(4):
                        ins = e.matmul(PS[hc][:], w1sb[:, l, hc * 128:(hc + 1) * 128], XL[:, xi, :],
                                       start=(l == 0), stop=(l == 31))
                    return ins
                P.op("pe", fn, reads=[("XL", xi), ("w1", l // 8)], writes=[("H", hc) for hc in range(4)])
            for hc in range(4):
                ti = P.rr("tAB", 2)
                P.op("act", lambda e, hc=hc, ti=ti: e.activation(out=tA[:, ti, :], in_=PS[hc][:], func=AF.Square),
                     reads=[("H", hc)], writes=[("tA", ti)])
                P.op("dve", lambda e, ti=ti: e.tensor_scalar(out=tA[:, ti, :], in0=tA[:, ti, :], scalar1=0.044715, scalar2=1.0,
                                                             op0=ALU.mult, op1=ALU.add),
                     reads=[("tA", ti)], writes=[("tA", ti)])
                P.op("dve", lambda e, hc=hc, ti=ti: e.tensor_tensor(out=tB[:, ti, :], in0=tA[:, ti, :], in1=PS[hc][:], op=ALU.mult),
                     reads=[("tA", ti), ("H", hc)], writes=[("tB", ti)])
                P.op("act", lambda e, ti=ti: e.activation(out=tB[:, ti, :], in_=tB[:, ti, :], func=AF.Tanh, scale=GC),
                     reads=[("tB", ti)], writes=[("tB", ti)])
                P.op("dve", lambda e, ti=ti: e.tensor_scalar(out=tB[:, ti, :], in0=tB[:, ti, :], scalar1=1.0, scalar2=0.5,
                                                             op0=ALU.add, op1=ALU.mult),
                     reads=[("tB", ti)], writes=[("tB", ti)])
                P.op("dve", lambda e, hc=hc, ti=ti: e.tensor_tensor(out=hidT[:, hc, :], in0=tB[:, ti, :], in1=PS[hc][:], op=ALU.mult),
                     reads=[("tB", ti), ("H", hc)], writes=[("hidT", hc)])
            hid_reads = [("hidT", hc) for hc in range(4)]
            if jv == 0:
                def fn(e):
                    ins = None
                    for hc in range(4):
                        ins = e.matmul(PS[4][:], w2sb[:, hc, :], hidT[:, hc, :], start=(hc == 0), stop=(hc == 3))
                    return ins
                P.op("pe", fn, reads=hid_reads + ["w2"], writes=[("ps", 4)])
                P.op("act", lambda e: e.activation(out=kcmpT[:], in_=PS[4][:], func=AF.Copy), reads=[("ps", 4)], writes=["kcmpT"])
            else:
                def fn(e):
                    ins = None
                    for nt in range(4):
                        for hc in range(4):
                            ins = e.matmul(PS[5][:, nt * 128:(nt + 1) * 128], hidT[:, hc, nt * 128:(nt + 1) * 128], w2sb[:, hc, :],
                                           start=(hc == 0), stop=(hc == 3))
                    return ins
                P.op("pe", fn, reads=hid_reads + ["w2"], writes=[("ps", 5)])
                P.op("act", lambda e: e.activation(out=vcmp[:].rearrange("p a b -> p (a b)"), in_=PS[5][:], func=AF.Copy),
                     reads=[("ps", 5)], writes=["vcmp"])
        ALLPS = [("H", h) for h in range(4)] + [("ps", 4), ("ps", 5)]
        PK = {0: ("S", 0), 1: ("S", 1), 2: ("O", 0), 3: ("O", 1), 4: ("D", 0), 5: ("D", 1), 6: ("S", 2)}
        P.op("pe", lambda e: e.matmul(PS[7][:, 0:128], ones[:], ones[:], start=True, stop=True),
             reads=["ones"], writes=ALLPS + [PK[i] for i in range(7)] + ["M"])

        def branch(tiles, qbuf_key, q_ap, ob, db, hooks=None):
            n = len(tiles)
            slots = {}

            def emit_s(ti_):
                kl, kkey, extra, vl, vkey = tiles[ti_]
                sbk = P.rr("S", 3)
                slots[ti_] = sbk

                def sfn(e, kl=kl, extra=extra, sbk=sbk):
                    ins = e.matmul(PS[Sb[sbk]][:], kl, q_ap, start=True, stop=(len(extra) == 0))
                    for xi_, (l_, r_, tp, _) in enumerate(extra):
                        kw = {} if tp is None else {"tile_position": tp}
                        ins = e.matmul(PS[Sb[sbk]][:], l_, r_, start=False, stop=(xi_ == len(extra) - 1), **kw)
                    return ins
                xkeys = [k for x_ in extra for k in x_[3]]
                P.op("pe", sfn, reads=[kkey, qbuf_key] + xkeys, writes=[("S", sbk)])

            def emit_rest(ti_):
                kl, kkey, extra, vl, vkey = tiles[ti_]
                sbk = slots[ti_]
                ei = P.rr("E", NE)
                P.op("act", lambda e, sbk=sbk, ei=ei: e.activation(out=Eb[:, ei, :], in_=PS[Sb[sbk]][:], func=AF.Exp, scale=scale),
                     reads=[("S", sbk)], writes=[("E", ei)])

                def ofn(e, vl=vl, ei=ei, ti_=ti_):
                    e.matmul(PS[Ob[ob]][:], vl, Eb[:, ei, :], start=(ti_ == 0), stop=(ti_ == n - 1))
                    return e.matmul(PS[Db[db]][:], ones[:], Eb[:, ei, :], start=(ti_ == 0), stop=(ti_ == n - 1))
                P.op("pe", ofn, reads=[vkey, ("E", ei), "ones"], writes=[("O", ob), ("D", db)])

            emit_s(0)
            if n > 1:
                emit_s(1)
            for ti_ in range(n):
                if ti_ + 2 < n:
                    emit_s(ti_ + 2)
                emit_rest(ti_)
                if hooks and ti_ in hooks:
                    for h_ in hooks[ti_]:
                        h_()

        def rden_of(db, rb):
            P.op("dve", lambda e: e.tensor_scalar(out=rden[:, rb, :], in0=PS[Db[db]][:], scalar1=1e-30, scalar2=None, op0=ALU.max),
                 reads=[("D", db)], writes=[("rden", rb)])
            P.op("dve", lambda e: e.reciprocal(out=rden[:, rb, :], in_=rden[:, rb, :]), reads=[("rden", rb)], writes=[("rden", rb)])

        cur_slot = [0]

        def dump(src_ap, key, idx):
            if DBGA and cur_slot[0] < 2:
                sl = cur_slot[0]
                P.op("sp", lambda e: e.dma_start(out=dbg_d.ap()[:, sl, idx, :], in_=src_ap), reads=[key], dma_sem="dbg")

        def combine(bi, ob, rb, qi, ai, first):
            P.op("pe", lambda e: e.matmul(PS[M1][:], sel3[:, bi, :], g3sb[:, qi, :], start=True, stop=True),
                 reads=["sel3", ("g3", qi)], writes=["M"])
            dump(rden[:, rb, :], ("rden", rb), bi * 2)
            P.op("dve", lambda e: e.tensor_tensor(out=wgt[:], in0=rden[:, rb, :], in1=PS[M1][:], op=ALU.mult),
                 reads=[("rden", rb), "M"], writes=["wgt"])
            dump(wgt[:], "wgt", bi * 2 + 1)
            if first:
                P.op("dve", lambda e: e.tensor_tensor(out=acc[:, ai, :], in0=wgt[:], in1=PS[Ob[ob]][:], op=ALU.mult),
                     reads=["wgt", ("O", ob)], writes=[("acc", ai)])
            else:
                P.op("dve", lambda e: e.tensor_tensor(out=tmp[:], in0=wgt[:], in1=PS[Ob[ob]][:], op=ALU.mult),
                     reads=["wgt", ("O", ob)], writes=["tmp"])
                P.op("pool", lambda e: e.tensor_tensor(out=acc[:, ai, :], in0=acc[:, ai, :], in1=tmp[:], op=ALU.add),
                     reads=["tmp", ("acc", ai)], writes=[("acc", ai)])

        def slot_loads(i):
            qi = i % 2
            ld("sp", qsb[:, qi, :], qT_d.ap()[:, i, :], ("q", qi), ("l_q", qi))
            ld("sp", qrsb[:, qi, :], qrT_d.ap()[:, i, :], ("qr", qi), ("l_qr", qi))
            ld("sp", g3sb[:, qi, :], g3_d.ap()[:, i, :], ("g3", qi), ("l_g3", qi))
            ld("sp", cmsb[:, qi, :], cmask_d.ap()[:, i, :], ("cm", qi), ("l_cm", qi))
            ld("sp", slc[:, qi, :], slotc_d.ap()[:, i, :], ("slc", qi), ("l_sl", qi))

        NSL = NSLOT
        SL = {}

        def slot_info(i):
            par = i % 2
            return par, 4 * (i // 2) + (1 if par == 0 else 3), i % 2

        def stage_a(i):
            par, qbmax, qi = slot_info(i)
            nkt = qbmax // 16 + 1
            obc, dbc = P.rr("O", 2), P.rr("D", 2)
            for kc_ in range(nkt):
                sbk = P.rr("S", 3)
                last = kc_ == nkt - 1

                def sfn(e, kc_=kc_, sbk=sbk, last=last, qi=qi):
                    ins = e.matmul(PS[Sb[sbk]][:], kcmpT[:, kc_ * 128:(kc_ + 1) * 128], qsb[:, qi, :], start=True, stop=not last)
                    if last:
                        ins = e.matmul(PS[Sb[sbk]][:], cmsb[:, qi, :], i4[:], start=False, stop=True)
                    return ins
                P.op("pe", sfn, reads=["kcmpT", ("q", qi), ("cm", qi), "i4"], writes=[("S", sbk)])
                P.op("act", lambda e, sbk=sbk, kc_=kc_: e.activation(out=Ec[:, kc_, :], in_=PS[Sb[sbk]][:], func=AF.Exp, scale=scale),
                     reads=[("S", sbk)], writes=[("Ec", kc_)])

                def ofn(e, kc_=kc_, last=last, obc=obc, dbc=dbc):
                    e.matmul(PS[Ob[obc]][:], vcmp[:, kc_, :], Ec[:, kc_, :], start=(kc_ == 0), stop=last)
                    return e.matmul(PS[Db[dbc]][:], ones[:], Ec[:, kc_, :], start=(kc_ == 0), stop=last)
                P.op("pe", ofn, reads=["vcmp", ("Ec", kc_), "ones"], writes=[("O", obc), ("D", dbc)])
            rbc = P.rr("rden", 3)
            rden_of(dbc, rbc)
            for kc_ in range(nkt):
                P.op("pool", lambda e, kc_=kc_, rbc=rbc: e.tensor_tensor(out=Pn[:, kc_, :], in0=Ec[:, kc_, :], in1=rden[:, rbc, :], op=ALU.mult),
                     reads=[("Ec", kc_), ("rden", rbc)], writes=[("Pn", kc_)])
            SL[i] = dict(nkt=nkt, obc=obc, rbc=rbc)
            combine(0, obc, rbc, qi, i % 2, True)

        def stage_b(i):
            par, qbmax, qi = slot_info(i)
            nkt = SL[i]["nkt"]

            def ifn(e, nkt=nkt):
                ins = None
                tot = nkt * 4
                c_ = 0
                for kc_ in range(nkt):
                    for g in range(4):
                        ins = e.matmul(PS[M0][:, 0:128], Pn[:, kc_, g * 128:(g + 1) * 128], ov[:, kc_, :],
                                       start=(c_ == 0), stop=(c_ == tot - 1))
                        c_ += 1
                return ins
            P.op("pe", ifn, reads=[("Pn", k) for k in range(nkt)] + ["ov"], writes=["M"])
            P.op("dve", lambda e, qi=qi: e.tensor_tensor(out=impm[:], in0=PS[M0][:, 0:128], in1=slc[:, qi, 0:128], op=ALU.mult),
                 reads=["M", ("slc", qi)], writes=["impm"])
            P.op("dve", lambda e, qi=qi: e.tensor_tensor(out=impm[:], in0=impm[:], in1=slc[:, qi, 128:256], op=ALU.add),
                 reads=["impm", ("slc", qi)], writes=["impm"])
            P.op("dve", lambda e: e.max(out=t8[:, 0:8], in_=impm[:]), reads=["impm"], writes=["t8a"])
            P.op("dve", lambda e: e.match_replace(out=impm2[:], in_to_replace=t8[:, 0:8], in_values=impm[:], imm_value=-1.0e9),
                 reads=["impm", "t8a"], writes=["impm2"])
            P.op("dve", lambda e: e.max(out=t8[:, 8:16], in_=impm2[:]), reads=["impm2"], writes=["t8b"])
            P.op("dve", lambda e: e.tensor_scalar(out=nsel[:], in0=impm[:], scalar1=t8[:, 15:16], scalar2=None, op0=ALU.is_lt),
                 reads=["impm", "t8b"], writes=["nsel"])

        def stage_c(i):
            par, qbmax, qi = slot_info(i)
            P.op("pe", lambda e: e.transpose(out=PS[M0][:, 128:256], in_=nsel[:], identity=identf[:]),
                 reads=["nsel", "identf"], writes=["M"])
            ni = P.rr("nselT", 2)
            for g in range(4):
                P.op("dve", lambda e, g=g, ni=ni: e.tensor_copy(out=nselT[:, ni, g, :], in_=PS[M0][:, 128:256]),
                     reads=["M"], writes=[("nselT", ni)])
            SL[i]["ni"] = ni

        if NSL > 0:
            slot_loads(0)
            stage_a(0)
            stage_b(0)
            stage_c(0)
        for i in range(NSL):
            par, qbmax, qi = slot_info(i)
            ai = i % 2
            nxt = i + 1 < NSL
            if nxt:
                slot_loads(i + 1)
                stage_a(i + 1)

            tiles = []
            for jj in range(6):
                kt = qbmax - 5 + jj
                if kt < 0:
                    continue
                extra = []
                if jj in (0, 1, 4, 5):
                    extra.append((winm[:, par * 4 + (0, 1, None, None, 2, 3)[jj], :], i4[:], None, ["winm", "i4"]))
                tiles.append((kwT[:, kt * 128:(kt + 1) * 128], "kwT", extra, vw[:, kt, :], "vw"))
            obw, dbw = P.rr("O", 2), P.rr("D", 2)
            branch(tiles, ("qr", qi), qrsb[:, qi, :], obw, dbw)
            rbw = P.rr("rden", 3)
            rden_of(dbw, rbw)
            combine(2, obw, rbw, qi, ai, False)

            ni = SL[i]["ni"]
            tiles = []
            for kt in range(qbmax + 1):
                j, r = kt // 16, kt % 16
                extra = [(acon[32 * j:32 * j + 32, r, :], nselT[32 * j:32 * j + 32, ni, :, :].rearrange("p a b -> p (a b)"),
                          (32 * j, 0), ["acon", ("nselT", ni)])]
                if kt == qbmax - 1:
                    extra.append((tailm[:, par * 2 + 0, :], i4[:], None, ["tailm", "i4"]))
                if kt == qbmax:
                    extra.append((tailm[:, par * 2 + 1, :], i4[:], None, ["tailm", "i4"]))
                tiles.append((ksT[:, kt * 128:(kt + 1) * 128], "ksT", extra, vs[:, kt, :], "vs"))
            hooks = {}
            if nxt:
                nt_ = len(tiles)
                hooks.setdefault(min(1, nt_ - 1), []).append(lambda i=i: stage_b(i + 1))
                hooks.setdefault(min(7, nt_ - 1), []).append(lambda i=i: stage_c(i + 1))
            obs, dbs = P.rr("O", 2), P.rr("D", 2)
            branch(tiles, ("qr", qi), qrsb[:, qi, :], obs, dbs, hooks)
            rbs = P.rr("rden", 3)
            rden_of(dbs, rbs)
            combine(1, obs, rbs, qi, ai, False)

            oi = P.rr("ost", 2)
            P.op("act", lambda e, oi=oi, ai=ai: e.activation(out=ost[:, oi, :], in_=acc[:, ai, :], func=AF.Copy),
                 reads=[("acc", ai)], writes=[("ost", oi)])
            P.op("sp", lambda e, oi=oi, i=i: e.dma_start(out=oT_d.ap()[:, i, :], in_=ost[:, oi, :]),
                 reads=[("ost", oi)], dma_sem=("st_o", oi))
        P.emit(st, final_waits=[("st_o", 0), ("st_o", 1), "dbg"])
    return nc


def nsa_attn_consts(half):
    c = {}
    c["identf"] = np.eye(128, dtype=np.float32)
    c["i4"] = np.tile(np.eye(128, dtype=np.float32), (1, 4)).astype(NPBF)
    c["ones"] = np.ones((128, 128), np.float32).astype(NPBF)
    p = np.arange(128)
    k = np.arange(128)
    acon = np.zeros((128, 16, 128), np.float32)
    for r in range(16):
        acon[:, r, :] = np.where((p[:, None] % 32) == 2 * r + (k[None, :] >= 64), NEG, 0.0)
    c["acon4"] = acon.astype(NPBF)
    ov = np.zeros((128, 4, 128), np.float32)
    for kt in range(4):
        n = 128 * kt + p
        cs = n * 16
        ss = np.arange(128) * 64
        o = (cs[:, None] < ss[None, :] + 64) & (cs[:, None] + 32 > ss[None, :]) & (n[:, None] <= 510)
        ov[:, kt, :] = o
    c["ov"] = ov.astype(NPBF)
    q = p[:, None]
    kk = k[None, :]
    zero = np.zeros((128, 128), np.float32)
    allneg = np.full((128, 128), NEG, np.float32)
    caus = np.where(kk <= q, 0.0, NEG).astype(np.float32)
    winold = np.where(kk > q, 0.0, NEG).astype(np.float32)
    tail = np.zeros((128, 4, 128), np.float32)
    winm = np.zeros((128, 8, 128), np.float32)
    for par in range(2):
        higher = (par == 1) if half == 0 else (par == 0)
        if higher:
            tail[:, par * 2 + 0] = zero
            tail[:, par * 2 + 1] = caus
            w = [allneg, winold, zero, caus]
        else:
            tail[:, par * 2 + 0] = caus
            tail[:, par * 2 + 1] = allneg
            w = [winold, zero, caus, allneg]
        for x_ in range(4):
            winm[:, par * 4 + x_] = w[x_]
    c["tailm"] = tail.astype(NPBF)
    c["winm"] = winm.astype(NPBF)
    cmask = np.zeros((128, NSLOT, 128), np.float32)
    slotc = np.zeros((128, NSLOT, 256), np.float32)
    s_ = np.arange(128)[None, :]
    for i in range(NSLOT):
        qb, qbmax = slot_qb(i, half)
        t = 128 * qb + p[:, None]
        ktc = qbmax // 16
        n = 128 * ktc + k[None, :]
        cmask[:, i, :] = np.where(16 * n + 31 <= t, 0.0, NEG)
        cur = t // 64
        m1 = np.ones((128, 128), np.float32)
        m2 = np.zeros((128, 128), np.float32)
        f0 = (s_ == 0) & (s_ <= cur)
        m1[np.broadcast_to(f0, m1.shape)] = 0.0
        m2[np.broadcast_to(f0, m2.shape)] = BIGV + 2
        fp = (s_ == cur - 1)
        m1[fp] = 0.0
        m2[fp] = BIGV + 1
        fc = (s_ == cur)
        m1[fc] = 0.0
        m2[fc] = BIGV
        nc_ = s_ > cur
        m1[nc_] = 0.0
        m2[nc_] = (-1.0 - np.broadcast_to(s_, m2.shape))[nc_]
        slotc[:, i, 0:128] = m1
        slotc[:, i, 128:256] = m2
    c["cmask"] = cmask.astype(NPBF)
    c["slotc"] = slotc
    sel3 = np.zeros((3, 3, 128), np.float32)
    for b in range(3):
        sel3[b, b, :] = 1.0
    c["sel3"] = sel3
    return c


_PROG = {}
_IDENT = np.eye(128, dtype=np.float32)


def _run(nc, maps):
    res = run_bass_kernel_spmd(nc, maps, core_ids=list(range(NCORES)))
    return res.results


def _cat(res, name, axis):
    return np.concatenate([np.asarray(r[name]) for r in res], axis=axis)


def dense_maps(xs, inp, layer_done, oT_full, proj, layer_next):
    maps = []
    for c in range(NCORES):
        m = {"x": xs[c], "ident": _IDENT}
        if layer_done is not None:
            L = layer_done
            m["oT"] = np.ascontiguousarray(oT_full[:, c * TPC:(c + 1) * TPC])
            m["w_o"] = inp["nsa_w_out"][L] if L < 2 else inp["diff_w_out"][L - 2]
            m["g_mlp"] = gain_fm(inp["mlp_norm_g"][L])
            m["w_up"] = inp["mlp_w_up"][L]
            m["w_down"] = inp["mlp_w_down"][L]
        if proj is None:
            m["g_final"] = np.asarray(inp["final_norm_g"], np.float32)
        else:
            m["g_attn"] = gain_fm(inp["attn_norm_g"][layer_next])
            C, Sn, RT = rope_consts(128 if proj == "nsa" else 64, c * TPC, TPC)
            m["cosT"], m["sinT"], m["rotT"] = C, Sn, RT
            if proj == "nsa":
                m["w_in"] = inp["nsa_w_in"][layer_next]
            else:
                m["w_q"] = inp["diff_w_q"][layer_next - 2]
            if proj == "diffkv":
                m["g_kv"] = gain_fm(inp["kv_norm_g"])
                m["w_kv"] = inp["kv_w_shared"]
        maps.append(m)
    return maps


def nsa_attn_maps(res, inp, layer):
    qT = _cat(res, "qT", 1).reshape(16, 128, 64, 128)
    qrT = _cat(res, "qrT", 1).reshape(16, 128, 64, 128)
    kcT, vcT = _cat(res, "kcT", 1), _cat(res, "vcT", 1)
    ksT, kwT = _cat(res, "ksT", 1), _cat(res, "kwT", 1)
    vs, vw = _cat(res, "vs", 0), _cat(res, "vw", 0)
    gT = _cat(res, "gT", 1)
    maps = []
    for c in range(NCORES):
        hk, half = c // 2, c % 2
        qbs = [slot_qb(i, half)[0] for i in range(NSLOT)]
        m = dict(nsa_attn_consts(half))
        rs = slice(hk * 128, (hk + 1) * 128)
        m["kcT"] = np.ascontiguousarray(kcT[rs])
        m["vcT"] = np.ascontiguousarray(vcT[rs])
        m["ksT"] = np.ascontiguousarray(ksT[rs])
        m["kwT"] = np.ascontiguousarray(kwT[rs])
        m["vs"] = np.ascontiguousarray(vs[:, rs])
        m["vw"] = np.ascontiguousarray(vw[:, rs])
        m["qT"] = np.ascontiguousarray(qT[4 * hk:4 * hk + 4][:, :, qbs, :].transpose(1, 2, 0, 3)).reshape(128, NSLOT, 512)
        m["qrT"] = np.ascontiguousarray(qrT[4 * hk:4 * hk + 4][:, :, qbs, :].transpose(1, 2, 0, 3)).reshape(128, NSLOT, 512)
        gv = gT[hk * 12:(hk + 1) * 12].reshape(4, 3, 64, 128)[:, :, qbs, :]
        m["g3"] = np.ascontiguousarray(gv.transpose(1, 2, 0, 3)).reshape(3, NSLOT, 512)
        m["w1"] = inp["nsa_cmp_w1"][layer]
        m["w2"] = inp["nsa_cmp_w2"][layer]
        m["posT"] = np.ascontiguousarray(np.asarray(inp["nsa_cmp_pos"][layer]).transpose(0, 2, 1))
        maps.append(m)
    return maps


def nsa_attn_gather(res):
    oT = np.zeros((16, 128, 64, 128), NPBF)
    for c in range(NCORES):
        hk, half = c // 2, c % 2
        qbs = [slot_qb(i, half)[0] for i in range(NSLOT)]
        o = np.asarray(res[c]["oT"]).reshape(128, NSLOT, 4, 128).transpose(2, 0, 1, 3)
        oT[4 * hk:4 * hk + 4][:, :, qbs, :] = o
    return oT.reshape(2048, 8192)


def build_diff_attn():
    nc = bass.Bass("TRN2", target_bir_lowering=False)
    din = lambda name, shape, dt=F32: nc.dram_tensor(name, list(shape), dt, kind="ExternalInput")
    kT_d = din("kT", [128, S], BF16)
    v_d = din("v", [S, 128], BF16)
    qT_d = din("qT", [128, 64, 512], BF16)
    lamv_d = din("lamv", [128, 256])
    g_d = din("subg", [128, 1])
    linit_d = din("linit", [128, 2])
    i4_d = din("i4", [128, 512], BF16)
    ones_d = din("ones", [128, 128], BF16)
    onesf_d = din("onesf", [128, 128])
    caus_d = din("causT", [128, 128], BF16)
    oT_d = nc.dram_tensor("oT", [128, 64, 256], BF16, kind="ExternalOutput")
    scale = 64.0 ** -0.5
    with ExitStack() as st:
        sb = lambda name, shape, dt: st.enter_context(nc.sbuf_tensor(name, list(shape), dt))
        kT = sb("kT_sb", [128, S], BF16)
        v = sb("v_sb", [128, 64, 128], BF16)
        qsb = sb("q_sb", [128, 64, 512], BF16)
        lamv = sb("lamv_sb", [128, 256], F32)
        gcol = sb("gcol", [128, 1], F32)
        linit = sb("linit_sb", [128, 2], F32)
        i4 = sb("i4_sb", [128, 512], BF16)
        ones = sb("ones_sb", [128, 128], BF16)
        onesf = sb("onesf_sb", [128, 128], F32)
        caus = sb("caus_sb", [128, 128], BF16)
        lw = sb("lw", [128, 128], F32)
        ls = sb("ls", [128, 4], F32)
        nlam = sb("nlam", [128, 1], F32)
        gsc = sb("gsc", [128, 1], F32)
        NE = 3
        Eb = sb("Eb", [128, NE, 512], BF16)
        rden = sb("rden", [128, 512], F32)
        A = sb("A", [128, 512], F32)
        o = sb("o", [128, 256], F32)
        sq = sb("sq", [128, 256], F32)
        rs = sb("rs", [128, 256], F32)
        on = sb("on", [128, 256], F32)
        ost = sb("ost", [128, 2, 256], BF16)
        PS = [st.enter_context(nc.psum_tensor("ps%d" % i, [128, 512], F32)) for i in range(8)]
        Sb, Ob, Db, M0 = (0, 1, 2), (3, 4), (5, 6), 7
        P = Prog(nc)
        ld = lambda eng, dst, src, key, sem: P.op(eng, lambda e: e.dma_start(out=dst, in_=src), writes=[key], dma_sem=sem)
        ld("sp", kT[:], kT_d.ap(), "kT", "l_k")
        ld("sp", v[:], v_d.ap().rearrange("(t p) d -> p t d", p=128), "v", "l_v")
        ld("sp", qsb[:], qT_d.ap(), "q", "l_q")
        ld("act", lamv[:], lamv_d.ap(), "lamv", "l_c0")
        ld("act", gcol[:], g_d.ap(), "gcol", "l_c1")
        ld("act", linit[:], linit_d.ap(), "linit", "l_c2")
        ld("act", i4[:], i4_d.ap(), "i4", "l_c3")
        ld("act", ones[:], ones_d.ap(), "ones", "l_c4")
        ld("act", onesf[:], onesf_d.ap(), "onesf", "l_c5")
        ld("act", caus[:], caus_d.ap(), "caus", "l_c6")
        P.op("dve", lambda e: e.tensor_tensor(out=lw[:, 0:64], in0=lamv[:, 0:64], in1=lamv[:, 64:128], op=ALU.mult),
             reads=["lamv"], writes=["lw0"])
        P.op("dve", lambda e: e.tensor_tensor(out=lw[:, 64:128], in0=lamv[:, 128:192], in1=lamv[:, 192:256], op=ALU.mult),
             reads=["lamv"], writes=["lw1"])
        P.op("dve", lambda e: e.reduce_sum(out=ls[:, 0:1], in_=lw[:, 0:64], axis=AX.X), reads=["lw0"], writes=["ls0"])
        P.op("dve", lambda e: e.reduce_sum(out=ls[:, 1:2], in_=lw[:, 64:128], axis=AX.X), reads=["lw1"], writes=["ls1"])
        P.op("act", lambda e: e.activation(out=ls[:, 2:4], in_=ls[:, 0:2], func=AF.Exp), reads=["ls0", "ls1"], writes=["ls23"])
        P.op("dve", lambda e: e.tensor_tensor(out=nlam[:], in0=ls[:, 3:4], in1=ls[:, 2:3], op=ALU.subtract),
             reads=["ls23"], writes=["nlam"])
        P.op("dve", lambda e: e.tensor_tensor(out=nlam[:], in0=nlam[:], in1=linit[:, 0:1], op=ALU.subtract),
             reads=["nlam", "linit"], writes=["nlam"])
        P.op("dve", lambda e: e.tensor_tensor(out=gsc[:], in0=gcol[:], in1=linit[:, 1:2], op=ALU.mult),
             reads=["gcol", "linit"], writes=["gsc"])

        NQ = 64
        for qb in range(NQ):
            ob, db = P.rr("O", 2), P.rr("D", 2)
            n = qb + 1
            sl_ = {}

            def emit_s(kt, qb=qb):
                sbk = P.rr("S", 3)
                sl_[kt] = sbk
                diag = kt == qb

                def sfn(e, kt=kt, sbk=sbk, diag=diag, qb=qb):
                    ksl = slice(kt * 128, (kt + 1) * 128)
                    ins = e.matmul(PS[Sb[sbk]][:], kT[:, ksl], qsb[:, qb, :], start=True, stop=not diag)
                    if diag:
                        ins = e.matmul(PS[Sb[sbk]][:], caus[:], i4[:], start=False, stop=True)
                    return ins
                P.op("pe", sfn, reads=["kT", "q", "caus", "i4"], writes=[("S", sbk)])

            def emit_rest(kt, n=n, ob=ob, db=db):
                sbk = sl_[kt]
                ei = P.rr("E", NE)
                P.op("act", lambda e, sbk=sbk, ei=ei: e.activation(out=Eb[:, ei, :], in_=PS[Sb[sbk]][:], func=AF.Exp, scale=scale),
                     reads=[("S", sbk)], writes=[("E", ei)])

                def ofn(e, kt=kt, ei=ei, n=n, ob=ob, db=db):
                    e.matmul(PS[Ob[ob]][:], v[:, kt, :], Eb[:, ei, :], start=(kt == 0), stop=(kt == n - 1))
                    return e.matmul(PS[Db[db]][:], ones[:], Eb[:, ei, :], start=(kt == 0), stop=(kt == n - 1))
                P.op("pe", ofn, reads=["v", ("E", ei), "ones"], writes=[("O", ob), ("D", db)])

            emit_s(0)
            if n > 1:
                emit_s(1)
            for kt in range(n):
                if kt + 2 < n:
                    emit_s(kt + 2)
                emit_rest(kt)
            P.op("dve", lambda e, db=db: e.tensor_scalar(out=rden[:], in0=PS[Db[db]][:], scalar1=1e-30, scalar2=None, op0=ALU.max),
                 reads=[("D", db)], writes=["rden"])
            P.op("dve", lambda e: e.reciprocal(out=rden[:], in_=rden[:]), reads=["rden"], writes=["rden"])
            P.op("dve", lambda e, ob=ob: e.tensor_tensor(out=A[:], in0=rden[:], in1=PS[Ob[ob]][:], op=ALU.mult),
                 reads=["rden", ("O", ob)], writes=["A"])
            P.op("dve", lambda e: e.scalar_tensor_tensor(out=o[:], in0=A[:, 256:512], scalar=nlam[:, 0:1], in1=A[:, 0:256],
                                                         op0=ALU.mult, op1=ALU.add),
                 reads=["A", "nlam"], writes=["o"])
            P.op("pool", lambda e: e.tensor_tensor(out=sq[:], in0=o[:], in1=o[:], op=ALU.mult), reads=["o"], writes=["sq"])
            P.op("pe", lambda e: e.matmul(PS[M0][:, 0:256], onesf[:], sq[:], start=True, stop=True),
                 reads=["sq", "onesf"], writes=["M0"])
            P.op("dve", lambda e: e.tensor_scalar(out=rs[:], in0=PS[M0][:, 0:256], scalar1=1.0 / 128, scalar2=EPS,
                                                  op0=ALU.mult, op1=ALU.add), reads=["M0"], writes=["rs"])
            P.op("act", lambda e: e.activation(out=rs[:], in_=rs[:], func=AF.Ln), reads=["rs"], writes=["rs"])
            P.op("act", lambda e: e.activation(out=rs[:], in_=rs[:], func=AF.Exp, scale=-0.5), reads=["rs"], writes=["rs"])
            P.op("dve", lambda e: e.tensor_tensor(out=on[:], in0=o[:], in1=rs[:], op=ALU.mult), reads=["o", "rs"], writes=["on"])
            oi = P.rr("ost", 2)
            P.op("act", lambda e, oi=oi: e.activation(out=ost[:, oi, :], in_=on[:], func=AF.Identity, scale=gsc[:, 0:1]),
                 reads=["on", "gsc"], writes=[("ost", oi)])
            P.op("sp", lambda e, oi=oi, qb=qb: e.dma_start(out=oT_d.ap()[:, qb, :], in_=ost[:, oi, :]),
                 reads=[("ost", oi)], dma_sem=("st_o", oi))
        P.emit(st, final_waits=[("st_o", 0), ("st_o", 1)])
    return nc


def diff_attn_maps(dqT, dkT, dv, inp, layer):
    j = layer - 2
    li = 0.8 - 0.6 * math.exp(-0.3 * layer)
    q4 = dqT.reshape(16, 128, 64, 128)
    p = np.arange(128)
    consts = {
        "i4": np.tile(np.eye(128, dtype=np.float32), (1, 4)).astype(NPBF),
        "ones": np.ones((128, 128), np.float32).astype(NPBF),
        "onesf": np.ones((128, 128), np.float32),
        "causT": np.where(p[None, :] <= p[:, None], 0.0, NEG).astype(np.float32).astype(NPBF),
        "lamv": np.ascontiguousarray(np.broadcast_to(np.asarray(inp["diff_lambda"][j], np.float32).reshape(1, 256), (128, 256))),
        "subg": np.ascontiguousarray(np.asarray(inp["diff_subln_g"][j], np.float32).reshape(128, 1)),
        "linit": np.ascontiguousarray(np.broadcast_to(np.array([[li, 1.0 - li]], np.float32), (128, 2))),
    }
    maps = []
    for c in range(NCORES):
        hk = c // 2
        m = dict(consts)
        m["kT"] = np.ascontiguousarray(dkT[hk * 128:(hk + 1) * 128])
        m["v"] = np.ascontiguousarray(dv[:, hk * 128:(hk + 1) * 128])
        qq = np.ascontiguousarray(q4[2 * c:2 * c + 2].transpose(1, 2, 0, 3))
        qz = np.zeros((128, 64, 2, 2, 128), NPBF)
        qz[0:64, :, 0] = qq[0:64]
        qz[64:128, :, 1] = qq[64:128]
        m["qT"] = qz.reshape(128, 64, 512)
        maps.append(m)
    return maps


def diff_attn_gather(res):
    oT = np.zeros((16, 128, 64, 128), NPBF)
    for c in range(NCORES):
        o = np.asarray(res[c]["oT"]).reshape(128, 64, 2, 128).transpose(2, 0, 1, 3)
        oT[2 * c:2 * c + 2] = o
    return oT.reshape(2048, 8192)


def kernel(x, attn_norm_g, mlp_norm_g, final_norm_g, nsa_w_in, nsa_cmp_pos, nsa_cmp_w1, nsa_cmp_w2, nsa_w_out,
           kv_norm_g, kv_w_shared, diff_w_q, diff_lambda, diff_subln_g, diff_w_out, mlp_w_up, mlp_w_down, _debug=None):
    inp = dict(attn_norm_g=attn_norm_g, mlp_norm_g=mlp_norm_g, final_norm_g=final_norm_g, nsa_w_in=nsa_w_in,
               nsa_cmp_pos=nsa_cmp_pos, nsa_cmp_w1=nsa_cmp_w1, nsa_cmp_w2=nsa_cmp_w2, nsa_w_out=nsa_w_out,
               kv_norm_g=kv_norm_g, kv_w_shared=kv_w_shared, diff_w_q=diff_w_q, diff_lambda=diff_lambda,
               diff_subln_g=diff_subln_g, diff_w_out=diff_w_out, mlp_w_up=mlp_w_up, mlp_w_down=mlp_w_down)
    inp = {k: np.asarray(v, np.float32) for k, v in inp.items()}
    x2 = np.asarray(x, np.float32).reshape(S, D)
    xs = [np.ascontiguousarray(x2[c * TPC:(c + 1) * TPC]) for c in range(NCORES)]
    if "nsa" not in _PROG:
        _PROG["nsa"] = build_nsa_attn()
        _PROG["diff"] = build_diff_attn()
    dbg = {}
    res = _run(get_dense(False, False, False, "nsa"), dense_maps(xs, inp, None, None, "nsa", 0))
    dkT = dv = None
    for layer in range(4):
        if layer < 2:
            ra = _run(_PROG["nsa"], nsa_attn_maps(res, inp, layer))
            oT = nsa_attn_gather(ra)
        else:
            if layer == 2:
                dkT, dv = _cat(res, "dkT", 1), _cat(res, "dv", 0)
            ra = _run(_PROG["diff"], diff_attn_maps(_cat(res, "dqT", 1), dkT, dv, inp, layer))
            oT = diff_attn_gather(ra)
        if _debug is not None:
            dbg["oT%d" % layer] = oT
        nxt = [("nsa", 1), ("diffkv", 2), ("diff", 3), (None, None)][layer]
        res = _run(get_dense(True, True, nxt[0] is None, nxt[0]), dense_maps(xs, inp, layer, oT, nxt[0], nxt[1]))
        if nxt[0] is not None:
            xs = [np.asarray(r["x_out"]) for r in res]
            if _debug is not None:
                dbg["x%d" % layer] = np.concatenate(xs, 0)
    y = np.concatenate([np.asarray(r["y"]) for r in res], 0).reshape(1, S, D).astype(np.float32)
    if _debug is not None:
        _debug.update(dbg)
    return y
```

```python
import math
from contextlib import ExitStack

import numpy as np
import ml_dtypes
import concourse.bass as bass
import concourse.mybir as mybir
from concourse.bass_utils import run_bass_kernel_spmd

F32 = mybir.dt.float32
BF16 = mybir.dt.bfloat16
AF = mybir.ActivationFunctionType
ALU = mybir.AluOpType
AX = mybir.AxisListType
NPBF = ml_dtypes.bfloat16

NCORES = 8
S = 8192
D = 2048
TPC = S // NCORES
NTT = TPC // 128
EPS = 1e-6
NEG = -30000.0
ROPE_THETA = 500000.0

ENGS = ("pe", "act", "dve", "pool", "sp")


class Op:
    __slots__ = ("eng", "fn", "deps", "signalled", "sig", "dma_sem", "dma_val")


class Prog:
    def __init__(self, nc):
        self.nc = nc
        self.ops = []
        self.last_w = {}
        self.readers = {}
        self.dma_cum = {}
        self.rot = {}
        self.dry = False

    def rr(self, name, n):
        if self.dry:
            return 0
        i = self.rot.get(name, 0)
        self.rot[name] = i + 1
        return i % n

    def op(self, eng, fn, reads=(), writes=(), dma_sem=None, ndma=1):
        if self.dry:
            return None
        o = Op()
        o.eng = eng
        o.fn = fn
        o.signalled = False
        o.sig = 0
        o.dma_sem = dma_sem
        o.dma_val = 0
        deps = set()
        for k in reads:
            w = self.last_w.get(k)
            if w is not None:
                deps.add(w)
        for k in writes:
            w = self.last_w.get(k)
            if w is not None:
                deps.add(w)
            for r in self.readers.get(k, ()):
                deps.add(r)
        o.deps = [d for d in deps
                  if not (d.eng == "pe" and eng == "pe" and d.dma_sem is None and dma_sem is None)]
        if dma_sem is not None:
            self.dma_cum[dma_sem] = self.dma_cum.get(dma_sem, 0) + 16 * ndma
            o.dma_val = self.dma_cum[dma_sem]
        for d in o.deps:
            if d.dma_sem is None:
                d.signalled = True
        for k in writes:
            self.last_w[k] = o
            self.readers[k] = []
        for k in reads:
            if k not in writes:
                self.readers.setdefault(k, []).append(o)
        self.ops.append(o)
        return o

    def emit(self, stack, final_waits=()):
        nc = self.nc
        cnt = {e: 0 for e in ENGS}
        for o in self.ops:
            if o.dma_sem is None and o.signalled:
                cnt[o.eng] += 1
                o.sig = cnt[o.eng]
        esem = {e: stack.enter_context(nc.semaphore("s_" + e)) for e in ENGS}
        dsem = {}
        for i, k in enumerate(self.dma_cum):
            dsem[k] = stack.enter_context(nc.semaphore("d%d" % i))
        block = stack.enter_context(nc.Block())
        per = {e: [o for o in self.ops if o.eng == e] for e in ENGS}
        dma_cum = self.dma_cum

        def run(e, eng):
            waited = {}
            for o in per[e]:
                need = {}
                for d in o.deps:
                    if d.dma_sem is not None:
                        key = ("d", d.dma_sem)
                        v = d.dma_val
                    else:
                        key = ("e", d.eng)
                        v = d.sig
                    if v > need.get(key, 0):
                        need[key] = v
                for key, v in need.items():
                    if v > waited.get(key, 0):
                        sem = dsem[key[1]] if key[0] == "d" else esem[key[1]]
                        eng.wait_ge(sem, v)
                        waited[key] = v
                r = o.fn(eng)
                if o.dma_sem is not None:
                    rs = r if isinstance(r, (list, tuple)) else [r]
                    for ins in rs:
                        ins.then_inc(dsem[o.dma_sem], 16)
                elif o.signalled:
                    r.then_inc(esem[e], 1)
            if e == "sp":
                for k in final_waits:
                    if k in dsem:
                        eng.wait_ge(dsem[k], dma_cum[k])

        @block.tensor
        def _(eng):
            run("pe", eng)

        @block.scalar
        def _(eng):
            run("act", eng)

        @block.vector
        def _(eng):
            run("dve", eng)

        @block.gpsimd
        def _(eng):
            run("pool", eng)

        @block.sync
        def _(eng):
            run("sp", eng)


def rope_consts(head_chunk, tok0, ntok):
    rot = head_chunk // 4
    half = rot // 2
    inv = 1.0 / (ROPE_THETA ** (np.arange(0, rot, 2, dtype=np.float32) / np.float32(rot)))
    inv = inv.astype(np.float32)
    pos = np.arange(tok0, tok0 + ntok, dtype=np.float32)
    ang = (pos[None, :] * inv[:, None]).astype(np.float32)
    cos = np.cos(ang).astype(np.float32)
    sin = np.sin(ang).astype(np.float32)
    C = np.ones((128, ntok), np.float32)
    Sn = np.zeros((128, ntok), np.float32)
    Rm = np.zeros((128, 128), np.float32)
    for base in range(0, 128, head_chunk):
        for j in range(half):
            C[base + j] = cos[j]
            C[base + half + j] = cos[j]
            Sn[base + j] = sin[j]
            Sn[base + half + j] = sin[j]
            Rm[base + j, base + half + j] = -1.0
            Rm[base + half + j, base + j] = 1.0
    return C, Sn, np.ascontiguousarray(Rm.T)


def build_dense(oproj, mlp, final, proj):
    nc = bass.Bass("TRN2", target_bir_lowering=False)
    T = TPC
    dr = {}

    def din(name, shape, dt=F32):
        dr[name] = nc.dram_tensor(name, list(shape), dt, kind="ExternalInput")
        return dr[name]

    def dout(name, shape, dt=F32):
        dr[name] = nc.dram_tensor(name, list(shape), dt, kind="ExternalOutput")
        return dr[name]

    x_d = din("x", [T, D])
    ident_d = din("ident", [128, 128])
    if oproj:
        oT_d = din("oT", [D, T], BF16)
        wo_d = din("w_o", [D, D])
    if mlp:
        gm_d = din("g_mlp", [128, 16])
        wu_d = din("w_up", [D, 4 * D])
        wd_d = din("w_down", [4 * D, D])
    if final:
        gf_d = din("g_final", [D])
        y_d = dout("y", [T, D])
    else:
        xo_d = dout("x_out", [T, D])
    if proj:
        ga_d = din("g_attn", [128, 16])
        cos_d = din("cosT", [128, T])
        sin_d = din("sinT", [128, T])
        rt_d = din("rotT", [128, 128])
    if proj == "nsa":
        win_d = din("w_in", [D, 5168])
        qT_d = dout("qT", [D, T], BF16)
        qrT_d = dout("qrT", [D, T], BF16)
        kcT_d = dout("kcT", [512, T], BF16)
        vcT_d = dout("vcT", [512, T], BF16)
        ksT_d = dout("ksT", [512, T], BF16)
        kwT_d = dout("kwT", [512, T], BF16)
        vs_d = dout("vs", [T, 512], BF16)
        vw_d = dout("vw", [T, 512], BF16)
        gT_d = dout("gT", [48, T])
    if proj in ("diff", "diffkv"):
        wq_d = din("w_q", [D, D])
        dqT_d = dout("dqT", [D, T], BF16)
    if proj == "diffkv":
        gk_d = din("g_kv", [128, 16])
        wkv_d = din("w_kv", [D, 1024])
        dkT_d = dout("dkT", [512, T], BF16)
        dv_d = dout("dv", [T, 512], BF16)

    with ExitStack() as st:
        sb = lambda name, shape, dt: st.enter_context(nc.sbuf_tensor(name, list(shape), dt))
        x = sb("x_sb", [128, NTT, D], F32)
        big1 = sb("big1", [128, 16, T], BF16)
        big2 = sb("big2", [128, 16, T], BF16)
        NWB = 2
        wb = sb("wb", [128, NWB, 16, 512], BF16)
        wst = sb("wst", [128, 2, 4, 512], F32)
        ident = sb("ident_sb", [128, 128], F32)
        xh = sb("xh", [128, 1, D], F32)
        ssq = sb("ssq", [128, 32], F32)
        rstd = sb("rstd", [128, 32], F32)
        gT = sb("gT_sb", [128, 4, 16], F32)
        NST = 4
        stg = sb("stg", [128, NST, 512], BF16)
        r32 = sb("r32", [128, 2, 512], F32)
        if proj:
            cosT = sb("cos_sb", [128, T], F32)
            sinT = sb("sin_sb", [128, T], F32)
            rotT = sb("rot_sb", [128, 128], F32)
            t1 = sb("t1", [128, 1, 512], F32)
            t2 = sb("t2", [128, 1, 512], F32)
            gst = sb("gst", [48, 1, 512], F32)
        if final:
            gfb = sb("gfb", [128, D], F32)
        psb = [st.enter_context(nc.psum_tensor("ps%d" % i, [128, 512], F32)) for i in range(8)]

        P = Prog(nc)
        out_sems = []
        B2 = [("big2", fc) for fc in range(16)]

        P.op("sp", lambda e: e.dma_start(out=x[:], in_=x_d.ap().rearrange("(t p) c -> p t c", p=128)),
             writes=[("x", t) for t in range(NTT)], dma_sem="ldx")
        P.op("act", lambda e: e.dma_start(out=ident[:], in_=ident_d.ap()), writes=["ident"], dma_sem="ld_ident")
        gains = {}

        def load_gain(name, d):
            gi = len(gains)
            gains[name] = gi
            P.op("act", lambda e: e.dma_start(out=gT[:, gi, :], in_=d.ap()),
                 writes=[("gain", gi)], dma_sem=("ldg", gi))

        if mlp:
            load_gain("mlp", gm_d)
        if proj:
            load_gain("attn", ga_d)
            P.op("act", lambda e: e.dma_start(out=cosT[:], in_=cos_d.ap()), writes=["cos"], dma_sem="ld_cos")
            P.op("act", lambda e: e.dma_start(out=sinT[:], in_=sin_d.ap()), writes=["sin"], dma_sem="ld_sin")
            P.op("act", lambda e: e.dma_start(out=rotT[:], in_=rt_d.ap()), writes=["rot"], dma_sem="ld_rot")
        if proj == "diffkv":
            load_gain("kv", gk_d)
        if final:
            P.op("act", lambda e: e.dma_start(out=gfb[:], in_=gf_d.ap().partition_broadcast(128)),
                 writes=["gfb"], dma_sem="ld_gfb")
        if oproj:
            P.op("sp", lambda e: e.dma_start(out=big2[:], in_=oT_d.ap().rearrange("(k p) t -> p k t", p=128)),
                 writes=B2, dma_sem="ldo")

        wlist = []
        wpos = [0]

        def issue_wblock(n):
            wd, r0, c0, ncols = wlist[n]
            i = n % NWB
            for qq in range(4):
                si = P.rr("wst", 2)
                src = wd.ap()[r0 + qq * 512:r0 + (qq + 1) * 512, c0:c0 + ncols].rearrange("(k p) c -> p k c", p=128)
                P.op("sp", lambda e, si=si, src=src: e.dma_start(out=wst[:, si, :, 0:ncols], in_=src),
                     writes=[("wst", si)], dma_sem=("wst", si))
                P.op("pool", lambda e, si=si, qq=qq: e.tensor_copy(out=wb[:, i, qq * 4:(qq + 1) * 4, 0:ncols],
                                                                   in_=wst[:, si, :, 0:ncols]),
                     reads=[("wst", si)], writes=[("wb", i)])

        def load_wblock(wd, r0, c0, ncols=512):
            if P.dry:
                wlist.append((wd, r0, c0, ncols))
                return 0
            n = wpos[0]
            wpos[0] += 1
            if n == 0:
                issue_wblock(0)
            if n + 1 < len(wlist):
                issue_wblock(n + 1)
            return n % NWB

        def next_ps():
            return P.rr("ps", 4)

        def mm_group(pb, pso, pairs, reads):
            def fn(e):
                n = len(pairs)
                ins = None
                for i, (l, r) in enumerate(pairs):
                    ins = e.matmul(pso, l, r, start=(i == 0), stop=(i == n - 1))
                return ins
            P.op("pe", fn, reads=reads, writes=[("ps", pb)])

        def resid_add(pb, tt, cc):
            xs = x[:, tt, cc * 512:(cc + 1) * 512]
            P.op("dve", lambda e: e.tensor_tensor(out=xs, in0=xs, in1=psb[pb][:], op=ALU.add),
                 reads=[("ps", pb), ("x", tt)], writes=[("x", tt)])

        nstat = [0]
        for PASS in (0, 1):
            P.dry = (PASS == 0)
            nstat[0] = 0
            if oproj:
                for cc in range(4):
                    wi = load_wblock(wo_d, 0, cc * 512)
                    for tt in range(NTT):
                        pb = next_ps()
                        mm_group(pb, psb[pb][:], [(big2[:, kc, tt * 128:(tt + 1) * 128], wb[:, wi, kc, :]) for kc in range(16)],
                                 reads=B2 + [("wb", wi)])
                        resid_add(pb, tt, cc)


            def make_hT(gname):
                HT = 9
                gi = gains[gname]
                for tt in range(NTT):
                    _make_hT_tile(gi, tt, HT)

            def _make_hT_tile(gi, tt, HT):
                if True:
                    si = nstat[0]
                    nstat[0] += 1
                    xi = P.rr("xh", 1)
                    P.op("dve", lambda e, si=si: e.memset(ssq[:, si:si + 1], 0.0), writes=[("ssq", si)])
                    P.op("act", lambda e, si=si, tt=tt: e.activation(out=xh[:, 0, :], in_=x[:, tt, :], func=AF.Square,
                                                                      accum_out=ssq[:, si:si + 1]),
                         reads=[("x", tt), ("ssq", si)], writes=[("xh", 0), ("ssq", si)])
                    P.op("dve", lambda e, si=si: e.tensor_scalar(out=rstd[:, si:si + 1], in0=ssq[:, si:si + 1],
                                                                 scalar1=1.0 / D, scalar2=EPS, op0=ALU.mult, op1=ALU.add),
                         reads=[("ssq", si)], writes=[("rstd", si)])
                    if HT < 2:
                        return
                    P.op("act", lambda e, si=si: e.activation(out=rstd[:, si:si + 1], in_=rstd[:, si:si + 1], func=AF.Sqrt),
                         reads=[("rstd", si)], writes=[("rstd", si)])
                    P.op("dve", lambda e, si=si: e.reciprocal(out=rstd[:, si:si + 1], in_=rstd[:, si:si + 1]),
                         reads=[("rstd", si)], writes=[("rstd", si)])
                    if HT < 3:
                        return
                    P.op("act", lambda e, si=si, tt=tt, xi=xi: e.activation(out=xh[:, xi, :], in_=x[:, tt, :], func=AF.Identity,
                                                                           scale=rstd[:, si:si + 1]),
                         reads=[("x", tt), ("rstd", si)], writes=[("xh", xi)])
                    if HT < 4:
                        return
                    for q4 in range(4):
                        pb = 4 + P.rr("pst", 2)

                        def tfn(e, q4=q4, xi=xi, pb=pb):
                            ins = None
                            for j in range(4):
                                kc = q4 * 4 + j
                                ins = e.transpose(out=psb[pb][:, j * 128:(j + 1) * 128],
                                                  in_=xh[:, xi, kc * 128:(kc + 1) * 128], identity=ident[:])
                            return ins
                        P.op("pe", tfn, reads=[("xh", xi), "ident"], writes=[("ps", pb)])
                        if HT < 5:
                            continue
                        for j in range(4):
                            kc = q4 * 4 + j
                            dst = big1[:, kc, tt * 128:(tt + 1) * 128]
                            src = psb[pb][:, j * 128:(j + 1) * 128]
                            EV = 2
                            if (q4 % 2 == 0 and EV == 2) or EV == 0:
                                P.op("dve", lambda e, dst=dst, src=src, kc=kc: e.tensor_scalar(
                                    out=dst, in0=src, scalar1=gT[:, gi, kc:kc + 1], scalar2=None, op0=ALU.mult),
                                    reads=[("ps", pb), ("gain", gi)], writes=[("big1", tt, q4 % 2)])
                            else:
                                P.op("act", lambda e, dst=dst, src=src, kc=kc: e.activation(
                                    out=dst, in_=src, func=AF.Identity, scale=gT[:, gi, kc:kc + 1]),
                                    reads=[("ps", pb), ("gain", gi)], writes=[("big1", tt, q4 % 2)])

            hT_reads = [("big1", t, u) for t in range(NTT) for u in range(2)]

            if mlp:
                make_hT("mlp")
                for qt in range(4):
                    for blk in range(4):
                        wi = load_wblock(wu_d, 0, qt * 2048 + blk * 512)
                        for j in range(4):
                            fc = blk * 4 + j
                            for tg in range(2):
                                pb = next_ps()
                                mm_group(pb, psb[pb][:],
                                         [(wb[:, wi, kc, j * 128:(j + 1) * 128], big1[:, kc, tg * 512:(tg + 1) * 512])
                                          for kc in range(16)],
                                         reads=hT_reads + [("wb", wi)])
                                ri = P.rr("r32", 2)
                                P.op("act", lambda e, pb=pb, ri=ri: e.activation(out=r32[:, ri, :], in_=psb[pb][:], func=AF.Relu),
                                     reads=[("ps", pb)], writes=[("r32", ri)])
                                if (fc + tg) % 2 == 0:
                                    P.op("act", lambda e, ri=ri, fc=fc, tg=tg: e.activation(
                                        out=big2[:, fc, tg * 512:(tg + 1) * 512], in_=r32[:, ri, :], func=AF.Square),
                                        reads=[("r32", ri)], writes=[("big2", fc)])
                                else:
                                    P.op("dve", lambda e, ri=ri, fc=fc, tg=tg: e.tensor_tensor(
                                        out=big2[:, fc, tg * 512:(tg + 1) * 512], in0=r32[:, ri, :], in1=r32[:, ri, :], op=ALU.mult),
                                        reads=[("r32", ri)], writes=[("big2", fc)])
                    for cc in range(4):
                        wi = load_wblock(wd_d, qt * 2048, cc * 512)
                        for tt in range(NTT):
                            pb = next_ps()
                            mm_group(pb, psb[pb][:],
                                     [(big2[:, fc, tt * 128:(tt + 1) * 128], wb[:, wi, fc, :]) for fc in range(16)],
                                     reads=B2 + [("wb", wi)])
                            resid_add(pb, tt, cc)

            if final:
                for tt in range(NTT):
                    si = nstat[0]
                    nstat[0] += 1
                    yi = 0
                    P.op("dve", lambda e, si=si: e.memset(ssq[:, si:si + 1], 0.0), writes=[("ssq", si)])
                    P.op("act", lambda e, si=si, tt=tt: e.activation(out=xh[:, 0, :], in_=x[:, tt, :], func=AF.Square,
                                                                      accum_out=ssq[:, si:si + 1]),
                         reads=[("x", tt), ("ssq", si)], writes=[("xh", 0), ("ssq", si)])
                    P.op("dve", lambda e, si=si: e.tensor_scalar(out=rstd[:, si:si + 1], in0=ssq[:, si:si + 1],
                                                                 scalar1=1.0 / D, scalar2=EPS, op0=ALU.mult, op1=ALU.add),
                         reads=[("ssq", si)], writes=[("rstd", si)])
                    P.op("act", lambda e, si=si: e.activation(out=rstd[:, si:si + 1], in_=rstd[:, si:si + 1], func=AF.Sqrt),
                         reads=[("rstd", si)], writes=[("rstd", si)])
                    P.op("dve", lambda e, si=si: e.reciprocal(out=rstd[:, si:si + 1], in_=rstd[:, si:si + 1]),
                         reads=[("rstd", si)], writes=[("rstd", si)])
                    P.op("dve", lambda e, si=si, tt=tt, yi=yi: e.scalar_tensor_tensor(
                        out=xh[:, 0, :], in0=x[:, tt, :], scalar=rstd[:, si:si + 1], in1=gfb[:], op0=ALU.mult, op1=ALU.mult),
                        reads=[("x", tt), ("rstd", si), "gfb"], writes=[("xh", 0)])
                    P.op("sp", lambda e, tt=tt, yi=yi: e.dma_start(out=y_d.ap()[tt * 128:(tt + 1) * 128, :], in_=xh[:, 0, :]),
                         reads=[("xh", 0)], dma_sem="yst")
                out_sems += ["yst"]
            else:
                P.op("sp", lambda e: e.dma_start(out=xo_d.ap().rearrange("(t p) c -> p t c", p=128), in_=x[:]),
                     reads=[("x", t) for t in range(NTT)], dma_sem="stx")
                out_sems.append("stx")

            def stage_out(dst_ap, src_fn, eng, reads):
                si = P.rr("stg", NST)
                P.op(eng, lambda e: src_fn(e, stg[:, si, :]), reads=reads, writes=[("stg", si)])
                P.op("sp", lambda e: e.dma_start(out=dst_ap, in_=stg[:, si, :]), reads=[("stg", si)], dma_sem=("stg", si))

            def proj_fm(wi, j, mode, outs):
                for tg in range(2):
                    pb = next_ps()
                    mm_group(pb, psb[pb][:],
                             [(wb[:, wi, kc, j * 128:(j + 1) * 128], big1[:, kc, tg * 512:(tg + 1) * 512]) for kc in range(16)],
                             reads=hT_reads + [("wb", wi)])
                    tsl = slice(tg * 512, (tg + 1) * 512)
                    if "rope" in outs:
                        ri = P.rr("r32", 2)
                        P.op("act", lambda e, pb=pb, ri=ri: e.activation(out=r32[:, ri, :], in_=psb[pb][:], func=AF.Copy),
                             reads=[("ps", pb)], writes=[("r32", ri)])
                        if "plain" in outs:
                            dd, r0 = outs["plain"]
                            stage_out(dd.ap()[r0:r0 + 128, tsl],
                                      lambda e, o, ri=ri: e.tensor_copy(out=o, in_=r32[:, ri, :]), "dve", [("r32", ri)])
                        pr = 6 + P.rr("psr", 2)
                        P.op("pe", lambda e, pr=pr, ri=ri: e.matmul(psb[pr][:], rotT[:], r32[:, ri, :], start=True, stop=True),
                             reads=[("r32", ri), "rot"], writes=[("ps", pr)])
                        ti = P.rr("t12", 1)
                        P.op("dve", lambda e, ri=ri, ti=ti, tsl=tsl: e.tensor_tensor(out=t1[:, ti, :], in0=r32[:, ri, :],
                                                                                     in1=cosT[:, tsl], op=ALU.mult),
                             reads=[("r32", ri), "cos"], writes=[("t1", ti)])
                        P.op("dve", lambda e, pr=pr, ti=ti, tsl=tsl: e.tensor_tensor(out=t2[:, ti, :], in0=psb[pr][:],
                                                                                     in1=sinT[:, tsl], op=ALU.mult),
                             reads=[("ps", pr), "sin"], writes=[("t2", ti)])
                        dd, r0 = outs["rope"]
                        stage_out(dd.ap()[r0:r0 + 128, tsl],
                                  lambda e, o, ti=ti: e.tensor_tensor(out=o, in0=t1[:, ti, :], in1=t2[:, ti, :], op=ALU.add),
                                  "dve", [("t1", ti), ("t2", ti)])
                    else:
                        dd, r0 = outs["plain"]
                        stage_out(dd.ap()[r0:r0 + 128, tsl],
                                  lambda e, o, pb=pb: e.activation(out=o, in_=psb[pb][:], func=AF.Copy), "act", [("ps", pb)])

            def proj_tm(wi, dd, c0):
                for tt in range(NTT):
                    pb = next_ps()
                    mm_group(pb, psb[pb][:],
                             [(big1[:, kc, tt * 128:(tt + 1) * 128], wb[:, wi, kc, :]) for kc in range(16)],
                             reads=hT_reads + [("wb", wi)])
                    stage_out(dd.ap()[tt * 128:(tt + 1) * 128, c0:c0 + 512],
                              lambda e, o, pb=pb: e.activation(out=o, in_=psb[pb][:], func=AF.Copy), "act", [("ps", pb)])

            DBG = 9
            if proj == "nsa" and DBG >= 2:
                make_hT("attn")
            if proj == "nsa" and DBG >= 3:
                for b in range(4 if DBG >= 4 else 1):
                    wi = load_wblock(win_d, 0, b * 512)
                    for j in range(4):
                        r0 = (b * 4 + j) * 128
                        proj_fm(wi, j, "both", {"plain": (qT_d, r0), "rope": (qrT_d, r0)})
            if proj == "nsa" and DBG >= 5:
                specs = [(kcT_d, False), (vcT_d, False), (ksT_d, True), (None, "vs"), (kwT_d, True), (None, "vw")]
                for pi, (dd, mode) in enumerate(specs):
                    wi = load_wblock(win_d, 0, 2048 + pi * 512)
                    if dd is None:
                        proj_tm(wi, vs_d if mode == "vs" else vw_d, 0)
                    else:
                        for j in range(4):
                            proj_fm(wi, j, "x", {"rope": (dd, j * 128)} if mode else {"plain": (dd, j * 128)})
                wi = load_wblock(win_d, 0, 2048 + 6 * 512, 48)
                for tg in range(2):
                    pb = next_ps()
                    mm_group(pb, psb[pb][0:48, :],
                             [(wb[:, wi, kc, 0:48], big1[:, kc, tg * 512:(tg + 1) * 512]) for kc in range(16)],
                             reads=hT_reads + [("wb", wi)])
                    gi2 = P.rr("gst", 1)
                    P.op("act", lambda e, pb=pb, gi2=gi2: e.activation(out=gst[:, gi2, :], in_=psb[pb][0:48, :], func=AF.Sigmoid),
                         reads=[("ps", pb)], writes=[("gst", gi2)])
                    P.op("sp", lambda e, gi2=gi2, tg=tg: e.dma_start(out=gT_d.ap()[:, tg * 512:(tg + 1) * 512], in_=gst[:, gi2, :]),
                         reads=[("gst", gi2)], dma_sem=("gst", gi2))
                out_sems += [("gst", 0)]
            if proj in ("diff", "diffkv"):
                make_hT("attn")
                for b in range(4):
                    wi = load_wblock(wq_d, 0, b * 512)
                    for j in range(4):
                        proj_fm(wi, j, "x", {"rope": (dqT_d, (b * 4 + j) * 128)})
            if proj == "diffkv":
                make_hT("kv")
                wi = load_wblock(wkv_d, 0, 0)
                for j in range(4):
                    proj_fm(wi, j, "x", {"rope": (dkT_d, j * 128)})
                wi = load_wblock(wkv_d, 0, 512)
                proj_tm(wi, dv_d, 0)
            if proj:
                out_sems += [("stg", i) for i in range(NST)]

        P.emit(st, final_waits=out_sems)
    return nc


_DENSE_CACHE = {}


def gain_fm(g):
    return np.ascontiguousarray(np.asarray(g, np.float32).reshape(16, 128).T)


def get_dense(oproj, mlp, final, proj):
    key = (oproj, mlp, final, proj)
    if key not in _DENSE_CACHE:
        _DENSE_CACHE[key] = build_dense(*key)
    return _DENSE_CACHE[key]


NSLOT = 32
BIGV = 10000.0


def slot_qb(i, half):
    m = i // 2
    if i % 2 == 0:
        return 4 * m + (0 if half == 0 else 1), 4 * m + 1
    return 4 * m + (3 if half == 0 else 2), 4 * m + 3


def build_nsa_attn():
    nc = bass.Bass("TRN2", target_bir_lowering=False)
    din = lambda name, shape, dt=F32: nc.dram_tensor(name, list(shape), dt, kind="ExternalInput")
    kcT_d = din("kcT", [128, S], BF16)
    vcT_d = din("vcT", [128, S], BF16)
    ksT_d = din("ksT", [128, S], BF16)
    kwT_d = din("kwT", [128, S], BF16)
    vs_d = din("vs", [S, 128], BF16)
    vw_d = din("vw", [S, 128], BF16)
    qT_d = din("qT", [128, NSLOT, 512], BF16)
    qrT_d = din("qrT", [128, NSLOT, 512], BF16)
    g3_d = din("g3", [3, NSLOT, 512])
    w1_d = din("w1", [2, 4096, 512])
    w2_d = din("w2", [2, 512, 128])
    posT_d = din("posT", [2, 128, 32])
    identf_d = din("identf", [128, 128])
    i4_d = din("i4", [128, 512], BF16)
    ones_d = din("ones", [128, 128], BF16)
    acon_d = din("acon4", [128, 64, 128], BF16)
    ov_d = din("ov", [128, 4, 128], BF16)
    tailm_d = din("tailm", [128, 4, 128], BF16)
    winm_d = din("winm", [128, 8, 128], BF16)
    cmask_d = din("cmask", [128, NSLOT, 128], BF16)
    slotc_d = din("slotc", [128, NSLOT, 256])
    sel3_d = din("sel3", [3, 3, 128])
    oT_d = nc.dram_tensor("oT", [128, NSLOT, 512], BF16, kind="ExternalOutput")
    DBGA = 0
    if DBGA:
        dbg_d = nc.dram_tensor("dbg", [128, 2, 8, 512], F32, kind="ExternalOutput")
    scale = 128.0 ** -0.5

    with ExitStack() as st:
        sb = lambda name, shape, dt: st.enter_context(nc.sbuf_tensor(name, list(shape), dt))
        ksT = sb("ksT_sb", [128, S], BF16)
        kwT = sb("kwT_sb", [128, S], BF16)
        vs = sb("vs_sb", [128, 64, 128], BF16)
        vw = sb("vw_sb", [128, 64, 128], BF16)
        xc = sb("xc", [128, S], BF16)
        w1sb = sb("w1sb", [128, 32, 512], BF16)
        w2sb = sb("w2sb", [128, 4, 128], BF16)
        posT = sb("posT_sb", [128, 32], F32)
        XL = sb("XL", [128, 2, 512], BF16)
        hidT = sb("hidT", [128, 4, 512], BF16)
        tA = sb("tA", [128, 2, 512], F32)
        tB = sb("tB", [128, 2, 512], F32)
        kcmpT = sb("kcmpT", [128, 512], BF16)
        vcmp = sb("vcmp", [128, 4, 128], BF16)
        identf = sb("identf_sb", [128, 128], F32)
        i4 = sb("i4_sb", [128, 512], BF16)
        ones = sb("ones_sb", [128, 128], BF16)
        acon = sb("acon_sb", [128, 64, 128], BF16)
        ov = sb("ov_sb", [128, 4, 128], BF16)
        tailm = sb("tailm_sb", [128, 4, 128], BF16)
        winm = sb("winm_sb", [128, 8, 128], BF16)
        sel3 = sb("sel3_sb", [3, 3, 128], F32)
        qsb = sb("qsb", [128, 2, 512], BF16)
        qrsb = sb("qrsb", [128, 2, 512], BF16)
        g3sb = sb("g3sb", [3, 2, 512], F32)
        cmsb = sb("cmsb", [128, 2, 128], BF16)
        slc = sb("slc", [128, 2, 256], F32)
        Ec = sb("Ec", [128, 4, 512], BF16)
        NE = 3
        Eb = sb("Eb", [128, NE, 512], BF16)
        Pn = sb("Pn", [128, 4, 512], BF16)
        rden = sb("rden", [128, 3, 512], F32)
        wgt = sb("wgt", [128, 512], F32)
        tmp = sb("tmp", [128, 512], F32)
        acc = sb("acc", [128, 2, 512], F32)
        ost = sb("ost", [128, 2, 512], BF16)
        impm = sb("impm", [128, 128], F32)
        impm2 = sb("impm2", [128, 128], F32)
        t8 = sb("t8", [128, 16], F32)
        nsel = sb("nsel", [128, 128], F32)
        nselT = sb("nselT", [128, 2, 4, 128], BF16)
        PS = [st.enter_context(nc.psum_tensor("ps%d" % i, [128, 512], F32)) for i in range(8)]
        Sb, Ob, Db, M0, M1 = (0, 1, 6), (2, 3), (4, 5), 7, 7

        P = Prog(nc)
        ld = lambda eng, dst, src, key, sem: P.op(eng, lambda e: e.dma_start(out=dst, in_=src), writes=[key], dma_sem=sem)
        ld("sp", ksT[:], ksT_d.ap(), "ksT", "l_ks")
        ld("sp", kwT[:], kwT_d.ap(), "kwT", "l_kw")
        ld("sp", vs[:], vs_d.ap().rearrange("(t p) d -> p t d", p=128), "vs", "l_vs")
        ld("sp", vw[:], vw_d.ap().rearrange("(t p) d -> p t d", p=128), "vw", "l_vw")
        ld("act", identf[:], identf_d.ap(), "identf", "l_c0")
        ld("act", i4[:], i4_d.ap(), "i4", "l_c1")
        ld("act", ones[:], ones_d.ap(), "ones", "l_c2")
        ld("act", acon[:], acon_d.ap(), "acon", "l_c3")
        ld("act", ov[:], ov_d.ap(), "ov", "l_c4")
        ld("act", tailm[:], tailm_d.ap(), "tailm", "l_c5")
        ld("act", winm[:], winm_d.ap(), "winm", "l_c6")
        ld("act", sel3[:], sel3_d.ap(), "sel3", "l_c7")
        P.op("dve", lambda e: e.memset(XL[:], 0.0), writes=[("XL", 0), ("XL", 1)])

        GC = math.sqrt(2.0 / math.pi)
        for jv in range(2):
            src_d = kcT_d if jv == 0 else vcT_d
            ld("sp", xc[:], src_d.ap(), "xc", "l_xc")
            for q4 in range(4):
                P.op("pool", lambda e, q4=q4, jv=jv: e.dma_start(
                    out=w1sb[:, q4 * 8:(q4 + 1) * 8, :],
                    in_=w1_d.ap()[jv, q4 * 1024:(q4 + 1) * 1024, :].rearrange("(l p) h -> p l h", p=128)),
                    writes=[("w1", q4)], dma_sem=("l_w1", q4))
            P.op("pool", lambda e, jv=jv: e.dma_start(out=w2sb[:], in_=w2_d.ap()[jv].rearrange("(c p) d -> p c d", p=128)),
                 writes=["w2"], dma_sem="l_w2")
            ld("act", posT[:], posT_d.ap()[jv], "posT", "l_pos")
            for l in range(32):
                xi = P.rr("xl", 2)
                src = bass.AP(xc, l, [[S, 128], [16, 511]])
                P.op("dve", lambda e, xi=xi, src=src, l=l: e.tensor_scalar(
                    out=XL[:, xi, 0:511], in0=src, scalar1=posT[:, l:l + 1], scalar2=None, op0=ALU.add),
                    reads=["xc", "posT"], writes=[("XL", xi)])

                def fn(e, xi=xi, l=l):
                    ins = None
                    for hc in range(4):
                        ins = e.matmul(PS[hc][:], w1sb[:, l, hc * 128:(hc + 1) * 128], XL[:, xi, :],
                                       start=(l == 0), stop=(l == 31))
                    return ins
                P.op("pe", fn, reads=[("XL", xi), ("w1", l // 8)], writes=[("H", hc) for hc in range(4)])
            for hc in range(4):
                ti = P.rr("tAB", 2)
                P.op("act", lambda e, hc=hc, ti=ti: e.activation(out=tA[:, ti, :], in_=PS[hc][:], func=AF.Square),
                     reads=[("H", hc)], writes=[("tA", ti)])
                P.op("dve", lambda e, ti=ti: e.tensor_scalar(out=tA[:, ti, :], in0=tA[:, ti, :], scalar1=0.044715, scalar2=1.0,
                                                             op0=ALU.mult, op1=ALU.add),
                     reads=[("tA", ti)], writes=[("tA", ti)])
                P.op("dve", lambda e, hc=hc, ti=ti: e.tensor_tensor(out=tB[:, ti, :], in0=tA[:, ti, :], in1=PS[hc][:], op=ALU.mult),
                     reads=[("tA", ti), ("H", hc)], writes=[("tB", ti)])
                P.op("act", lambda e, ti=ti: e.activation(out=tB[:, ti, :], in_=tB[:, ti, :], func=AF.Tanh, scale=GC),
                     reads=[("tB", ti)], writes=[("tB", ti)])
                P.op("dve", lambda e, ti=ti: e.tensor_scalar(out=tB[:, ti, :], in0=tB[:, ti, :], scalar1=1.0, scalar2=0.5,
                                                             op0=ALU.add, op1=ALU.mult),
                     reads=[("tB", ti)], writes=[("tB", ti)])
                P.op("dve", lambda e, hc=hc, ti=ti: e.tensor_tensor(out=hidT[:, hc, :], in0=tB[:, ti, :], in1=PS[hc][:], op=ALU.mult),
                     reads=[("tB", ti), ("H", hc)], writes=[("hidT", hc)])
            hid_reads = [("hidT", hc) for hc in range(4)]
            if jv == 0:
                def fn(e):
                    ins = None
                    for hc in range(4):
                        ins = e.matmul(PS[4][:], w2sb[:, hc, :], hidT[:, hc, :], start=(hc == 0), stop=(hc == 3))
                    return ins
                P.op("pe", fn, reads=hid_reads + ["w2"], writes=[("ps", 4)])
                P.op("act", lambda e: e.activation(out=kcmpT[:], in_=PS[4][:], func=AF.Copy), reads=[("ps", 4)], writes=["kcmpT"])
            else:
                def fn(e):
                    ins = None
                    for nt in range(4):
                        for hc in range(4):
                            ins = e.matmul(PS[5][:, nt * 128:(nt + 1) * 128], hidT[:, hc, nt * 128:(nt + 1) * 128], w2sb[:, hc, :],
                                           start=(hc == 0), stop=(hc == 3))
                    return ins
                P.op("pe", fn, reads=hid_reads + ["w2"], writes=[("ps", 5)])
                P.op("act", lambda e: e.activation(out=vcmp[:].rearrange("p a b -> p (a b)"), in_=PS[5][:], func=AF.Copy),
                     reads=[("ps", 5)], writes=["vcmp"])
        ALLPS = [("H", h) for h in range(4)] + [("ps", 4), ("ps", 5)]
        PK = {0: ("S", 0), 1: ("S", 1), 2: ("O", 0), 3: ("O", 1), 4: ("D", 0), 5: ("D", 1), 6: ("S", 2)}
        P.op("pe", lambda e: e.matmul(PS[7][:, 0:128], ones[:], ones[:], start=True, stop=True),
             reads=["ones"], writes=ALLPS + [PK[i] for i in range(7)] + ["M"])

        def branch(tiles, qbuf_key, q_ap, ob, db, hooks=None):
            n = len(tiles)
            slots = {}

            def emit_s(ti_):
                kl, kkey, extra, vl, vkey = tiles[ti_]
                sbk = P.rr("S", 3)
                slots[ti_] = sbk

                def sfn(e, kl=kl, extra=extra, sbk=sbk):
                    ins = e.matmul(PS[Sb[sbk]][:], kl, q_ap, start=True, stop=(len(extra) == 0))
                    for xi_, (l_, r_, tp, _) in enumerate(extra):
                        kw = {} if tp is None else {"tile_position": tp}
                        ins = e.matmul(PS[Sb[sbk]][:], l_, r_, start=False, stop=(xi_ == len(extra) - 1), **kw)
                    return ins
                xkeys = [k for x_ in extra for k in x_[3]]
                P.op("pe", sfn, reads=[kkey, qbuf_key] + xkeys, writes=[("S", sbk)])

            def emit_rest(ti_):
                kl, kkey, extra, vl, vkey = tiles[ti_]
                sbk = slots[ti_]
                ei = P.rr("E", NE)
                P.op("act", lambda e, sbk=sbk, ei=ei: e.activation(out=Eb[:, ei, :], in_=PS[Sb[sbk]][:], func=AF.Exp, scale=scale),
                     reads=[("S", sbk)], writes=[("E", ei)])

                def ofn(e, vl=vl, ei=ei, ti_=ti_):
                    e.matmul(PS[Ob[ob]][:], vl, Eb[:, ei, :], start=(ti_ == 0), stop=(ti_ == n - 1))
                    return e.matmul(PS[Db[db]][:], ones[:], Eb[:, ei, :], start=(ti_ == 0), stop=(ti_ == n - 1))
                P.op("pe", ofn, reads=[vkey, ("E", ei), "ones"], writes=[("O", ob), ("D", db)])

            emit_s(0)
            if n > 1:
                emit_s(1)
            for ti_ in range(n):
                if ti_ + 2 < n:
                    emit_s(ti_ + 2)
                emit_rest(ti_)
                if hooks and ti_ in hooks:
                    for h_ in hooks[ti_]:
                        h_()

        def rden_of(db, rb):
            P.op("dve", lambda e: e.tensor_scalar(out=rden[:, rb, :], in0=PS[Db[db]][:], scalar1=1e-30, scalar2=None, op0=ALU.max),
                 reads=[("D", db)], writes=[("rden", rb)])
            P.op("dve", lambda e: e.reciprocal(out=rden[:, rb, :], in_=rden[:, rb, :]), reads=[("rden", rb)], writes=[("rden", rb)])

        cur_slot = [0]

        def dump(src_ap, key, idx):
            if DBGA and cur_slot[0] < 2:
                sl = cur_slot[0]
                P.op("sp", lambda e: e.dma_start(out=dbg_d.ap()[:, sl, idx, :], in_=src_ap), reads=[key], dma_sem="dbg")

        def combine(bi, ob, rb, qi, ai, first):
            P.op("pe", lambda e: e.matmul(PS[M1][:], sel3[:, bi, :], g3sb[:, qi, :], start=True, stop=True),
                 reads=["sel3", ("g3", qi)], writes=["M"])
            dump(rden[:, rb, :], ("rden", rb), bi * 2)
            P.op("dve", lambda e: e.tensor_tensor(out=wgt[:], in0=rden[:, rb, :], in1=PS[M1][:], op=ALU.mult),
                 reads=[("rden", rb), "M"], writes=["wgt"])
            dump(wgt[:], "wgt", bi * 2 + 1)
            if first:
                P.op("dve", lambda e: e.tensor_tensor(out=acc[:, ai, :], in0=wgt[:], in1=PS[Ob[ob]][:], op=ALU.mult),
                     reads=["wgt", ("O", ob)], writes=[("acc", ai)])
            else:
                P.op("dve", lambda e: e.tensor_tensor(out=tmp[:], in0=wgt[:], in1=PS[Ob[ob]][:], op=ALU.mult),
                     reads=["wgt", ("O", ob)], writes=["tmp"])
                P.op("pool", lambda e: e.tensor_tensor(out=acc[:, ai, :], in0=acc[:, ai, :], in1=tmp[:], op=ALU.add),
                     reads=["tmp", ("acc", ai)], writes=[("acc", ai)])

        def slot_loads(i):
            qi = i % 2
            ld("sp", qsb[:, qi, :], qT_d.ap()[:, i, :], ("q", qi), ("l_q", qi))
            ld("sp", qrsb[:, qi, :], qrT_d.ap()[:, i, :], ("qr", qi), ("l_qr", qi))
            ld("sp", g3sb[:, qi, :], g3_d.ap()[:, i, :], ("g3", qi), ("l_g3", qi))
            ld("sp", cmsb[:, qi, :], cmask_d.ap()[:, i, :], ("cm", qi), ("l_cm", qi))
            ld("sp", slc[:, qi, :], slotc_d.ap()[:, i, :], ("slc", qi), ("l_sl", qi))

        NSL = NSLOT
        SL = {}

        def slot_info(i):
            par = i % 2
            return par, 4 * (i // 2) + (1 if par == 0 else 3), i % 2

        def stage_a(i):
            par, qbmax, qi = slot_info(i)
            nkt = qbmax // 16 + 1
            obc, dbc = P.rr("O", 2), P.rr("D", 2)
            for kc_ in range(nkt):
                sbk = P.rr("S", 3)
                last = kc_ == nkt - 1

                def sfn(e, kc_=kc_, sbk=sbk, last=last, qi=qi):
                    ins = e.matmul(PS[Sb[sbk]][:], kcmpT[:, kc_ * 128:(kc_ + 1) * 128], qsb[:, qi, :], start=True, stop=not last)
                    if last:
                        ins = e.matmul(PS[Sb[sbk]][:], cmsb[:, qi, :], i4[:], start=False, stop=True)
                    return ins
                P.op("pe", sfn, reads=["kcmpT", ("q", qi), ("cm", qi), "i4"], writes=[("S", sbk)])
                P.op("act", lambda e, sbk=sbk, kc_=kc_: e.activation(out=Ec[:, kc_, :], in_=PS[Sb[sbk]][:], func=AF.Exp, scale=scale),
                     reads=[("S", sbk)], writes=[("Ec", kc_)])

                def ofn(e, kc_=kc_, last=last, obc=obc, dbc=dbc):
                    e.matmul(PS[Ob[obc]][:], vcmp[:, kc_, :], Ec[:, kc_, :], start=(kc_ == 0), stop=last)
                    return e.matmul(PS[Db[dbc]][:], ones[:], Ec[:, kc_, :], start=(kc_ == 0), stop=last)
                P.op("pe", ofn, reads=["vcmp", ("Ec", kc_), "ones"], writes=[("O", obc), ("D", dbc)])
            rbc = P.rr("rden", 3)
            rden_of(dbc, rbc)
            for kc_ in range(nkt):
                P.op("pool", lambda e, kc_=kc_, rbc=rbc: e.tensor_tensor(out=Pn[:, kc_, :], in0=Ec[:, kc_, :], in1=rden[:, rbc, :], op=ALU.mult),
                     reads=[("Ec", kc_), ("rden", rbc)], writes=[("Pn", kc_)])
            SL[i] = dict(nkt=nkt, obc=obc, rbc=rbc)
            combine(0, obc, rbc, qi, i % 2, True)

        def stage_b(i):
            par, qbmax, qi = slot_info(i)
            nkt = SL[i]["nkt"]

            def ifn(e, nkt=nkt):
                ins = None
                tot = nkt * 4
                c_ = 0
                for kc_ in range(nkt):
                    for g in range(4):
                        ins = e.matmul(PS[M0][:, 0:128], Pn[:, kc_, g * 128:(g + 1) * 128], ov[:, kc_, :],
                                       start=(c_ == 0), stop=(c_ == tot - 1))
                        c_ += 1
                return ins
            P.op("pe", ifn, reads=[("Pn", k) for k in range(nkt)] + ["ov"], writes=["M"])
            P.op("dve", lambda e, qi=qi: e.tensor_tensor(out=impm[:], in0=PS[M0][:, 0:128], in1=slc[:, qi, 0:128], op=ALU.mult),
                 reads=["M", ("slc", qi)], writes=["impm"])
            P.op("dve", lambda e, qi=qi: e.tensor_tensor(out=impm[:], in0=impm[:], in1=slc[:, qi, 128:256], op=ALU.add),
                 reads=["impm", ("slc", qi)], writes=["impm"])
            P.op("dve", lambda e: e.max(out=t8[:, 0:8], in_=impm[:]), reads=["impm"], writes=["t8a"])
            P.op("dve", lambda e: e.match_replace(out=impm2[:], in_to_replace=t8[:, 0:8], in_values=impm[:], imm_value=-1.0e9),
                 reads=["impm", "t8a"], writes=["impm2"])
            P.op("dve", lambda e: e.max(out=t8[:, 8:16], in_=impm2[:]), reads=["impm2"], writes=["t8b"])
            P.op("dve", lambda e: e.tensor_scalar(out=nsel[:], in0=impm[:], scalar1=t8[:, 15:16], scalar2=None, op0=ALU.is_lt),
                 reads=["impm", "t8b"], writes=["nsel"])

        def stage_c(i):
            par, qbmax, qi = slot_info(i)
            P.op("pe", lambda e: e.transpose(out=PS[M0][:, 128:256], in_=nsel[:], identity=identf[:]),
                 reads=["nsel", "identf"], writes=["M"])
            ni = P.rr("nselT", 2)
            for g in range(4):
                P.op("dve", lambda e, g=g, ni=ni: e.tensor_copy(out=nselT[:, ni, g, :], in_=PS[M0][:, 128:256]),
                     reads=["M"], writes=[("nselT", ni)])
            SL[i]["ni"] = ni

        if NSL > 0:
            slot_loads(0)
            stage_a(0)
            stage_b(0)
            stage_c(0)
        for i in range(NSL):
            par, qbmax, qi = slot_info(i)
            ai = i % 2
            nxt = i + 1 < NSL
            if nxt:
                slot_loads(i + 1)
                stage_a(i + 1)

            tiles = []
            for jj in range(6):
                kt = qbmax - 5 + jj
                if kt < 0:
                    continue
                extra = []
                if jj in (0, 1, 4, 5):
                    extra.append((winm[:, par * 4 + (0, 1, None, None, 2, 3)[jj], :], i4[:], None, ["winm", "i4"]))
                tiles.append((kwT[:, kt * 128:(kt + 1) * 128], "kwT", extra, vw[:, kt, :], "vw"))
            obw, dbw = P.rr("O", 2), P.rr("D", 2)
            branch(tiles, ("qr", qi), qrsb[:, qi, :], obw, dbw)
            rbw = P.rr("rden", 3)
            rden_of(dbw, rbw)
            combine(2, obw, rbw, qi, ai, False)

            ni = SL[i]["ni"]
            tiles = []
            for kt in range(qbmax + 1):
                extra = [(acon[:, kt, :], nselT[:, ni, :, :].rearrange("p a b -> p (a b)"), None, ["acon", ("nselT", ni)])]
                if kt == qbmax - 1:
                    extra.append((tailm[:, par * 2 + 0, :], i4[:], None, ["tailm", "i4"]))
                if kt == qbmax:
                    extra.append((tailm[:, par * 2 + 1, :], i4[:], None, ["tailm", "i4"]))
                tiles.append((ksT[:, kt * 128:(kt + 1) * 128], "ksT", extra, vs[:, kt, :], "vs"))
            hooks = {}
            if nxt:
                nt_ = len(tiles)
                hooks.setdefault(min(1, nt_ - 1), []).append(lambda i=i: stage_b(i + 1))
                hooks.setdefault(min(7, nt_ - 1), []).append(lambda i=i: stage_c(i + 1))
            obs, dbs = P.rr("O", 2), P.rr("D", 2)
            branch(tiles, ("qr", qi), qrsb[:, qi, :], obs, dbs, hooks)
            rbs = P.rr("rden", 3)
            rden_of(dbs, rbs)
            combine(1, obs, rbs, qi, ai, False)

            oi = P.rr("ost", 2)
            P.op("act", lambda e, oi=oi, ai=ai: e.activation(out=ost[:, oi, :], in_=acc[:, ai, :], func=AF.Copy),
                 reads=[("acc", ai)], writes=[("ost", oi)])
            P.op("sp", lambda e, oi=oi, i=i: e.dma_start(out=oT_d.ap()[:, i, :], in_=ost[:, oi, :]),
                 reads=[("ost", oi)], dma_sem=("st_o", oi))
        P.emit(st, final_waits=[("st_o", 0), ("st_o", 1), "dbg"])
    return nc


def nsa_attn_consts(half):
    c = {}
    c["identf"] = np.eye(128, dtype=np.float32)
    c["i4"] = np.tile(np.eye(128, dtype=np.float32), (1, 4)).astype(NPBF)
    c["ones"] = np.ones((128, 128), np.float32).astype(NPBF)
    p = np.arange(128)
    k = np.arange(128)
    acon = np.zeros((128, 64, 128), np.float32)
    for kt in range(64):
        acon[:, kt, :] = np.where(p[:, None] == 2 * kt + (k[None, :] >= 64), NEG, 0.0)
    c["acon4"] = acon.astype(NPBF)
    ov = np.zeros((128, 4, 128), np.float32)
    for kt in range(4):
        n = 128 * kt + p
        cs = n * 16
        ss = np.arange(128) * 64
        o = (cs[:, None] < ss[None, :] + 64) & (cs[:, None] + 32 > ss[None, :]) & (n[:, None] <= 510)
        ov[:, kt, :] = o
    c["ov"] = ov.astype(NPBF)
    q = p[:, None]
    kk = k[None, :]
    zero = np.zeros((128, 128), np.float32)
    allneg = np.full((128, 128), NEG, np.float32)
    caus = np.where(kk <= q, 0.0, NEG).astype(np.float32)
    winold = np.where(kk > q, 0.0, NEG).astype(np.float32)
    tail = np.zeros((128, 4, 128), np.float32)
    winm = np.zeros((128, 8, 128), np.float32)
    for par in range(2):
        higher = (par == 1) if half == 0 else (par == 0)
        if higher:
            tail[:, par * 2 + 0] = zero
            tail[:, par * 2 + 1] = caus
            w = [allneg, winold, zero, caus]
        else:
            tail[:, par * 2 + 0] = caus
            tail[:, par * 2 + 1] = allneg
            w = [winold, zero, caus, allneg]
        for x_ in range(4):
            winm[:, par * 4 + x_] = w[x_]
    c["tailm"] = tail.astype(NPBF)
    c["winm"] = winm.astype(NPBF)
    cmask = np.zeros((128, NSLOT, 128), np.float32)
    slotc = np.zeros((128, NSLOT, 256), np.float32)
    s_ = np.arange(128)[None, :]
    for i in range(NSLOT):
        qb, qbmax = slot_qb(i, half)
        t = 128 * qb + p[:, None]
        ktc = qbmax // 16
        n = 128 * ktc + k[None, :]
        cmask[:, i, :] = np.where(16 * n + 31 <= t, 0.0, NEG)
        cur = t // 64
        m1 = np.ones((128, 128), np.float32)
        m2 = np.zeros((128, 128), np.float32)
        f0 = (s_ == 0) & (s_ <= cur)
        m1[np.broadcast_to(f0, m1.shape)] = 0.0
        m2[np.broadcast_to(f0, m2.shape)] = BIGV + 2
        fp = (s_ == cur - 1)
        m1[fp] = 0.0
        m2[fp] = BIGV + 1
        fc = (s_ == cur)
        m1[fc] = 0.0
        m2[fc] = BIGV
        nc_ = s_ > cur
        m1[nc_] = 0.0
        m2[nc_] = (-1.0 - np.broadcast_to(s_, m2.shape))[nc_]
        slotc[:, i, 0:128] = m1
        slotc[:, i, 128:256] = m2
    c["cmask"] = cmask.astype(NPBF)
    c["slotc"] = slotc
    sel3 = np.zeros((3, 3, 128), np.float32)
    for b in range(3):
        sel3[b, b, :] = 1.0
    c["sel3"] = sel3
    return c


_PROG = {}
_IDENT = np.eye(128, dtype=np.float32)


def _run(nc, maps):
    res = run_bass_kernel_spmd(nc, maps, core_ids=list(range(NCORES)))
    return res.results


def _cat(res, name, axis):
    return np.concatenate([np.asarray(r[name]) for r in res], axis=axis)


def dense_maps(xs, inp, layer_done, oT_full, proj, layer_next):
    maps = []
    for c in range(NCORES):
        m = {"x": xs[c], "ident": _IDENT}
        if layer_done is not None:
            L = layer_done
            m["oT"] = np.ascontiguousarray(oT_full[:, c * TPC:(c + 1) * TPC])
            m["w_o"] = inp["nsa_w_out"][L] if L < 2 else inp["diff_w_out"][L - 2]
            m["g_mlp"] = gain_fm(inp["mlp_norm_g"][L])
            m["w_up"] = inp["mlp_w_up"][L]
            m["w_down"] = inp["mlp_w_down"][L]
        if proj is None:
            m["g_final"] = np.asarray(inp["final_norm_g"], np.float32)
        else:
            m["g_attn"] = gain_fm(inp["attn_norm_g"][layer_next])
            C, Sn, RT = rope_consts(128 if proj == "nsa" else 64, c * TPC, TPC)
            m["cosT"], m["sinT"], m["rotT"] = C, Sn, RT
            if proj == "nsa":
                m["w_in"] = inp["nsa_w_in"][layer_next]
            else:
                m["w_q"] = inp["diff_w_q"][layer_next - 2]
            if proj == "diffkv":
                m["g_kv"] = gain_fm(inp["kv_norm_g"])
                m["w_kv"] = inp["kv_w_shared"]
        maps.append(m)
    return maps


def nsa_attn_maps(res, inp, layer):
    qT = _cat(res, "qT", 1).reshape(16, 128, 64, 128)
    qrT = _cat(res, "qrT", 1).reshape(16, 128, 64, 128)
    kcT, vcT = _cat(res, "kcT", 1), _cat(res, "vcT", 1)
    ksT, kwT = _cat(res, "ksT", 1), _cat(res, "kwT", 1)
    vs, vw = _cat(res, "vs", 0), _cat(res, "vw", 0)
    gT = _cat(res, "gT", 1)
    maps = []
    for c in range(NCORES):
        hk, half = c // 2, c % 2
        qbs = [slot_qb(i, half)[0] for i in range(NSLOT)]
        m = dict(nsa_attn_consts(half))
        rs = slice(hk * 128, (hk + 1) * 128)
        m["kcT"] = np.ascontiguousarray(kcT[rs])
        m["vcT"] = np.ascontiguousarray(vcT[rs])
        m["ksT"] = np.ascontiguousarray(ksT[rs])
        m["kwT"] = np.ascontiguousarray(kwT[rs])
        m["vs"] = np.ascontiguousarray(vs[:, rs])
        m["vw"] = np.ascontiguousarray(vw[:, rs])
        m["qT"] = np.ascontiguousarray(qT[4 * hk:4 * hk + 4][:, :, qbs, :].transpose(1, 2, 0, 3)).reshape(128, NSLOT, 512)
        m["qrT"] = np.ascontiguousarray(qrT[4 * hk:4 * hk + 4][:, :, qbs, :].transpose(1, 2, 0, 3)).reshape(128, NSLOT, 512)
        gv = gT[hk * 12:(hk + 1) * 12].reshape(4, 3, 64, 128)[:, :, qbs, :]
        m["g3"] = np.ascontiguousarray(gv.transpose(1, 2, 0, 3)).reshape(3, NSLOT, 512)
        m["w1"] = inp["nsa_cmp_w1"][layer]
        m["w2"] = inp["nsa_cmp_w2"][layer]
        m["posT"] = np.ascontiguousarray(np.asarray(inp["nsa_cmp_pos"][layer]).transpose(0, 2, 1))
        maps.append(m)
    return maps


def nsa_attn_gather(res):
    oT = np.zeros((16, 128, 64, 128), NPBF)
    for c in range(NCORES):
        hk, half = c // 2, c % 2
        qbs = [slot_qb(i, half)[0] for i in range(NSLOT)]
        o = np.asarray(res[c]["oT"]).reshape(128, NSLOT, 4, 128).transpose(2, 0, 1, 3)
        oT[4 * hk:4 * hk + 4][:, :, qbs, :] = o
    return oT.reshape(2048, 8192)


def build_diff_attn():
    nc = bass.Bass("TRN2", target_bir_lowering=False)
    din = lambda name, shape, dt=F32: nc.dram_tensor(name, list(shape), dt, kind="ExternalInput")
    kT_d = din("kT", [128, S], BF16)
    v_d = din("v", [S, 128], BF16)
    qT_d = din("qT", [128, 64, 512], BF16)
    lamv_d = din("lamv", [128, 256])
    g_d = din("subg", [128, 1])
    linit_d = din("linit", [128, 2])
    i4_d = din("i4", [128, 512], BF16)
    ones_d = din("ones", [128, 128], BF16)
    onesf_d = din("onesf", [128, 128])
    caus_d = din("causT", [128, 128], BF16)
    oT_d = nc.dram_tensor("oT", [128, 64, 256], BF16, kind="ExternalOutput")
    scale = 64.0 ** -0.5
    with ExitStack() as st:
        sb = lambda name, shape, dt: st.enter_context(nc.sbuf_tensor(name, list(shape), dt))
        kT = sb("kT_sb", [128, S], BF16)
        v = sb("v_sb", [128, 64, 128], BF16)
        qsb = sb("q_sb", [128, 64, 512], BF16)
        lamv = sb("lamv_sb", [128, 256], F32)
        gcol = sb("gcol", [128, 1], F32)
        linit = sb("linit_sb", [128, 2], F32)
        i4 = sb("i4_sb", [128, 512], BF16)
        ones = sb("ones_sb", [128, 128], BF16)
        onesf = sb("onesf_sb", [128, 128], F32)
        caus = sb("caus_sb", [128, 128], BF16)
        lw = sb("lw", [128, 128], F32)
        ls = sb("ls", [128, 4], F32)
        nlam = sb("nlam", [128, 1], F32)
        gsc = sb("gsc", [128, 1], F32)
        NE = 3
        Eb = sb("Eb", [128, NE, 512], BF16)
        rden = sb("rden", [128, 512], F32)
        A = sb("A", [128, 512], F32)
        o = sb("o", [128, 256], F32)
        sq = sb("sq", [128, 256], F32)
        rs = sb("rs", [128, 256], F32)
        on = sb("on", [128, 256], F32)
        ost = sb("ost", [128, 2, 256], BF16)
        PS = [st.enter_context(nc.psum_tensor("ps%d" % i, [128, 512], F32)) for i in range(8)]
        Sb, Ob, Db, M0 = (0, 1, 2), (3, 4), (5, 6), 7
        P = Prog(nc)
        ld = lambda eng, dst, src, key, sem: P.op(eng, lambda e: e.dma_start(out=dst, in_=src), writes=[key], dma_sem=sem)
        ld("sp", kT[:], kT_d.ap(), "kT", "l_k")
        ld("sp", v[:], v_d.ap().rearrange("(t p) d -> p t d", p=128), "v", "l_v")
        ld("sp", qsb[:], qT_d.ap(), "q", "l_q")
        ld("act", lamv[:], lamv_d.ap(), "lamv", "l_c0")
        ld("act", gcol[:], g_d.ap(), "gcol", "l_c1")
        ld("act", linit[:], linit_d.ap(), "linit", "l_c2")
        ld("act", i4[:], i4_d.ap(), "i4", "l_c3")
        ld("act", ones[:], ones_d.ap(), "ones", "l_c4")
        ld("act", onesf[:], onesf_d.ap(), "onesf", "l_c5")
        ld("act", caus[:], caus_d.ap(), "caus", "l_c6")
        P.op("dve", lambda e: e.tensor_tensor(out=lw[:, 0:64], in0=lamv[:, 0:64], in1=lamv[:, 64:128], op=ALU.mult),
             reads=["lamv"], writes=["lw0"])
        P.op("dve", lambda e: e.tensor_tensor(out=lw[:, 64:128], in0=lamv[:, 128:192], in1=lamv[:, 192:256], op=ALU.mult),
             reads=["lamv"], writes=["lw1"])
        P.op("dve", lambda e: e.reduce_sum(out=ls[:, 0:1], in_=lw[:, 0:64], axis=AX.X), reads=["lw0"], writes=["ls0"])
        P.op("dve", lambda e: e.reduce_sum(out=ls[:, 1:2], in_=lw[:, 64:128], axis=AX.X), reads=["lw1"], writes=["ls1"])
        P.op("act", lambda e: e.activation(out=ls[:, 2:4], in_=ls[:, 0:2], func=AF.Exp), reads=["ls0", "ls1"], writes=["ls23"])
        P.op("dve", lambda e: e.tensor_tensor(out=nlam[:], in0=ls[:, 3:4], in1=ls[:, 2:3], op=ALU.subtract),
             reads=["ls23"], writes=["nlam"])
        P.op("dve", lambda e: e.tensor_tensor(out=nlam[:], in0=nlam[:], in1=linit[:, 0:1], op=ALU.subtract),
             reads=["nlam", "linit"], writes=["nlam"])
        P.op("dve", lambda e: e.tensor_tensor(out=gsc[:], in0=gcol[:], in1=linit[:, 1:2], op=ALU.mult),
             reads=["gcol", "linit"], writes=["gsc"])

        NQ = 64
        for qb in range(NQ):
            ob, db = P.rr("O", 2), P.rr("D", 2)
            n = qb + 1
            sl_ = {}

            def emit_s(kt, qb=qb):
                sbk = P.rr("S", 3)
                sl_[kt] = sbk
                diag = kt == qb

                def sfn(e, kt=kt, sbk=sbk, diag=diag, qb=qb):
                    ksl = slice(kt * 128, (kt + 1) * 128)
                    ins = e.matmul(PS[Sb[sbk]][:], kT[:, ksl], qsb[:, qb, :], start=True, stop=not diag)
                    if diag:
                        ins = e.matmul(PS[Sb[sbk]][:], caus[:], i4[:], start=False, stop=True)
                    return ins
                P.op("pe", sfn, reads=["kT", "q", "caus", "i4"], writes=[("S", sbk)])

            def emit_rest(kt, n=n, ob=ob, db=db):
                sbk = sl_[kt]
                ei = P.rr("E", NE)
                P.op("act", lambda e, sbk=sbk, ei=ei: e.activation(out=Eb[:, ei, :], in_=PS[Sb[sbk]][:], func=AF.Exp, scale=scale),
                     reads=[("S", sbk)], writes=[("E", ei)])

                def ofn(e, kt=kt, ei=ei, n=n, ob=ob, db=db):
                    e.matmul(PS[Ob[ob]][:], v[:, kt, :], Eb[:, ei, :], start=(kt == 0), stop=(kt == n - 1))
                    return e.matmul(PS[Db[db]][:], ones[:], Eb[:, ei, :], start=(kt == 0), stop=(kt == n - 1))
                P.op("pe", ofn, reads=["v", ("E", ei), "ones"], writes=[("O", ob), ("D", db)])

            emit_s(0)
            if n > 1:
                emit_s(1)
            for kt in range(n):
                if kt + 2 < n:
                    emit_s(kt + 2)
                emit_rest(kt)
            P.op("dve", lambda e, db=db: e.tensor_scalar(out=rden[:], in0=PS[Db[db]][:], scalar1=1e-30, scalar2=None, op0=ALU.max),
                 reads=[("D", db)], writes=["rden"])
            P.op("dve", lambda e: e.reciprocal(out=rden[:], in_=rden[:]), reads=["rden"], writes=["rden"])
            P.op("dve", lambda e, ob=ob: e.tensor_tensor(out=A[:], in0=rden[:], in1=PS[Ob[ob]][:], op=ALU.mult),
                 reads=["rden", ("O", ob)], writes=["A"])
            P.op("dve", lambda e: e.scalar_tensor_tensor(out=o[:], in0=A[:, 256:512], scalar=nlam[:, 0:1], in1=A[:, 0:256],
                                                         op0=ALU.mult, op1=ALU.add),
                 reads=["A", "nlam"], writes=["o"])
            P.op("pool", lambda e: e.tensor_tensor(out=sq[:], in0=o[:], in1=o[:], op=ALU.mult), reads=["o"], writes=["sq"])
            P.op("pe", lambda e: e.matmul(PS[M0][:, 0:256], onesf[:], sq[:], start=True, stop=True),
                 reads=["sq", "onesf"], writes=["M0"])
            P.op("dve", lambda e: e.tensor_scalar(out=rs[:], in0=PS[M0][:, 0:256], scalar1=1.0 / 128, scalar2=EPS,
                                                  op0=ALU.mult, op1=ALU.add), reads=["M0"], writes=["rs"])
            P.op("act", lambda e: e.activation(out=rs[:], in_=rs[:], func=AF.Ln), reads=["rs"], writes=["rs"])
            P.op("act", lambda e: e.activation(out=rs[:], in_=rs[:], func=AF.Exp, scale=-0.5), reads=["rs"], writes=["rs"])
            P.op("dve", lambda e: e.tensor_tensor(out=on[:], in0=o[:], in1=rs[:], op=ALU.mult), reads=["o", "rs"], writes=["on"])
            oi = P.rr("ost", 2)
            P.op("act", lambda e, oi=oi: e.activation(out=ost[:, oi, :], in_=on[:], func=AF.Identity, scale=gsc[:, 0:1]),
                 reads=["on", "gsc"], writes=[("ost", oi)])
            P.op("sp", lambda e, oi=oi, qb=qb: e.dma_start(out=oT_d.ap()[:, qb, :], in_=ost[:, oi, :]),
                 reads=[("ost", oi)], dma_sem=("st_o", oi))
        P.emit(st, final_waits=[("st_o", 0), ("st_o", 1)])
    return nc


def diff_attn_maps(dqT, dkT, dv, inp, layer):
    j = layer - 2
    li = 0.8 - 0.6 * math.exp(-0.3 * layer)
    q4 = dqT.reshape(16, 128, 64, 128)
    p = np.arange(128)
    consts = {
        "i4": np.tile(np.eye(128, dtype=np.float32), (1, 4)).astype(NPBF),
        "ones": np.ones((128, 128), np.float32).astype(NPBF),
        "onesf": np.ones((128, 128), np.float32),
        "causT": np.where(p[None, :] <= p[:, None], 0.0, NEG).astype(np.float32).astype(NPBF),
        "lamv": np.ascontiguousarray(np.broadcast_to(np.asarray(inp["diff_lambda"][j], np.float32).reshape(1, 256), (128, 256))),
        "subg": np.ascontiguousarray(np.asarray(inp["diff_subln_g"][j], np.float32).reshape(128, 1)),
        "linit": np.ascontiguousarray(np.broadcast_to(np.array([[li, 1.0 - li]], np.float32), (128, 2))),
    }
    maps = []
    for c in range(NCORES):
        hk = c // 2
        m = dict(consts)
        m["kT"] = np.ascontiguousarray(dkT[hk * 128:(hk + 1) * 128])
        m["v"] = np.ascontiguousarray(dv[:, hk * 128:(hk + 1) * 128])
        qq = np.ascontiguousarray(q4[2 * c:2 * c + 2].transpose(1, 2, 0, 3))
        qz = np.zeros((128, 64, 2, 2, 128), NPBF)
        qz[0:64, :, 0] = qq[0:64]
        qz[64:128, :, 1] = qq[64:128]
        m["qT"] = qz.reshape(128, 64, 512)
        maps.append(m)
    return maps


def diff_attn_gather(res):
    oT = np.zeros((16, 128, 64, 128), NPBF)
    for c in range(NCORES):
        o = np.asarray(res[c]["oT"]).reshape(128, 64, 2, 128).transpose(2, 0, 1, 3)
        oT[2 * c:2 * c + 2] = o
    return oT.reshape(2048, 8192)


def kernel(x, attn_norm_g, mlp_norm_g, final_norm_g, nsa_w_in, nsa_cmp_pos, nsa_cmp_w1, nsa_cmp_w2, nsa_w_out,
           kv_norm_g, kv_w_shared, diff_w_q, diff_lambda, diff_subln_g, diff_w_out, mlp_w_up, mlp_w_down, _debug=None):
    inp = dict(attn_norm_g=attn_norm_g, mlp_norm_g=mlp_norm_g, final_norm_g=final_norm_g, nsa_w_in=nsa_w_in,
               nsa_cmp_pos=nsa_cmp_pos, nsa_cmp_w1=nsa_cmp_w1, nsa_cmp_w2=nsa_cmp_w2, nsa_w_out=nsa_w_out,
               kv_norm_g=kv_norm_g, kv_w_shared=kv_w_shared, diff_w_q=diff_w_q, diff_lambda=diff_lambda,
               diff_subln_g=diff_subln_g, diff_w_out=diff_w_out, mlp_w_up=mlp_w_up, mlp_w_down=mlp_w_down)
    inp = {k: np.asarray(v, np.float32) for k, v in inp.items()}
    x2 = np.asarray(x, np.float32).reshape(S, D)
    xs = [np.ascontiguousarray(x2[c * TPC:(c + 1) * TPC]) for c in range(NCORES)]
    if "nsa" not in _PROG:
        _PROG["nsa"] = build_nsa_attn()
        _PROG["diff"] = build_diff_attn()
    dbg = {}
    res = _run(get_dense(False, False, False, "nsa"), dense_maps(xs, inp, None, None, "nsa", 0))
    dkT = dv = None
    for layer in range(4):
        if layer < 2:
            ra = _run(_PROG["nsa"], nsa_attn_maps(res, inp, layer))
            oT = nsa_attn_gather(ra)
        else:
            if layer == 2:
                dkT, dv = _cat(res, "dkT", 1), _cat(res, "dv", 0)
            ra = _run(_PROG["diff"], diff_attn_maps(_cat(res, "dqT", 1), dkT, dv, inp, layer))
            oT = diff_attn_gather(ra)
        if _debug is not None:
            dbg["oT%d" % layer] = oT
        nxt = [("nsa", 1), ("diffkv", 2), ("diff", 3), (None, None)][layer]
        res = _run(get_dense(True, True, nxt[0] is None, nxt[0]), dense_maps(xs, inp, layer, oT, nxt[0], nxt[1]))
        if nxt[0] is not None:
            xs = [np.asarray(r["x_out"]) for r in res]
            if _debug is not None:
                dbg["x%d" % layer] = np.concatenate(xs, 0)
    y = np.concatenate([np.asarray(r["y"]) for r in res], 0).reshape(1, S, D).astype(np.float32)
    if _debug is not None:
        _debug.update(dbg)
    return y
```

```python
import math
from contextlib import ExitStack

import numpy as np
import ml_dtypes
import concourse.bass as bass
import concourse.mybir as mybir
from concourse.bass_utils import run_bass_kernel_spmd

F32 = mybir.dt.float32
BF16 = mybir.dt.bfloat16
AF = mybir.ActivationFunctionType
ALU = mybir.AluOpType
AX = mybir.AxisListType
NPBF = ml_dtypes.bfloat16

NCORES = 8
S = 8192
D = 2048
TPC = S // NCORES
NTT = TPC // 128
EPS = 1e-6
NEG = -30000.0
ROPE_THETA = 500000.0

ENGS = ("pe", "act", "dve", "pool", "sp")


class Op:
    __slots__ = ("eng", "fn", "deps", "signalled", "sig", "dma_sem", "dma_val")


class Prog:
    def __init__(self, nc):
        self.nc = nc
        self.ops = []
        self.last_w = {}
        self.readers = {}
        self.dma_cum = {}
        self.rot = {}
        self.dry = False

    def rr(self, name, n):
        if self.dry:
            return 0
        i = self.rot.get(name, 0)
        self.rot[name] = i + 1
        return i % n

    def op(self, eng, fn, reads=(), writes=(), dma_sem=None, ndma=1):
        if self.dry:
            return None
        o = Op()
        o.eng = eng
        o.fn = fn
        o.signalled = False
        o.sig = 0
        o.dma_sem = dma_sem
        o.dma_val = 0
        deps = set()
        for k in reads:
            w = self.last_w.get(k)
            if w is not None:
                deps.add(w)
        for k in writes:
            w = self.last_w.get(k)
            if w is not None:
                deps.add(w)
            for r in self.readers.get(k, ()):
                deps.add(r)
        o.deps = [d for d in deps
                  if not (d.eng == "pe" and eng == "pe" and d.dma_sem is None and dma_sem is None)]
        if dma_sem is not None:
            self.dma_cum[dma_sem] = self.dma_cum.get(dma_sem, 0) + 16 * ndma
            o.dma_val = self.dma_cum[dma_sem]
        for d in o.deps:
            if d.dma_sem is None:
                d.signalled = True
        for k in writes:
            self.last_w[k] = o
            self.readers[k] = []
        for k in reads:
            if k not in writes:
                self.readers.setdefault(k, []).append(o)
        self.ops.append(o)
        return o

    def emit(self, stack, final_waits=()):
        nc = self.nc
        cnt = {e: 0 for e in ENGS}
        for o in self.ops:
            if o.dma_sem is None and o.signalled:
                cnt[o.eng] += 1
                o.sig = cnt[o.eng]
        esem = {e: stack.enter_context(nc.semaphore("s_" + e)) for e in ENGS}
        dsem = {}
        for i, k in enumerate(self.dma_cum):
            dsem[k] = stack.enter_context(nc.semaphore("d%d" % i))
        block = stack.enter_context(nc.Block())
        per = {e: [o for o in self.ops if o.eng == e] for e in ENGS}
        dma_cum = self.dma_cum

        def run(e, eng):
            waited = {}
            for o in per[e]:
                need = {}
                for d in o.deps:
                    if d.dma_sem is not None:
                        key = ("d", d.dma_sem)
                        v = d.dma_val
                    else:
                        key = ("e", d.eng)
                        v = d.sig
                    if v > need.get(key, 0):
                        need[key] = v
                for key, v in need.items():
                    if v > waited.get(key, 0):
                        sem = dsem[key[1]] if key[0] == "d" else esem[key[1]]
                        eng.wait_ge(sem, v)
                        waited[key] = v
                r = o.fn(eng)
                if o.dma_sem is not None:
                    rs = r if isinstance(r, (list, tuple)) else [r]
                    for ins in rs:
                        ins.then_inc(dsem[o.dma_sem], 16)
                elif o.signalled:
                    r.then_inc(esem[e], 1)
            if e == "sp":
                for k in final_waits:
                    if k in dsem:
                        eng.wait_ge(dsem[k], dma_cum[k])

        @block.tensor
        def _(eng):
            run("pe", eng)

        @block.scalar
        def _(eng):
            run("act", eng)

        @block.vector
        def _(eng):
            run("dve", eng)

        @block.gpsimd
        def _(eng):
            run("pool", eng)

        @block.sync
        def _(eng):
            run("sp", eng)


def rope_consts(head_chunk, tok0, ntok):
    rot = head_chunk // 4
    half = rot // 2
    inv = 1.0 / (ROPE_THETA ** (np.arange(0, rot, 2, dtype=np.float32) / np.float32(rot)))
    inv = inv.astype(np.float32)
    pos = np.arange(tok0, tok0 + ntok, dtype=np.float32)
    ang = (pos[None, :] * inv[:, None]).astype(np.float32)
    cos = np.cos(ang).astype(np.float32)
    sin = np.sin(ang).astype(np.float32)
    C = np.ones((128, ntok), np.float32)
    Sn = np.zeros((128, ntok), np.float32)
    Rm = np.zeros((128, 128), np.float32)
    for base in range(0, 128, head_chunk):
        for j in range(half):
            C[base + j] = cos[j]
            C[base + half + j] = cos[j]
            Sn[base + j] = sin[j]
            Sn[base + half + j] = sin[j]
            Rm[base + j, base + half + j] = -1.0
            Rm[base + half + j, base + j] = 1.0
    return C, Sn, np.ascontiguousarray(Rm.T)


def build_dense(oproj, mlp, final, proj):
    nc = bass.Bass("TRN2", target_bir_lowering=False)
    T = TPC
    dr = {}

    def din(name, shape, dt=F32):
        dr[name] = nc.dram_tensor(name, list(shape), dt, kind="ExternalInput")
        return dr[name]

    def dout(name, shape, dt=F32):
        dr[name] = nc.dram_tensor(name, list(shape), dt, kind="ExternalOutput")
        return dr[name]

    x_d = din("x", [T, D])
    ident_d = din("ident", [128, 128])
    if oproj:
        oT_d = din("oT", [D, T], BF16)
        wo_d = din("w_o", [D, D])
    if mlp:
        gm_d = din("g_mlp", [128, 16])
        wu_d = din("w_up", [D, 4 * D])
        wd_d = din("w_down", [4 * D, D])
    if final:
        gf_d = din("g_final", [D])
        y_d = dout("y", [T, D])
    else:
        xo_d = dout("x_out", [T, D])
    if proj:
        ga_d = din("g_attn", [128, 16])
        cos_d = din("cosT", [128, T])
        sin_d = din("sinT", [128, T])
        rt_d = din("rotT", [128, 128])
    if proj == "nsa":
        win_d = din("w_in", [D, 5168])
        qT_d = dout("qT", [D, T], BF16)
        qrT_d = dout("qrT", [D, T], BF16)
        kcT_d = dout("kcT", [512, T], BF16)
        vcT_d = dout("vcT", [512, T], BF16)
        ksT_d = dout("ksT", [512, T], BF16)
        kwT_d = dout("kwT", [512, T], BF16)
        vs_d = dout("vs", [T, 512], BF16)
        vw_d = dout("vw", [T, 512], BF16)
        gT_d = dout("gT", [48, T])
    if proj in ("diff", "diffkv"):
        wq_d = din("w_q", [D, D])
        dqT_d = dout("dqT", [D, T], BF16)
    if proj == "diffkv":
        gk_d = din("g_kv", [128, 16])
        wkv_d = din("w_kv", [D, 1024])
        dkT_d = dout("dkT", [512, T], BF16)
        dv_d = dout("dv", [T, 512], BF16)

    with ExitStack() as st:
        sb = lambda name, shape, dt: st.enter_context(nc.sbuf_tensor(name, list(shape), dt))
        x = sb("x_sb", [128, NTT, D], F32)
        big1 = sb("big1", [128, 16, T], BF16)
        big2 = sb("big2", [128, 16, T], BF16)
        NWB = 2
        wb = sb("wb", [128, NWB, 16, 512], BF16)
        wst = sb("wst", [128, 2, 4, 512], F32)
        ident = sb("ident_sb", [128, 128], F32)
        xh = sb("xh", [128, 1, D], F32) if final else None
        xhb = sb("xhb", [128, D], BF16)
        identb = sb("identb_sb", [128, 128], BF16)
        ssq = sb("ssq", [128, 32], F32)
        rstd = sb("rstd", [128, 32], F32)
        gT = sb("gT_sb", [128, 4, 16], F32)
        NST = 4
        stg = sb("stg", [128, NST, 512], BF16)
        r32 = sb("r32", [128, 2, 512], F32)
        if proj:
            cosT = sb("cos_sb", [128, T], F32)
            sinT = sb("sin_sb", [128, T], F32)
            rotT = sb("rot_sb", [128, 128], F32)
            t1 = sb("t1", [128, 1, 512], F32)
            t2 = sb("t2", [128, 1, 512], F32)
            gst = sb("gst", [48, 1, 512], F32)
        if final:
            gfb = sb("gfb", [128, D], F32)
        psb = [st.enter_context(nc.psum_tensor("ps%d" % i, [128, 512], F32)) for i in range(8)]
        psbb = [t_.bitcast(BF16) for t_ in psb]

        P = Prog(nc)
        out_sems = []
        B2 = [("big2", fc) for fc in range(16)]

        P.op("sp", lambda e: e.dma_start(out=x[:], in_=x_d.ap().rearrange("(t p) c -> p t c", p=128)),
             writes=[("x", t) for t in range(NTT)], dma_sem="ldx")
        P.op("act", lambda e: e.dma_start(out=ident[:], in_=ident_d.ap()), writes=["ident"], dma_sem="ld_ident")
        P.op("dve", lambda e: e.tensor_copy(out=identb[:], in_=ident[:]), reads=["ident"], writes=["identb"])
        gains = {}

        def load_gain(name, d):
            gi = len(gains)
            gains[name] = gi
            P.op("act", lambda e: e.dma_start(out=gT[:, gi, :], in_=d.ap()),
                 writes=[("gain", gi)], dma_sem=("ldg", gi))

        if mlp:
            load_gain("mlp", gm_d)
        if proj:
            load_gain("attn", ga_d)
            P.op("act", lambda e: e.dma_start(out=cosT[:], in_=cos_d.ap()), writes=["cos"], dma_sem="ld_cos")
            P.op("act", lambda e: e.dma_start(out=sinT[:], in_=sin_d.ap()), writes=["sin"], dma_sem="ld_sin")
            P.op("act", lambda e: e.dma_start(out=rotT[:], in_=rt_d.ap()), writes=["rot"], dma_sem="ld_rot")
        if proj == "diffkv":
            load_gain("kv", gk_d)
        if final:
            P.op("act", lambda e: e.dma_start(out=gfb[:], in_=gf_d.ap().partition_broadcast(128)),
                 writes=["gfb"], dma_sem="ld_gfb")
        if oproj:
            P.op("sp", lambda e: e.dma_start(out=big2[:], in_=oT_d.ap().rearrange("(k p) t -> p k t", p=128)),
                 writes=B2, dma_sem="ldo")

        wlist = []
        wpos = [0]

        def issue_wblock(n):
            wd, r0, c0, ncols = wlist[n]
            i = n % NWB
            for qq in range(4):
                si = P.rr("wst", 2)
                src = wd.ap()[r0 + qq * 512:r0 + (qq + 1) * 512, c0:c0 + ncols].rearrange("(k p) c -> p k c", p=128)
                P.op("sp", lambda e, si=si, src=src: e.dma_start(out=wst[:, si, :, 0:ncols], in_=src),
                     writes=[("wst", si)], dma_sem=("wst", si))
                P.op("pool", lambda e, si=si, qq=qq: e.tensor_copy(out=wb[:, i, qq * 4:(qq + 1) * 4, 0:ncols],
                                                                   in_=wst[:, si, :, 0:ncols]),
                     reads=[("wst", si)], writes=[("wb", i)])

        def load_wblock(wd, r0, c0, ncols=512):
            if P.dry:
                wlist.append((wd, r0, c0, ncols))
                return 0
            n = wpos[0]
            wpos[0] += 1
            if n == 0:
                issue_wblock(0)
            if n + 1 < len(wlist):
                issue_wblock(n + 1)
            return n % NWB

        def next_ps():
            return P.rr("ps", 4)

        def mm_group(pb, pso, pairs, reads):
            def fn(e):
                n = len(pairs)
                ins = None
                for i, (l, r) in enumerate(pairs):
                    ins = e.matmul(pso, l, r, start=(i == 0), stop=(i == n - 1))
                return ins
            P.op("pe", fn, reads=reads, writes=[("ps", pb)])

        def resid_add(pb, tt, cc):
            xs = x[:, tt, cc * 512:(cc + 1) * 512]
            P.op("dve", lambda e: e.tensor_tensor(out=xs, in0=xs, in1=psb[pb][:], op=ALU.add),
                 reads=[("ps", pb), ("x", tt)], writes=[("x", tt)])

        nstat = [0]
        for PASS in (0, 1):
            P.dry = (PASS == 0)
            nstat[0] = 0
            if oproj:
                for cc in range(4):
                    wi = load_wblock(wo_d, 0, cc * 512)
                    for tt in range(NTT):
                        pb = next_ps()
                        mm_group(pb, psb[pb][:], [(big2[:, kc, tt * 128:(tt + 1) * 128], wb[:, wi, kc, :]) for kc in range(16)],
                                 reads=B2 + [("wb", wi)])
                        resid_add(pb, tt, cc)


            def make_hT(gname):
                HT = 9
                gi = gains[gname]
                for tt in range(NTT):
                    _make_hT_tile(gi, tt, HT)

            def _make_hT_tile(gi, tt, HT):
                if True:
                    si = nstat[0]
                    nstat[0] += 1
                    xi = P.rr("xh", 1)
                    P.op("dve", lambda e, si=si: e.memset(ssq[:, si:si + 1], 0.0), writes=[("ssq", si)])
                    P.op("act", lambda e, si=si, tt=tt: e.activation(out=xhb[:], in_=x[:, tt, :], func=AF.Square,
                                                                      accum_out=ssq[:, si:si + 1]),
                         reads=[("x", tt), ("ssq", si)], writes=["xhb", ("ssq", si)])
                    P.op("dve", lambda e, si=si: e.tensor_scalar(out=rstd[:, si:si + 1], in0=ssq[:, si:si + 1],
                                                                 scalar1=1.0 / D, scalar2=EPS, op0=ALU.mult, op1=ALU.add),
                         reads=[("ssq", si)], writes=[("rstd", si)])
                    if HT < 2:
                        return
                    P.op("act", lambda e, si=si: e.activation(out=rstd[:, si:si + 1], in_=rstd[:, si:si + 1], func=AF.Sqrt),
                         reads=[("rstd", si)], writes=[("rstd", si)])
                    P.op("dve", lambda e, si=si: e.reciprocal(out=rstd[:, si:si + 1], in_=rstd[:, si:si + 1]),
                         reads=[("rstd", si)], writes=[("rstd", si)])
                    if HT < 3:
                        return
                    P.op("act", lambda e, si=si, tt=tt, xi=xi: e.activation(out=xhb[:], in_=x[:, tt, :], func=AF.Identity,
                                                                           scale=rstd[:, si:si + 1]),
                         reads=[("x", tt), ("rstd", si)], writes=["xhb"])
                    if HT < 4:
                        return
                    for q4 in range(4):
                        pb = 4 + P.rr("pst", 2)

                        def tfn(e, q4=q4, xi=xi, pb=pb):
                            ins = None
                            for j in range(4):
                                kc = q4 * 4 + j
                                ins = e.transpose(out=psbb[pb][:, j * 128:(j + 1) * 128],
                                                  in_=xhb[:, kc * 128:(kc + 1) * 128], identity=identb[:])
                            return ins
                        P.op("pe", tfn, reads=["xhb", "identb"], writes=[("ps", pb)])
                        if HT < 5:
                            continue
                        for j in range(4):
                            kc = q4 * 4 + j
                            dst = big1[:, kc, tt * 128:(tt + 1) * 128]
                            src = psbb[pb][:, j * 128:(j + 1) * 128]
                            EV = 2
                            if (q4 % 2 == 0 and EV == 2) or EV == 0:
                                P.op("dve", lambda e, dst=dst, src=src, kc=kc: e.tensor_scalar(
                                    out=dst, in0=src, scalar1=gT[:, gi, kc:kc + 1], scalar2=None, op0=ALU.mult),
                                    reads=[("ps", pb), ("gain", gi)], writes=[("big1", tt, q4 % 2)])
                            else:
                                P.op("act", lambda e, dst=dst, src=src, kc=kc: e.activation(
                                    out=dst, in_=src, func=AF.Identity, scale=gT[:, gi, kc:kc + 1]),
                                    reads=[("ps", pb), ("gain", gi)], writes=[("big1", tt, q4 % 2)])

            hT_reads = [("big1", t, u) for t in range(NTT) for u in range(2)]

            if mlp:
                make_hT("mlp")
                for qt in range(4):
                    for blk in range(4):
                        wi = load_wblock(wu_d, 0, qt * 2048 + blk * 512)
                        for j in range(4):
                            fc = blk * 4 + j
                            for tg in range(2):
                                pb = next_ps()
                                mm_group(pb, psb[pb][:],
                                         [(wb[:, wi, kc, j * 128:(j + 1) * 128], big1[:, kc, tg * 512:(tg + 1) * 512])
                                          for kc in range(16)],
                                         reads=hT_reads + [("wb", wi)])
                                ri = P.rr("r32", 2)
                                P.op("act", lambda e, pb=pb, ri=ri: e.activation(out=r32[:, ri, :], in_=psb[pb][:], func=AF.Relu),
                                     reads=[("ps", pb)], writes=[("r32", ri)])
                                if (fc + tg) % 2 == 0:
                                    P.op("act", lambda e, ri=ri, fc=fc, tg=tg: e.activation(
                                        out=big2[:, fc, tg * 512:(tg + 1) * 512], in_=r32[:, ri, :], func=AF.Square),
                                        reads=[("r32", ri)], writes=[("big2", fc)])
                                else:
                                    P.op("dve", lambda e, ri=ri, fc=fc, tg=tg: e.tensor_tensor(
                                        out=big2[:, fc, tg * 512:(tg + 1) * 512], in0=r32[:, ri, :], in1=r32[:, ri, :], op=ALU.mult),
                                        reads=[("r32", ri)], writes=[("big2", fc)])
                    for cc in range(4):
                        wi = load_wblock(wd_d, qt * 2048, cc * 512)
                        for tt in range(NTT):
                            pb = next_ps()
                            mm_group(pb, psb[pb][:],
                                     [(big2[:, fc, tt * 128:(tt + 1) * 128], wb[:, wi, fc, :]) for fc in range(16)],
                                     reads=B2 + [("wb", wi)])
                            resid_add(pb, tt, cc)

            if final:
                for tt in range(NTT):
                    si = nstat[0]
                    nstat[0] += 1
                    yi = 0
                    P.op("dve", lambda e, si=si: e.memset(ssq[:, si:si + 1], 0.0), writes=[("ssq", si)])
                    P.op("act", lambda e, si=si, tt=tt: e.activation(out=xh[:, 0, :], in_=x[:, tt, :], func=AF.Square,
                                                                      accum_out=ssq[:, si:si + 1]),
                         reads=[("x", tt), ("ssq", si)], writes=[("xh", 0), ("ssq", si)])
                    P.op("dve", lambda e, si=si: e.tensor_scalar(out=rstd[:, si:si + 1], in0=ssq[:, si:si + 1],
                                                                 scalar1=1.0 / D, scalar2=EPS, op0=ALU.mult, op1=ALU.add),
                         reads=[("ssq", si)], writes=[("rstd", si)])
                    P.op("act", lambda e, si=si: e.activation(out=rstd[:, si:si + 1], in_=rstd[:, si:si + 1], func=AF.Sqrt),
                         reads=[("rstd", si)], writes=[("rstd", si)])
                    P.op("dve", lambda e, si=si: e.reciprocal(out=rstd[:, si:si + 1], in_=rstd[:, si:si + 1]),
                         reads=[("rstd", si)], writes=[("rstd", si)])
                    P.op("dve", lambda e, si=si, tt=tt, yi=yi: e.scalar_tensor_tensor(
                        out=xh[:, 0, :], in0=x[:, tt, :], scalar=rstd[:, si:si + 1], in1=gfb[:], op0=ALU.mult, op1=ALU.mult),
                        reads=[("x", tt), ("rstd", si), "gfb"], writes=[("xh", 0)])
                    P.op("sp", lambda e, tt=tt, yi=yi: e.dma_start(out=y_d.ap()[tt * 128:(tt + 1) * 128, :], in_=xh[:, 0, :]),
                         reads=[("xh", 0)], dma_sem="yst")
                out_sems += ["yst"]
            else:
                P.op("sp", lambda e: e.dma_start(out=xo_d.ap().rearrange("(t p) c -> p t c", p=128), in_=x[:]),
                     reads=[("x", t) for t in range(NTT)], dma_sem="stx")
                out_sems.append("stx")

            def stage_out(dst_ap, src_fn, eng, reads):
                si = P.rr("stg", NST)
                P.op(eng, lambda e: src_fn(e, stg[:, si, :]), reads=reads, writes=[("stg", si)])
                P.op("sp", lambda e: e.dma_start(out=dst_ap, in_=stg[:, si, :]), reads=[("stg", si)], dma_sem=("stg", si))

            def proj_fm(wi, j, mode, outs):
                for tg in range(2):
                    pb = next_ps()
                    mm_group(pb, psb[pb][:],
                             [(wb[:, wi, kc, j * 128:(j + 1) * 128], big1[:, kc, tg * 512:(tg + 1) * 512]) for kc in range(16)],
                             reads=hT_reads + [("wb", wi)])
                    tsl = slice(tg * 512, (tg + 1) * 512)
                    if "rope" in outs:
                        ri = P.rr("r32", 2)
                        P.op("act", lambda e, pb=pb, ri=ri: e.activation(out=r32[:, ri, :], in_=psb[pb][:], func=AF.Copy),
                             reads=[("ps", pb)], writes=[("r32", ri)])
                        if "plain" in outs:
                            dd, r0 = outs["plain"]
                            stage_out(dd.ap()[r0:r0 + 128, tsl],
                                      lambda e, o, ri=ri: e.tensor_copy(out=o, in_=r32[:, ri, :]), "dve", [("r32", ri)])
                        pr = 6 + P.rr("psr", 2)
                        P.op("pe", lambda e, pr=pr, ri=ri: e.matmul(psb[pr][:], rotT[:], r32[:, ri, :], start=True, stop=True),
                             reads=[("r32", ri), "rot"], writes=[("ps", pr)])
                        ti = P.rr("t12", 1)
                        P.op("dve", lambda e, ri=ri, ti=ti, tsl=tsl: e.tensor_tensor(out=t1[:, ti, :], in0=r32[:, ri, :],
                                                                                     in1=cosT[:, tsl], op=ALU.mult),
                             reads=[("r32", ri), "cos"], writes=[("t1", ti)])
                        P.op("dve", lambda e, pr=pr, ti=ti, tsl=tsl: e.tensor_tensor(out=t2[:, ti, :], in0=psb[pr][:],
                                                                                     in1=sinT[:, tsl], op=ALU.mult),
                             reads=[("ps", pr), "sin"], writes=[("t2", ti)])
                        dd, r0 = outs["rope"]
                        stage_out(dd.ap()[r0:r0 + 128, tsl],
                                  lambda e, o, ti=ti: e.tensor_tensor(out=o, in0=t1[:, ti, :], in1=t2[:, ti, :], op=ALU.add),
                                  "dve", [("t1", ti), ("t2", ti)])
                    else:
                        dd, r0 = outs["plain"]
                        stage_out(dd.ap()[r0:r0 + 128, tsl],
                                  lambda e, o, pb=pb: e.activation(out=o, in_=psb[pb][:], func=AF.Copy), "act", [("ps", pb)])

            def proj_tm(wi, dd, c0):
                for tt in range(NTT):
                    pb = next_ps()
                    mm_group(pb, psb[pb][:],
                             [(big1[:, kc, tt * 128:(tt + 1) * 128], wb[:, wi, kc, :]) for kc in range(16)],
                             reads=hT_reads + [("wb", wi)])
                    stage_out(dd.ap()[tt * 128:(tt + 1) * 128, c0:c0 + 512],
                              lambda e, o, pb=pb: e.activation(out=o, in_=psb[pb][:], func=AF.Copy), "act", [("ps", pb)])

            DBG = 9
            if proj == "nsa" and DBG >= 2:
                make_hT("attn")
            if proj == "nsa" and DBG >= 3:
                for b in range(4 if DBG >= 4 else 1):
                    wi = load_wblock(win_d, 0, b * 512)
                    for j in range(4):
                        r0 = (b * 4 + j) * 128
                        proj_fm(wi, j, "both", {"plain": (qT_d, r0), "rope": (qrT_d, r0)})
            if proj == "nsa" and DBG >= 5:
                specs = [(kcT_d, False), (vcT_d, False), (ksT_d, True), (None, "vs"), (kwT_d, True), (None, "vw")]
                for pi, (dd, mode) in enumerate(specs):
                    wi = load_wblock(win_d, 0, 2048 + pi * 512)
                    if dd is None:
                        proj_tm(wi, vs_d if mode == "vs" else vw_d, 0)
                    else:
                        for j in range(4):
                            proj_fm(wi, j, "x", {"rope": (dd, j * 128)} if mode else {"plain": (dd, j * 128)})
                wi = load_wblock(win_d, 0, 2048 + 6 * 512, 48)
                for tg in range(2):
                    pb = next_ps()
                    mm_group(pb, psb[pb][0:48, :],
                             [(wb[:, wi, kc, 0:48], big1[:, kc, tg * 512:(tg + 1) * 512]) for kc in range(16)],
                             reads=hT_reads + [("wb", wi)])
                    gi2 = P.rr("gst", 1)
                    P.op("act", lambda e, pb=pb, gi2=gi2: e.activation(out=gst[:, gi2, :], in_=psb[pb][0:48, :], func=AF.Sigmoid),
                         reads=[("ps", pb)], writes=[("gst", gi2)])
                    P.op("sp", lambda e, gi2=gi2, tg=tg: e.dma_start(out=gT_d.ap()[:, tg * 512:(tg + 1) * 512], in_=gst[:, gi2, :]),
                         reads=[("gst", gi2)], dma_sem=("gst", gi2))
                out_sems += [("gst", 0)]
            if proj in ("diff", "diffkv"):
                make_hT("attn")
                for b in range(4):
                    wi = load_wblock(wq_d, 0, b * 512)
                    for j in range(4):
                        proj_fm(wi, j, "x", {"rope": (dqT_d, (b * 4 + j) * 128)})
            if proj == "diffkv":
                make_hT("kv")
                wi = load_wblock(wkv_d, 0, 0)
                for j in range(4):
                    proj_fm(wi, j, "x", {"rope": (dkT_d, j * 128)})
                wi = load_wblock(wkv_d, 0, 512)
                proj_tm(wi, dv_d, 0)
            if proj:
                out_sems += [("stg", i) for i in range(NST)]

        P.emit(st, final_waits=out_sems)
    return nc


_DENSE_CACHE = {}


def gain_fm(g):
    return np.ascontiguousarray(np.asarray(g, np.float32).reshape(16, 128).T)


def get_dense(oproj, mlp, final, proj):
    key = (oproj, mlp, final, proj)
    if key not in _DENSE_CACHE:
        _DENSE_CACHE[key] = build_dense(*key)
    return _DENSE_CACHE[key]


NSLOT = 32
BIGV = 10000.0


def slot_qb(i, half):
    m = i // 2
    if i % 2 == 0:
        return 4 * m + (0 if half == 0 else 1), 4 * m + 1
    return 4 * m + (3 if half == 0 else 2), 4 * m + 3


def build_nsa_attn():
    nc = bass.Bass("TRN2", target_bir_lowering=False)
    din = lambda name, shape, dt=F32: nc.dram_tensor(name, list(shape), dt, kind="ExternalInput")
    kcT_d = din("kcT", [128, S], BF16)
    vcT_d = din("vcT", [128, S], BF16)
    ksT_d = din("ksT", [128, S], BF16)
    kwT_d = din("kwT", [128, S], BF16)
    vs_d = din("vs", [S, 128], BF16)
    vw_d = din("vw", [S, 128], BF16)
    qT_d = din("qT", [128, NSLOT, 512], BF16)
    qrT_d = din("qrT", [128, NSLOT, 512], BF16)
    g3_d = din("g3", [3, NSLOT, 512])
    w1_d = din("w1", [2, 4096, 512])
    w2_d = din("w2", [2, 512, 128])
    posT_d = din("posT", [2, 128, 32])
    identf_d = din("identf", [128, 128])
    i4_d = din("i4", [128, 512], BF16)
    ones_d = din("ones", [128, 128], BF16)
    acon_d = din("acon4", [128, 64, 128], BF16)
    ov_d = din("ov", [128, 4, 128], BF16)
    tailm_d = din("tailm", [128, 4, 128], BF16)
    winm_d = din("winm", [128, 8, 128], BF16)
    cmask_d = din("cmask", [128, NSLOT, 128], BF16)
    slotc_d = din("slotc", [128, NSLOT, 256])
    sel3_d = din("sel3", [3, 3, 128])
    oT_d = nc.dram_tensor("oT", [128, NSLOT, 512], BF16, kind="ExternalOutput")
    DBGA = 0
    if DBGA:
        dbg_d = nc.dram_tensor("dbg", [128, 2, 8, 512], F32, kind="ExternalOutput")
    scale = 128.0 ** -0.5

    with ExitStack() as st:
        sb = lambda name, shape, dt: st.enter_context(nc.sbuf_tensor(name, list(shape), dt))
        ksT = sb("ksT_sb", [128, S], BF16)
        kwT = sb("kwT_sb", [128, S], BF16)
        vs = sb("vs_sb", [128, 64, 128], BF16)
        vw = sb("vw_sb", [128, 64, 128], BF16)
        xc = sb("xc", [128, S], BF16)
        w1sb = sb("w1sb", [128, 32, 512], BF16)
        w2sb = sb("w2sb", [128, 4, 128], BF16)
        posT = sb("posT_sb", [128, 32], F32)
        XL = sb("XL", [128, 2, 512], BF16)
        hidT = sb("hidT", [128, 4, 512], BF16)
        tA = sb("tA", [128, 2, 512], F32)
        tB = sb("tB", [128, 2, 512], F32)
        kcmpT = sb("kcmpT", [128, 512], BF16)
        vcmp = sb("vcmp", [128, 4, 128], BF16)
        identf = sb("identf_sb", [128, 128], F32)
        i4 = sb("i4_sb", [128, 512], BF16)
        ones = sb("ones_sb", [128, 128], BF16)
        acon = sb("acon_sb", [128, 64, 128], BF16)
        ov = sb("ov_sb", [128, 4, 128], BF16)
        tailm = sb("tailm_sb", [128, 4, 128], BF16)
        winm = sb("winm_sb", [128, 8, 128], BF16)
        sel3 = sb("sel3_sb", [3, 3, 128], F32)
        qsb = sb("qsb", [128, 2, 512], BF16)
        qrsb = sb("qrsb", [128, 2, 512], BF16)
        g3sb = sb("g3sb", [3, 2, 512], F32)
        cmsb = sb("cmsb", [128, 2, 128], BF16)
        slc = sb("slc", [128, 2, 256], F32)
        Ec = sb("Ec", [128, 4, 512], BF16)
        NE = 3
        Eb = sb("Eb", [128, NE, 512], BF16)
        Pn = sb("Pn", [128, 4, 512], BF16)
        rden = sb("rden", [128, 3, 512], F32)
        wgt = sb("wgt", [128, 512], F32)
        tmp = sb("tmp", [128, 512], F32)
        acc = sb("acc", [128, 2, 512], F32)
        ost = sb("ost", [128, 2, 512], BF16)
        impm = sb("impm", [128, 128], F32)
        impm2 = sb("impm2", [128, 128], F32)
        t8 = sb("t8", [128, 16], F32)
        nsel = sb("nsel", [128, 128], F32)
        nselT = sb("nselT", [128, 2, 4, 128], BF16)
        PS = [st.enter_context(nc.psum_tensor("ps%d" % i, [128, 512], F32)) for i in range(8)]
        Sb, Ob, Db, M0, M1 = (0, 1, 6), (2, 3), (4, 5), 7, 7

        P = Prog(nc)
        ld = lambda eng, dst, src, key, sem: P.op(eng, lambda e: e.dma_start(out=dst, in_=src), writes=[key], dma_sem=sem)
        ld("sp", ksT[:], ksT_d.ap(), "ksT", "l_ks")
        ld("sp", kwT[:], kwT_d.ap(), "kwT", "l_kw")
        ld("sp", vs[:], vs_d.ap().rearrange("(t p) d -> p t d", p=128), "vs", "l_vs")
        ld("sp", vw[:], vw_d.ap().rearrange("(t p) d -> p t d", p=128), "vw", "l_vw")
        ld("act", identf[:], identf_d.ap(), "identf", "l_c0")
        ld("act", i4[:], i4_d.ap(), "i4", "l_c1")
        ld("act", ones[:], ones_d.ap(), "ones", "l_c2")
        ld("act", acon[:], acon_d.ap(), "acon", "l_c3")
        ld("act", ov[:], ov_d.ap(), "ov", "l_c4")
        ld("act", tailm[:], tailm_d.ap(), "tailm", "l_c5")
        ld("act", winm[:], winm_d.ap(), "winm", "l_c6")
        ld("act", sel3[:], sel3_d.ap(), "sel3", "l_c7")
        P.op("dve", lambda e: e.memset(XL[:], 0.0), writes=[("XL", 0), ("XL", 1)])

        GC = math.sqrt(2.0 / math.pi)
        for jv in range(2):
            src_d = kcT_d if jv == 0 else vcT_d
            ld("sp", xc[:], src_d.ap(), "xc", "l_xc")
            for q4 in range(4):
                P.op("pool", lambda e, q4=q4, jv=jv: e.dma_start(
                    out=w1sb[:, q4 * 8:(q4 + 1) * 8, :],
                    in_=w1_d.ap()[jv, q4 * 1024:(q4 + 1) * 1024, :].rearrange("(l p) h -> p l h", p=128)),
                    writes=[("w1", q4)], dma_sem=("l_w1", q4))
            P.op("pool", lambda e, jv=jv: e.dma_start(out=w2sb[:], in_=w2_d.ap()[jv].rearrange("(c p) d -> p c d", p=128)),
                 writes=["w2"], dma_sem="l_w2")
            ld("act", posT[:], posT_d.ap()[jv], "posT", "l_pos")
            for l in range(32):
                xi = P.rr("xl", 2)
                src = bass.AP(xc, l, [[S, 128], [16, 511]])
                P.op("dve", lambda e, xi=xi, src=src, l=l: e.tensor_scalar(
                    out=XL[:, xi, 0:511], in0=src, scalar1=posT[:, l:l + 1], scalar2=None, op0=ALU.add),
                    reads=["xc", "posT"], writes=[("XL", xi)])

                def fn(e, xi=xi, l=l):
                    ins = None
                    for hc in range(4):
                        ins = e.matmul(PS[hc][:], w1sb[:, l, hc * 128:(hc + 1) * 128], XL[:, xi, :],
                                       start=(l == 0), stop=(l == 31))
                    return ins
                P.op("pe", fn, reads=[("XL", xi), ("w1", l // 8)], writes=[("H", hc) for hc in range(4)])
            for hc in range(4):
                ti = P.rr("tAB", 2)
                P.op("act", lambda e, hc=hc, ti=ti: e.activation(out=tA[:, ti, :], in_=PS[hc][:], func=AF.Square),
                     reads=[("H", hc)], writes=[("tA", ti)])
                P.op("dve", lambda e, ti=ti: e.tensor_scalar(out=tA[:, ti, :], in0=tA[:, ti, :], scalar1=0.044715, scalar2=1.0,
                                                             op0=ALU.mult, op1=ALU.add),
                     reads=[("tA", ti)], writes=[("tA", ti)])
                P.op("dve", lambda e, hc=hc, ti=ti: e.tensor_tensor(out=tB[:, ti, :], in0=tA[:, ti, :], in1=PS[hc][:], op=ALU.mult),
                     reads=[("tA", ti), ("H", hc)], writes=[("tB", ti)])
                P.op("act", lambda e, ti=ti: e.activation(out=tB[:, ti, :], in_=tB[:, ti, :], func=AF.Tanh, scale=GC),
                     reads=[("tB", ti)], writes=[("tB", ti)])
                P.op("dve", lambda e, ti=ti: e.tensor_scalar(out=tB[:, ti, :], in0=tB[:, ti, :], scalar1=1.0, scalar2=0.5,
                                                             op0=ALU.add, op1=ALU.mult),
                     reads=[("tB", ti)], writes=[("tB", ti)])
                P.op("dve", lambda e, hc=hc, ti=ti: e.tensor_tensor(out=hidT[:, hc, :], in0=tB[:, ti, :], in1=PS[hc][:], op=ALU.mult),
                     reads=[("tB", ti), ("H", hc)], writes=[("hidT", hc)])
            hid_reads = [("hidT", hc) for hc in range(4)]
            if jv == 0:
                def fn(e):
                    ins = None
                    for hc in range(4):
                        ins = e.matmul(PS[4][:], w2sb[:, hc, :], hidT[:, hc, :], start=(hc == 0), stop=(hc == 3))
                    return ins
                P.op("pe", fn, reads=hid_reads + ["w2"], writes=[("ps", 4)])
                P.op("act", lambda e: e.activation(out=kcmpT[:], in_=PS[4][:], func=AF.Copy), reads=[("ps", 4)], writes=["kcmpT"])
            else:
                def fn(e):
                    ins = None
                    for nt in range(4):
                        for hc in range(4):
                            ins = e.matmul(PS[5][:, nt * 128:(nt + 1) * 128], hidT[:, hc, nt * 128:(nt + 1) * 128], w2sb[:, hc, :],
                                           start=(hc == 0), stop=(hc == 3))
                    return ins
                P.op("pe", fn, reads=hid_reads + ["w2"], writes=[("ps", 5)])
                P.op("act", lambda e: e.activation(out=vcmp[:].rearrange("p a b -> p (a b)"), in_=PS[5][:], func=AF.Copy),
                     reads=[("ps", 5)], writes=["vcmp"])
        ALLPS = [("H", h) for h in range(4)] + [("ps", 4), ("ps", 5)]
        PK = {0: ("S", 0), 1: ("S", 1), 2: ("O", 0), 3: ("O", 1), 4: ("D", 0), 5: ("D", 1), 6: ("S", 2)}
        P.op("pe", lambda e: e.matmul(PS[7][:, 0:128], ones[:], ones[:], start=True, stop=True),
             reads=["ones"], writes=ALLPS + [PK[i] for i in range(7)] + ["M"])

        def branch(tiles, qbuf_key, q_ap, ob, db, hooks=None):
            n = len(tiles)
            slots = {}

            def emit_s(ti_):
                kl, kkey, extra, vl, vkey = tiles[ti_]
                sbk = P.rr("S", 3)
                slots[ti_] = sbk

                def sfn(e, kl=kl, extra=extra, sbk=sbk):
                    ins = e.matmul(PS[Sb[sbk]][:], kl, q_ap, start=True, stop=(len(extra) == 0))
                    for xi_, (l_, r_, tp, _) in enumerate(extra):
                        kw = {} if tp is None else {"tile_position": tp}
                        ins = e.matmul(PS[Sb[sbk]][:], l_, r_, start=False, stop=(xi_ == len(extra) - 1), **kw)
                    return ins
                xkeys = [k for x_ in extra for k in x_[3]]
                P.op("pe", sfn, reads=[kkey, qbuf_key] + xkeys, writes=[("S", sbk)])

            def emit_rest(ti_):
                kl, kkey, extra, vl, vkey = tiles[ti_]
                sbk = slots[ti_]
                ei = P.rr("E", NE)
                P.op("act", lambda e, sbk=sbk, ei=ei: e.activation(out=Eb[:, ei, :], in_=PS[Sb[sbk]][:], func=AF.Exp, scale=scale),
                     reads=[("S", sbk)], writes=[("E", ei)])

                def ofn(e, vl=vl, ei=ei, ti_=ti_):
                    e.matmul(PS[Ob[ob]][:], vl, Eb[:, ei, :], start=(ti_ == 0), stop=(ti_ == n - 1))
                    return e.matmul(PS[Db[db]][:], ones[:], Eb[:, ei, :], start=(ti_ == 0), stop=(ti_ == n - 1))
                P.op("pe", ofn, reads=[vkey, ("E", ei), "ones"], writes=[("O", ob), ("D", db)])

            emit_s(0)
            if n > 1:
                emit_s(1)
            for ti_ in range(n):
                if ti_ + 2 < n:
                    emit_s(ti_ + 2)
                emit_rest(ti_)
                if hooks and ti_ in hooks:
                    for h_ in hooks[ti_]:
                        h_()

        def rden_of(db, rb):
            P.op("dve", lambda e: e.tensor_scalar(out=rden[:, rb, :], in0=PS[Db[db]][:], scalar1=1e-30, scalar2=None, op0=ALU.max),
                 reads=[("D", db)], writes=[("rden", rb)])
            P.op("dve", lambda e: e.reciprocal(out=rden[:, rb, :], in_=rden[:, rb, :]), reads=[("rden", rb)], writes=[("rden", rb)])

        cur_slot = [0]

        def dump(src_ap, key, idx):
            if DBGA and cur_slot[0] < 2:
                sl = cur_slot[0]
                P.op("sp", lambda e: e.dma_start(out=dbg_d.ap()[:, sl, idx, :], in_=src_ap), reads=[key], dma_sem="dbg")

        def combine(bi, ob, rb, qi, ai, first):
            P.op("pe", lambda e: e.matmul(PS[M1][:], sel3[:, bi, :], g3sb[:, qi, :], start=True, stop=True),
                 reads=["sel3", ("g3", qi)], writes=["M"])
            dump(rden[:, rb, :], ("rden", rb), bi * 2)
            P.op("dve", lambda e: e.tensor_tensor(out=wgt[:], in0=rden[:, rb, :], in1=PS[M1][:], op=ALU.mult),
                 reads=[("rden", rb), "M"], writes=["wgt"])
            dump(wgt[:], "wgt", bi * 2 + 1)
            if first:
                P.op("dve", lambda e: e.tensor_tensor(out=acc[:, ai, :], in0=wgt[:], in1=PS[Ob[ob]][:], op=ALU.mult),
                     reads=["wgt", ("O", ob)], writes=[("acc", ai)])
            else:
                P.op("dve", lambda e: e.tensor_tensor(out=tmp[:], in0=wgt[:], in1=PS[Ob[ob]][:], op=ALU.mult),
                     reads=["wgt", ("O", ob)], writes=["tmp"])
                P.op("pool", lambda e: e.tensor_tensor(out=acc[:, ai, :], in0=acc[:, ai, :], in1=tmp[:], op=ALU.add),
                     reads=["tmp", ("acc", ai)], writes=[("acc", ai)])

        def slot_loads(i):
            qi = i % 2
            ld("sp", qsb[:, qi, :], qT_d.ap()[:, i, :], ("q", qi), ("l_q", qi))
            ld("sp", qrsb[:, qi, :], qrT_d.ap()[:, i, :], ("qr", qi), ("l_qr", qi))
            ld("sp", g3sb[:, qi, :], g3_d.ap()[:, i, :], ("g3", qi), ("l_g3", qi))
            ld("sp", cmsb[:, qi, :], cmask_d.ap()[:, i, :], ("cm", qi), ("l_cm", qi))
            ld("sp", slc[:, qi, :], slotc_d.ap()[:, i, :], ("slc", qi), ("l_sl", qi))

        NSL = NSLOT
        SL = {}

        def slot_info(i):
            par = i % 2
            return par, 4 * (i // 2) + (1 if par == 0 else 3), i % 2

        def stage_a(i):
            par, qbmax, qi = slot_info(i)
            nkt = qbmax // 16 + 1
            obc, dbc = P.rr("O", 2), P.rr("D", 2)
            for kc_ in range(nkt):
                sbk = P.rr("S", 3)
                last = kc_ == nkt - 1

                def sfn(e, kc_=kc_, sbk=sbk, last=last, qi=qi):
                    ins = e.matmul(PS[Sb[sbk]][:], kcmpT[:, kc_ * 128:(kc_ + 1) * 128], qsb[:, qi, :], start=True, stop=not last)
                    if last:
                        ins = e.matmul(PS[Sb[sbk]][:], cmsb[:, qi, :], i4[:], start=False, stop=True)
                    return ins
                P.op("pe", sfn, reads=["kcmpT", ("q", qi), ("cm", qi), "i4"], writes=[("S", sbk)])
                P.op("act", lambda e, sbk=sbk, kc_=kc_: e.activation(out=Ec[:, kc_, :], in_=PS[Sb[sbk]][:], func=AF.Exp, scale=scale),
                     reads=[("S", sbk)], writes=[("Ec", kc_)])

                def ofn(e, kc_=kc_, last=last, obc=obc, dbc=dbc):
                    e.matmul(PS[Ob[obc]][:], vcmp[:, kc_, :], Ec[:, kc_, :], start=(kc_ == 0), stop=last)
                    return e.matmul(PS[Db[dbc]][:], ones[:], Ec[:, kc_, :], start=(kc_ == 0), stop=last)
                P.op("pe", ofn, reads=["vcmp", ("Ec", kc_), "ones"], writes=[("O", obc), ("D", dbc)])
            rbc = P.rr("rden", 3)
            rden_of(dbc, rbc)
            for kc_ in range(nkt):
                P.op("pool", lambda e, kc_=kc_, rbc=rbc: e.tensor_tensor(out=Pn[:, kc_, :], in0=Ec[:, kc_, :], in1=rden[:, rbc, :], op=ALU.mult),
                     reads=[("Ec", kc_), ("rden", rbc)], writes=[("Pn", kc_)])
            SL[i] = dict(nkt=nkt, obc=obc, rbc=rbc)
            combine(0, obc, rbc, qi, i % 2, True)

        def stage_b(i):
            par, qbmax, qi = slot_info(i)
            nkt = SL[i]["nkt"]

            def ifn(e, nkt=nkt):
                ins = None
                tot = nkt * 4
                c_ = 0
                for kc_ in range(nkt):
                    for g in range(4):
                        ins = e.matmul(PS[M0][:, 0:128], Pn[:, kc_, g * 128:(g + 1) * 128], ov[:, kc_, :],
                                       start=(c_ == 0), stop=(c_ == tot - 1))
                        c_ += 1
                return ins
            P.op("pe", ifn, reads=[("Pn", k) for k in range(nkt)] + ["ov"], writes=["M"])
            P.op("dve", lambda e, qi=qi: e.tensor_tensor(out=impm[:], in0=PS[M0][:, 0:128], in1=slc[:, qi, 0:128], op=ALU.mult),
                 reads=["M", ("slc", qi)], writes=["impm"])
            P.op("dve", lambda e, qi=qi: e.tensor_tensor(out=impm[:], in0=impm[:], in1=slc[:, qi, 128:256], op=ALU.add),
                 reads=["impm", ("slc", qi)], writes=["impm"])
            P.op("dve", lambda e: e.max(out=t8[:, 0:8], in_=impm[:]), reads=["impm"], writes=["t8a"])
            P.op("dve", lambda e: e.match_replace(out=impm2[:], in_to_replace=t8[:, 0:8], in_values=impm[:], imm_value=-1.0e9),
                 reads=["impm", "t8a"], writes=["impm2"])
            P.op("dve", lambda e: e.max(out=t8[:, 8:16], in_=impm2[:]), reads=["impm2"], writes=["t8b"])
            P.op("dve", lambda e: e.tensor_scalar(out=nsel[:], in0=impm[:], scalar1=t8[:, 15:16], scalar2=None, op0=ALU.is_lt),
                 reads=["impm", "t8b"], writes=["nsel"])

        def stage_c(i):
            par, qbmax, qi = slot_info(i)
            P.op("pe", lambda e: e.transpose(out=PS[M0][:, 128:256], in_=nsel[:], identity=identf[:]),
                 reads=["nsel", "identf"], writes=["M"])
            ni = P.rr("nselT", 2)
            for g in range(4):
                P.op("dve", lambda e, g=g, ni=ni: e.tensor_copy(out=nselT[:, ni, g, :], in_=PS[M0][:, 128:256]),
                     reads=["M"], writes=[("nselT", ni)])
            SL[i]["ni"] = ni

        if NSL > 0:
            slot_loads(0)
            stage_a(0)
            stage_b(0)
            stage_c(0)
        for i in range(NSL):
            par, qbmax, qi = slot_info(i)
            ai = i % 2
            nxt = i + 1 < NSL
            if nxt:
                slot_loads(i + 1)
                stage_a(i + 1)

            tiles = []
            for jj in range(6):
                kt = qbmax - 5 + jj
                if kt < 0:
                    continue
                extra = []
                if jj in (0, 1, 4, 5):
                    extra.append((winm[:, par * 4 + (0, 1, None, None, 2, 3)[jj], :], i4[:], None, ["winm", "i4"]))
                tiles.append((kwT[:, kt * 128:(kt + 1) * 128], "kwT", extra, vw[:, kt, :], "vw"))
            obw, dbw = P.rr("O", 2), P.rr("D", 2)
            branch(tiles, ("qr", qi), qrsb[:, qi, :], obw, dbw)
            rbw = P.rr("rden", 3)
            rden_of(dbw, rbw)
            combine(2, obw, rbw, qi, ai, False)

            ni = SL[i]["ni"]
            tiles = []
            for kt in range(qbmax + 1):
                extra = [(acon[:, kt, :], nselT[:, ni, :, :].rearrange("p a b -> p (a b)"), None, ["acon", ("nselT", ni)])]
                if kt == qbmax - 1:
                    extra.append((tailm[:, par * 2 + 0, :], i4[:], None, ["tailm", "i4"]))
                if kt == qbmax:
                    extra.append((tailm[:, par * 2 + 1, :], i4[:], None, ["tailm", "i4"]))
                tiles.append((ksT[:, kt * 128:(kt + 1) * 128], "ksT", extra, vs[:, kt, :], "vs"))
            hooks = {}
            if nxt:
                nt_ = len(tiles)
                hooks.setdefault(min(1, nt_ - 1), []).append(lambda i=i: stage_b(i + 1))
                hooks.setdefault(min(7, nt_ - 1), []).append(lambda i=i: stage_c(i + 1))
            obs, dbs = P.rr("O", 2), P.rr("D", 2)
            branch(tiles, ("qr", qi), qrsb[:, qi, :], obs, dbs, hooks)
            rbs = P.rr("rden", 3)
            rden_of(dbs, rbs)
            combine(1, obs, rbs, qi, ai, False)

            oi = P.rr("ost", 2)
            P.op("act", lambda e, oi=oi, ai=ai: e.activation(out=ost[:, oi, :], in_=acc[:, ai, :], func=AF.Copy),
                 reads=[("acc", ai)], writes=[("ost", oi)])
            P.op("sp", lambda e, oi=oi, i=i: e.dma_start(out=oT_d.ap()[:, i, :], in_=ost[:, oi, :]),
                 reads=[("ost", oi)], dma_sem=("st_o", oi))
        P.emit(st, final_waits=[("st_o", 0), ("st_o", 1), "dbg"])
    return nc


def nsa_attn_consts(half):
    c = {}
    c["identf"] = np.eye(128, dtype=np.float32)
    c["i4"] = np.tile(np.eye(128, dtype=np.float32), (1, 4)).astype(NPBF)
    c["ones"] = np.ones((128, 128), np.float32).astype(NPBF)
    p = np.arange(128)
    k = np.arange(128)
    acon = np.zeros((128, 64, 128), np.float32)
    for kt in range(64):
        acon[:, kt, :] = np.where(p[:, None] == 2 * kt + (k[None, :] >= 64), NEG, 0.0)
    c["acon4"] = acon.astype(NPBF)
    ov = np.zeros((128, 4, 128), np.float32)
    for kt in range(4):
        n = 128 * kt + p
        cs = n * 16
        ss = np.arange(128) * 64
        o = (cs[:, None] < ss[None, :] + 64) & (cs[:, None] + 32 > ss[None, :]) & (n[:, None] <= 510)
        ov[:, kt, :] = o
    c["ov"] = ov.astype(NPBF)
    q = p[:, None]
    kk = k[None, :]
    zero = np.zeros((128, 128), np.float32)
    allneg = np.full((128, 128), NEG, np.float32)
    caus = np.where(kk <= q, 0.0, NEG).astype(np.float32)
    winold = np.where(kk > q, 0.0, NEG).astype(np.float32)
    tail = np.zeros((128, 4, 128), np.float32)
    winm = np.zeros((128, 8, 128), np.float32)
    for par in range(2):
        higher = (par == 1) if half == 0 else (par == 0)
        if higher:
            tail[:, par * 2 + 0] = zero
            tail[:, par * 2 + 1] = caus
            w = [allneg, winold, zero, caus]
        else:
            tail[:, par * 2 + 0] = caus
            tail[:, par * 2 + 1] = allneg
            w = [winold, zero, caus, allneg]
        for x_ in range(4):
            winm[:, par * 4 + x_] = w[x_]
    c["tailm"] = tail.astype(NPBF)
    c["winm"] = winm.astype(NPBF)
    cmask = np.zeros((128, NSLOT, 128), np.float32)
    slotc = np.zeros((128, NSLOT, 256), np.float32)
    s_ = np.arange(128)[None, :]
    for i in range(NSLOT):
        qb, qbmax = slot_qb(i, half)
        t = 128 * qb + p[:, None]
        ktc = qbmax // 16
        n = 128 * ktc + k[None, :]
        cmask[:, i, :] = np.where(16 * n + 31 <= t, 0.0, NEG)
        cur = t // 64
        m1 = np.ones((128, 128), np.float32)
        m2 = np.zeros((128, 128), np.float32)
        f0 = (s_ == 0) & (s_ <= cur)
        m1[np.broadcast_to(f0, m1.shape)] = 0.0
        m2[np.broadcast_to(f0, m2.shape)] = BIGV + 2
        fp = (s_ == cur - 1)
        m1[fp] = 0.0
        m2[fp] = BIGV + 1
        fc = (s_ == cur)
        m1[fc] = 0.0
        m2[fc] = BIGV
        nc_ = s_ > cur
        m1[nc_] = 0.0
        m2[nc_] = (-1.0 - np.broadcast_to(s_, m2.shape))[nc_]
        slotc[:, i, 0:128] = m1
        slotc[:, i, 128:256] = m2
    c["cmask"] = cmask.astype(NPBF)
    c["slotc"] = slotc
    sel3 = np.zeros((3, 3, 128), np.float32)
    for b in range(3):
        sel3[b, b, :] = 1.0
    c["sel3"] = sel3
    return c


_PROG = {}
_IDENT = np.eye(128, dtype=np.float32)


def _run(nc, maps):
    res = run_bass_kernel_spmd(nc, maps, core_ids=list(range(NCORES)))
    return res.results


def _cat(res, name, axis):
    return np.concatenate([np.asarray(r[name]) for r in res], axis=axis)


def dense_maps(xs, inp, layer_done, oT_full, proj, layer_next):
    maps = []
    for c in range(NCORES):
        m = {"x": xs[c], "ident": _IDENT}
        if layer_done is not None:
            L = layer_done
            m["oT"] = np.ascontiguousarray(oT_full[:, c * TPC:(c + 1) * TPC])
            m["w_o"] = inp["nsa_w_out"][L] if L < 2 else inp["diff_w_out"][L - 2]
            m["g_mlp"] = gain_fm(inp["mlp_norm_g"][L])
            m["w_up"] = inp["mlp_w_up"][L]
            m["w_down"] = inp["mlp_w_down"][L]
        if proj is None:
            m["g_final"] = np.asarray(inp["final_norm_g"], np.float32)
        else:
            m["g_attn"] = gain_fm(inp["attn_norm_g"][layer_next])
            C, Sn, RT = rope_consts(128 if proj == "nsa" else 64, c * TPC, TPC)
            m["cosT"], m["sinT"], m["rotT"] = C, Sn, RT
            if proj == "nsa":
                m["w_in"] = inp["nsa_w_in"][layer_next]
            else:
                m["w_q"] = inp["diff_w_q"][layer_next - 2]
            if proj == "diffkv":
                m["g_kv"] = gain_fm(inp["kv_norm_g"])
                m["w_kv"] = inp["kv_w_shared"]
        maps.append(m)
    return maps


def nsa_attn_maps(res, inp, layer):
    qT = _cat(res, "qT", 1).reshape(16, 128, 64, 128)
    qrT = _cat(res, "qrT", 1).reshape(16, 128, 64, 128)
    kcT, vcT = _cat(res, "kcT", 1), _cat(res, "vcT", 1)
    ksT, kwT = _cat(res, "ksT", 1), _cat(res, "kwT", 1)
    vs, vw = _cat(res, "vs", 0), _cat(res, "vw", 0)
    gT = _cat(res, "gT", 1)
    maps = []
    for c in range(NCORES):
        hk, half = c // 2, c % 2
        qbs = [slot_qb(i, half)[0] for i in range(NSLOT)]
        m = dict(nsa_attn_consts(half))
        rs = slice(hk * 128, (hk + 1) * 128)
        m["kcT"] = np.ascontiguousarray(kcT[rs])
        m["vcT"] = np.ascontiguousarray(vcT[rs])
        m["ksT"] = np.ascontiguousarray(ksT[rs])
        m["kwT"] = np.ascontiguousarray(kwT[rs])
        m["vs"] = np.ascontiguousarray(vs[:, rs])
        m["vw"] = np.ascontiguousarray(vw[:, rs])
        m["qT"] = np.ascontiguousarray(qT[4 * hk:4 * hk + 4][:, :, qbs, :].transpose(1, 2, 0, 3)).reshape(128, NSLOT, 512)
        m["qrT"] = np.ascontiguousarray(qrT[4 * hk:4 * hk + 4][:, :, qbs, :].transpose(1, 2, 0, 3)).reshape(128, NSLOT, 512)
        gv = gT[hk * 12:(hk + 1) * 12].reshape(4, 3, 64, 128)[:, :, qbs, :]
        m["g3"] = np.ascontiguousarray(gv.transpose(1, 2, 0, 3)).reshape(3, NSLOT, 512)
        m["w1"] = inp["nsa_cmp_w1"][layer]
        m["w2"] = inp["nsa_cmp_w2"][layer]
        m["posT"] = np.ascontiguousarray(np.asarray(inp["nsa_cmp_pos"][layer]).transpose(0, 2, 1))
        maps.append(m)
    return maps


def nsa_attn_gather(res):
    oT = np.zeros((16, 128, 64, 128), NPBF)
    for c in range(NCORES):
        hk, half = c // 2, c % 2
        qbs = [slot_qb(i, half)[0] for i in range(NSLOT)]
        o = np.asarray(res[c]["oT"]).reshape(128, NSLOT, 4, 128).transpose(2, 0, 1, 3)
        oT[4 * hk:4 * hk + 4][:, :, qbs, :] = o
    return oT.reshape(2048, 8192)


def build_diff_attn():
    nc = bass.Bass("TRN2", target_bir_lowering=False)
    din = lambda name, shape, dt=F32: nc.dram_tensor(name, list(shape), dt, kind="ExternalInput")
    kT_d = din("kT", [128, S], BF16)
    v_d = din("v", [S, 128], BF16)
    qT_d = din("qT", [128, 64, 512], BF16)
    lamv_d = din("lamv", [128, 256])
    g_d = din("subg", [128, 1])
    linit_d = din("linit", [128, 2])
    i4_d = din("i4", [128, 512], BF16)
    ones_d = din("ones", [128, 128], BF16)
    onesf_d = din("onesf", [128, 128])
    caus_d = din("causT", [128, 128], BF16)
    oT_d = nc.dram_tensor("oT", [128, 64, 256], BF16, kind="ExternalOutput")
    scale = 64.0 ** -0.5
    with ExitStack() as st:
        sb = lambda name, shape, dt: st.enter_context(nc.sbuf_tensor(name, list(shape), dt))
        kT = sb("kT_sb", [128, S], BF16)
        v = sb("v_sb", [128, 64, 128], BF16)
        qsb = sb("q_sb", [128, 64, 512], BF16)
        lamv = sb("lamv_sb", [128, 256], F32)
        gcol = sb("gcol", [128, 1], F32)
        linit = sb("linit_sb", [128, 2], F32)
        i4 = sb("i4_sb", [128, 512], BF16)
        ones = sb("ones_sb", [128, 128], BF16)
        onesf = sb("onesf_sb", [128, 128], F32)
        caus = sb("caus_sb", [128, 128], BF16)
        lw = sb("lw", [128, 128], F32)
        ls = sb("ls", [128, 4], F32)
        nlam = sb("nlam", [128, 1], F32)
        gsc = sb("gsc", [128, 1], F32)
        NE = 3
        Eb = sb("Eb", [128, NE, 512], BF16)
        rden = sb("rden", [128, 512], F32)
        A = sb("A", [128, 512], F32)
        o = sb("o", [128, 256], F32)
        sq = sb("sq", [128, 256], F32)
        rs = sb("rs", [128, 256], F32)
        on = sb("on", [128, 256], F32)
        ost = sb("ost", [128, 2, 256], BF16)
        PS = [st.enter_context(nc.psum_tensor("ps%d" % i, [128, 512], F32)) for i in range(8)]
        Sb, Ob, Db, M0 = (0, 1, 2), (3, 4), (5, 6), 7
        P = Prog(nc)
        ld = lambda eng, dst, src, key, sem: P.op(eng, lambda e: e.dma_start(out=dst, in_=src), writes=[key], dma_sem=sem)
        ld("sp", kT[:], kT_d.ap(), "kT", "l_k")
        ld("sp", v[:], v_d.ap().rearrange("(t p) d -> p t d", p=128), "v", "l_v")
        ld("sp", qsb[:], qT_d.ap(), "q", "l_q")
        ld("act", lamv[:], lamv_d.ap(), "lamv", "l_c0")
        ld("act", gcol[:], g_d.ap(), "gcol", "l_c1")
        ld("act", linit[:], linit_d.ap(), "linit", "l_c2")
        ld("act", i4[:], i4_d.ap(), "i4", "l_c3")
        ld("act", ones[:], ones_d.ap(), "ones", "l_c4")
        ld("act", onesf[:], onesf_d.ap(), "onesf", "l_c5")
        ld("act", caus[:], caus_d.ap(), "caus", "l_c6")
        P.op("dve", lambda e: e.tensor_tensor(out=lw[:, 0:64], in0=lamv[:, 0:64], in1=lamv[:, 64:128], op=ALU.mult),
             reads=["lamv"], writes=["lw0"])
        P.op("dve", lambda e: e.tensor_tensor(out=lw[:, 64:128], in0=lamv[:, 128:192], in1=lamv[:, 192:256], op=ALU.mult),
             reads=["lamv"], writes=["lw1"])
        P.op("dve", lambda e: e.reduce_sum(out=ls[:, 0:1], in_=lw[:, 0:64], axis=AX.X), reads=["lw0"], writes=["ls0"])
        P.op("dve", lambda e: e.reduce_sum(out=ls[:, 1:2], in_=lw[:, 64:128], axis=AX.X), reads=["lw1"], writes=["ls1"])
        P.op("act", lambda e: e.activation(out=ls[:, 2:4], in_=ls[:, 0:2], func=AF.Exp), reads=["ls0", "ls1"], writes=["ls23"])
        P.op("dve", lambda e: e.tensor_tensor(out=nlam[:], in0=ls[:, 3:4], in1=ls[:, 2:3], op=ALU.subtract),
             reads=["ls23"], writes=["nlam"])
        P.op("dve", lambda e: e.tensor_tensor(out=nlam[:], in0=nlam[:], in1=linit[:, 0:1], op=ALU.subtract),
             reads=["nlam", "linit"], writes=["nlam"])
        P.op("dve", lambda e: e.tensor_tensor(out=gsc[:], in0=gcol[:], in1=linit[:, 1:2], op=ALU.mult),
             reads=["gcol", "linit"], writes=["gsc"])

        NQ = 64
        for qb in range(NQ):
            ob, db = P.rr("O", 2), P.rr("D", 2)
            n = qb + 1
            sl_ = {}

            def emit_s(kt, qb=qb):
                sbk = P.rr("S", 3)
                sl_[kt] = sbk
                diag = kt == qb

                def sfn(e, kt=kt, sbk=sbk, diag=diag, qb=qb):
                    ksl = slice(kt * 128, (kt + 1) * 128)
                    ins = e.matmul(PS[Sb[sbk]][:], kT[:, ksl], qsb[:, qb, :], start=True, stop=not diag)
                    if diag:
                        ins = e.matmul(PS[Sb[sbk]][:], caus[:], i4[:], start=False, stop=True)
                    return ins
                P.op("pe", sfn, reads=["kT", "q", "caus", "i4"], writes=[("S", sbk)])

            def emit_rest(kt, n=n, ob=ob, db=db):
                sbk = sl_[kt]
                ei = P.rr("E", NE)
                P.op("act", lambda e, sbk=sbk, ei=ei: e.activation(out=Eb[:, ei, :], in_=PS[Sb[sbk]][:], func=AF.Exp, scale=scale),
                     reads=[("S", sbk)], writes=[("E", ei)])

                def ofn(e, kt=kt, ei=ei, n=n, ob=ob, db=db):
                    e.matmul(PS[Ob[ob]][:], v[:, kt, :], Eb[:, ei, :], start=(kt == 0), stop=(kt == n - 1))
                    return e.matmul(PS[Db[db]][:], ones[:], Eb[:, ei, :], start=(kt == 0), stop=(kt == n - 1))
                P.op("pe", ofn, reads=["v", ("E", ei), "ones"], writes=[("O", ob), ("D", db)])

            emit_s(0)
            if n > 1:
                emit_s(1)
            for kt in range(n):
                if kt + 2 < n:
                    emit_s(kt + 2)
                emit_rest(kt)
            P.op("dve", lambda e, db=db: e.tensor_scalar(out=rden[:], in0=PS[Db[db]][:], scalar1=1e-30, scalar2=None, op0=ALU.max),
                 reads=[("D", db)], writes=["rden"])
            P.op("dve", lambda e: e.reciprocal(out=rden[:], in_=rden[:]), reads=["rden"], writes=["rden"])
            P.op("dve", lambda e, ob=ob: e.tensor_tensor(out=A[:], in0=rden[:], in1=PS[Ob[ob]][:], op=ALU.mult),
                 reads=["rden", ("O", ob)], writes=["A"])
            P.op("dve", lambda e: e.scalar_tensor_tensor(out=o[:], in0=A[:, 256:512], scalar=nlam[:, 0:1], in1=A[:, 0:256],
                                                         op0=ALU.mult, op1=ALU.add),
                 reads=["A", "nlam"], writes=["o"])
            P.op("pool", lambda e: e.tensor_tensor(out=sq[:], in0=o[:], in1=o[:], op=ALU.mult), reads=["o"], writes=["sq"])
            P.op("pe", lambda e: e.matmul(PS[M0][:, 0:256], onesf[:], sq[:], start=True, stop=True),
                 reads=["sq", "onesf"], writes=["M0"])
            P.op("dve", lambda e: e.tensor_scalar(out=rs[:], in0=PS[M0][:, 0:256], scalar1=1.0 / 128, scalar2=EPS,
                                                  op0=ALU.mult, op1=ALU.add), reads=["M0"], writes=["rs"])
            P.op("act", lambda e: e.activation(out=rs[:], in_=rs[:], func=AF.Ln), reads=["rs"], writes=["rs"])
            P.op("act", lambda e: e.activation(out=rs[:], in_=rs[:], func=AF.Exp, scale=-0.5), reads=["rs"], writes=["rs"])
            P.op("dve", lambda e: e.tensor_tensor(out=on[:], in0=o[:], in1=rs[:], op=ALU.mult), reads=["o", "rs"], writes=["on"])
            oi = P.rr("ost", 2)
            P.op("act", lambda e, oi=oi: e.activation(out=ost[:, oi, :], in_=on[:], func=AF.Identity, scale=gsc[:, 0:1]),
                 reads=["on", "gsc"], writes=[("ost", oi)])
            P.op("sp", lambda e, oi=oi, qb=qb: e.dma_start(out=oT_d.ap()[:, qb, :], in_=ost[:, oi, :]),
                 reads=[("ost", oi)], dma_sem=("st_o", oi))
        P.emit(st, final_waits=[("st_o", 0), ("st_o", 1)])
    return nc


def diff_attn_maps(dqT, dkT, dv, inp, layer):
    j = layer - 2
    li = 0.8 - 0.6 * math.exp(-0.3 * layer)
    q4 = dqT.reshape(16, 128, 64, 128)
    p = np.arange(128)
    consts = {
        "i4": np.tile(np.eye(128, dtype=np.float32), (1, 4)).astype(NPBF),
        "ones": np.ones((128, 128), np.float32).astype(NPBF),
        "onesf": np.ones((128, 128), np.float32),
        "causT": np.where(p[None, :] <= p[:, None], 0.0, NEG).astype(np.float32).astype(NPBF),
        "lamv": np.ascontiguousarray(np.broadcast_to(np.asarray(inp["diff_lambda"][j], np.float32).reshape(1, 256), (128, 256))),
        "subg": np.ascontiguousarray(np.asarray(inp["diff_subln_g"][j], np.float32).reshape(128, 1)),
        "linit": np.ascontiguousarray(np.broadcast_to(np.array([[li, 1.0 - li]], np.float32), (128, 2))),
    }
    maps = []
    for c in range(NCORES):
        hk = c // 2
        m = dict(consts)
        m["kT"] = np.ascontiguousarray(dkT[hk * 128:(hk + 1) * 128])
        m["v"] = np.ascontiguousarray(dv[:, hk * 128:(hk + 1) * 128])
        qq = np.ascontiguousarray(q4[2 * c:2 * c + 2].transpose(1, 2, 0, 3))
        qz = np.zeros((128, 64, 2, 2, 128), NPBF)
        qz[0:64, :, 0] = qq[0:64]
        qz[64:128, :, 1] = qq[64:128]
        m["qT"] = qz.reshape(128, 64, 512)
        maps.append(m)
    return maps


def diff_attn_gather(res):
    oT = np.zeros((16, 128, 64, 128), NPBF)
    for c in range(NCORES):
        o = np.asarray(res[c]["oT"]).reshape(128, 64, 2, 128).transpose(2, 0, 1, 3)
        oT[2 * c:2 * c + 2] = o
    return oT.reshape(2048, 8192)


def kernel(x, attn_norm_g, mlp_norm_g, final_norm_g, nsa_w_in, nsa_cmp_pos, nsa_cmp_w1, nsa_cmp_w2, nsa_w_out,
           kv_norm_g, kv_w_shared, diff_w_q, diff_lambda, diff_subln_g, diff_w_out, mlp_w_up, mlp_w_down, _debug=None):
    inp = dict(attn_norm_g=attn_norm_g, mlp_norm_g=mlp_norm_g, final_norm_g=final_norm_g, nsa_w_in=nsa_w_in,
               nsa_cmp_pos=nsa_cmp_pos, nsa_cmp_w1=nsa_cmp_w1, nsa_cmp_w2=nsa_cmp_w2, nsa_w_out=nsa_w_out,
               kv_norm_g=kv_norm_g, kv_w_shared=kv_w_shared, diff_w_q=diff_w_q, diff_lambda=diff_lambda,
               diff_subln_g=diff_subln_g, diff_w_out=diff_w_out, mlp_w_up=mlp_w_up, mlp_w_down=mlp_w_down)
    inp = {k: np.asarray(v, np.float32) for k, v in inp.items()}
    x2 = np.asarray(x, np.float32).reshape(S, D)
    xs = [np.ascontiguousarray(x2[c * TPC:(c + 1) * TPC]) for c in range(NCORES)]
    if "nsa" not in _PROG:
        _PROG["nsa"] = build_nsa_attn()
        _PROG["diff"] = build_diff_attn()
    dbg = {}
    res = _run(get_dense(False, False, False, "nsa"), dense_maps(xs, inp, None, None, "nsa", 0))
    dkT = dv = None
    for layer in range(4):
        if layer < 2:
            ra = _run(_PROG["nsa"], nsa_attn_maps(res, inp, layer))
            oT = nsa_attn_gather(ra)
        else:
            if layer == 2:
                dkT, dv = _cat(res, "dkT", 1), _cat(res, "dv", 0)
            ra = _run(_PROG["diff"], diff_attn_maps(_cat(res, "dqT", 1), dkT, dv, inp, layer))
            oT = diff_attn_gather(ra)
        if _debug is not None:
            dbg["oT%d" % layer] = oT
        nxt = [("nsa", 1), ("diffkv", 2), ("diff", 3), (None, None)][layer]
        res = _run(get_dense(True, True, nxt[0] is None, nxt[0]), dense_maps(xs, inp, layer, oT, nxt[0], nxt[1]))
        if nxt[0] is not None:
            xs = [np.asarray(r["x_out"]) for r in res]
            if _debug is not None:
                dbg["x%d" % layer] = np.concatenate(xs, 0)
    y = np.concatenate([np.asarray(r["y"]) for r in res], 0).reshape(1, S, D).astype(np.float32)
    if _debug is not None:
        _debug.update(dbg)
    return y
```
